# Optimizing a Trainium2 kernel written in Bass

```python
import math
import jax
import jax.numpy as jnp
from jax import lax
import numpy as np

D_MODEL = 1024
BATCH = 2
SEQ = 8192
DEPTH = 4

HEAD_DIM = 64
N_GROUPS = 4
GROUP_WIDTH = D_MODEL // N_GROUPS
H_FOX = GROUP_WIDTH // HEAD_DIM
H_DIFF = GROUP_WIDTH // HEAD_DIM
DIFF_QK_DIM = HEAD_DIM // 2
H_DSA = GROUP_WIDTH // HEAD_DIM
H_IDX = 4
D_IDX = 64
K_SEL_MAX = 256
CONV_CH = GROUP_WIDTH
CONV_WIDTH = 31
Q_BLOCK = 128
NUM_BUCKETS = 32
MAX_DISTANCE = 128
N_BIAS_HEADS = H_DIFF + H_DSA
N_EXPERTS = 32
TOP_K = 4
D_FF = D_MODEL
SWIGLU_LIMIT = 7.0
SWIGLU_ALPHA = 1.702
FORGET_BIAS_INIT = 4.0
DEEPNORM_ALPHA = (2 * DEPTH) ** 0.25
DEEPNORM_BETA = (8 * DEPTH) ** -0.25
LN_EPS = 1e-5

IN_WIDTHS = (GROUP_WIDTH, GROUP_WIDTH, GROUP_WIDTH, H_FOX,
             GROUP_WIDTH, GROUP_WIDTH, GROUP_WIDTH,
             GROUP_WIDTH, GROUP_WIDTH, GROUP_WIDTH, H_IDX * D_IDX, D_IDX, H_IDX,
             2 * CONV_CH)
N_IN = sum(IN_WIDTHS)
SPLIT_POINTS = tuple(sum(IN_WIDTHS[:i + 1]) for i in range(len(IN_WIDTHS) - 1))
V_SLOTS = (2, 6, 9)
F_SLOT = 3

kernel_name = "hybrid_fox_diff_dsa_conformer_moe_deepnorm"


def _to_blocks(a):
    b, l = a.shape[0], a.shape[1]
    a = a.reshape((b, l // Q_BLOCK, Q_BLOCK) + a.shape[2:])
    return jnp.swapaxes(a, 0, 1)


def _from_blocks(a):
    a = jnp.swapaxes(a, 0, 1)
    return a.reshape((a.shape[0], a.shape[1] * a.shape[2]) + a.shape[3:])


def _layernorm(x, g, b):
    xf = x.astype(jnp.float32)
    mu = jnp.mean(xf, axis=-1, keepdims=True)
    var = jnp.mean(jnp.square(xf - mu), axis=-1, keepdims=True)
    y = (xf - mu) * lax.rsqrt(var + LN_EPS)
    return (y * g.astype(jnp.float32) + b.astype(jnp.float32)).astype(x.dtype)


def _rel_bucket(dist):
    n = jnp.maximum(dist, 0)
    max_exact = NUM_BUCKETS // 2
    nf = jnp.maximum(n, 1).astype(jnp.float32)
    large = max_exact + (jnp.log(nf / max_exact) / math.log(MAX_DISTANCE / max_exact)
                         * (NUM_BUCKETS - max_exact)).astype(jnp.int32)
    large = jnp.minimum(large, NUM_BUCKETS - 1)
    return jnp.where(n < max_exact, n, large)


def forgetting_attention(q, k, v, log_f):
    L = q.shape[1]
    c = jnp.cumsum(log_f.astype(jnp.float32), axis=1)
    c_k = jnp.swapaxes(c, 1, 2)[:, :, None, :]
    k_pos = jnp.arange(L, dtype=jnp.int32)
    scale = HEAD_DIM ** -0.5

    def block(args):
        qb, cb, i = args
        q_pos = i * Q_BLOCK + jnp.arange(Q_BLOCK, dtype=jnp.int32)
        s = jnp.einsum("bqhd,bkhd->bhqk", qb, k).astype(jnp.float32) * scale
        s = s + jnp.swapaxes(cb, 1, 2)[..., None] - c_k
        s = jnp.where(k_pos[None, :] <= q_pos[:, None], s, -jnp.inf)
        p = jax.nn.softmax(s, axis=-1).astype(v.dtype)
        return jnp.einsum("bhqk,bkhd->bqhd", p, v)

    nb = L // Q_BLOCK
    out = lax.map(block, (_to_blocks(q), _to_blocks(c), jnp.arange(nb, dtype=jnp.int32)))
    return _from_blocks(out)


def differential_attention(q, k, v, lam, bias_table):
    L = q.shape[1]
    k_pos = jnp.arange(L, dtype=jnp.int32)
    scale = DIFF_QK_DIM ** -0.5

    def block(args):
        qb, i = args
        q_pos = i * Q_BLOCK + jnp.arange(Q_BLOCK, dtype=jnp.int32)
        s = jnp.einsum("bqhmd,bkhmd->bhmqk", qb, k).astype(jnp.float32) * scale
        bias = bias_table[_rel_bucket(q_pos[:, None] - k_pos[None, :])]
        s = s + jnp.transpose(bias, (2, 0, 1))[None, :, None].astype(jnp.float32)
        s = jnp.where(k_pos[None, :] <= q_pos[:, None], s, -jnp.inf)
        p = jax.nn.softmax(s, axis=-1)
        p = p[:, :, 0] - lam * p[:, :, 1]
        return jnp.einsum("bhqk,bkhd->bqhd", p.astype(v.dtype), v)

    nb = L // Q_BLOCK
    out = lax.map(block, (_to_blocks(q), jnp.arange(nb, dtype=jnp.int32)))
    return _from_blocks(out)


def indexed_sparse_attention(q, k, v, q_idx, k_idx, w_idx, bias_table, top_k):
    L = q.shape[1]
    k_pos = jnp.arange(L, dtype=jnp.int32)
    scale = HEAD_DIM ** -0.5

    def block(args):
        qb, qib, wb, i = args
        q_pos = i * Q_BLOCK + jnp.arange(Q_BLOCK, dtype=jnp.int32)
        sc = jnp.einsum("bqhd,bkd->bqhk", qib, k_idx).astype(jnp.float32) * (D_IDX ** -0.5)
        score = jnp.einsum("bqh,bqhk->bqk", wb.astype(jnp.float32), jax.nn.relu(sc)) * (H_IDX ** -0.5)
        score = jnp.where(k_pos[None, None, :] <= q_pos[None, :, None], score, -jnp.inf)
        _, idx = lax.top_k(score, top_k)
        valid = idx <= q_pos[None, :, None]
        kg = jax.vmap(lambda a, j: a[j])(k, idx)
        vg = jax.vmap(lambda a, j: a[j])(v, idx)
        s = jnp.einsum("bqhd,bqkhd->bhqk", qb, kg).astype(jnp.float32) * scale
        bias = bias_table[_rel_bucket(q_pos[None, :, None] - idx)]
        s = s + jnp.transpose(bias, (0, 3, 1, 2)).astype(jnp.float32)
        s = jnp.where(valid[:, None], s, -jnp.inf)
        p = jax.nn.softmax(s, axis=-1).astype(v.dtype)
        return jnp.einsum("bhqk,bqkhd->bqhd", p, vg)

    nb = L // Q_BLOCK
    out = lax.map(block, (_to_blocks(q), _to_blocks(q_idx), _to_blocks(w_idx),
                          jnp.arange(nb, dtype=jnp.int32)))
    return _from_blocks(out)


def conformer_conv(u, conv_w, conv_b, ln_g, ln_b):
    a, g = jnp.split(u, 2, axis=-1)
    h = a * jax.nn.sigmoid(g)
    h = lax.conv_general_dilated(h, conv_w[:, None, :], window_strides=(1,),
                                 padding=((CONV_WIDTH - 1, 0),),
                                 dimension_numbers=("NWC", "WIO", "NWC"),
                                 feature_group_count=CONV_CH) + conv_b
    h = _layernorm(h, ln_g, ln_b)
    return h * jax.nn.sigmoid(h)


def token_mixers(x, rel_bias, w_in, forget_b, diff_lambda, diff_norm_g, conv_w, conv_b,
                 conv_ln_g, conv_ln_b, w_out, lambda_init):
    B, L, _ = x.shape
    proj = x @ w_in
    (fq, fk, fv, ff, dq, dk, dv, sq, sk, sv, iq, ik, iw, cu) = jnp.split(proj, SPLIT_POINTS, axis=-1)

    log_f = jax.nn.log_sigmoid((ff + forget_b).astype(jnp.float32))
    y_fox = forgetting_attention(fq.reshape(B, L, H_FOX, HEAD_DIM), fk.reshape(B, L, H_FOX, HEAD_DIM),
                                 fv.reshape(B, L, H_FOX, HEAD_DIM), log_f)

    lp = diff_lambda.astype(jnp.float32)
    lam = jnp.exp(jnp.sum(lp[0] * lp[1])) - jnp.exp(jnp.sum(lp[2] * lp[3])) + lambda_init
    y_diff = differential_attention(dq.reshape(B, L, H_DIFF, 2, DIFF_QK_DIM),
                                    dk.reshape(B, L, H_DIFF, 2, DIFF_QK_DIM),
                                    dv.reshape(B, L, H_DIFF, HEAD_DIM), lam, rel_bias[:, :H_DIFF])
    yf = y_diff.astype(jnp.float32)
    yf = yf * lax.rsqrt(jnp.mean(jnp.square(yf), axis=-1, keepdims=True) + LN_EPS)
    y_diff = (yf * diff_norm_g.astype(jnp.float32) * (1.0 - lambda_init)).astype(x.dtype)

    top_k = min(K_SEL_MAX, L // 4)
    y_dsa = indexed_sparse_attention(sq.reshape(B, L, H_DSA, HEAD_DIM), sk.reshape(B, L, H_DSA, HEAD_DIM),
                                     sv.reshape(B, L, H_DSA, HEAD_DIM), iq.reshape(B, L, H_IDX, D_IDX),
                                     ik, iw, rel_bias[:, H_DIFF:], top_k)

    y_conv = conformer_conv(cu, conv_w, conv_b, conv_ln_g, conv_ln_b)

    y = jnp.concatenate([y_fox.reshape(B, L, GROUP_WIDTH), y_diff.reshape(B, L, GROUP_WIDTH),
                         y_dsa.reshape(B, L, GROUP_WIDTH), y_conv], axis=-1)
    return y @ w_out


def moe_ffn(x, router_w, router_b, w_gu, b_gu, w_down, b_down):
    B, L, D = x.shape
    t = x.reshape(B * L, D)
    logits = (t @ router_w + router_b).astype(jnp.float32)
    top_val, top_idx = lax.top_k(logits, TOP_K)
    wts = jax.nn.softmax(top_val, axis=-1)
    gates = jnp.sum(jax.nn.one_hot(top_idx, N_EXPERTS, dtype=jnp.float32) * wts[..., None], axis=1)
    out = jnp.zeros((B * L, D), jnp.float32)
    for e in range(N_EXPERTS):
        gu = t @ w_gu[e] + b_gu[e]
        gate = jnp.minimum(gu[:, ::2], SWIGLU_LIMIT)
        up = jnp.clip(gu[:, 1::2], -SWIGLU_LIMIT, SWIGLU_LIMIT)
        h = (up + 1.0) * gate * jax.nn.sigmoid(gate * SWIGLU_ALPHA)
        out = out + gates[:, e:e + 1] * (h @ w_down[e] + b_down[e])
    return out.astype(x.dtype).reshape(B, L, D)


def setup_inputs(seed: int = 0) -> dict:
    key = jax.random.key(seed)
    ks = jax.random.split(key, 22)

    def nrm(k, shape, scale):
        return jax.random.normal(k, shape, jnp.float32) * scale

    slot_scale = [1.0] * len(IN_WIDTHS)
    for s in V_SLOTS:
        slot_scale[s] = DEEPNORM_BETA
    slot_scale[F_SLOT] = 0.1
    col_scale = jnp.asarray(np.concatenate(
        [np.full((w,), sc, np.float32) for w, sc in zip(IN_WIDTHS, slot_scale)]))

    return {
        "x": nrm(ks[0], (BATCH, SEQ, D_MODEL), 1.0),
        "rel_bias": nrm(ks[1], (NUM_BUCKETS, N_BIAS_HEADS), 0.2),
        "w_in": nrm(ks[2], (DEPTH, D_MODEL, N_IN), D_MODEL ** -0.5) * col_scale,
        "forget_b": FORGET_BIAS_INIT + nrm(ks[3], (DEPTH, H_FOX), 0.1),
        "diff_lambda": nrm(ks[4], (DEPTH, 4, DIFF_QK_DIM), 0.1),
        "diff_norm_g": 1.0 + nrm(ks[5], (DEPTH, HEAD_DIM), 0.01),
        "conv_w": nrm(ks[6], (DEPTH, CONV_WIDTH, CONV_CH), CONV_WIDTH ** -0.5),
        "conv_b": nrm(ks[7], (DEPTH, CONV_CH), 0.01),
        "conv_ln_g": 1.0 + nrm(ks[8], (DEPTH, CONV_CH), 0.01),
        "conv_ln_b": nrm(ks[9], (DEPTH, CONV_CH), 0.01),
        "w_out": nrm(ks[10], (DEPTH, D_MODEL, D_MODEL), D_MODEL ** -0.5 * DEEPNORM_BETA),
        "ln1_g": 1.0 + nrm(ks[11], (DEPTH, D_MODEL), 0.01),
        "ln1_b": nrm(ks[12], (DEPTH, D_MODEL), 0.01),
        "router_w": nrm(ks[13], (DEPTH, D_MODEL, N_EXPERTS), D_MODEL ** -0.5),
        "router_b": nrm(ks[14], (DEPTH, N_EXPERTS), 0.01),
        "w_gu": nrm(ks[15], (DEPTH, N_EXPERTS, D_MODEL, 2 * D_FF), D_MODEL ** -0.5 * DEEPNORM_BETA),
        "b_gu": nrm(ks[16], (DEPTH, N_EXPERTS, 2 * D_FF), 0.01),
        "w_down": nrm(ks[17], (DEPTH, N_EXPERTS, D_FF, D_MODEL), D_FF ** -0.5 * DEEPNORM_BETA),
        "b_down": nrm(ks[18], (DEPTH, N_EXPERTS, D_MODEL), 0.01),
        "ln2_g": 1.0 + nrm(ks[19], (DEPTH, D_MODEL), 0.01),
        "ln2_b": nrm(ks[20], (DEPTH, D_MODEL), 0.01),
    }


def reference(x, rel_bias, w_in, forget_b, diff_lambda, diff_norm_g, conv_w, conv_b, conv_ln_g,
              conv_ln_b, w_out, ln1_g, ln1_b, router_w, router_b, w_gu, b_gu, w_down, b_down,
              ln2_g, ln2_b):
    for l in range(DEPTH):
        lambda_init = 0.8 - 0.6 * math.exp(-0.3 * l)
        mix = token_mixers(x, rel_bias, w_in[l], forget_b[l], diff_lambda[l], diff_norm_g[l],
                           conv_w[l], conv_b[l], conv_ln_g[l], conv_ln_b[l], w_out[l], lambda_init)
        x = _layernorm(DEEPNORM_ALPHA * x + mix, ln1_g[l], ln1_b[l])
        ffn = moe_ffn(x, router_w[l], router_b[l], w_gu[l], b_gu[l], w_down[l], b_down[l])
        x = _layernorm(DEEPNORM_ALPHA * x + ffn, ln2_g[l], ln2_b[l])
    return x
```

```python
import math
from contextlib import ExitStack

import numpy as np
import concourse.bass as bass
import concourse.mybir as mybir
from concourse.bass_utils import run_bass_kernel_spmd

F32 = mybir.dt.float32
BF16 = mybir.dt.bfloat16
ALU = mybir.AluOpType
AF = mybir.ActivationFunctionType

D = 1024
NIN = 3144
DEPTH = 4
NE = 32
ALPHA = (2 * DEPTH) ** 0.25
EPS = 1e-5
NEG = -30000.0
REPL = -1.0e30
C_FQ, C_FK, C_FV, C_FF = 0, 256, 512, 768
C_DQ, C_DK, C_DV = 772, 1028, 1284
C_SQ, C_SK, C_SV = 1540, 1796, 2052
C_IQ, C_IK, C_IW, C_CU = 2308, 2564, 2628, 2632
KT_ROWS = 832
NQCH = 10


class Buf:
    __slots__ = ("w", "r")

    def __init__(self):
        self.w = None
        self.r = {}


class Prog:
    NDS = 24

    def __init__(self, nc):
        self.nc = nc
        self.st = ExitStack()
        self.engs = {"pe": nc.tensor, "act": nc.scalar, "dve": nc.vector, "pool": nc.gpsimd, "sp": nc.sync}
        self.semobj = {}
        self.cnt = {}
        for k in self.engs:
            self.semobj[k] = self.st.enter_context(nc.semaphore("s_" + k))
            self.cnt[k] = 0
        self.seen = {k: {} for k in self.engs}
        self.dcnt = [0] * self.NDS
        for i in range(self.NDS):
            self.semobj[("d", i)] = self.st.enter_context(nc.semaphore("s_d%d" % i))
        self.dnext = 0
        self.nins = 0

    def sb(self, name, shape, dt, st=None):
        self.uid = getattr(self, "uid", 0) + 1
        return (st or self.st).enter_context(self.nc.sbuf_tensor("%s_%d" % (name, self.uid), shape, dt))

    def ps(self, name, shape, dt, st=None):
        return (st or self.st).enter_context(self.nc.psum_tensor(name, shape, dt))

    def _wait(self, eng, key, val):
        if self.seen[eng].get(key, 0) >= val:
            return
        self.engs[eng].wait_ge(self.semobj[key], val)
        self.seen[eng][key] = val
        self.nins += 1

    def _deps(self, eng, reads, writes):
        need = {}

        def add(k, v, kind):
            if k == eng:
                if eng == "pe" or kind == "war":
                    return
            if need.get(k, 0) < v:
                need[k] = v
        for b in reads:
            if b.w is not None:
                add(b.w[0], b.w[1], "raw")
        for b in writes:
            if b.w is not None:
                add(b.w[0], b.w[1], "waw")
            for k, v in b.r.items():
                add(k, v, "war")
        for k, v in need.items():
            self._wait(eng, k, v)

    def _mark(self, key, val, reads, writes):
        for b in reads:
            b.r[key] = val
        for b in writes:
            b.w = (key, val)
            b.r = {}

    def op(self, eng, fn, reads=(), writes=()):
        self._deps(eng, reads, writes)
        ins = fn(self.engs[eng])
        self.cnt[eng] += 1
        ins.then_inc(self.semobj[eng], 1)
        self.nins += 1
        self._mark(eng, self.cnt[eng], reads, writes)

    def dma(self, eng, out, in_, reads=(), writes=()):
        i = self.dnext
        self.dnext = (i + 1) % self.NDS
        key = ("d", i)
        if self.dcnt[i] > 0:
            self._wait(eng, key, self.dcnt[i])
        self._deps(eng, reads, writes)
        ins = self.engs[eng].dma_start(out=out, in_=in_)
        self.dcnt[i] += 16
        ins.then_inc(self.semobj[key], 16)
        self.nins += 1
        self._mark(key, self.dcnt[i], reads, writes)

    def collective(self, src, dst, groups):
        if "cc" not in self.semobj:
            self.semobj["cc"] = self.st.enter_context(self.nc.semaphore("s_cc"))
            self.cccnt = 0
        self.barrier(["pool"])
        ins = self.nc.gpsimd.collective_compute("AllGather", ALU.bypass, replica_groups=groups, ins=[src.opt()], outs=[dst.opt()])
        self.cccnt += 1
        ins.then_inc(self.semobj["cc"])
        self.nins += 1
        self._wait("pool", "cc", self.cccnt)

    def barrier(self, engs=None):
        engs = engs or list(self.engs)
        for e in engs:
            for k in self.engs:
                if k != e and self.cnt[k] > 0:
                    self._wait(e, k, self.cnt[k])
            for i in range(self.NDS):
                if self.dcnt[i] > 0:
                    self._wait(e, ("d", i), self.dcnt[i])
            if getattr(self, "cccnt", 0) > 0:
                self._wait(e, "cc", self.cccnt)

    def mm(self, out, lhsT, rhs, start, stop, reads, writes):
        self.op("pe", lambda e: e.matmul(out, lhsT=lhsT, rhs=rhs, start=start, stop=stop), reads, writes)

    def tr(self, out, in_, ident, reads, writes):
        self.op("pe", lambda e: e.transpose(out, in_, ident), reads, writes)

    def act(self, out, in_, func, reads, writes, bias=None, scale=1.0, accum=None):
        kw = {}
        if bias is not None:
            kw["bias"] = bias
        if accum is not None:
            kw["accum_out"] = accum
        self.op("act", lambda e: e.activation(out=out, in_=in_, func=func, scale=scale, **kw), reads, writes)

    def ts(self, eng, out, in0, s1, s2, op0, op1, reads, writes, accum=None):
        kw = {}
        if accum is not None:
            kw["accum_out"] = accum
        if op1 is None:
            self.op(eng, lambda e: e.tensor_scalar(out=out, in0=in0, scalar1=s1, scalar2=None, op0=op0, **kw), reads, writes)
        else:
            self.op(eng, lambda e: e.tensor_scalar(out=out, in0=in0, scalar1=s1, scalar2=s2, op0=op0, op1=op1, **kw), reads, writes)

    def tt(self, eng, out, in0, in1, op, reads, writes):
        self.op(eng, lambda e: e.tensor_tensor(out=out, in0=in0, in1=in1, op=op), reads, writes)

    def stt(self, eng, out, in0, scalar, in1, op0, op1, reads, writes):
        self.op(eng, lambda e: e.scalar_tensor_tensor(out=out, in0=in0, scalar=scalar, in1=in1, op0=op0, op1=op1), reads, writes)

    def copy(self, eng, out, in_, reads, writes):
        if eng == "act":
            self.op("act", lambda e: e.copy(out=out, in_=in_), reads, writes)
        else:
            self.op(eng, lambda e: e.tensor_copy(out=out, in_=in_), reads, writes)

    def memset(self, eng, ap, val, writes):
        self.op(eng, lambda e: e.memset(ap, val), (), writes)

    def finish(self):
        self.barrier(["sp"])
        self.st.close()


def _bufs(n):
    return [Buf() for _ in range(n)]


class Ctx:
    pass


class Gat:
    def __init__(self, R, mono=None, S=None, chunks=None):
        self.R, self.mono, self.S, self.chunks = R, mono, S, chunks

    def rows(self, c, r0, n):
        if self.mono is not None:
            return [(self.mono[c * self.R + r0:c * self.R + r0 + n, :], 0, n)]
        out = []
        r = r0
        while r < r0 + n:
            k, o = r // self.S, r % self.S
            m = min(self.S - o, r0 + n - r)
            out.append((self.chunks[k][c * self.S + o:c * self.S + o + m, :], r - r0, m))
            r += m
        return out


def setup_common(p, J, consts):
    nc = p.nc
    g = Ctx()
    g.J = J
    g.T = J * 128
    g.banks = [p.ps("bank%d" % i, [128, 512], F32) for i in range(8)]
    g.bb = _bufs(8)
    g.identf = p.sb("identf", [128, 128], F32)
    g.identb = p.sb("identb", [128, 128], BF16)
    g.b_ident = Buf()
    p.dma("sp", g.identf[:], consts["identf"], (), [g.b_ident])
    p.copy("dve", g.identb[:], g.identf[:], [g.b_ident], [g.b_ident])
    return g


def transpose_block(p, g, src, src_b, dst_bf, dst_b, blk, dst_f32=None, dst_f32_b=None, bank0=0):
    for half in range(2):
        bk = bank0 + half
        for q in range(4):
            dc = half * 4 + q
            p.tr(g.banks[bk][:, q * 128:(q + 1) * 128], src[:, dc * 128:(dc + 1) * 128], g.identf[:],
                 [src_b, g.b_ident], [g.bb[bk]])
        outv = dst_bf[:, half * 4:(half + 1) * 4, blk * 128:(blk + 1) * 128]
        inv = g.banks[bk][:].rearrange("p (q t) -> p q t", q=4)
        if dst_f32 is None:
            p.copy("act" if half == 0 else "dve", outv, inv, [g.bb[bk]], [dst_b])
        else:
            f32v = dst_f32[:, half * 4:(half + 1) * 4, :]
            p.copy("dve", f32v, inv, [g.bb[bk]], [dst_f32_b])
            p.copy("act", outv, f32v, [dst_f32_b], [dst_b])


def emit_A(p, g, X, w_in, forget_b, consts, KT_loc, V_loc, HT_loc, LF_loc, QT_loc, IW_loc):
    J, T = g.J, g.T
    st = ExitStack()
    xT = p.sb("a_xT", [128, 8, T], BF16, st)
    b_xT = Buf()
    win = p.sb("a_win", [128, 8, NIN], BF16, st)
    b_win = _bufs(8)
    xt = [p.sb("a_xt%d" % i, [128, D], F32, st) for i in range(2)]
    b_xt = _bufs(2)
    fb = p.sb("a_fb", [128, 4], F32, st)
    b_fb = Buf()
    dqm = p.sb("a_dqm", [128, 2], F32, st)
    b_dqm = Buf()
    p.dma("sp", fb[:], forget_b.partition_broadcast(128), (), [b_fb])
    p.dma("sp", dqm[:], consts["dqmask"], (), [b_dqm])
    w_r = w_in.rearrange("(dc q) f -> q dc f", q=128)
    for dc in range(8):
        p.dma("pool", win[:, dc, :], w_r[:, dc, :], (), [b_win[dc]])
    Xr = X.rearrange("(j q) d -> q j d", q=128)
    for j in range(J):
        s = j % 2
        p.dma("sp", xt[s][:], Xr[:, j, :], (), [b_xt[s]])
        transpose_block(p, g, xt[s], b_xt[s], xT, b_xT, j)

    ev = p.sb("a_ev", [128, 4, 512], BF16, st)
    b_ev = _bufs(4)
    evf = p.sb("a_evf", [128, 2, 512], F32, st)
    b_evf = _bufs(2)
    sg = p.sb("a_sg", [128, 2, 512], F32, st)
    b_sg = _bufs(2)
    ntg = T // 512 if T >= 512 else 1
    TG = min(512, T)
    state = {"bk": 2, "ev": 0, "evf": 0}

    def fm_chunk(col0, ncols, tg):
        bk = state["bk"]
        state["bk"] = 2 + (bk - 2 + 1) % 4
        for dc in range(8):
            p.mm(g.banks[bk][0:ncols, 0:TG], win[:, dc, col0:col0 + ncols], xT[:, dc, tg * TG:(tg + 1) * TG],
                 dc == 0, dc == 7, [b_win[dc], b_xT], [g.bb[bk]])
        return bk

    def ev_slot():
        s = state["ev"]
        state["ev"] = (s + 1) % 4
        return s

    for tg in range(ntg):
        tsl = slice(tg * TG, (tg + 1) * TG)
        qspec = [(0, C_FQ, None), (1, C_FQ + 128, None), (2, C_DQ, 0), (3, C_DQ + 128, 0), (4, C_DQ, 1), (5, C_DQ + 128, 1),
                 (6, C_SQ, None), (7, C_SQ + 128, None), (8, C_IQ, None), (9, C_IQ + 128, None)]
        cache = {}
        for qi, col0, m in qspec:
            if m == 1 and col0 in cache:
                bk = cache[col0]
            else:
                bk = fm_chunk(col0, 128, tg)
                cache[col0] = bk
            s = ev_slot()
            if m is None:
                p.copy("act", ev[:, s, 0:TG], g.banks[bk][:, 0:TG], [g.bb[bk]], [b_ev[s]])
            else:
                p.ts("dve", ev[:, s, 0:TG], g.banks[bk][:, 0:TG], dqm[:, m:m + 1], None, ALU.mult, None,
                     [g.bb[bk], b_dqm], [b_ev[s]])
            p.dma("sp", QT_loc[qi, :, tsl], ev[:, s, 0:TG], [b_ev[s]], ())
        for ki, col0, ncols in [(0, C_FK, 128), (1, C_FK + 128, 128), (2, C_DK, 128), (3, C_DK + 128, 128),
                                (4, C_SK, 128), (5, C_SK + 128, 128), (6, C_IK, 64)]:
            bk = fm_chunk(col0, ncols, tg)
            s = ev_slot()
            p.copy("act" if ki % 2 == 0 else "dve", ev[0:ncols, s, 0:TG], g.banks[bk][0:ncols, 0:TG], [g.bb[bk]], [b_ev[s]])
            p.dma("sp", KT_loc[ki * 128:ki * 128 + ncols, tsl], ev[0:ncols, s, 0:TG], [b_ev[s]], ())
        for cc in range(2):
            bka = fm_chunk(C_CU + cc * 128, 128, tg)
            bkg = fm_chunk(C_CU + 256 + cc * 128, 128, tg)
            s = state["evf"]
            state["evf"] = 1 - s
            p.act(sg[:, s, 0:TG], g.banks[bkg][:, 0:TG], AF.Sigmoid, [g.bb[bkg]], [b_sg[s]])
            p.tt("dve", evf[:, s, 0:TG], g.banks[bka][:, 0:TG], sg[:, s, 0:TG], ALU.mult, [g.bb[bka], b_sg[s]], [b_evf[s]])
            p.dma("sp", HT_loc[cc * 128:(cc + 1) * 128, tsl], evf[:, s, 0:TG], [b_evf[s]], ())

    vt = p.sb("a_vt", [128, 2, 768], BF16, st)
    b_vt = _bufs(2)
    lf = p.sb("a_lf", [128, J, 4], F32, st)
    b_lf = Buf()
    iw = p.sb("a_iw", [128, J, 4], F32, st)
    b_iw = Buf()
    for j in range(J):
        s = j % 2
        lt = slice(j * 128, (j + 1) * 128)
        for gi, col0 in enumerate([C_FV, C_DV]):
            for dc in range(8):
                p.mm(g.banks[0][:, gi * 256:(gi + 1) * 256], xT[:, dc, lt], win[:, dc, col0:col0 + 256], dc == 0, dc == 7,
                     [b_win[dc], b_xT], [g.bb[0]])
        for dc in range(8):
            p.mm(g.banks[1][:, 0:256], xT[:, dc, lt], win[:, dc, C_SV:C_SV + 256], dc == 0, dc == 7, [b_win[dc], b_xT], [g.bb[1]])
        for dc in range(8):
            p.mm(g.banks[1][:, 256:260], xT[:, dc, lt], win[:, dc, C_FF:C_FF + 4], dc == 0, dc == 7, [b_win[dc], b_xT], [g.bb[1]])
        for dc in range(8):
            p.mm(g.banks[1][:, 260:264], xT[:, dc, lt], win[:, dc, C_IW:C_IW + 4], dc == 0, dc == 7, [b_win[dc], b_xT], [g.bb[1]])
        p.copy("act", vt[:, s, 0:512], g.banks[0][:, 0:512], [g.bb[0]], [b_vt[s]])
        p.copy("dve", vt[:, s, 512:768], g.banks[1][:, 0:256], [g.bb[1]], [b_vt[s]])
        p.tt("dve", lf[:, j, :], g.banks[1][:, 256:260], fb[:], ALU.add, [g.bb[1], b_fb], [b_lf])
        p.copy("dve", iw[:, j, :], g.banks[1][:, 260:264], [g.bb[1]], [b_iw])
        p.dma("sp", V_loc[lt, :], vt[:, s, :], [b_vt[s]], ())
    lfv = lf[:].rearrange("p j h -> p (j h)")
    p.act(lfv, lfv, AF.Exp, [b_lf], [b_lf], scale=-1.0)
    p.act(lfv, lfv, AF.Ln, [b_lf], [b_lf], bias=1.0)
    p.ts("dve", lfv, lfv, -1.0, None, ALU.mult, None, [b_lf], [b_lf])
    p.tr(g.banks[2][0:J * 4, 0:128], lfv, g.identf[:], [b_lf, g.b_ident], [g.bb[2]])
    lft = p.sb("a_lft", [J * 4, 128], F32, st)
    b_lft = Buf()
    p.copy("dve", lft[:], g.banks[2][0:J * 4, 0:128], [g.bb[2]], [b_lft])
    p.dma("sp", LF_loc, lft[:], [b_lft], ())
    p.dma("sp", IW_loc, iw[:].rearrange("p j h -> p (j h)"), [b_iw], ())
    p.barrier()
    st.close()


def ln_rows(p, g, z, b_z, out, b_out, gt, bt, b_gb, tmp, b_tmp, st4, b_st4):
    p.memset("dve", st4[:, 0:2], 0.0, [b_st4])
    p.ts("dve", tmp, z, 1.0, 0.0, ALU.mult, ALU.add, [b_z], [b_tmp, b_st4], accum=st4[:, 0:1])
    p.tt("pool", tmp, z, z, ALU.mult, [b_z], [b_tmp])
    p.ts("dve", tmp, tmp, 1.0, 0.0, ALU.mult, ALU.add, [b_tmp], [b_tmp, b_st4], accum=st4[:, 1:2])
    p.ts("dve", st4[:, 0:1], st4[:, 0:1], 1.0 / D, None, ALU.mult, None, [b_st4], [b_st4])
    p.tt("dve", st4[:, 2:3], st4[:, 0:1], st4[:, 0:1], ALU.mult, [b_st4], [b_st4])
    p.stt("dve", st4[:, 1:2], st4[:, 1:2], 1.0 / D, st4[:, 2:3], ALU.mult, ALU.subtract, [b_st4], [b_st4])
    p.act(st4[:, 1:2], st4[:, 1:2], AF.Ln, [b_st4], [b_st4], bias=EPS)
    p.act(st4[:, 1:2], st4[:, 1:2], AF.Exp, [b_st4], [b_st4], scale=-0.5)
    p.stt("dve", st4[:, 2:3], st4[:, 0:1], -1.0, st4[:, 1:2], ALU.mult, ALU.mult, [b_st4], [b_st4])
    p.act(tmp, z, AF.Identity, [b_z, b_st4], [b_tmp], bias=st4[:, 2:3], scale=st4[:, 1:2])
    p.tt("dve", tmp, tmp, gt, ALU.mult, [b_tmp, b_gb], [b_tmp])
    p.tt("pool", out, tmp, bt, ALU.add, [b_tmp, b_gb], [b_out])


NOMASK = False


def emit_B(p, g, l, X, KT_all, V_all, HT_all, LF_all, QT_loc, IW_loc, HT_loc, NEGT, W, consts, x1, b_x1, lamc=None, dbg=None):
    J, T = g.J, g.T
    NB = 4 * J
    L = NB * 128
    lambda_init = 0.8 - 0.6 * math.exp(-0.3 * l)
    stB = ExitStack()
    yT = p.sb("b_yT", [128, 8, T], BF16, stB)
    b_yT = Buf()
    sel = p.sb("b_sel", [128, 4], F32, stB)
    b_c = Buf()
    p.dma("sp", sel[:], consts["sel"], (), [b_c])
    maskT = p.sb("b_maskT", [128, 4, 128], F32, stB)
    p.dma("sp", maskT[:], consts["maskT"].rearrange("c k q -> k c q"), (), [b_c])
    b31 = p.sb("b_b31", [128, 8], F32, stB)
    p.dma("sp", b31[:], W["rel31"].partition_broadcast(128), (), [b_c])
    ones = p.sb("b_ones", [128, 128], F32, stB)
    p.memset("pool", ones[:], 1.0, [b_c])

    st = ExitStack()
    hc = p.sb("c_hc", [128, 2, J, 158], F32, st)
    b_hc = Buf()
    cand = p.sb("c_cand", [128, 4, 2, J, 30], F32, st)
    b_cand = Buf()
    cw = p.sb("c_cw", [128, 2, 31], F32, st)
    cpar = p.sb("c_par", [128, 2, 3], F32, st)
    b_cw = Buf()
    cwr = p.sb("c_cwr", [31, 256], F32, st)
    p.dma("sp", cwr[:], W["conv_w"], (), [b_cw])
    for cc in range(2):
        p.tr(g.banks[0][:, cc * 32:cc * 32 + 31], cwr[:, cc * 128:(cc + 1) * 128], g.identf[0:31, 0:31], [b_cw, g.b_ident], [g.bb[0]])
    p.copy("dve", cw[:], g.banks[0][:, 0:64].rearrange("c (cc w) -> c cc w", cc=2)[:, :, 0:31], [g.bb[0]], [b_cw])
    for i, nm in enumerate(["conv_b", "conv_ln_g", "conv_ln_b"]):
        for cc in range(2):
            p.dma("sp", cpar[:, cc, i:i + 1], W[nm].rearrange("(cc c o) -> cc c o", c=128, o=1)[cc], (), [b_cw])
    p.memset("pool", cand[:], 0.0, [b_cand])
    for cc in range(2):
        p.dma("sp", hc[:, cc, :, 30:158], HT_loc[cc * 128:(cc + 1) * 128, :].rearrange("c (j t) -> c j t", t=128), (), [b_hc])
        for c in range(4):
            if c >= 1:
                for ap_, off, n in HT_all.rows(c - 1, cc * 128, 128):
                    p.dma("sp", cand[off:off + n, c, cc, :, :], ap_.rearrange("c (j t) -> c j t", t=128)[:, :, 98:128], (), [b_cand])
            elif J > 1:
                for ap_, off, n in HT_all.rows(3, cc * 128, 128):
                    p.dma("sp", cand[off:off + n, c, cc, 1:J, :], ap_.rearrange("c (j t) -> c j t", t=128)[:, 0:J - 1, 98:128], (), [b_cand])
    for cc in range(2):
        e = "dve"
        p.ts(e, hc[:, cc, :, 0:30], cand[:, 0, cc, :, :], sel[:, 0:1], None, ALU.mult, None, [b_cand, b_c], [b_hc])
        for c in range(1, 4):
            p.stt(e, hc[:, cc, :, 0:30], cand[:, c, cc, :, :], sel[:, c:c + 1], hc[:, cc, :, 0:30], ALU.mult, ALU.add,
                  [b_cand, b_c, b_hc], [b_hc])
    cacc = p.sb("c_acc", [128, 2, J, 128], F32, st)
    b_cacc = _bufs(2)
    for cc in range(2):
        e = "dve"
        p.ts(e, cacc[:, cc], hc[:, cc, :, 0:128], cw[:, cc, 0:1], cpar[:, cc, 0:1], ALU.mult, ALU.add, [b_hc, b_cw], [b_cacc[cc]])
        for w in range(1, 31):
            p.stt(e, cacc[:, cc], hc[:, cc, :, w:w + 128], cw[:, cc, w:w + 1], cacc[:, cc], ALU.mult, ALU.add,
                  [b_hc, b_cw, b_cacc[cc]], [b_cacc[cc]])
    TG = min(512, T)
    csq = p.sb("c_sq", [128, 2, TG], F32, st)
    b_csq = Buf()
    cmean = p.sb("c_mean", [128, TG], F32, st)
    crstd = p.sb("c_rstd", [128, TG], F32, st)
    b_cst = Buf()
    cd = p.sb("c_d", [128, 2, TG], F32, st)
    b_cd = _bufs(2)
    p.ts("dve", ones[:], ones[:], 1.0 / 256.0, None, ALU.mult, None, [b_c], [b_c])
    for tg in range(T // TG):
        flat = [cacc[:, cc].rearrange("c j t -> c (j t)")[:, tg * TG:(tg + 1) * TG] for cc in range(2)]
        for cc in range(2):
            p.tt("pool", csq[:, cc, :], flat[cc], flat[cc], ALU.mult, [b_cacc[cc]], [b_csq])
        for cc in range(2):
            p.mm(g.banks[0][:, 0:TG], ones[:], flat[cc], cc == 0, cc == 1, [b_c, b_cacc[cc]], [g.bb[0]])
        for cc in range(2):
            p.mm(g.banks[1][:, 0:TG], ones[:], csq[:, cc, :], cc == 0, cc == 1, [b_c, b_csq], [g.bb[1]])
        p.copy("act", cmean[:], g.banks[0][:, 0:TG], [g.bb[0]], [b_cst])
        p.tt("dve", crstd[:], cmean[:], cmean[:], ALU.mult, [b_cst], [b_cst])
        p.tt("dve", crstd[:], g.banks[1][:, 0:TG], crstd[:], ALU.subtract, [g.bb[1], b_cst], [b_cst])
        p.act(crstd[:], crstd[:], AF.Ln, [b_cst], [b_cst], bias=EPS)
        p.act(crstd[:], crstd[:], AF.Exp, [b_cst], [b_cst], scale=-0.5)
        for cc in range(2):
            p.tt("dve", cd[:, cc, :], flat[cc], cmean[:], ALU.subtract, [b_cacc[cc], b_cst], [b_cd[cc]])
            p.tt("dve", cd[:, cc, :], cd[:, cc, :], crstd[:], ALU.mult, [b_cd[cc], b_cst], [b_cd[cc]])
            p.ts("dve", cd[:, cc, :], cd[:, cc, :], cpar[:, cc, 1:2], cpar[:, cc, 2:3], ALU.mult, ALU.add, [b_cd[cc], b_cw], [b_cd[cc]])
            p.act(csq[:, cc, :], cd[:, cc, :], AF.Sigmoid, [b_cd[cc]], [b_csq])
            p.tt("dve", yT[:, 6 + cc, tg * TG:(tg + 1) * TG], cd[:, cc, :], csq[:, cc, :], ALU.mult, [b_cd[cc], b_csq], [b_yT])
    p.ts("dve", ones[:], ones[:], 256.0, None, ALU.mult, None, [b_c], [b_c])
    p.barrier()
    st.close()

    R = 4 * J * 4
    cn = p.sb("f_cn", [128, J, 4, 4], F32, stB)
    offs = p.sb("f_offs", [128, J, 4, 4], F32, stB)
    offl = p.sb("f_offl", [128, J, 4], F32, stB)
    vs = p.sb("f_vs", [128, J, 4, 4], F32, stB)
    bjr = p.sb("f_bjr", [128, J, J, 4], F32, stB)
    lam = p.sb("d_lam", [128, 4], F32, stB)
    dng = p.sb("d_g", [128, 64], F32, stB)
    b_f = Buf()
    st = ExitStack()
    triu = p.sb("f_triu", [128, 128], F32, st)
    p.dma("sp", triu[:], consts["triu"], (), [b_f])
    nt = (R + 127) // 128
    lfr = p.sb("f_lfr", [128, nt, 128], F32, st)
    lfn = p.sb("f_lfn", [128, R], F32, st)
    sc_a = p.sb("f_sca", [128, NB, 4], F32, st)
    sc_b = p.sb("f_scb", [128, NB, 4], F32, st)
    for t in range(nt):
        rr = min(128, R - t * 128)
        p.dma("sp", lfr[0:rr, t, :], LF_all[t * 128:t * 128 + rr, :], (), [b_f])
        p.tr(g.banks[0][:, t * 128:t * 128 + rr], lfr[0:rr, t, :], g.identf[0:rr, 0:rr], [b_f, g.b_ident], [g.bb[0]])
    p.copy("dve", lfn[:], g.banks[0][:, 0:R], [g.bb[0]], [b_f])
    p.mm(g.banks[1][:, 0:R], triu[:], lfn[:], True, True, [b_f], [g.bb[1]])
    p.memset("pool", ones[:], 1.0, [b_c])
    p.mm(g.banks[2][:, 0:R], ones[:], lfn[:], True, True, [b_f, b_c], [g.bb[2]])
    nat = lambda ap: ap.rearrange("p (c j h) -> p j c h", c=4, j=J)
    p.copy("dve", sc_a[:].rearrange("p (j c) h -> p j c h", c=4), nat(g.banks[2][:, 0:R]), [g.bb[2]], [b_f])
    p.copy("dve", offs[:], nat(g.banks[2][:, 0:R]), [g.bb[2]], [b_f])
    a, b = sc_a, sc_b
    s = 1
    while s < NB:
        p.copy("dve", b[:, 0:s, :], a[:, 0:s, :], [b_f], [b_f])
        p.tt("dve", b[:, s:NB, :], a[:, s:NB, :], a[:, 0:NB - s, :], ALU.add, [b_f], [b_f])
        a, b = b, a
        s *= 2
    offs_f = offs[:].rearrange("p j c h -> p (j c) h")
    p.tt("dve", offs_f, a[:], offs_f, ALU.subtract, [b_f], [b_f])
    p.tt("dve", cn[:], nat(g.banks[1][:, 0:R]), offs[:], ALU.add, [g.bb[1], b_f], [b_f])
    p.ts("dve", offl[:], offs[:, :, 0, :], sel[:, 0:1], None, ALU.mult, None, [b_f, b_c], [b_f])
    for c in range(1, 4):
        p.stt("dve", offl[:], offs[:, :, c, :], sel[:, c:c + 1], offl[:], ALU.mult, ALU.add, [b_f, b_c], [b_f])
    for c in range(4):
        p.tt("dve", vs[:, :, c, :], offs[:, :, 0, :], cn[:, :, c, :], ALU.subtract, [b_f], [b_f])
    vsf = vs[:].rearrange("p j c h -> p (j c h)")
    p.act(vsf, vsf, AF.Exp, [b_f], [b_f])
    for j in range(J):
        for h in range(4):
            p.ts("dve", bjr[:, j, :, h], offs[:, :, 0, h], -1.0, offl[:, j, h:h + 1], ALU.mult, ALU.add, [b_f], [b_f])
    dlf = p.sb("d_dl", [128, 128], F32, st)
    p.dma("sp", dlf[:], W["diff_lambda"].rearrange("a b -> (a b)").partition_broadcast(128), (), [b_f])
    dl = dlf[:].rearrange("p (a b) -> p a b", a=4)
    p.memset("dve", lam[:], 0.0, [b_f])
    p.dma("sp", dng[:], W["diff_norm_g"].partition_broadcast(128), (), [b_f])
    p.tt("dve", dl[:, 0, :], dl[:, 0, :], dl[:, 1, :], ALU.mult, [b_f], [b_f])
    p.tt("dve", dl[:, 2, :], dl[:, 2, :], dl[:, 3, :], ALU.mult, [b_f], [b_f])
    p.ts("dve", dl[:, 1, :], dl[:, 0, :], 1.0, 0.0, ALU.mult, ALU.add, [b_f], [b_f], accum=lam[:, 0:1])
    p.ts("dve", dl[:, 3, :], dl[:, 2, :], 1.0, 0.0, ALU.mult, ALU.add, [b_f], [b_f], accum=lam[:, 1:2])
    p.act(lam[:, 0:2], lam[:, 0:2], AF.Exp, [b_f], [b_f])
    p.tt("dve", lam[:, 2:3], lam[:, 1:2], lam[:, 0:1], ALU.subtract, [b_f], [b_f])
    if lamc is None:
        p.ts("dve", lam[:, 2:3], lam[:, 2:3], -lambda_init, None, ALU.add, None, [b_f], [b_f])
        p.ts("dve", dng[:], dng[:], 1.0 - lambda_init, None, ALU.mult, None, [b_f], [b_f])
    else:
        lc = p.sb("d_lc", [128, 2], F32, st)
        p.dma("sp", lc[:], lamc, (), [b_f])
        p.ts("dve", lam[:, 2:3], lam[:, 2:3], lc[:, 0:1], None, ALU.add, None, [b_f], [b_f])
        p.ts("dve", dng[:], dng[:], lc[:, 1:2], None, ALU.mult, None, [b_f], [b_f])
    p.barrier()
    st.close()

    stA = ExitStack()
    ALIAS = (J == 16)
    x1flat = x1[:].rearrange("p a b -> p (a b)")
    if ALIAS:
        kt = [x1flat[:, i * 4096:(i + 1) * 4096].bitcast(BF16) for i in range(2)]
        vt = [x1flat[:, 8192:8192 + 4160].bitcast(BF16).rearrange("p (a b c) -> p a b c", a=NB, b=2, c=65),
              p.sb("a_v1", [128, NB, 2, 65], BF16, stA)]
    else:
        kt = [p.sb("a_kt%d" % i, [128, L], BF16, stA) for i in range(2)]
        vt = [p.sb("a_v%d" % i, [128, NB, 2, 65], BF16, stA) for i in range(2)]
    qt = [p.sb("a_q%d" % i, [128, 2, T], BF16, stA) for i in range(2)]
    bti = [p.sb("a_bt%d" % i, [128, 2, 8, 128], F32, stA) for i in range(2)]
    b_kv = _bufs(2)
    pT = p.sb("a_pT", [128, 4, 512], BF16, stA)
    b_pT = _bufs(4)
    ssb = p.sb("a_ssb", [128, 2, 512], F32, stA)
    b_ssb = _bufs(2)
    ysb = p.sb("a_ysb", [128, 2, 128], F32, stA)
    b_ysb = _bufs(2)
    yd = p.sb("a_yd", [128, 2, 66], F32, stA)
    b_yd = _bufs(2)
    rec = p.sb("a_rec", [128, 8], F32, stA)
    b_rec = Buf()
    for i in range(2):
        p.memset("pool", vt[i][:, :, :, 64:65], 1.0, [b_kv[i]])
    rot = {"s": 0, "pt": 0, "ss": 0, "o": 0}
    SB = [0, 1, 2]
    OB = [3, 4, 5, 6]
    TB = 7

    def load_pair(slot, kind, pr):
        kch = kind * 2 + pr
        for c in range(4):
            for ap_, off, n in KT_all.rows(c, kch * 128, 128):
                dst = kt[slot][off:off + n, :].rearrange("r (j c t) -> r j c t", c=4, t=128)[:, :, c, :]
                p.dma("sp", dst, ap_.rearrange("r (j t) -> r j t", t=128), (), [b_kv[slot]])
            for hh in range(2):
                c0 = kind * 256 + pr * 128 + hh * 64
                for j_ in range(J):
                    for ap_, off, n in V_all.rows(c, j_ * 128, 128):
                        p.dma("sp", vt[slot][off:off + n, 4 * j_ + c, hh, 0:64], ap_[:, c0:c0 + 64], (), [b_kv[slot]])
        if kind == 0:
            p.dma("sp", qt[slot][:, 0, :], QT_loc[pr], (), [b_kv[slot]])
            for hh in range(2):
                h = pr * 2 + hh
                vsh = vs[:].rearrange("p j c h -> p (j c) h")[:, :, h:h + 1]
                p.tt("pool", vt[slot][:, :, hh, 0:64], vt[slot][:, :, hh, 0:64], vsh.to_broadcast([128, NB, 64]), ALU.mult,
                     [b_f, b_kv[slot]], [b_kv[slot]])
                p.copy("pool", vt[slot][:, :, hh, 64:65], vsh, [b_f, b_kv[slot]], [b_kv[slot]])
        elif kind == 1:
            p.dma("sp", qt[slot][:, 0, :], QT_loc[2 + pr], (), [b_kv[slot]])
            p.dma("sp", qt[slot][:, 1, :], QT_loc[4 + pr], (), [b_kv[slot]])
        else:
            p.dma("sp", qt[slot][:, 0, :], QT_loc[6 + pr], (), [b_kv[slot]])
        if kind > 0:
            hb = (kind - 1) * 4 + pr * 2
            p.dma("sp", bti[slot][:], W["biasT"][hb:hb + 2].rearrange("h t k q -> k h t q"), (), [b_kv[slot]])
            for hh in range(2):
                p.tt("pool", bti[slot][:, hh, 0:4, :], bti[slot][:, hh, 0:4, :], maskT[:], ALU.add, [b_kv[slot], b_c], [b_kv[slot]])
            p.memset("pool", vt[slot][:, :, :, 64:65], 1.0, [b_kv[slot]])

    def run_maps(slot, kind, pr, j, ngT=None, b_ng=None):
        nrow = j + 1
        if kind == 1:
            maps = [(hh, m) for hh in range(2) for m in range(2)]
        else:
            maps = [(hh, 0) for hh in range(2)]
        scale = (32 ** -0.5) if kind == 1 else 0.125
        obank = {}
        items = [(mi, r) for mi in range(len(maps)) for r in range(nrow)]
        pend = {}

        def qk(it):
            mi, r = it
            hh, m = maps[mi]
            sbk = SB[rot["s"] % 3]
            rot["s"] += 1
            rows = slice(hh * 64, hh * 64 + 64)
            for c in range(4):
                blk = 4 * r + c
                p.mm(g.banks[sbk][:, c * 128:(c + 1) * 128], kt[slot][rows, blk * 128:(blk + 1) * 128],
                     qt[slot][rows, m, j * 128:(j + 1) * 128], True, True, [b_kv[slot]], [g.bb[sbk]])
            ps_ = rot["pt"] % 4
            rot["pt"] += 1
            h = pr * 2 + hh
            if kind == 0:
                bias = bjr[:, j, r, h:h + 1]
                near = (r == j)
                bread = [b_f]
            else:
                gh = (kind - 1) * 4 + h
                bias = b31[:, gh:gh + 1]
                near = (r >= j - 1)
                bread = [b_c]
            if near:
                ss = rot["ss"] % 2
                rot["ss"] += 1
                for c in range(4):
                    if kind == 0:
                        tile_ = maskT[:, c, :]
                        rd = [b_c]
                    else:
                        tile_ = bti[slot][:, hh, (0 if r == j else 4) + c, :]
                        rd = [b_kv[slot]]
                    p.stt("dve", ssb[:, ss, c * 128:(c + 1) * 128], g.banks[sbk][:, c * 128:(c + 1) * 128], scale, tile_,
                          ALU.mult, ALU.add, [g.bb[sbk]] + rd, [b_ssb[ss]])
                if kind == 0:
                    p.act(pT[:, ps_, :], ssb[:, ss, :], AF.Exp, [b_ssb[ss]] + bread, [b_pT[ps_]], bias=bias)
                else:
                    p.act(pT[:, ps_, :], ssb[:, ss, :], AF.Exp, [b_ssb[ss]], [b_pT[ps_]])
            else:
                p.act(pT[:, ps_, :], g.banks[sbk][:, :], AF.Exp, [g.bb[sbk]] + bread, [b_pT[ps_]], bias=bias, scale=scale)
            if kind == 2:
                p.tt("pool", pT[:, ps_, :], pT[:, ps_, :], ngT[:, 4 * r:4 * r + 4, :].rearrange("k c q -> k (c q)"), ALU.mult,
                     [b_pT[ps_], b_ng], [b_pT[ps_]])
            pend[it] = ps_

        def pv(it):
            mi, r = it
            hh, m = maps[mi]
            ps_ = pend.pop(it)
            if r == 0:
                obank[mi] = OB[rot["o"] % 4]
                rot["o"] += 1
            ob = obank[mi]
            for c in range(4):
                blk = 4 * r + c
                p.mm(g.banks[ob][:, 0:65], pT[:, ps_, c * 128:(c + 1) * 128], vt[slot][:, blk, hh, :],
                     r == 0 and c == 0, r == nrow - 1 and c == 3, [b_pT[ps_], b_kv[slot]], [g.bb[ob]])
            if r == nrow - 1:
                finalize(mi)

        def finalize(mi):
            hh, m = maps[mi]
            ob = obank[mi]
            ys = (j * 2 + pr) % 2
            if kind != 1:
                col = mi
                p.op("dve", lambda e: e.reciprocal(out=rec[:, col:col + 1], in_=g.banks[ob][:, 64:65]), [g.bb[ob]], [b_rec])
                p.ts("dve", ysb[:, ys, hh * 64:(hh + 1) * 64], g.banks[ob][:, 0:64], rec[:, col:col + 1], None, ALU.mult, None,
                     [g.bb[ob], b_rec], [b_ysb[ys]])
            else:
                col = mi
                p.op("dve", lambda e: e.reciprocal(out=rec[:, col:col + 1], in_=g.banks[ob][:, 64:65]), [g.bb[ob]], [b_rec])
                if m == 0:
                    p.ts("dve", yd[:, hh, 0:64], g.banks[ob][:, 0:64], rec[:, col:col + 1], None, ALU.mult, None,
                         [g.bb[ob], b_rec], [b_yd[hh]])
                else:
                    p.ts("dve", ssb[:, 0, 0:64], g.banks[ob][:, 0:64], rec[:, col:col + 1], lam[:, 2:3], ALU.mult, ALU.mult,
                         [g.bb[ob], b_rec, b_f], [b_ssb[0]])
                    p.tt("dve", yd[:, hh, 0:64], yd[:, hh, 0:64], ssb[:, 0, 0:64], ALU.add, [b_ssb[0], b_yd[hh]], [b_yd[hh]])
                    p.tt("dve", ssb[:, 0, 64:128], yd[:, hh, 0:64], yd[:, hh, 0:64], ALU.mult, [b_yd[hh]], [b_ssb[0]])
                    p.memset("dve", yd[:, hh, 64:65], 0.0, [b_yd[hh]])
                    p.ts("dve", ssb[:, 0, 64:128], ssb[:, 0, 64:128], 1.0 / 64, 0.0, ALU.mult, ALU.add, [b_ssb[0]], [b_ssb[0], b_yd[hh]],
                         accum=yd[:, hh, 64:65])
                    p.act(yd[:, hh, 64:65], yd[:, hh, 64:65], AF.Ln, [b_yd[hh]], [b_yd[hh]], bias=EPS)
                    p.act(yd[:, hh, 64:65], yd[:, hh, 64:65], AF.Exp, [b_yd[hh]], [b_yd[hh]], scale=-0.5)
                    p.stt("dve", ysb[:, ys, hh * 64:(hh + 1) * 64], yd[:, hh, 0:64], yd[:, hh, 64:65], dng[:], ALU.mult, ALU.mult,
                          [b_yd[hh], b_f], [b_ysb[ys]])
            last = (mi == len(maps) - 1)
            if dbg is not None and kind == 1 and pr == 0 and j == 0 and mi == 1:
                def dd(name, ap, shape, bufs):
                    t_ = p.nc.dram_tensor("dbg_" + name, shape, F32, kind="ExternalOutput").ap()
                    p.dma("sp", t_, ap, bufs, ())
                dd("yd", yd[:], [128, 2, 66], [b_yd[0], b_yd[1]])
                dd("rec", rec[:], [128, 8], [b_rec])
                dd("bti", bti[slot][:, 0, 0, :], [128, 128], [b_kv[slot]])
                dd("lam", lam[:], [128, 4], [b_f])
                dd("dng", dng[:], [128, 64], [b_f])
                dd("ssb", ssb[:], [128, 2, 512], [b_ssb[0], b_ssb[1]])
                dd("ysb", ysb[:], [128, 2, 128], [b_ysb[0], b_ysb[1]])
            if last:
                p.tr(g.banks[TB][:, 0:128], ysb[:, ys, :], g.identf[:], [b_ysb[ys], g.b_ident], [g.bb[TB]])
                p.copy("act", yT[:, kind * 2 + pr, j * 128:(j + 1) * 128], g.banks[TB][:, 0:128], [g.bb[TB]], [b_yT])

        n = len(items)
        for i in range(n + 1):
            if i < n:
                qk(items[i])
            if i > 0:
                pv(items[i - 1])

    seq = [(0, 0), (0, 1), (1, 0), (1, 1)]
    load_pair(0, *seq[0])
    for i, (kind, pr) in enumerate(seq):
        if i + 1 < len(seq):
            load_pair((i + 1) % 2, *seq[i + 1])
        for j in range(J):
            run_maps(i % 2, kind, pr, j)
    p.barrier()

    st = ExitStack()
    if ALIAS:
        sc = x1flat[:, 0:8192]
        ik2 = x1flat[:, 8192:12288].bitcast(BF16)
        iq = x1flat[:, 12288:14336].bitcast(BF16).rearrange("p (a b) -> p a b", a=2)
    else:
        sc = p.sb("i_sc", [128, L], F32, st)
        ik2 = p.sb("i_ik2", [128, L], BF16, st)
        iq = p.sb("i_iq", [128, 2, T], BF16, st)
    b_sc = Buf()
    iwt = p.sb("i_iw", [128, J, 4], F32, st)
    mq = p.sb("i_mq", [128, 4, 128], F32, st)
    mqa = p.sb("i_mqa", [128, 4, 128], F32, st)
    b_i = Buf()
    if ALIAS:
        rl = x1flat[:, 14336:16384].rearrange("p (a b) -> p a b", a=4)
    else:
        rl = p.sb("i_rl", [128, 4, 512], F32, st)
    b_rl = _bufs(4)
    m8 = p.sb("i_m8", [128, 8], F32, st)
    b_m8 = Buf()
    nev = p.sb("i_nev", [128, 2, 512], BF16, st)
    b_nev = _bufs(2)
    for c in range(4):
        for ap_, off, n in KT_all.rows(c, 768, 64):
            for half in range(2):
                dst = ik2[half * 64 + off:half * 64 + off + n, :].rearrange("r (j c t) -> r j c t", c=4, t=128)[:, :, c, :]
                p.dma("sp", dst, ap_.rearrange("r (j t) -> r j t", t=128), (), [b_i])
    p.dma("sp", iq[:, 0, :], QT_loc[8], (), [b_i])
    p.dma("sp", iq[:, 1, :], QT_loc[9], (), [b_i])
    p.dma("sp", iwt[:].rearrange("p j h -> p (j h)"), IW_loc, (), [b_i])
    p.dma("sp", mq[:], consts["maskQ"].rearrange("c q k -> q c k"), (), [b_i])
    p.ts("dve", mqa[:], mq[:], 1.0e30, None, ALU.mult, None, [b_i], [b_i])
    for j in range(J):
        Nk = 512 * (j + 1)
        for r in range(j + 1):
            ks = slice(r * 512, (r + 1) * 512)
            for hi in range(4):
                rows = slice((hi % 2) * 64, (hi % 2) * 64 + 64)
                bk = hi
                p.mm(g.banks[bk][:, :], iq[rows, hi // 2, j * 128:(j + 1) * 128], ik2[rows, ks], True, True, [b_i], [g.bb[bk]])
                p.act(rl[:, hi, :], g.banks[bk][:, :], AF.Relu, [g.bb[bk]], [b_rl[hi]])
                if hi == 0:
                    p.ts("dve", sc[:, ks], rl[:, 0, :], iwt[:, j, 0:1], None, ALU.mult, None, [b_rl[0], b_i], [b_sc])
                else:
                    p.stt("dve", sc[:, ks], rl[:, hi, :], iwt[:, j, hi:hi + 1], sc[:, ks], ALU.mult, ALU.add, [b_rl[hi], b_i, b_sc], [b_sc])
            if r == j:
                p.tt("dve", sc[:, ks], sc[:, ks], mqa[:].rearrange("q c k -> q (c k)"), ALU.add, [b_sc, b_i], [b_sc])
        for it in range(32):
            p.op("dve", lambda e: e.max(out=m8[:], in_=sc[:, 0:Nk]), [b_sc], [b_m8])
            p.op("dve", lambda e: e.match_replace(out=sc[:, 0:Nk], in_to_replace=m8[:], in_values=sc[:, 0:Nk], imm_value=-3.0e38),
                 [b_sc, b_m8], [b_sc])
        p.ts("dve", sc[:, 0:Nk], sc[:, 0:Nk], -1.0e35, 1.0, ALU.is_le, ALU.subtract, [b_sc], [b_sc])
        ks = slice(j * 512, (j + 1) * 512)
        p.tt("dve", sc[:, ks], sc[:, ks], mq[:].rearrange("q c k -> q (c k)"), ALU.min, [b_sc, b_i], [b_sc])
        if dbg is not None and j == 0:
            t_ = p.nc.dram_tensor("dbg_sc", [128, 512], F32, kind="ExternalOutput").ap()
            p.dma("sp", t_, sc[:, 0:512], [b_sc], ())
        for r in range(j + 1):
            bk = 4 + (r % 2)
            for c in range(4):
                blk = 4 * r + c
                p.tr(g.banks[bk][:, c * 128:(c + 1) * 128], sc[:, blk * 128:(blk + 1) * 128], g.identf[:], [b_sc, g.b_ident], [g.bb[bk]])
            s = r % 2
            p.act(nev[:, s, :], g.banks[bk][:, :], AF.Identity, [g.bb[bk]], [b_nev[s]], bias=1.0)
            p.dma("sp", NEGT[j, :, 4 * r:4 * r + 4, :], nev[:, s, :].rearrange("k (c q) -> k c q", c=4), [b_nev[s]], ())
    p.barrier()
    st.close()

    st = ExitStack()
    ng = [p.sb("s_ng%d" % i, [128, NB, 128], BF16, st) for i in range(2)]
    b_ngs = _bufs(2)
    load_pair(0, 2, 0)
    load_pair(1, 2, 1)
    p.dma("sp", ng[0][:, 0:4, :], NEGT[0, :, 0:4, :], (), [b_ngs[0]])
    for j in range(J):
        if j + 1 < J:
            p.dma("sp", ng[(j + 1) % 2][:, 0:4 * (j + 2), :], NEGT[j + 1, :, 0:4 * (j + 2), :], (), [b_ngs[(j + 1) % 2]])
        for pr in range(2):
            run_maps(pr, 2, pr, j, ngT=ng[j % 2], b_ng=b_ngs[j % 2])
    p.barrier()
    st.close()
    stA.close()

    if dbg is not None:
        p.dma("sp", dbg, yT[:], [b_yT], ())
    st = ExitStack()
    wo = p.sb("o_wo", [128, 8, D], BF16, st)
    b_wo = Buf()
    p.dma("pool", wo[:], W["w_out"].rearrange("(fc f) m -> f fc m", f=128), (), [b_wo])
    gt = p.sb("o_g", [128, D], F32, st)
    bt = p.sb("o_b", [128, D], F32, st)
    b_gb = Buf()
    p.dma("sp", gt[:], W["ln1_g"].partition_broadcast(128), (), [b_gb])
    p.dma("sp", bt[:], W["ln1_b"].partition_broadcast(128), (), [b_gb])
    xin = [p.sb("o_x%d" % i, [128, D], F32, st) for i in range(2)]
    b_xin = _bufs(2)
    tmp = p.sb("o_tmp", [128, D], F32, st)
    b_tmp = Buf()
    st4 = p.sb("o_st4", [128, 4], F32, st)
    b_st4 = Buf()
    Xr = X.rearrange("(j q) d -> q j d", q=128)
    for j in range(J):
        s = j % 2
        p.dma("sp", xin[s][:], Xr[:, j, :], (), [b_xin[s]])
        for mh in range(2):
            bk = mh
            for fc in range(8):
                p.mm(g.banks[bk][:, :], yT[:, fc, j * 128:(j + 1) * 128], wo[:, fc, mh * 512:(mh + 1) * 512], fc == 0, fc == 7,
                     [b_yT, b_wo], [g.bb[bk]])
            p.stt("dve", xin[s][:, mh * 512:(mh + 1) * 512], xin[s][:, mh * 512:(mh + 1) * 512], ALPHA, g.banks[bk][:, :],
                  ALU.mult, ALU.add, [b_xin[s], g.bb[bk]], [b_xin[s]])
        ln_rows(p, g, xin[s][:], b_xin[s], x1[:, j, :], b_x1, gt[:], bt[:], b_gb, tmp[:], b_tmp, st4, b_st4)
    p.barrier()
    st.close()
    stB.close()

def emit_C(p, g, W, x1, b_x1, Xout):
    J, T = g.J, g.T
    st = ExitStack()
    xT = p.sb("m_xT", [128, 8, T], BF16, st)
    b_xT = Buf()
    xTf = p.sb("m_xTf", [128, 8, 128], F32, st)
    b_xTf = Buf()
    rw = p.sb("m_rw", [128, 8, NE], F32, st)
    b_w = Buf()
    p.dma("sp", rw[:], W["router_w"].rearrange("(dc q) e -> q dc e", q=128), (), [b_w])
    rb = p.sb("m_rb", [128, NE], F32, st)
    p.dma("sp", rb[:], W["router_b"].partition_broadcast(128), (), [b_w])
    bgu = p.sb("m_bgu", [128, 16, NE], F32, st)
    bd = p.sb("m_bd", [NE, D], F32, st)
    gates = p.sb("m_gates", [128, J, NE], F32, st)
    gT = p.sb("m_gT", [NE, 128], F32, st)
    lg = p.sb("m_lg", [128, NE], F32, st)
    m8 = p.sb("m_m8", [128, 8], F32, st)
    sm = p.sb("m_sm", [128, 4], F32, st)
    st0 = ExitStack()
    bgr = p.sb("m_bgr", [NE, 2 * D], F32, st0)
    p.dma("sp", bgr[:], W["b_gu"], (), [b_w])
    bgv = bgr[:].rearrange("e (jc q t) -> e jc t q", q=128, t=2)
    for jc in range(8):
        for t in range(2):
            i = jc * 2 + t
            p.tr(g.banks[7][:, i * NE:(i + 1) * NE], bgv[:, jc, t, :], g.identf[0:NE, 0:NE], [b_w, g.b_ident], [g.bb[7]])
    p.copy("dve", bgu[:], g.banks[7][:, :].rearrange("q (i e) -> q i e", e=NE), [g.bb[7]], [b_w])
    p.barrier()
    st0.close()
    p.dma("sp", bd[:], W["b_down"], (), [b_w])
    b_gates = Buf()
    b_gT = Buf()
    b_r = Buf()
    for j in range(J):
        transpose_block(p, g, x1[:, j, :], b_x1, xT, b_xT, j, dst_f32=xTf, dst_f32_b=b_xTf, bank0=0)
        for dc in range(8):
            p.mm(g.banks[2][:, 0:NE], xTf[:, dc, :], rw[:, dc, :], dc == 0, dc == 7, [b_xTf, b_w], [g.bb[2]])
        p.tt("dve", lg[:], g.banks[2][:, 0:NE], rb[:], ALU.add, [g.bb[2], b_w], [b_r])
        p.op("dve", lambda e: e.max(out=m8[:], in_=lg[:]), [b_r], [b_r])
        p.ts("dve", sm[:, 0:1], m8[:, 0:1], -1.0, None, ALU.mult, None, [b_r], [b_r])
        p.act(gates[:, j, :], lg[:], AF.Exp, [b_r], [b_gates], bias=sm[:, 0:1])
        p.ts("dve", lg[:], lg[:], m8[:, 3:4], None, ALU.is_ge, None, [b_r], [b_r])
        p.tt("dve", gates[:, j, :], gates[:, j, :], lg[:], ALU.mult, [b_r, b_gates], [b_gates])
        p.memset("dve", sm[:, 1:2], 0.0, [b_r])
        p.ts("dve", lg[:], gates[:, j, :], 1.0, 0.0, ALU.mult, ALU.add, [b_gates], [b_r], accum=sm[:, 1:2])
        p.op("dve", lambda e: e.reciprocal(out=sm[:, 2:3], in_=sm[:, 1:2]), [b_r], [b_r])
        p.ts("dve", gates[:, j, :], gates[:, j, :], sm[:, 2:3], None, ALU.mult, None, [b_r, b_gates], [b_gates])
        p.tr(g.banks[3][0:NE, 0:128], gates[:, j, :], g.identf[:], [b_gates, g.b_ident], [g.bb[3]])
        p.copy("dve", gT[:], g.banks[3][0:NE, 0:128], [g.bb[3]], [b_gT])
        for mh in range(2):
            p.mm(g.banks[4 + mh][:, :], gT[:], bd[:, mh * 512:(mh + 1) * 512], True, True, [b_gT, b_w], [g.bb[4 + mh]])
            p.stt("dve", x1[:, j, mh * 512:(mh + 1) * 512], x1[:, j, mh * 512:(mh + 1) * 512], ALPHA, g.banks[4 + mh][:, :],
                  ALU.mult, ALU.add, [b_x1, g.bb[4 + mh]], [b_x1])
    NW = 4
    st2 = ExitStack()
    st_save, st = st, st2
    wgu = [p.sb("m_wgu%d" % i, [128, 8, 256], BF16, st) for i in range(NW)]
    b_wgu = _bufs(NW)
    wd = [p.sb("m_wd%d" % i, [128, 8, D], BF16, st) for i in range(2)]
    b_wd = _bufs(2)
    hT = p.sb("m_hT", [128, 8, T], BF16, st)
    b_hT = _bufs(8)
    TG = min(512, T)
    ntg = T // TG
    gc = p.sb("m_gc", [128, 2, TG], F32, st)
    sg = p.sb("m_sg", [128, 2, TG], F32, st)
    uc = p.sb("m_uc", [128, 2, TG], F32, st)
    b_gc, b_sg, b_uc = _bufs(2), _bufs(2), _bufs(2)
    b_acc = _bufs(J)
    wq = {"n": 0}

    def load_gu(e, jc):
        s = wq["n"] % NW
        wq["n"] += 1
        src = W["w_gu"][e].rearrange("(dc q) f -> q dc f", q=128)[:, :, jc * 256:(jc + 1) * 256]
        p.dma("pool", wgu[s][:], src, (), [b_wgu[s]])
        return s

    def load_d(e):
        s = e % 2
        p.dma("pool", wd[s][:], W["w_down"][e].rearrange("(jc q) m -> q jc m", q=128), (), [b_wd[s]])

    pre = []
    for jc in range(2):
        pre.append(load_gu(0, jc))
    load_d(0)
    k = 0
    for e in range(NE):
        for jc in range(8):
            nxt = e * 8 + jc + 2
            if nxt < NE * 8:
                pre.append(load_gu(nxt // 8, nxt % 8))
            if jc == 4 and e + 1 < NE:
                load_d(e + 1)
            s = pre.pop(0)
            wv = wgu[s][:].rearrange("q dc (j t) -> q dc t j", t=2)
            for tg in range(ntg):
                kk = k % 2
                k += 1
                tsl = slice(tg * TG, (tg + 1) * TG)
                for t, bk in ((0, kk), (1, 2 + kk)):
                    for dc in range(8):
                        p.mm(g.banks[bk][:, 0:TG], wv[:, dc, t, :], xT[:, dc, tsl], dc == 0, dc == 7, [b_wgu[s], b_xT], [g.bb[bk]])
                p.ts("dve", gc[:, kk, :], g.banks[kk][:, 0:TG], bgu[:, jc * 2, e:e + 1], 7.0, ALU.add, ALU.min, [g.bb[kk], b_w], [b_gc[kk]])
                p.act(sg[:, kk, :], gc[:, kk, :], AF.Sigmoid, [b_gc[kk]], [b_sg[kk]], scale=1.702)
                p.act(uc[:, kk, :], g.banks[2 + kk][:, 0:TG], AF.Identity, [g.bb[2 + kk], b_w], [b_uc[kk]], bias=bgu[:, jc * 2 + 1, e:e + 1])
                p.ts("pool", uc[:, kk, :], uc[:, kk, :], 7.0, -7.0, ALU.min, ALU.max, [b_uc[kk]], [b_uc[kk]])
                p.tt("pool", gc[:, kk, :], gc[:, kk, :], sg[:, kk, :], ALU.mult, [b_gc[kk], b_sg[kk]], [b_gc[kk]])
                p.stt("dve", hT[:, jc, tsl], uc[:, kk, :], 1.0, gc[:, kk, :], ALU.add, ALU.mult, [b_uc[kk], b_gc[kk]], [b_hT[jc]])
        s = e % 2
        for j in range(J):
            for mh in range(2):
                bk = 4 + (j * 2 + mh) % 4
                for jc in range(8):
                    p.mm(g.banks[bk][:, :], hT[:, jc, j * 128:(j + 1) * 128], wd[s][:, jc, mh * 512:(mh + 1) * 512], jc == 0, jc == 7,
                         [b_hT[jc], b_wd[s]], [g.bb[bk]])
                p.stt("dve", x1[:, j, mh * 512:(mh + 1) * 512], g.banks[bk][:, :], gates[:, j, e:e + 1], x1[:, j, mh * 512:(mh + 1) * 512],
                      ALU.mult, ALU.add, [g.bb[bk], b_gates, b_acc[j], b_x1], [b_acc[j]])
    p.barrier()
    st2.close()
    st = st_save
    gt = p.sb("m_g", [128, D], F32, st)
    bt = p.sb("m_b", [128, D], F32, st)
    b_gb = Buf()
    p.dma("sp", gt[:], W["ln2_g"].partition_broadcast(128), (), [b_gb])
    p.dma("sp", bt[:], W["ln2_b"].partition_broadcast(128), (), [b_gb])
    tmp = p.sb("m_tmp", [128, D], F32, st)
    b_tmp = Buf()
    st4 = p.sb("m_st4", [128, 4], F32, st)
    b_st4 = Buf()
    Xr = Xout.rearrange("(j q) d -> q j d", q=128)
    for j in range(J):
        ln_rows(p, g, x1[:, j, :], b_acc[j], x1[:, j, :], b_acc[j], gt[:], bt[:], b_gb, tmp[:], b_tmp, st4, b_st4)
        p.dma("sp", Xr[:, j, :], x1[:, j, :], [b_acc[j]], ())
    p.barrier()
    st.close()


def _bucket(d):
    n = np.maximum(d, 0)
    nf = np.maximum(n, 1).astype(np.float32)
    large = 16 + (np.log(nf / np.float32(16)) / np.float32(math.log(8.0)) * np.float32(16)).astype(np.int32)
    large = np.minimum(large, 31)
    return np.where(n < 16, n, large)


def make_consts(c_me):
    q = np.arange(128)
    cs = {}
    cs["identf"] = np.eye(128, dtype=np.float32)
    dqm = np.zeros((128, 2), np.float32)
    dqm[:, 0] = ((q % 64) < 32)
    dqm[:, 1] = 1.0 - dqm[:, 0]
    cs["dqmask"] = dqm
    sel = np.zeros((128, 4), np.float32)
    sel[:, c_me] = 1.0
    cs["sel"] = sel
    cs["triu"] = np.triu(np.ones((128, 128), np.float32))
    mT = np.zeros((4, 128, 128), np.float32)
    for c in range(4):
        dl = c_me - c
        if dl < 0:
            mT[c] = 1.0
        elif dl == 0:
            mT[c] = (q[:, None] > q[None, :]).astype(np.float32)
    cs["maskT"] = mT * np.float32(NEG)
    cs["maskQ"] = -np.ascontiguousarray(mT.transpose(0, 2, 1))
    return cs


def bias_tile_index(c_me):
    q = np.arange(128)
    idx = np.zeros((8, 128, 128), np.int64)
    for t in range(8):
        dl = (c_me - t) if t < 4 else (4 + c_me - (t - 4))
        d = 128 * dl + q[None, :] - q[:, None]
        idx[t] = _bucket(d)
    return idx


CONST_SHAPES = {"identf": [128, 128], "dqmask": [128, 2], "sel": [128, 4], "triu": [128, 128],
                "maskT": [4, 128, 128], "maskQ": [4, 128, 128]}


def declare_consts(nc):
    return {k: nc.dram_tensor("c_" + k, shp, F32, kind="ExternalInput").ap() for k, shp in CONST_SHAPES.items()}


WSHAPES = {"rel31": [8], "biasT": [8, 8, 128, 128], "diff_lambda": [4, 32], "diff_norm_g": [64], "conv_w": [31, 256],
           "conv_b": [256], "conv_ln_g": [256], "conv_ln_b": [256], "w_out": [D, D], "ln1_g": [D], "ln1_b": [D],
           "router_w": [D, NE], "router_b": [NE], "w_gu": [NE, D, 2 * D], "b_gu": [NE, 2 * D], "w_down": [NE, D, D],
           "b_down": [NE, D], "ln2_g": [D], "ln2_b": [D]}


def build_A(J):
    nc = bass.Bass("TRN2", target_bir_lowering=False)
    T = J * 128
    X = nc.dram_tensor("X", [T, D], F32, kind="ExternalInput").ap()
    w_in = nc.dram_tensor("w_in", [D, NIN], F32, kind="ExternalInput").ap()
    forget_b = nc.dram_tensor("forget_b", [4], F32, kind="ExternalInput").ap()
    consts = declare_consts(nc)
    KT_loc = nc.dram_tensor("KT_loc", [KT_ROWS, T], BF16, kind="ExternalOutput").ap()
    V_loc = nc.dram_tensor("V_loc", [T, 768], BF16, kind="ExternalOutput").ap()
    HT_loc = nc.dram_tensor("HT_loc", [256, T], F32, kind="ExternalOutput").ap()
    LF_loc = nc.dram_tensor("LF_loc", [J * 4, 128], F32, kind="ExternalOutput").ap()
    QT_loc = nc.dram_tensor("QT_loc", [NQCH, 128, T], BF16, kind="ExternalOutput").ap()
    IW_loc = nc.dram_tensor("IW_loc", [128, J * 4], F32, kind="ExternalOutput").ap()
    p = Prog(nc)
    g = setup_common(p, J, consts)
    emit_A(p, g, X, w_in, forget_b, consts, KT_loc, V_loc, HT_loc, LF_loc, QT_loc, IW_loc)
    p.finish()
    return nc


def build_BC(J, do_C=True, debug=False):
    nc = bass.Bass("TRN2", target_bir_lowering=False)
    T = J * 128
    NB = 4 * J
    X = nc.dram_tensor("X", [T, D], F32, kind="ExternalInput").ap()
    KT_all = nc.dram_tensor("KT_all", [4 * KT_ROWS, T], BF16, kind="ExternalInput").ap()
    V_all = nc.dram_tensor("V_all", [4 * T, 768], BF16, kind="ExternalInput").ap()
    HT_all = nc.dram_tensor("HT_all", [4 * 256, T], F32, kind="ExternalInput").ap()
    LF_all = nc.dram_tensor("LF_all", [4 * J * 4, 128], F32, kind="ExternalInput").ap()
    QT_loc = nc.dram_tensor("QT_loc", [NQCH, 128, T], BF16, kind="ExternalInput").ap()
    IW_loc = nc.dram_tensor("IW_loc", [128, J * 4], F32, kind="ExternalInput").ap()
    HT_loc = nc.dram_tensor("HT_loc", [256, T], F32, kind="ExternalInput").ap()
    lamc = nc.dram_tensor("lamc", [128, 2], F32, kind="ExternalInput").ap()
    W = {k: nc.dram_tensor(k, shp, F32, kind="ExternalInput").ap() for k, shp in WSHAPES.items()}
    consts = declare_consts(nc)
    NEGT = nc.dram_tensor("NEGT", [J, 128, NB, 128], BF16, kind="Internal").ap()
    Xout = nc.dram_tensor("Xout", [T, D], F32, kind="ExternalOutput").ap()
    p = Prog(nc)
    g = setup_common(p, J, consts)
    x1 = p.sb("x1", [128, J, D], F32)
    b_x1 = Buf()
    dbg = nc.dram_tensor("dbg_yT", [128, 8, T], BF16, kind="ExternalOutput").ap() if debug else None
    emit_B(p, g, 0, X, Gat(KT_ROWS, mono=KT_all), Gat(T, mono=V_all), Gat(256, mono=HT_all), LF_all, QT_loc, IW_loc, HT_loc, NEGT, W,
           consts, x1, b_x1, lamc=lamc, dbg=dbg)
    if do_C:
        emit_C(p, g, W, x1, b_x1, Xout)
    else:
        p.dma("sp", Xout.rearrange("(j q) d -> q j d", q=128), x1[:], [b_x1], ())
    p.finish()
    return nc


def build_fused(J, nlayers=DEPTH):
    nc = bass.Bass("TRN2", target_bir_lowering=False)
    T = J * 128
    NB = 4 * J
    Xin = nc.dram_tensor("X", [T, D], F32, kind="ExternalInput").ap()
    w_in = nc.dram_tensor("w_in", [nlayers, D, NIN], F32, kind="ExternalInput").ap()
    forget_b = nc.dram_tensor("forget_b", [nlayers, 4], F32, kind="ExternalInput").ap()
    Wf = {}
    for k, shp in WSHAPES.items():
        if k in ("rel31", "biasT"):
            Wf[k] = nc.dram_tensor(k, shp, F32, kind="ExternalInput").ap()
        else:
            Wf[k] = nc.dram_tensor(k, [nlayers] + shp, F32, kind="ExternalInput").ap()
    consts = declare_consts(nc)
    Xout = nc.dram_tensor("Xout", [T, D], F32, kind="ExternalOutput").ap()
    Xbuf = nc.dram_tensor("Xbuf", [T, D], F32, kind="Internal").ap()
    KT_loc = nc.dram_tensor("KT_loc", [KT_ROWS, T], BF16, kind="Internal").ap()
    V_loc = nc.dram_tensor("V_loc", [T, 768], BF16, kind="Internal").ap()
    HT_loc = nc.dram_tensor("HT_loc", [256, T], F32, kind="Internal").ap()
    LF_loc = nc.dram_tensor("LF_loc", [J * 4, 128], F32, kind="Internal").ap()
    QT_loc = nc.dram_tensor("QT_loc", [NQCH, 128, T], BF16, kind="Internal").ap()
    IW_loc = nc.dram_tensor("IW_loc", [128, J * 4], F32, kind="Internal").ap()
    SK, SV, SH = 64, 128, 32
    def mk(name, R, S, W_, dt):
        return [Gat(R, S=S, chunks=[nc.dram_tensor("%s%d_%d" % (name, i, k), [4 * S, W_], dt, kind="Internal").ap()
                                    for k in range(R // S)]) for i in range(2)]
    KT_alls = mk("KTa", KT_ROWS, SK, T, BF16)
    V_alls = mk("Va", T, SV, 768, BF16)
    HT_alls = mk("HTa", 256, SH, T, F32)
    LF_alls = [nc.dram_tensor("LF_all%d" % i, [4 * J * 4, 128], F32, kind="Internal").ap() for i in range(2)]
    NEGT = nc.dram_tensor("NEGT", [J, 128, NB, 128], BF16, kind="Internal").ap()
    p = Prog(nc)
    g = setup_common(p, J, consts)
    x1 = p.sb("x1", [128, J, D], F32)
    b_x1 = Buf()
    groups = [[0, 1, 2, 3], [4, 5, 6, 7]]
    for l in range(nlayers):
        Xsrc = Xin if l == 0 else Xbuf
        Xdst = Xout if l == nlayers - 1 else Xbuf
        KT_all, V_all, HT_all, LF_all = KT_alls[l % 2], V_alls[l % 2], HT_alls[l % 2], LF_alls[l % 2]
        emit_A(p, g, Xsrc, w_in[l], forget_b[l], consts, KT_loc, V_loc, HT_loc, LF_loc, QT_loc, IW_loc)
        for src, gat in ((KT_loc, KT_all), (V_loc, V_all), (HT_loc, HT_all)):
            for k, ch in enumerate(gat.chunks):
                p.collective(src[k * gat.S:(k + 1) * gat.S, :], ch, groups)
        p.collective(LF_loc, LF_all, groups)
        p.barrier()
        W = {k: (v if k in ("rel31", "biasT") else v[l]) for k, v in Wf.items()}
        emit_B(p, g, l, Xsrc, KT_all, V_all, HT_all, LF_all, QT_loc, IW_loc, HT_loc, NEGT, W, consts, x1, b_x1)
        emit_C(p, g, W, x1, b_x1, Xdst)
    p.finish()
    return nc


J_FULL = 16
FUSED = True
_CACHE = {}


def _loc(a, J):
    L = a.shape[0]
    v = a.reshape(J, 4, 128, -1)
    return [np.ascontiguousarray(v[:, c].reshape(J * 128, -1)) for c in range(4)]


def _unloc(parts, J):
    Dd = parts[0].shape[-1]
    out = np.empty((J, 4, 128, Dd), parts[0].dtype)
    for c in range(4):
        out[:, c] = parts[c].reshape(J, 128, Dd)
    return out.reshape(J * 4 * 128, Dd)


def kernel(x, rel_bias, w_in, forget_b, diff_lambda, diff_norm_g, conv_w, conv_b, conv_ln_g, conv_ln_b, w_out,
           ln1_g, ln1_b, router_w, router_b, w_gu, b_gu, w_down, b_down, ln2_g, ln2_b):
    J = J_FULL
    f32 = lambda a: np.ascontiguousarray(np.asarray(a, dtype=np.float32))
    x = f32(x)
    rel_bias = f32(rel_bias)
    P = dict(w_in=f32(w_in), forget_b=f32(forget_b), diff_lambda=f32(diff_lambda), diff_norm_g=f32(diff_norm_g), conv_w=f32(conv_w),
             conv_b=f32(conv_b), conv_ln_g=f32(conv_ln_g), conv_ln_b=f32(conv_ln_b), w_out=f32(w_out), ln1_g=f32(ln1_g), ln1_b=f32(ln1_b),
             router_w=f32(router_w), router_b=f32(router_b), w_gu=f32(w_gu), b_gu=f32(b_gu), w_down=f32(w_down), b_down=f32(b_down),
             ln2_g=f32(ln2_g), ln2_b=f32(ln2_b))
    B = x.shape[0]
    cs = [make_consts(c) for c in range(4)]
    biasT = [np.ascontiguousarray(rel_bias[bias_tile_index(c)].transpose(3, 0, 1, 2)) for c in range(4)]
    rel31 = np.ascontiguousarray(rel_bias[31])
    xs = [None] * 8
    for b in range(B):
        for c, part in enumerate(_loc(x[b], J)):
            xs[4 * b + c] = part
    cores = list(range(8))
    if FUSED:
        if "F" not in _CACHE:
            _CACHE["F"] = build_fused(J)
        in_maps = []
        for r in cores:
            m = {"X": xs[r], "w_in": P["w_in"], "forget_b": P["forget_b"], "biasT": biasT[r % 4], "rel31": rel31}
            for k in WSHAPES:
                if k not in m:
                    m[k] = P[k]
            for k, v in cs[r % 4].items():
                m["c_" + k] = v
            in_maps.append(m)
        res = run_bass_kernel_spmd(_CACHE["F"], in_maps, core_ids=cores).results
        xs = [np.asarray(res[r]["Xout"]) for r in cores]
    else:
        if "A" not in _CACHE:
            _CACHE["A"] = build_A(J)
            _CACHE["BC"] = build_BC(J)
        for l in range(DEPTH):
            li = 0.8 - 0.6 * math.exp(-0.3 * l)
            lamc = np.tile(np.array([[-li, 1.0 - li]], np.float32), (128, 1))
            in_maps = []
            for r in cores:
                m = {"X": xs[r], "w_in": P["w_in"][l], "forget_b": P["forget_b"][l]}
                for k, v in cs[r % 4].items():
                    m["c_" + k] = v
                in_maps.append(m)
            resA = run_bass_kernel_spmd(_CACHE["A"], in_maps, core_ids=cores).results
            in_maps = []
            for r in cores:
                b = r // 4
                m = {"X": xs[r], "lamc": lamc, "biasT": biasT[r % 4], "rel31": rel31}
                for nm in ("KT", "V", "HT", "LF"):
                    m[nm + "_all"] = np.concatenate([np.asarray(resA[4 * b + cc][nm + "_loc"]) for cc in range(4)], 0)
                for nm in ("QT_loc", "IW_loc", "HT_loc"):
                    m[nm] = np.asarray(resA[r][nm])
                for k in WSHAPES:
                    if k not in m:
                        m[k] = P[k][l]
                for k, v in cs[r % 4].items():
                    m["c_" + k] = v
                in_maps.append(m)
            resB = run_bass_kernel_spmd(_CACHE["BC"], in_maps, core_ids=cores).results
            xs = [np.asarray(resB[r]["Xout"]) for r in cores]
    out = np.stack([_unloc([xs[4 * b + c] for c in range(4)], J) for b in range(B)], 0)
    return out.astype(np.float32)
```

```python
import math
from contextlib import ExitStack

import numpy as np
import concourse.bass as bass
import concourse.mybir as mybir
from concourse.bass_utils import run_bass_kernel_spmd

F32 = mybir.dt.float32
BF16 = mybir.dt.bfloat16
ALU = mybir.AluOpType
AF = mybir.ActivationFunctionType

D = 1024
NIN = 3144
DEPTH = 4
NE = 32
ALPHA = (2 * DEPTH) ** 0.25
EPS = 1e-5
NEG = -30000.0
REPL = -1.0e30
C_FQ, C_FK, C_FV, C_FF = 0, 256, 512, 768
C_DQ, C_DK, C_DV = 772, 1028, 1284
C_SQ, C_SK, C_SV = 1540, 1796, 2052
C_IQ, C_IK, C_IW, C_CU = 2308, 2564, 2628, 2632
KT_ROWS = 832
NQCH = 10


class Buf:
    __slots__ = ("w", "r")

    def __init__(self):
        self.w = None
        self.r = {}


class Prog:
    NDS = 24

    def __init__(self, nc):
        self.nc = nc
        self.st = ExitStack()
        self.engs = {"pe": nc.tensor, "act": nc.scalar, "dve": nc.vector, "pool": nc.gpsimd, "sp": nc.sync}
        self.semobj = {}
        self.cnt = {}
        for k in self.engs:
            self.semobj[k] = self.st.enter_context(nc.semaphore("s_" + k))
            self.cnt[k] = 0
        self.seen = {k: {} for k in self.engs}
        self.dcnt = [0] * self.NDS
        for i in range(self.NDS):
            self.semobj[("d", i)] = self.st.enter_context(nc.semaphore("s_d%d" % i))
        self.dnext = 0
        self.nins = 0

    def sb(self, name, shape, dt, st=None):
        self.uid = getattr(self, "uid", 0) + 1
        return (st or self.st).enter_context(self.nc.sbuf_tensor("%s_%d" % (name, self.uid), shape, dt))

    def ps(self, name, shape, dt, st=None):
        return (st or self.st).enter_context(self.nc.psum_tensor(name, shape, dt))

    def _wait(self, eng, key, val):
        if self.seen[eng].get(key, 0) >= val:
            return
        self.engs[eng].wait_ge(self.semobj[key], val)
        self.seen[eng][key] = val
        self.nins += 1

    def _deps(self, eng, reads, writes, extra=None):
        need = {}

        def add(k, v, kind):
            if k == eng:
                if eng == "pe" or kind == "war":
                    return
            if need.get(k, 0) < v:
                need[k] = v
        if extra is not None:
            add(extra[0], extra[1], "raw")
        for b in reads:
            if b.w is not None:
                add(b.w[0], b.w[1], "raw")
        for b in writes:
            if b.w is not None:
                add(b.w[0], b.w[1], "waw")
            for k, v in b.r.items():
                add(k, v, "war")
        items = [(k, v) for k, v in need.items() if self.seen[eng].get(k, 0) < v]
        if not items:
            return None
        for k, v in items[:-1]:
            self._wait(eng, k, v)
        return items[-1]

    def _embed(self, eng, ins, last):
        if last is not None:
            ins._wait_ge(self.semobj[last[0]], last[1])
            self.seen[eng][last[0]] = last[1]

    def _mark(self, key, val, reads, writes):
        for b in reads:
            b.r[key] = val
        for b in writes:
            b.w = (key, val)
            b.r = {}

    def op(self, eng, fn, reads=(), writes=()):
        last = self._deps(eng, reads, writes)
        ins = fn(self.engs[eng])
        self._embed(eng, ins, last)
        self.cnt[eng] += 1
        ins.then_inc(self.semobj[eng], 1)
        self.nins += 1
        self._mark(eng, self.cnt[eng], reads, writes)

    def dma(self, eng, out, in_, reads=(), writes=()):
        i = self.dnext
        self.dnext = (i + 1) % self.NDS
        key = ("d", i)
        last = self._deps(eng, reads, writes, extra=(key, self.dcnt[i]) if self.dcnt[i] > 0 else None)
        ins = self.engs[eng].dma_start(out=out, in_=in_)
        self._embed(eng, ins, last)
        self.dcnt[i] += 16
        ins.then_inc(self.semobj[key], 16)
        self.nins += 1
        self._mark(key, self.dcnt[i], reads, writes)

    def collective(self, src, dst, groups):
        if "cc" not in self.semobj:
            self.semobj["cc"] = self.st.enter_context(self.nc.semaphore("s_cc"))
            self.cccnt = 0
        self.barrier(["pool"])
        ins = self.nc.gpsimd.collective_compute("AllGather", ALU.bypass, replica_groups=groups, ins=[src.opt()], outs=[dst.opt()])
        self.cccnt += 1
        ins.then_inc(self.semobj["cc"])
        self.nins += 1
        self._wait("pool", "cc", self.cccnt)

    def barrier(self, engs=None):
        engs = engs or list(self.engs)
        for e in engs:
            for k in self.engs:
                if k != e and self.cnt[k] > 0:
                    self._wait(e, k, self.cnt[k])
            for i in range(self.NDS):
                if self.dcnt[i] > 0:
                    self._wait(e, ("d", i), self.dcnt[i])
            if getattr(self, "cccnt", 0) > 0:
                self._wait(e, "cc", self.cccnt)

    def mm(self, out, lhsT, rhs, start, stop, reads, writes):
        self.op("pe", lambda e: e.matmul(out, lhsT=lhsT, rhs=rhs, start=start, stop=stop), reads, writes)

    def tr(self, out, in_, ident, reads, writes):
        self.op("pe", lambda e: e.transpose(out, in_, ident), reads, writes)

    def act(self, out, in_, func, reads, writes, bias=None, scale=1.0, accum=None):
        kw = {}
        if bias is not None:
            kw["bias"] = bias
        if accum is not None:
            kw["accum_out"] = accum
        self.op("act", lambda e: e.activation(out=out, in_=in_, func=func, scale=scale, **kw), reads, writes)

    def ts(self, eng, out, in0, s1, s2, op0, op1, reads, writes, accum=None):
        kw = {}
        if accum is not None:
            kw["accum_out"] = accum
        if op1 is None:
            self.op(eng, lambda e: e.tensor_scalar(out=out, in0=in0, scalar1=s1, scalar2=None, op0=op0, **kw), reads, writes)
        else:
            self.op(eng, lambda e: e.tensor_scalar(out=out, in0=in0, scalar1=s1, scalar2=s2, op0=op0, op1=op1, **kw), reads, writes)

    def tt(self, eng, out, in0, in1, op, reads, writes):
        self.op(eng, lambda e: e.tensor_tensor(out=out, in0=in0, in1=in1, op=op), reads, writes)

    def stt(self, eng, out, in0, scalar, in1, op0, op1, reads, writes):
        self.op(eng, lambda e: e.scalar_tensor_tensor(out=out, in0=in0, scalar=scalar, in1=in1, op0=op0, op1=op1), reads, writes)

    def copy(self, eng, out, in_, reads, writes):
        if eng == "act":
            self.op("act", lambda e: e.copy(out=out, in_=in_), reads, writes)
        else:
            self.op(eng, lambda e: e.tensor_copy(out=out, in_=in_), reads, writes)

    def memset(self, eng, ap, val, writes):
        self.op(eng, lambda e: e.memset(ap, val), (), writes)

    def finish(self):
        self.barrier(["sp"])
        self.st.close()


def _bufs(n):
    return [Buf() for _ in range(n)]


class Ctx:
    pass


class Gat:
    def __init__(self, R, mono=None, S=None, chunks=None):
        self.R, self.mono, self.S, self.chunks = R, mono, S, chunks

    def rows(self, c, r0, n):
        if self.mono is not None:
            return [(self.mono[c * self.R + r0:c * self.R + r0 + n, :], 0, n)]
        out = []
        r = r0
        while r < r0 + n:
            k, o = r // self.S, r % self.S
            m = min(self.S - o, r0 + n - r)
            out.append((self.chunks[k][c * self.S + o:c * self.S + o + m, :], r - r0, m))
            r += m
        return out


def setup_common(p, J, consts):
    nc = p.nc
    g = Ctx()
    g.J = J
    g.T = J * 128
    g.banks = [p.ps("bank%d" % i, [128, 512], F32) for i in range(8)]
    g.bb = _bufs(8)
    g.identf = p.sb("identf", [128, 128], F32)
    g.identb = p.sb("identb", [128, 128], BF16)
    g.b_ident = Buf()
    p.dma("sp", g.identf[:], consts["identf"], (), [g.b_ident])
    p.copy("dve", g.identb[:], g.identf[:], [g.b_ident], [g.b_ident])
    return g


def transpose_block(p, g, src, src_b, dst_bf, dst_b, blk, dst_f32=None, dst_f32_b=None, bank0=0):
    for half in range(2):
        bk = bank0 + half
        for q in range(4):
            dc = half * 4 + q
            p.tr(g.banks[bk][:, q * 128:(q + 1) * 128], src[:, dc * 128:(dc + 1) * 128], g.identf[:],
                 [src_b, g.b_ident], [g.bb[bk]])
        outv = dst_bf[:, half * 4:(half + 1) * 4, blk * 128:(blk + 1) * 128]
        inv = g.banks[bk][:].rearrange("p (q t) -> p q t", q=4)
        if dst_f32 is None:
            p.copy("act" if half == 0 else "dve", outv, inv, [g.bb[bk]], [dst_b])
        else:
            f32v = dst_f32[:, half * 4:(half + 1) * 4, :]
            p.copy("dve", f32v, inv, [g.bb[bk]], [dst_f32_b])
            p.copy("act", outv, f32v, [dst_f32_b], [dst_b])


def emit_A(p, g, X, w_in, forget_b, consts, KT_loc, V_loc, HT_loc, LF_loc, QT_loc, IW_loc):
    J, T = g.J, g.T
    st = ExitStack()
    xT = p.sb("a_xT", [128, 8, T], BF16, st)
    b_xT = Buf()
    win = p.sb("a_win", [128, 8, NIN], BF16, st)
    b_win = _bufs(8)
    xt = [p.sb("a_xt%d" % i, [128, D], F32, st) for i in range(2)]
    b_xt = _bufs(2)
    fb = p.sb("a_fb", [128, 4], F32, st)
    b_fb = Buf()
    dqm = p.sb("a_dqm", [128, 2], F32, st)
    b_dqm = Buf()
    p.dma("sp", fb[:], forget_b.partition_broadcast(128), (), [b_fb])
    p.dma("sp", dqm[:], consts["dqmask"], (), [b_dqm])
    w_r = w_in.rearrange("(dc q) f -> q dc f", q=128)
    for dc in range(8):
        p.dma("pool", win[:, dc, :], w_r[:, dc, :], (), [b_win[dc]])
    Xr = X.rearrange("(j q) d -> q j d", q=128)
    for j in range(J):
        s = j % 2
        p.dma("sp", xt[s][:], Xr[:, j, :], (), [b_xt[s]])
        transpose_block(p, g, xt[s], b_xt[s], xT, b_xT, j)

    ev = p.sb("a_ev", [128, 4, 512], BF16, st)
    b_ev = _bufs(4)
    evf = p.sb("a_evf", [128, 2, 512], F32, st)
    b_evf = _bufs(2)
    sg = p.sb("a_sg", [128, 2, 512], F32, st)
    b_sg = _bufs(2)
    ntg = T // 512 if T >= 512 else 1
    TG = min(512, T)
    state = {"bk": 2, "ev": 0, "evf": 0}

    def fm_chunk(col0, ncols, tg):
        bk = state["bk"]
        state["bk"] = 2 + (bk - 2 + 1) % 4
        for dc in range(8):
            p.mm(g.banks[bk][0:ncols, 0:TG], win[:, dc, col0:col0 + ncols], xT[:, dc, tg * TG:(tg + 1) * TG],
                 dc == 0, dc == 7, [b_win[dc], b_xT], [g.bb[bk]])
        return bk

    def ev_slot():
        s = state["ev"]
        state["ev"] = (s + 1) % 4
        return s

    for tg in range(ntg):
        tsl = slice(tg * TG, (tg + 1) * TG)
        qspec = [(0, C_FQ, None), (1, C_FQ + 128, None), (2, C_DQ, 0), (3, C_DQ + 128, 0), (4, C_DQ, 1), (5, C_DQ + 128, 1),
                 (6, C_SQ, None), (7, C_SQ + 128, None), (8, C_IQ, None), (9, C_IQ + 128, None)]
        cache = {}
        for qi, col0, m in qspec:
            if m == 1 and col0 in cache:
                bk = cache[col0]
            else:
                bk = fm_chunk(col0, 128, tg)
                cache[col0] = bk
            s = ev_slot()
            if m is None:
                p.copy("act", ev[:, s, 0:TG], g.banks[bk][:, 0:TG], [g.bb[bk]], [b_ev[s]])
            else:
                p.ts("dve", ev[:, s, 0:TG], g.banks[bk][:, 0:TG], dqm[:, m:m + 1], None, ALU.mult, None,
                     [g.bb[bk], b_dqm], [b_ev[s]])
            p.dma("sp", QT_loc[qi, :, tsl], ev[:, s, 0:TG], [b_ev[s]], ())
        for ki, col0, ncols in [(0, C_FK, 128), (1, C_FK + 128, 128), (2, C_DK, 128), (3, C_DK + 128, 128),
                                (4, C_SK, 128), (5, C_SK + 128, 128), (6, C_IK, 64)]:
            bk = fm_chunk(col0, ncols, tg)
            s = ev_slot()
            p.copy("act" if ki % 2 == 0 else "dve", ev[0:ncols, s, 0:TG], g.banks[bk][0:ncols, 0:TG], [g.bb[bk]], [b_ev[s]])
            p.dma("sp", KT_loc[ki * 128:ki * 128 + ncols, tsl], ev[0:ncols, s, 0:TG], [b_ev[s]], ())
        for cc in range(2):
            bka = fm_chunk(C_CU + cc * 128, 128, tg)
            bkg = fm_chunk(C_CU + 256 + cc * 128, 128, tg)
            s = state["evf"]
            state["evf"] = 1 - s
            p.act(sg[:, s, 0:TG], g.banks[bkg][:, 0:TG], AF.Sigmoid, [g.bb[bkg]], [b_sg[s]])
            p.tt("dve", evf[:, s, 0:TG], g.banks[bka][:, 0:TG], sg[:, s, 0:TG], ALU.mult, [g.bb[bka], b_sg[s]], [b_evf[s]])
            p.dma("sp", HT_loc[cc * 128:(cc + 1) * 128, tsl], evf[:, s, 0:TG], [b_evf[s]], ())

    vt = p.sb("a_vt", [128, 2, 768], BF16, st)
    b_vt = _bufs(2)
    lf = p.sb("a_lf", [128, J, 4], F32, st)
    b_lf = Buf()
    iw = p.sb("a_iw", [128, J, 4], F32, st)
    b_iw = Buf()
    for j in range(J):
        s = j % 2
        lt = slice(j * 128, (j + 1) * 128)
        for gi, col0 in enumerate([C_FV, C_DV]):
            for dc in range(8):
                p.mm(g.banks[0][:, gi * 256:(gi + 1) * 256], xT[:, dc, lt], win[:, dc, col0:col0 + 256], dc == 0, dc == 7,
                     [b_win[dc], b_xT], [g.bb[0]])
        for dc in range(8):
            p.mm(g.banks[1][:, 0:256], xT[:, dc, lt], win[:, dc, C_SV:C_SV + 256], dc == 0, dc == 7, [b_win[dc], b_xT], [g.bb[1]])
        for dc in range(8):
            p.mm(g.banks[1][:, 256:260], xT[:, dc, lt], win[:, dc, C_FF:C_FF + 4], dc == 0, dc == 7, [b_win[dc], b_xT], [g.bb[1]])
        for dc in range(8):
            p.mm(g.banks[1][:, 260:264], xT[:, dc, lt], win[:, dc, C_IW:C_IW + 4], dc == 0, dc == 7, [b_win[dc], b_xT], [g.bb[1]])
        p.copy("act", vt[:, s, 0:512], g.banks[0][:, 0:512], [g.bb[0]], [b_vt[s]])
        p.copy("dve", vt[:, s, 512:768], g.banks[1][:, 0:256], [g.bb[1]], [b_vt[s]])
        p.tt("dve", lf[:, j, :], g.banks[1][:, 256:260], fb[:], ALU.add, [g.bb[1], b_fb], [b_lf])
        p.copy("dve", iw[:, j, :], g.banks[1][:, 260:264], [g.bb[1]], [b_iw])
        p.dma("sp", V_loc[lt, :], vt[:, s, :], [b_vt[s]], ())
    lfv = lf[:].rearrange("p j h -> p (j h)")
    p.act(lfv, lfv, AF.Exp, [b_lf], [b_lf], scale=-1.0)
    p.act(lfv, lfv, AF.Ln, [b_lf], [b_lf], bias=1.0)
    p.ts("dve", lfv, lfv, -1.0, None, ALU.mult, None, [b_lf], [b_lf])
    p.tr(g.banks[2][0:J * 4, 0:128], lfv, g.identf[:], [b_lf, g.b_ident], [g.bb[2]])
    lft = p.sb("a_lft", [J * 4, 128], F32, st)
    b_lft = Buf()
    p.copy("dve", lft[:], g.banks[2][0:J * 4, 0:128], [g.bb[2]], [b_lft])
    p.dma("sp", LF_loc, lft[:], [b_lft], ())
    p.dma("sp", IW_loc, iw[:].rearrange("p j h -> p (j h)"), [b_iw], ())
    p.barrier()
    st.close()


def ln_rows(p, g, z, b_z, out, b_out, gt, bt, b_gb, tmp, b_tmp, st4, b_st4):
    p.memset("dve", st4[:, 0:2], 0.0, [b_st4])
    p.ts("dve", tmp, z, 1.0, 0.0, ALU.mult, ALU.add, [b_z], [b_tmp, b_st4], accum=st4[:, 0:1])
    p.tt("pool", tmp, z, z, ALU.mult, [b_z], [b_tmp])
    p.ts("dve", tmp, tmp, 1.0, 0.0, ALU.mult, ALU.add, [b_tmp], [b_tmp, b_st4], accum=st4[:, 1:2])
    p.ts("dve", st4[:, 0:1], st4[:, 0:1], 1.0 / D, None, ALU.mult, None, [b_st4], [b_st4])
    p.tt("dve", st4[:, 2:3], st4[:, 0:1], st4[:, 0:1], ALU.mult, [b_st4], [b_st4])
    p.stt("dve", st4[:, 1:2], st4[:, 1:2], 1.0 / D, st4[:, 2:3], ALU.mult, ALU.subtract, [b_st4], [b_st4])
    p.act(st4[:, 1:2], st4[:, 1:2], AF.Ln, [b_st4], [b_st4], bias=EPS)
    p.act(st4[:, 1:2], st4[:, 1:2], AF.Exp, [b_st4], [b_st4], scale=-0.5)
    p.stt("dve", st4[:, 2:3], st4[:, 0:1], -1.0, st4[:, 1:2], ALU.mult, ALU.mult, [b_st4], [b_st4])
    p.act(tmp, z, AF.Identity, [b_z, b_st4], [b_tmp], bias=st4[:, 2:3], scale=st4[:, 1:2])
    p.tt("dve", tmp, tmp, gt, ALU.mult, [b_tmp, b_gb], [b_tmp])
    p.tt("pool", out, tmp, bt, ALU.add, [b_tmp, b_gb], [b_out])


NOMASK = False


def emit_B(p, g, l, X, KT_all, V_all, HT_all, LF_all, QT_loc, IW_loc, HT_loc, NEGT, W, consts, x1, b_x1, lamc=None, dbg=None):
    J, T = g.J, g.T
    NB = 4 * J
    L = NB * 128
    lambda_init = 0.8 - 0.6 * math.exp(-0.3 * l)
    stB = ExitStack()
    yT = p.sb("b_yT", [128, 8, T], BF16, stB)
    b_yT = Buf()
    sel = p.sb("b_sel", [128, 4], F32, stB)
    b_c = Buf()
    p.dma("sp", sel[:], consts["sel"], (), [b_c])
    maskT = p.sb("b_maskT", [128, 4, 128], F32, stB)
    p.dma("sp", maskT[:], consts["maskT"].rearrange("c k q -> k c q"), (), [b_c])
    b31 = p.sb("b_b31", [128, 8], F32, stB)
    p.dma("sp", b31[:], W["rel31"].partition_broadcast(128), (), [b_c])
    ones = p.sb("b_ones", [128, 128], F32, stB)
    p.memset("pool", ones[:], 1.0, [b_c])

    st = ExitStack()
    hc = p.sb("c_hc", [128, 2, J, 158], F32, st)
    b_hc = Buf()
    cand = p.sb("c_cand", [128, 4, 2, J, 30], F32, st)
    b_cand = Buf()
    cw = p.sb("c_cw", [128, 2, 31], F32, st)
    cpar = p.sb("c_par", [128, 2, 3], F32, st)
    b_cw = Buf()
    cwr = p.sb("c_cwr", [31, 256], F32, st)
    p.dma("sp", cwr[:], W["conv_w"], (), [b_cw])
    for cc in range(2):
        p.tr(g.banks[0][:, cc * 32:cc * 32 + 31], cwr[:, cc * 128:(cc + 1) * 128], g.identf[0:31, 0:31], [b_cw, g.b_ident], [g.bb[0]])
    p.copy("dve", cw[:], g.banks[0][:, 0:64].rearrange("c (cc w) -> c cc w", cc=2)[:, :, 0:31], [g.bb[0]], [b_cw])
    for i, nm in enumerate(["conv_b", "conv_ln_g", "conv_ln_b"]):
        for cc in range(2):
            p.dma("sp", cpar[:, cc, i:i + 1], W[nm].rearrange("(cc c o) -> cc c o", c=128, o=1)[cc], (), [b_cw])
    p.memset("pool", cand[:], 0.0, [b_cand])
    for cc in range(2):
        p.dma("sp", hc[:, cc, :, 30:158], HT_loc[cc * 128:(cc + 1) * 128, :].rearrange("c (j t) -> c j t", t=128), (), [b_hc])
        for c in range(4):
            if c >= 1:
                for ap_, off, n in HT_all.rows(c - 1, cc * 128, 128):
                    p.dma("sp", cand[off:off + n, c, cc, :, :], ap_.rearrange("c (j t) -> c j t", t=128)[:, :, 98:128], (), [b_cand])
            elif J > 1:
                for ap_, off, n in HT_all.rows(3, cc * 128, 128):
                    p.dma("sp", cand[off:off + n, c, cc, 1:J, :], ap_.rearrange("c (j t) -> c j t", t=128)[:, 0:J - 1, 98:128], (), [b_cand])
    for cc in range(2):
        e = "dve"
        p.ts(e, hc[:, cc, :, 0:30], cand[:, 0, cc, :, :], sel[:, 0:1], None, ALU.mult, None, [b_cand, b_c], [b_hc])
        for c in range(1, 4):
            p.stt(e, hc[:, cc, :, 0:30], cand[:, c, cc, :, :], sel[:, c:c + 1], hc[:, cc, :, 0:30], ALU.mult, ALU.add,
                  [b_cand, b_c, b_hc], [b_hc])
    cacc = p.sb("c_acc", [128, 2, J, 128], F32, st)
    b_cacc = _bufs(2)
    for cc in range(2):
        e = "dve"
        p.ts(e, cacc[:, cc], hc[:, cc, :, 0:128], cw[:, cc, 0:1], cpar[:, cc, 0:1], ALU.mult, ALU.add, [b_hc, b_cw], [b_cacc[cc]])
        for w in range(1, 31):
            p.stt(e, cacc[:, cc], hc[:, cc, :, w:w + 128], cw[:, cc, w:w + 1], cacc[:, cc], ALU.mult, ALU.add,
                  [b_hc, b_cw, b_cacc[cc]], [b_cacc[cc]])
    TG = min(512, T)
    csq = p.sb("c_sq", [128, 2, TG], F32, st)
    b_csq = Buf()
    cmean = p.sb("c_mean", [128, TG], F32, st)
    crstd = p.sb("c_rstd", [128, TG], F32, st)
    b_cst = Buf()
    cd = p.sb("c_d", [128, 2, TG], F32, st)
    b_cd = _bufs(2)
    p.ts("dve", ones[:], ones[:], 1.0 / 256.0, None, ALU.mult, None, [b_c], [b_c])
    for tg in range(T // TG):
        flat = [cacc[:, cc].rearrange("c j t -> c (j t)")[:, tg * TG:(tg + 1) * TG] for cc in range(2)]
        for cc in range(2):
            p.tt("pool", csq[:, cc, :], flat[cc], flat[cc], ALU.mult, [b_cacc[cc]], [b_csq])
        for cc in range(2):
            p.mm(g.banks[0][:, 0:TG], ones[:], flat[cc], cc == 0, cc == 1, [b_c, b_cacc[cc]], [g.bb[0]])
        for cc in range(2):
            p.mm(g.banks[1][:, 0:TG], ones[:], csq[:, cc, :], cc == 0, cc == 1, [b_c, b_csq], [g.bb[1]])
        p.copy("act", cmean[:], g.banks[0][:, 0:TG], [g.bb[0]], [b_cst])
        p.tt("dve", crstd[:], cmean[:], cmean[:], ALU.mult, [b_cst], [b_cst])
        p.tt("dve", crstd[:], g.banks[1][:, 0:TG], crstd[:], ALU.subtract, [g.bb[1], b_cst], [b_cst])
        p.act(crstd[:], crstd[:], AF.Ln, [b_cst], [b_cst], bias=EPS)
        p.act(crstd[:], crstd[:], AF.Exp, [b_cst], [b_cst], scale=-0.5)
        for cc in range(2):
            p.tt("dve", cd[:, cc, :], flat[cc], cmean[:], ALU.subtract, [b_cacc[cc], b_cst], [b_cd[cc]])
            p.tt("dve", cd[:, cc, :], cd[:, cc, :], crstd[:], ALU.mult, [b_cd[cc], b_cst], [b_cd[cc]])
            p.ts("dve", cd[:, cc, :], cd[:, cc, :], cpar[:, cc, 1:2], cpar[:, cc, 2:3], ALU.mult, ALU.add, [b_cd[cc], b_cw], [b_cd[cc]])
            p.act(csq[:, cc, :], cd[:, cc, :], AF.Sigmoid, [b_cd[cc]], [b_csq])
            p.tt("dve", yT[:, 6 + cc, tg * TG:(tg + 1) * TG], cd[:, cc, :], csq[:, cc, :], ALU.mult, [b_cd[cc], b_csq], [b_yT])
    p.ts("dve", ones[:], ones[:], 256.0, None, ALU.mult, None, [b_c], [b_c])
    p.barrier()
    st.close()

    R = 4 * J * 4
    cn = p.sb("f_cn", [128, J, 4, 4], F32, stB)
    offs = p.sb("f_offs", [128, J, 4, 4], F32, stB)
    offl = p.sb("f_offl", [128, J, 4], F32, stB)
    vs = p.sb("f_vs", [128, J, 4, 4], F32, stB)
    bjr = p.sb("f_bjr", [128, J, J, 4], F32, stB)
    lam = p.sb("d_lam", [128, 4], F32, stB)
    dng = p.sb("d_g", [128, 64], F32, stB)
    b_f = Buf()
    st = ExitStack()
    triu = p.sb("f_triu", [128, 128], F32, st)
    p.dma("sp", triu[:], consts["triu"], (), [b_f])
    nt = (R + 127) // 128
    lfr = p.sb("f_lfr", [128, nt, 128], F32, st)
    lfn = p.sb("f_lfn", [128, R], F32, st)
    sc_a = p.sb("f_sca", [128, NB, 4], F32, st)
    sc_b = p.sb("f_scb", [128, NB, 4], F32, st)
    for t in range(nt):
        rr = min(128, R - t * 128)
        p.dma("sp", lfr[0:rr, t, :], LF_all[t * 128:t * 128 + rr, :], (), [b_f])
        p.tr(g.banks[0][:, t * 128:t * 128 + rr], lfr[0:rr, t, :], g.identf[0:rr, 0:rr], [b_f, g.b_ident], [g.bb[0]])
    p.copy("dve", lfn[:], g.banks[0][:, 0:R], [g.bb[0]], [b_f])
    p.mm(g.banks[1][:, 0:R], triu[:], lfn[:], True, True, [b_f], [g.bb[1]])
    p.memset("pool", ones[:], 1.0, [b_c])
    p.mm(g.banks[2][:, 0:R], ones[:], lfn[:], True, True, [b_f, b_c], [g.bb[2]])
    nat = lambda ap: ap.rearrange("p (c j h) -> p j c h", c=4, j=J)
    p.copy("dve", sc_a[:].rearrange("p (j c) h -> p j c h", c=4), nat(g.banks[2][:, 0:R]), [g.bb[2]], [b_f])
    p.copy("dve", offs[:], nat(g.banks[2][:, 0:R]), [g.bb[2]], [b_f])
    a, b = sc_a, sc_b
    s = 1
    while s < NB:
        p.copy("dve", b[:, 0:s, :], a[:, 0:s, :], [b_f], [b_f])
        p.tt("dve", b[:, s:NB, :], a[:, s:NB, :], a[:, 0:NB - s, :], ALU.add, [b_f], [b_f])
        a, b = b, a
        s *= 2
    offs_f = offs[:].rearrange("p j c h -> p (j c) h")
    p.tt("dve", offs_f, a[:], offs_f, ALU.subtract, [b_f], [b_f])
    p.tt("dve", cn[:], nat(g.banks[1][:, 0:R]), offs[:], ALU.add, [g.bb[1], b_f], [b_f])
    p.ts("dve", offl[:], offs[:, :, 0, :], sel[:, 0:1], None, ALU.mult, None, [b_f, b_c], [b_f])
    for c in range(1, 4):
        p.stt("dve", offl[:], offs[:, :, c, :], sel[:, c:c + 1], offl[:], ALU.mult, ALU.add, [b_f, b_c], [b_f])
    for c in range(4):
        p.tt("dve", vs[:, :, c, :], offs[:, :, 0, :], cn[:, :, c, :], ALU.subtract, [b_f], [b_f])
    vsf = vs[:].rearrange("p j c h -> p (j c h)")
    p.act(vsf, vsf, AF.Exp, [b_f], [b_f])
    for j in range(J):
        for h in range(4):
            p.ts("dve", bjr[:, j, :, h], offs[:, :, 0, h], -1.0, offl[:, j, h:h + 1], ALU.mult, ALU.add, [b_f], [b_f])
    dlf = p.sb("d_dl", [128, 128], F32, st)
    p.dma("sp", dlf[:], W["diff_lambda"].rearrange("a b -> (a b)").partition_broadcast(128), (), [b_f])
    dl = dlf[:].rearrange("p (a b) -> p a b", a=4)
    p.memset("dve", lam[:], 0.0, [b_f])
    p.dma("sp", dng[:], W["diff_norm_g"].partition_broadcast(128), (), [b_f])
    p.tt("dve", dl[:, 0, :], dl[:, 0, :], dl[:, 1, :], ALU.mult, [b_f], [b_f])
    p.tt("dve", dl[:, 2, :], dl[:, 2, :], dl[:, 3, :], ALU.mult, [b_f], [b_f])
    p.ts("dve", dl[:, 1, :], dl[:, 0, :], 1.0, 0.0, ALU.mult, ALU.add, [b_f], [b_f], accum=lam[:, 0:1])
    p.ts("dve", dl[:, 3, :], dl[:, 2, :], 1.0, 0.0, ALU.mult, ALU.add, [b_f], [b_f], accum=lam[:, 1:2])
    p.act(lam[:, 0:2], lam[:, 0:2], AF.Exp, [b_f], [b_f])
    p.tt("dve", lam[:, 2:3], lam[:, 1:2], lam[:, 0:1], ALU.subtract, [b_f], [b_f])
    if lamc is None:
        p.ts("dve", lam[:, 2:3], lam[:, 2:3], -lambda_init, None, ALU.add, None, [b_f], [b_f])
        p.ts("dve", dng[:], dng[:], 1.0 - lambda_init, None, ALU.mult, None, [b_f], [b_f])
    else:
        lc = p.sb("d_lc", [128, 2], F32, st)
        p.dma("sp", lc[:], lamc, (), [b_f])
        p.ts("dve", lam[:, 2:3], lam[:, 2:3], lc[:, 0:1], None, ALU.add, None, [b_f], [b_f])
        p.ts("dve", dng[:], dng[:], lc[:, 1:2], None, ALU.mult, None, [b_f], [b_f])
    p.barrier()
    st.close()

    stA = ExitStack()
    ALIAS = (J == 16)
    x1flat = x1[:].rearrange("p a b -> p (a b)")
    if ALIAS:
        kt = [x1flat[:, i * 4096:(i + 1) * 4096].bitcast(BF16) for i in range(2)]
        vt = [x1flat[:, 8192:8192 + 4160].bitcast(BF16).rearrange("p (a b c) -> p a b c", a=NB, b=2, c=65),
              p.sb("a_v1", [128, NB, 2, 65], BF16, stA)]
    else:
        kt = [p.sb("a_kt%d" % i, [128, L], BF16, stA) for i in range(2)]
        vt = [p.sb("a_v%d" % i, [128, NB, 2, 65], BF16, stA) for i in range(2)]
    qt = [p.sb("a_q%d" % i, [128, 2, T], BF16, stA) for i in range(2)]
    bti = [p.sb("a_bt%d" % i, [128, 2, 8, 128], F32, stA) for i in range(2)]
    b_kv = _bufs(2)
    pT = p.sb("a_pT", [128, 4, 512], BF16, stA)
    b_pT = _bufs(4)
    ssb = p.sb("a_ssb", [128, 2, 512], F32, stA)
    b_ssb = _bufs(2)
    ysb = p.sb("a_ysb", [128, 2, 128], F32, stA)
    b_ysb = _bufs(2)
    yd = p.sb("a_yd", [128, 2, 66], F32, stA)
    b_yd = _bufs(2)
    rec = p.sb("a_rec", [128, 8], F32, stA)
    b_rec = Buf()
    for i in range(2):
        p.memset("pool", vt[i][:, :, :, 64:65], 1.0, [b_kv[i]])
    rot = {"s": 0, "pt": 0, "ss": 0, "o": 0}
    SB = [0, 1, 2]
    OB = [3, 4, 5, 6]
    TB = 7

    def load_pair(slot, kind, pr):
        kch = kind * 2 + pr
        for c in range(4):
            for ap_, off, n in KT_all.rows(c, kch * 128, 128):
                dst = kt[slot][off:off + n, :].rearrange("r (j c t) -> r j c t", c=4, t=128)[:, :, c, :]
                p.dma("sp", dst, ap_.rearrange("r (j t) -> r j t", t=128), (), [b_kv[slot]])
            for hh in range(2):
                c0 = kind * 256 + pr * 128 + hh * 64
                for j_ in range(J):
                    for ap_, off, n in V_all.rows(c, j_ * 128, 128):
                        p.dma("sp", vt[slot][off:off + n, 4 * j_ + c, hh, 0:64], ap_[:, c0:c0 + 64], (), [b_kv[slot]])
        if kind == 0:
            p.dma("sp", qt[slot][:, 0, :], QT_loc[pr], (), [b_kv[slot]])
            for hh in range(2):
                h = pr * 2 + hh
                vsh = vs[:].rearrange("p j c h -> p (j c) h")[:, :, h:h + 1]
                p.tt("pool", vt[slot][:, :, hh, 0:64], vt[slot][:, :, hh, 0:64], vsh.to_broadcast([128, NB, 64]), ALU.mult,
                     [b_f, b_kv[slot]], [b_kv[slot]])
                p.copy("pool", vt[slot][:, :, hh, 64:65], vsh, [b_f, b_kv[slot]], [b_kv[slot]])
        elif kind == 1:
            p.dma("sp", qt[slot][:, 0, :], QT_loc[2 + pr], (), [b_kv[slot]])
            p.dma("sp", qt[slot][:, 1, :], QT_loc[4 + pr], (), [b_kv[slot]])
        else:
            p.dma("sp", qt[slot][:, 0, :], QT_loc[6 + pr], (), [b_kv[slot]])
        if kind > 0:
            hb = (kind - 1) * 4 + pr * 2
            p.dma("sp", bti[slot][:], W["biasT"][hb:hb + 2].rearrange("h t k q -> k h t q"), (), [b_kv[slot]])
            for hh in range(2):
                p.tt("pool", bti[slot][:, hh, 0:4, :], bti[slot][:, hh, 0:4, :], maskT[:], ALU.add, [b_kv[slot], b_c], [b_kv[slot]])
            p.memset("pool", vt[slot][:, :, :, 64:65], 1.0, [b_kv[slot]])

    def run_maps(slot, kind, pr, j, ngT=None, b_ng=None):
        nrow = j + 1
        if kind == 1:
            maps = [(hh, m) for hh in range(2) for m in range(2)]
        else:
            maps = [(hh, 0) for hh in range(2)]
        scale = (32 ** -0.5) if kind == 1 else 0.125
        obank = {}
        items = [(mi, r) for mi in range(len(maps)) for r in range(nrow)]
        pend = {}

        def qk(it):
            mi, r = it
            hh, m = maps[mi]
            sbk = SB[rot["s"] % 3]
            rot["s"] += 1
            rows = slice(hh * 64, hh * 64 + 64)
            for c in range(4):
                blk = 4 * r + c
                p.mm(g.banks[sbk][:, c * 128:(c + 1) * 128], kt[slot][rows, blk * 128:(blk + 1) * 128],
                     qt[slot][rows, m, j * 128:(j + 1) * 128], True, True, [b_kv[slot]], [g.bb[sbk]])
            ps_ = rot["pt"] % 4
            rot["pt"] += 1
            h = pr * 2 + hh
            if kind == 0:
                bias = bjr[:, j, r, h:h + 1]
                near = (r == j)
                bread = [b_f]
            else:
                gh = (kind - 1) * 4 + h
                bias = b31[:, gh:gh + 1]
                near = (r >= j - 1)
                bread = [b_c]
            if near:
                ss = rot["ss"] % 2
                rot["ss"] += 1
                for c in range(4):
                    if kind == 0:
                        tile_ = maskT[:, c, :]
                        rd = [b_c]
                    else:
                        tile_ = bti[slot][:, hh, (0 if r == j else 4) + c, :]
                        rd = [b_kv[slot]]
                    p.stt("dve", ssb[:, ss, c * 128:(c + 1) * 128], g.banks[sbk][:, c * 128:(c + 1) * 128], scale, tile_,
                          ALU.mult, ALU.add, [g.bb[sbk]] + rd, [b_ssb[ss]])
                if kind == 0:
                    p.act(pT[:, ps_, :], ssb[:, ss, :], AF.Exp, [b_ssb[ss]] + bread, [b_pT[ps_]], bias=bias)
                else:
                    p.act(pT[:, ps_, :], ssb[:, ss, :], AF.Exp, [b_ssb[ss]], [b_pT[ps_]])
            else:
                p.act(pT[:, ps_, :], g.banks[sbk][:, :], AF.Exp, [g.bb[sbk]] + bread, [b_pT[ps_]], bias=bias, scale=scale)
            if kind == 2:
                p.tt("pool", pT[:, ps_, :], pT[:, ps_, :], ngT[:, 4 * r:4 * r + 4, :].rearrange("k c q -> k (c q)"), ALU.mult,
                     [b_pT[ps_], b_ng], [b_pT[ps_]])
            pend[it] = ps_

        def pv(it):
            mi, r = it
            hh, m = maps[mi]
            ps_ = pend.pop(it)
            if r == 0:
                obank[mi] = OB[rot["o"] % 4]
                rot["o"] += 1
            ob = obank[mi]
            for c in range(4):
                blk = 4 * r + c
                p.mm(g.banks[ob][:, 0:65], pT[:, ps_, c * 128:(c + 1) * 128], vt[slot][:, blk, hh, :],
                     r == 0 and c == 0, r == nrow - 1 and c == 3, [b_pT[ps_], b_kv[slot]], [g.bb[ob]])
            if r == nrow - 1:
                finalize(mi)

        def finalize(mi):
            hh, m = maps[mi]
            ob = obank[mi]
            ys = (j * 2 + pr) % 2
            if kind != 1:
                col = mi
                p.op("dve", lambda e: e.reciprocal(out=rec[:, col:col + 1], in_=g.banks[ob][:, 64:65]), [g.bb[ob]], [b_rec])
                p.ts("dve", ysb[:, ys, hh * 64:(hh + 1) * 64], g.banks[ob][:, 0:64], rec[:, col:col + 1], None, ALU.mult, None,
                     [g.bb[ob], b_rec], [b_ysb[ys]])
            else:
                col = mi
                p.op("dve", lambda e: e.reciprocal(out=rec[:, col:col + 1], in_=g.banks[ob][:, 64:65]), [g.bb[ob]], [b_rec])
                if m == 0:
                    p.ts("dve", yd[:, hh, 0:64], g.banks[ob][:, 0:64], rec[:, col:col + 1], None, ALU.mult, None,
                         [g.bb[ob], b_rec], [b_yd[hh]])
                else:
                    p.ts("dve", ssb[:, 0, 0:64], g.banks[ob][:, 0:64], rec[:, col:col + 1], lam[:, 2:3], ALU.mult, ALU.mult,
                         [g.bb[ob], b_rec, b_f], [b_ssb[0]])
                    p.tt("dve", yd[:, hh, 0:64], yd[:, hh, 0:64], ssb[:, 0, 0:64], ALU.add, [b_ssb[0], b_yd[hh]], [b_yd[hh]])
                    p.tt("dve", ssb[:, 0, 64:128], yd[:, hh, 0:64], yd[:, hh, 0:64], ALU.mult, [b_yd[hh]], [b_ssb[0]])
                    p.memset("dve", yd[:, hh, 64:65], 0.0, [b_yd[hh]])
                    p.ts("dve", ssb[:, 0, 64:128], ssb[:, 0, 64:128], 1.0 / 64, 0.0, ALU.mult, ALU.add, [b_ssb[0]], [b_ssb[0], b_yd[hh]],
                         accum=yd[:, hh, 64:65])
                    p.act(yd[:, hh, 64:65], yd[:, hh, 64:65], AF.Ln, [b_yd[hh]], [b_yd[hh]], bias=EPS)
                    p.act(yd[:, hh, 64:65], yd[:, hh, 64:65], AF.Exp, [b_yd[hh]], [b_yd[hh]], scale=-0.5)
                    p.stt("dve", ysb[:, ys, hh * 64:(hh + 1) * 64], yd[:, hh, 0:64], yd[:, hh, 64:65], dng[:], ALU.mult, ALU.mult,
                          [b_yd[hh], b_f], [b_ysb[ys]])
            last = (mi == len(maps) - 1)
            if dbg is not None and kind == 1 and pr == 0 and j == 0 and mi == 1:
                def dd(name, ap, shape, bufs):
                    t_ = p.nc.dram_tensor("dbg_" + name, shape, F32, kind="ExternalOutput").ap()
                    p.dma("sp", t_, ap, bufs, ())
                dd("yd", yd[:], [128, 2, 66], [b_yd[0], b_yd[1]])
                dd("rec", rec[:], [128, 8], [b_rec])
                dd("bti", bti[slot][:, 0, 0, :], [128, 128], [b_kv[slot]])
                dd("lam", lam[:], [128, 4], [b_f])
                dd("dng", dng[:], [128, 64], [b_f])
                dd("ssb", ssb[:], [128, 2, 512], [b_ssb[0], b_ssb[1]])
                dd("ysb", ysb[:], [128, 2, 128], [b_ysb[0], b_ysb[1]])
            if last:
                p.tr(g.banks[TB][:, 0:128], ysb[:, ys, :], g.identf[:], [b_ysb[ys], g.b_ident], [g.bb[TB]])
                p.copy("act", yT[:, kind * 2 + pr, j * 128:(j + 1) * 128], g.banks[TB][:, 0:128], [g.bb[TB]], [b_yT])

        n = len(items)
        for i in range(n + 1):
            if i < n:
                qk(items[i])
            if i > 0:
                pv(items[i - 1])

    seq = [(0, 0), (0, 1), (1, 0), (1, 1)]
    load_pair(0, *seq[0])
    for i, (kind, pr) in enumerate(seq):
        if i + 1 < len(seq):
            load_pair((i + 1) % 2, *seq[i + 1])
        for j in range(J):
            run_maps(i % 2, kind, pr, j)
    p.barrier()

    st = ExitStack()
    if ALIAS:
        sc = x1flat[:, 0:8192]
        ik2 = x1flat[:, 8192:12288].bitcast(BF16)
        iq = x1flat[:, 12288:14336].bitcast(BF16).rearrange("p (a b) -> p a b", a=2)
    else:
        sc = p.sb("i_sc", [128, L], F32, st)
        ik2 = p.sb("i_ik2", [128, L], BF16, st)
        iq = p.sb("i_iq", [128, 2, T], BF16, st)
    b_sc = Buf()
    iwt = p.sb("i_iw", [128, J, 4], F32, st)
    mq = p.sb("i_mq", [128, 4, 128], F32, st)
    mqa = p.sb("i_mqa", [128, 4, 128], F32, st)
    b_i = Buf()
    if ALIAS:
        rl = x1flat[:, 14336:16384].rearrange("p (a b) -> p a b", a=4)
    else:
        rl = p.sb("i_rl", [128, 4, 512], F32, st)
    b_rl = _bufs(4)
    m8 = p.sb("i_m8", [128, 8], F32, st)
    b_m8 = Buf()
    nev = p.sb("i_nev", [128, 2, 512], BF16, st)
    b_nev = _bufs(2)
    for c in range(4):
        for ap_, off, n in KT_all.rows(c, 768, 64):
            for half in range(2):
                dst = ik2[half * 64 + off:half * 64 + off + n, :].rearrange("r (j c t) -> r j c t", c=4, t=128)[:, :, c, :]
                p.dma("sp", dst, ap_.rearrange("r (j t) -> r j t", t=128), (), [b_i])
    p.dma("sp", iq[:, 0, :], QT_loc[8], (), [b_i])
    p.dma("sp", iq[:, 1, :], QT_loc[9], (), [b_i])
    p.dma("sp", iwt[:].rearrange("p j h -> p (j h)"), IW_loc, (), [b_i])
    p.dma("sp", mq[:], consts["maskQ"].rearrange("c q k -> q c k"), (), [b_i])
    p.ts("dve", mqa[:], mq[:], 1.0e30, None, ALU.mult, None, [b_i], [b_i])
    for j in range(J):
        Nk = 512 * (j + 1)
        for r in range(j + 1):
            ks = slice(r * 512, (r + 1) * 512)
            for hi in range(4):
                rows = slice((hi % 2) * 64, (hi % 2) * 64 + 64)
                bk = hi
                p.mm(g.banks[bk][:, :], iq[rows, hi // 2, j * 128:(j + 1) * 128], ik2[rows, ks], True, True, [b_i], [g.bb[bk]])
                p.act(rl[:, hi, :], g.banks[bk][:, :], AF.Relu, [g.bb[bk]], [b_rl[hi]])
                if hi == 0:
                    p.ts("dve", sc[:, ks], rl[:, 0, :], iwt[:, j, 0:1], None, ALU.mult, None, [b_rl[0], b_i], [b_sc])
                else:
                    p.stt("dve", sc[:, ks], rl[:, hi, :], iwt[:, j, hi:hi + 1], sc[:, ks], ALU.mult, ALU.add, [b_rl[hi], b_i, b_sc], [b_sc])
            if r == j:
                p.tt("dve", sc[:, ks], sc[:, ks], mqa[:].rearrange("q c k -> q (c k)"), ALU.add, [b_sc, b_i], [b_sc])
        for it in range(32):
            p.op("dve", lambda e: e.max(out=m8[:], in_=sc[:, 0:Nk]), [b_sc], [b_m8])
            p.op("dve", lambda e: e.match_replace(out=sc[:, 0:Nk], in_to_replace=m8[:], in_values=sc[:, 0:Nk], imm_value=-3.0e38),
                 [b_sc, b_m8], [b_sc])
        p.ts("dve", sc[:, 0:Nk], sc[:, 0:Nk], -1.0e35, 1.0, ALU.is_le, ALU.subtract, [b_sc], [b_sc])
        ks = slice(j * 512, (j + 1) * 512)
        p.tt("dve", sc[:, ks], sc[:, ks], mq[:].rearrange("q c k -> q (c k)"), ALU.min, [b_sc, b_i], [b_sc])
        if dbg is not None and j == 0:
            t_ = p.nc.dram_tensor("dbg_sc", [128, 512], F32, kind="ExternalOutput").ap()
            p.dma("sp", t_, sc[:, 0:512], [b_sc], ())
        for r in range(j + 1):
            bk = 4 + (r % 2)
            for c in range(4):
                blk = 4 * r + c
                p.tr(g.banks[bk][:, c * 128:(c + 1) * 128], sc[:, blk * 128:(blk + 1) * 128], g.identf[:], [b_sc, g.b_ident], [g.bb[bk]])
            s = r % 2
            p.act(nev[:, s, :], g.banks[bk][:, :], AF.Identity, [g.bb[bk]], [b_nev[s]], bias=1.0)
            p.dma("sp", NEGT[j, :, 4 * r:4 * r + 4, :], nev[:, s, :].rearrange("k (c q) -> k c q", c=4), [b_nev[s]], ())
    p.barrier()
    st.close()

    st = ExitStack()
    ng = [p.sb("s_ng%d" % i, [128, NB, 128], BF16, st) for i in range(2)]
    b_ngs = _bufs(2)
    load_pair(0, 2, 0)
    load_pair(1, 2, 1)
    p.dma("sp", ng[0][:, 0:4, :], NEGT[0, :, 0:4, :], (), [b_ngs[0]])
    for j in range(J):
        if j + 1 < J:
            p.dma("sp", ng[(j + 1) % 2][:, 0:4 * (j + 2), :], NEGT[j + 1, :, 0:4 * (j + 2), :], (), [b_ngs[(j + 1) % 2]])
        for pr in range(2):
            run_maps(pr, 2, pr, j, ngT=ng[j % 2], b_ng=b_ngs[j % 2])
    p.barrier()
    st.close()
    stA.close()

    if dbg is not None:
        p.dma("sp", dbg, yT[:], [b_yT], ())
    st = ExitStack()
    wo = p.sb("o_wo", [128, 8, D], BF16, st)
    b_wo = Buf()
    p.dma("pool", wo[:], W["w_out"].rearrange("(fc f) m -> f fc m", f=128), (), [b_wo])
    gt = p.sb("o_g", [128, D], F32, st)
    bt = p.sb("o_b", [128, D], F32, st)
    b_gb = Buf()
    p.dma("sp", gt[:], W["ln1_g"].partition_broadcast(128), (), [b_gb])
    p.dma("sp", bt[:], W["ln1_b"].partition_broadcast(128), (), [b_gb])
    xin = [p.sb("o_x%d" % i, [128, D], F32, st) for i in range(2)]
    b_xin = _bufs(2)
    tmp = p.sb("o_tmp", [128, D], F32, st)
    b_tmp = Buf()
    st4 = p.sb("o_st4", [128, 4], F32, st)
    b_st4 = Buf()
    Xr = X.rearrange("(j q) d -> q j d", q=128)
    for j in range(J):
        s = j % 2
        p.dma("sp", xin[s][:], Xr[:, j, :], (), [b_xin[s]])
        for mh in range(2):
            bk = mh
            for fc in range(8):
                p.mm(g.banks[bk][:, :], yT[:, fc, j * 128:(j + 1) * 128], wo[:, fc, mh * 512:(mh + 1) * 512], fc == 0, fc == 7,
                     [b_yT, b_wo], [g.bb[bk]])
            p.stt("dve", xin[s][:, mh * 512:(mh + 1) * 512], xin[s][:, mh * 512:(mh + 1) * 512], ALPHA, g.banks[bk][:, :],
                  ALU.mult, ALU.add, [b_xin[s], g.bb[bk]], [b_xin[s]])
        ln_rows(p, g, xin[s][:], b_xin[s], x1[:, j, :], b_x1, gt[:], bt[:], b_gb, tmp[:], b_tmp, st4, b_st4)
    p.barrier()
    st.close()
    stB.close()

def emit_C(p, g, W, x1, b_x1, Xout):
    J, T = g.J, g.T
    st = ExitStack()
    xT = p.sb("m_xT", [128, 8, T], BF16, st)
    b_xT = Buf()
    xTf = p.sb("m_xTf", [128, 8, 128], F32, st)
    b_xTf = Buf()
    rw = p.sb("m_rw", [128, 8, NE], F32, st)
    b_w = Buf()
    p.dma("sp", rw[:], W["router_w"].rearrange("(dc q) e -> q dc e", q=128), (), [b_w])
    rb = p.sb("m_rb", [128, NE], F32, st)
    p.dma("sp", rb[:], W["router_b"].partition_broadcast(128), (), [b_w])
    bgu = p.sb("m_bgu", [128, 16, NE], F32, st)
    bd = p.sb("m_bd", [NE, D], F32, st)
    gates = p.sb("m_gates", [128, J, NE], F32, st)
    gT = p.sb("m_gT", [NE, 128], F32, st)
    lg = p.sb("m_lg", [128, NE], F32, st)
    m8 = p.sb("m_m8", [128, 8], F32, st)
    sm = p.sb("m_sm", [128, 4], F32, st)
    st0 = ExitStack()
    bgr = p.sb("m_bgr", [NE, 2 * D], F32, st0)
    p.dma("sp", bgr[:], W["b_gu"], (), [b_w])
    bgv = bgr[:].rearrange("e (jc q t) -> e jc t q", q=128, t=2)
    for jc in range(8):
        for t in range(2):
            i = jc * 2 + t
            p.tr(g.banks[7][:, i * NE:(i + 1) * NE], bgv[:, jc, t, :], g.identf[0:NE, 0:NE], [b_w, g.b_ident], [g.bb[7]])
    p.copy("dve", bgu[:], g.banks[7][:, :].rearrange("q (i e) -> q i e", e=NE), [g.bb[7]], [b_w])
    p.barrier()
    st0.close()
    p.dma("sp", bd[:], W["b_down"], (), [b_w])
    b_gates = Buf()
    b_gT = Buf()
    b_r = Buf()
    for j in range(J):
        transpose_block(p, g, x1[:, j, :], b_x1, xT, b_xT, j, dst_f32=xTf, dst_f32_b=b_xTf, bank0=0)
        for dc in range(8):
            p.mm(g.banks[2][:, 0:NE], xTf[:, dc, :], rw[:, dc, :], dc == 0, dc == 7, [b_xTf, b_w], [g.bb[2]])
        p.tt("dve", lg[:], g.banks[2][:, 0:NE], rb[:], ALU.add, [g.bb[2], b_w], [b_r])
        p.op("dve", lambda e: e.max(out=m8[:], in_=lg[:]), [b_r], [b_r])
        p.ts("dve", sm[:, 0:1], m8[:, 0:1], -1.0, None, ALU.mult, None, [b_r], [b_r])
        p.act(gates[:, j, :], lg[:], AF.Exp, [b_r], [b_gates], bias=sm[:, 0:1])
        p.ts("dve", lg[:], lg[:], m8[:, 3:4], None, ALU.is_ge, None, [b_r], [b_r])
        p.tt("dve", gates[:, j, :], gates[:, j, :], lg[:], ALU.mult, [b_r, b_gates], [b_gates])
        p.memset("dve", sm[:, 1:2], 0.0, [b_r])
        p.ts("dve", lg[:], gates[:, j, :], 1.0, 0.0, ALU.mult, ALU.add, [b_gates], [b_r], accum=sm[:, 1:2])
        p.op("dve", lambda e: e.reciprocal(out=sm[:, 2:3], in_=sm[:, 1:2]), [b_r], [b_r])
        p.ts("dve", gates[:, j, :], gates[:, j, :], sm[:, 2:3], None, ALU.mult, None, [b_r, b_gates], [b_gates])
        p.tr(g.banks[3][0:NE, 0:128], gates[:, j, :], g.identf[:], [b_gates, g.b_ident], [g.bb[3]])
        p.copy("dve", gT[:], g.banks[3][0:NE, 0:128], [g.bb[3]], [b_gT])
        for mh in range(2):
            p.mm(g.banks[4 + mh][:, :], gT[:], bd[:, mh * 512:(mh + 1) * 512], True, True, [b_gT, b_w], [g.bb[4 + mh]])
            p.stt("dve", x1[:, j, mh * 512:(mh + 1) * 512], x1[:, j, mh * 512:(mh + 1) * 512], ALPHA, g.banks[4 + mh][:, :],
                  ALU.mult, ALU.add, [b_x1, g.bb[4 + mh]], [b_x1])
    NW = 4
    st2 = ExitStack()
    st_save, st = st, st2
    wgu = [p.sb("m_wgu%d" % i, [128, 8, 256], BF16, st) for i in range(NW)]
    b_wgu = _bufs(NW)
    wd = [p.sb("m_wd%d" % i, [128, 8, D], BF16, st) for i in range(2)]
    b_wd = _bufs(2)
    hT = p.sb("m_hT", [128, 8, T], BF16, st)
    b_hT = _bufs(8)
    TG = min(512, T)
    ntg = T // TG
    gc = p.sb("m_gc", [128, 2, TG], F32, st)
    sg = p.sb("m_sg", [128, 2, TG], F32, st)
    uc = p.sb("m_uc", [128, 2, TG], F32, st)
    b_gc, b_sg, b_uc = _bufs(2), _bufs(2), _bufs(2)
    b_acc = _bufs(J)
    wq = {"n": 0}

    def load_gu(e, jc):
        s = wq["n"] % NW
        wq["n"] += 1
        src = W["w_gu"][e].rearrange("(dc q) f -> q dc f", q=128)[:, :, jc * 256:(jc + 1) * 256]
        p.dma("pool", wgu[s][:], src, (), [b_wgu[s]])
        return s

    def load_d(e):
        s = e % 2
        p.dma("pool", wd[s][:], W["w_down"][e].rearrange("(jc q) m -> q jc m", q=128), (), [b_wd[s]])

    pre = []
    for jc in range(2):
        pre.append(load_gu(0, jc))
    load_d(0)
    k = 0
    for e in range(NE):
        for jc in range(8):
            nxt = e * 8 + jc + 2
            if nxt < NE * 8:
                pre.append(load_gu(nxt // 8, nxt % 8))
            if jc == 4 and e + 1 < NE:
                load_d(e + 1)
            s = pre.pop(0)
            wv = wgu[s][:].rearrange("q dc (j t) -> q dc t j", t=2)
            for tg in range(ntg):
                kk = k % 2
                k += 1
                tsl = slice(tg * TG, (tg + 1) * TG)
                for t, bk in ((0, kk), (1, 2 + kk)):
                    for dc in range(8):
                        p.mm(g.banks[bk][:, 0:TG], wv[:, dc, t, :], xT[:, dc, tsl], dc == 0, dc == 7, [b_wgu[s], b_xT], [g.bb[bk]])
                p.ts("dve", gc[:, kk, :], g.banks[kk][:, 0:TG], bgu[:, jc * 2, e:e + 1], 7.0, ALU.add, ALU.min, [g.bb[kk], b_w], [b_gc[kk]])
                p.act(sg[:, kk, :], gc[:, kk, :], AF.Sigmoid, [b_gc[kk]], [b_sg[kk]], scale=1.702)
                p.act(uc[:, kk, :], g.banks[2 + kk][:, 0:TG], AF.Identity, [g.bb[2 + kk], b_w], [b_uc[kk]], bias=bgu[:, jc * 2 + 1, e:e + 1])
                p.ts("pool", uc[:, kk, :], uc[:, kk, :], 7.0, -7.0, ALU.min, ALU.max, [b_uc[kk]], [b_uc[kk]])
                p.tt("pool", gc[:, kk, :], gc[:, kk, :], sg[:, kk, :], ALU.mult, [b_gc[kk], b_sg[kk]], [b_gc[kk]])
                p.stt("dve", hT[:, jc, tsl], uc[:, kk, :], 1.0, gc[:, kk, :], ALU.add, ALU.mult, [b_uc[kk], b_gc[kk]], [b_hT[jc]])
        s = e % 2
        for j in range(J):
            for mh in range(2):
                bk = 4 + (j * 2 + mh) % 4
                for jc in range(8):
                    p.mm(g.banks[bk][:, :], hT[:, jc, j * 128:(j + 1) * 128], wd[s][:, jc, mh * 512:(mh + 1) * 512], jc == 0, jc == 7,
                         [b_hT[jc], b_wd[s]], [g.bb[bk]])
                p.stt("dve", x1[:, j, mh * 512:(mh + 1) * 512], g.banks[bk][:, :], gates[:, j, e:e + 1], x1[:, j, mh * 512:(mh + 1) * 512],
                      ALU.mult, ALU.add, [g.bb[bk], b_gates, b_acc[j], b_x1], [b_acc[j]])
    p.barrier()
    st2.close()
    st = st_save
    gt = p.sb("m_g", [128, D], F32, st)
    bt = p.sb("m_b", [128, D], F32, st)
    b_gb = Buf()
    p.dma("sp", gt[:], W["ln2_g"].partition_broadcast(128), (), [b_gb])
    p.dma("sp", bt[:], W["ln2_b"].partition_broadcast(128), (), [b_gb])
    tmp = p.sb("m_tmp", [128, D], F32, st)
    b_tmp = Buf()
    st4 = p.sb("m_st4", [128, 4], F32, st)
    b_st4 = Buf()
    Xr = Xout.rearrange("(j q) d -> q j d", q=128)
    for j in range(J):
        ln_rows(p, g, x1[:, j, :], b_acc[j], x1[:, j, :], b_acc[j], gt[:], bt[:], b_gb, tmp[:], b_tmp, st4, b_st4)
        p.dma("sp", Xr[:, j, :], x1[:, j, :], [b_acc[j]], ())
    p.barrier()
    st.close()


def _bucket(d):
    n = np.maximum(d, 0)
    nf = np.maximum(n, 1).astype(np.float32)
    large = 16 + (np.log(nf / np.float32(16)) / np.float32(math.log(8.0)) * np.float32(16)).astype(np.int32)
    large = np.minimum(large, 31)
    return np.where(n < 16, n, large)


def make_consts(c_me):
    q = np.arange(128)
    cs = {}
    cs["identf"] = np.eye(128, dtype=np.float32)
    dqm = np.zeros((128, 2), np.float32)
    dqm[:, 0] = ((q % 64) < 32)
    dqm[:, 1] = 1.0 - dqm[:, 0]
    cs["dqmask"] = dqm
    sel = np.zeros((128, 4), np.float32)
    sel[:, c_me] = 1.0
    cs["sel"] = sel
    cs["triu"] = np.triu(np.ones((128, 128), np.float32))
    mT = np.zeros((4, 128, 128), np.float32)
    for c in range(4):
        dl = c_me - c
        if dl < 0:
            mT[c] = 1.0
        elif dl == 0:
            mT[c] = (q[:, None] > q[None, :]).astype(np.float32)
    cs["maskT"] = mT * np.float32(NEG)
    cs["maskQ"] = -np.ascontiguousarray(mT.transpose(0, 2, 1))
    return cs


def bias_tile_index(c_me):
    q = np.arange(128)
    idx = np.zeros((8, 128, 128), np.int64)
    for t in range(8):
        dl = (c_me - t) if t < 4 else (4 + c_me - (t - 4))
        d = 128 * dl + q[None, :] - q[:, None]
        idx[t] = _bucket(d)
    return idx


CONST_SHAPES = {"identf": [128, 128], "dqmask": [128, 2], "sel": [128, 4], "triu": [128, 128],
                "maskT": [4, 128, 128], "maskQ": [4, 128, 128]}


def declare_consts(nc):
    return {k: nc.dram_tensor("c_" + k, shp, F32, kind="ExternalInput").ap() for k, shp in CONST_SHAPES.items()}


WSHAPES = {"rel31": [8], "biasT": [8, 8, 128, 128], "diff_lambda": [4, 32], "diff_norm_g": [64], "conv_w": [31, 256],
           "conv_b": [256], "conv_ln_g": [256], "conv_ln_b": [256], "w_out": [D, D], "ln1_g": [D], "ln1_b": [D],
           "router_w": [D, NE], "router_b": [NE], "w_gu": [NE, D, 2 * D], "b_gu": [NE, 2 * D], "w_down": [NE, D, D],
           "b_down": [NE, D], "ln2_g": [D], "ln2_b": [D]}


def build_A(J):
    nc = bass.Bass("TRN2", target_bir_lowering=False)
    T = J * 128
    X = nc.dram_tensor("X", [T, D], F32, kind="ExternalInput").ap()
    w_in = nc.dram_tensor("w_in", [D, NIN], F32, kind="ExternalInput").ap()
    forget_b = nc.dram_tensor("forget_b", [4], F32, kind="ExternalInput").ap()
    consts = declare_consts(nc)
    KT_loc = nc.dram_tensor("KT_loc", [KT_ROWS, T], BF16, kind="ExternalOutput").ap()
    V_loc = nc.dram_tensor("V_loc", [T, 768], BF16, kind="ExternalOutput").ap()
    HT_loc = nc.dram_tensor("HT_loc", [256, T], F32, kind="ExternalOutput").ap()
    LF_loc = nc.dram_tensor("LF_loc", [J * 4, 128], F32, kind="ExternalOutput").ap()
    QT_loc = nc.dram_tensor("QT_loc", [NQCH, 128, T], BF16, kind="ExternalOutput").ap()
    IW_loc = nc.dram_tensor("IW_loc", [128, J * 4], F32, kind="ExternalOutput").ap()
    p = Prog(nc)
    g = setup_common(p, J, consts)
    emit_A(p, g, X, w_in, forget_b, consts, KT_loc, V_loc, HT_loc, LF_loc, QT_loc, IW_loc)
    p.finish()
    return nc


def build_BC(J, do_C=True, debug=False):
    nc = bass.Bass("TRN2", target_bir_lowering=False)
    T = J * 128
    NB = 4 * J
    X = nc.dram_tensor("X", [T, D], F32, kind="ExternalInput").ap()
    KT_all = nc.dram_tensor("KT_all", [4 * KT_ROWS, T], BF16, kind="ExternalInput").ap()
    V_all = nc.dram_tensor("V_all", [4 * T, 768], BF16, kind="ExternalInput").ap()
    HT_all = nc.dram_tensor("HT_all", [4 * 256, T], F32, kind="ExternalInput").ap()
    LF_all = nc.dram_tensor("LF_all", [4 * J * 4, 128], F32, kind="ExternalInput").ap()
    QT_loc = nc.dram_tensor("QT_loc", [NQCH, 128, T], BF16, kind="ExternalInput").ap()
    IW_loc = nc.dram_tensor("IW_loc", [128, J * 4], F32, kind="ExternalInput").ap()
    HT_loc = nc.dram_tensor("HT_loc", [256, T], F32, kind="ExternalInput").ap()
    lamc = nc.dram_tensor("lamc", [128, 2], F32, kind="ExternalInput").ap()
    W = {k: nc.dram_tensor(k, shp, F32, kind="ExternalInput").ap() for k, shp in WSHAPES.items()}
    consts = declare_consts(nc)
    NEGT = nc.dram_tensor("NEGT", [J, 128, NB, 128], BF16, kind="Internal").ap()
    Xout = nc.dram_tensor("Xout", [T, D], F32, kind="ExternalOutput").ap()
    p = Prog(nc)
    g = setup_common(p, J, consts)
    x1 = p.sb("x1", [128, J, D], F32)
    b_x1 = Buf()
    dbg = nc.dram_tensor("dbg_yT", [128, 8, T], BF16, kind="ExternalOutput").ap() if debug else None
    emit_B(p, g, 0, X, Gat(KT_ROWS, mono=KT_all), Gat(T, mono=V_all), Gat(256, mono=HT_all), LF_all, QT_loc, IW_loc, HT_loc, NEGT, W,
           consts, x1, b_x1, lamc=lamc, dbg=dbg)
    if do_C:
        emit_C(p, g, W, x1, b_x1, Xout)
    else:
        p.dma("sp", Xout.rearrange("(j q) d -> q j d", q=128), x1[:], [b_x1], ())
    p.finish()
    return nc


def build_fused(J, nlayers=DEPTH):
    nc = bass.Bass("TRN2", target_bir_lowering=False)
    T = J * 128
    NB = 4 * J
    Xin = nc.dram_tensor("X", [T, D], F32, kind="ExternalInput").ap()
    w_in = nc.dram_tensor("w_in", [nlayers, D, NIN], F32, kind="ExternalInput").ap()
    forget_b = nc.dram_tensor("forget_b", [nlayers, 4], F32, kind="ExternalInput").ap()
    Wf = {}
    for k, shp in WSHAPES.items():
        if k in ("rel31", "biasT"):
            Wf[k] = nc.dram_tensor(k, shp, F32, kind="ExternalInput").ap()
        else:
            Wf[k] = nc.dram_tensor(k, [nlayers] + shp, F32, kind="ExternalInput").ap()
    consts = declare_consts(nc)
    Xout = nc.dram_tensor("Xout", [T, D], F32, kind="ExternalOutput").ap()
    Xbuf = nc.dram_tensor("Xbuf", [T, D], F32, kind="Internal").ap()
    KT_loc = nc.dram_tensor("KT_loc", [KT_ROWS, T], BF16, kind="Internal").ap()
    V_loc = nc.dram_tensor("V_loc", [T, 768], BF16, kind="Internal").ap()
    HT_loc = nc.dram_tensor("HT_loc", [256, T], F32, kind="Internal").ap()
    LF_loc = nc.dram_tensor("LF_loc", [J * 4, 128], F32, kind="Internal").ap()
    QT_loc = nc.dram_tensor("QT_loc", [NQCH, 128, T], BF16, kind="Internal").ap()
    IW_loc = nc.dram_tensor("IW_loc", [128, J * 4], F32, kind="Internal").ap()
    SK, SV, SH = 64, 128, 32
    def mk(name, R, S, W_, dt):
        return [Gat(R, S=S, chunks=[nc.dram_tensor("%s%d_%d" % (name, i, k), [4 * S, W_], dt, kind="Internal").ap()
                                    for k in range(R // S)]) for i in range(2)]
    KT_alls = mk("KTa", KT_ROWS, SK, T, BF16)
    V_alls = mk("Va", T, SV, 768, BF16)
    HT_alls = mk("HTa", 256, SH, T, F32)
    LF_alls = [nc.dram_tensor("LF_all%d" % i, [4 * J * 4, 128], F32, kind="Internal").ap() for i in range(2)]
    NEGT = nc.dram_tensor("NEGT", [J, 128, NB, 128], BF16, kind="Internal").ap()
    p = Prog(nc)
    g = setup_common(p, J, consts)
    x1 = p.sb("x1", [128, J, D], F32)
    b_x1 = Buf()
    groups = [[0, 1, 2, 3], [4, 5, 6, 7]]
    for l in range(nlayers):
        Xsrc = Xin if l == 0 else Xbuf
        Xdst = Xout if l == nlayers - 1 else Xbuf
        KT_all, V_all, HT_all, LF_all = KT_alls[l % 2], V_alls[l % 2], HT_alls[l % 2], LF_alls[l % 2]
        emit_A(p, g, Xsrc, w_in[l], forget_b[l], consts, KT_loc, V_loc, HT_loc, LF_loc, QT_loc, IW_loc)
        for src, gat in ((KT_loc, KT_all), (V_loc, V_all), (HT_loc, HT_all)):
            for k, ch in enumerate(gat.chunks):
                p.collective(src[k * gat.S:(k + 1) * gat.S, :], ch, groups)
        p.collective(LF_loc, LF_all, groups)
        p.barrier()
        W = {k: (v if k in ("rel31", "biasT") else v[l]) for k, v in Wf.items()}
        emit_B(p, g, l, Xsrc, KT_all, V_all, HT_all, LF_all, QT_loc, IW_loc, HT_loc, NEGT, W, consts, x1, b_x1)
        emit_C(p, g, W, x1, b_x1, Xdst)
    p.finish()
    return nc


J_FULL = 16
FUSED = True
_CACHE = {}


def _loc(a, J):
    L = a.shape[0]
    v = a.reshape(J, 4, 128, -1)
    return [np.ascontiguousarray(v[:, c].reshape(J * 128, -1)) for c in range(4)]


def _unloc(parts, J):
    Dd = parts[0].shape[-1]
    out = np.empty((J, 4, 128, Dd), parts[0].dtype)
    for c in range(4):
        out[:, c] = parts[c].reshape(J, 128, Dd)
    return out.reshape(J * 4 * 128, Dd)


def kernel(x, rel_bias, w_in, forget_b, diff_lambda, diff_norm_g, conv_w, conv_b, conv_ln_g, conv_ln_b, w_out,
           ln1_g, ln1_b, router_w, router_b, w_gu, b_gu, w_down, b_down, ln2_g, ln2_b):
    J = J_FULL
    f32 = lambda a: np.ascontiguousarray(np.asarray(a, dtype=np.float32))
    x = f32(x)
    rel_bias = f32(rel_bias)
    P = dict(w_in=f32(w_in), forget_b=f32(forget_b), diff_lambda=f32(diff_lambda), diff_norm_g=f32(diff_norm_g), conv_w=f32(conv_w),
             conv_b=f32(conv_b), conv_ln_g=f32(conv_ln_g), conv_ln_b=f32(conv_ln_b), w_out=f32(w_out), ln1_g=f32(ln1_g), ln1_b=f32(ln1_b),
             router_w=f32(router_w), router_b=f32(router_b), w_gu=f32(w_gu), b_gu=f32(b_gu), w_down=f32(w_down), b_down=f32(b_down),
             ln2_g=f32(ln2_g), ln2_b=f32(ln2_b))
    B = x.shape[0]
    cs = [make_consts(c) for c in range(4)]
    biasT = [np.ascontiguousarray(rel_bias[bias_tile_index(c)].transpose(3, 0, 1, 2)) for c in range(4)]
    rel31 = np.ascontiguousarray(rel_bias[31])
    xs = [None] * 8
    for b in range(B):
        for c, part in enumerate(_loc(x[b], J)):
            xs[4 * b + c] = part
    cores = list(range(8))
    if FUSED:
        if "F" not in _CACHE:
            _CACHE["F"] = build_fused(J)
        in_maps = []
        for r in cores:
            m = {"X": xs[r], "w_in": P["w_in"], "forget_b": P["forget_b"], "biasT": biasT[r % 4], "rel31": rel31}
            for k in WSHAPES:
                if k not in m:
                    m[k] = P[k]
            for k, v in cs[r % 4].items():
                m["c_" + k] = v
            in_maps.append(m)
        res = run_bass_kernel_spmd(_CACHE["F"], in_maps, core_ids=cores).results
        xs = [np.asarray(res[r]["Xout"]) for r in cores]
    else:
        if "A" not in _CACHE:
            _CACHE["A"] = build_A(J)
            _CACHE["BC"] = build_BC(J)
        for l in range(DEPTH):
            li = 0.8 - 0.6 * math.exp(-0.3 * l)
            lamc = np.tile(np.array([[-li, 1.0 - li]], np.float32), (128, 1))
            in_maps = []
            for r in cores:
                m = {"X": xs[r], "w_in": P["w_in"][l], "forget_b": P["forget_b"][l]}
                for k, v in cs[r % 4].items():
                    m["c_" + k] = v
                in_maps.append(m)
            resA = run_bass_kernel_spmd(_CACHE["A"], in_maps, core_ids=cores).results
            in_maps = []
            for r in cores:
                b = r // 4
                m = {"X": xs[r], "lamc": lamc, "biasT": biasT[r % 4], "rel31": rel31}
                for nm in ("KT", "V", "HT", "LF"):
                    m[nm + "_all"] = np.concatenate([np.asarray(resA[4 * b + cc][nm + "_loc"]) for cc in range(4)], 0)
                for nm in ("QT_loc", "IW_loc", "HT_loc"):
                    m[nm] = np.asarray(resA[r][nm])
                for k in WSHAPES:
                    if k not in m:
                        m[k] = P[k][l]
                for k, v in cs[r % 4].items():
                    m["c_" + k] = v
                in_maps.append(m)
            resB = run_bass_kernel_spmd(_CACHE["BC"], in_maps, core_ids=cores).results
            xs = [np.asarray(resB[r]["Xout"]) for r in cores]
    out = np.stack([_unloc([xs[4 * b + c] for c in range(4)], J) for b in range(B)], 0)
    return out.astype(np.float32)
```

```python
import math
from contextlib import ExitStack

import numpy as np
import concourse.bass as bass
import concourse.mybir as mybir
from concourse.bass_utils import run_bass_kernel_spmd

F32 = mybir.dt.float32
BF16 = mybir.dt.bfloat16
ALU = mybir.AluOpType
AF = mybir.ActivationFunctionType

D = 1024
NIN = 3144
DEPTH = 4
NE = 32
ALPHA = (2 * DEPTH) ** 0.25
EPS = 1e-5
NEG = -30000.0
REPL = -1.0e30
C_FQ, C_FK, C_FV, C_FF = 0, 256, 512, 768
C_DQ, C_DK, C_DV = 772, 1028, 1284
C_SQ, C_SK, C_SV = 1540, 1796, 2052
C_IQ, C_IK, C_IW, C_CU = 2308, 2564, 2628, 2632
KT_ROWS = 832
NQCH = 10


class Buf:
    __slots__ = ("w", "r", "wx")

    def __init__(self):
        self.w = None
        self.r = {}
        self.wx = []


class Prog:
    NDS = 24

    def __init__(self, nc):
        self.nc = nc
        self.st = ExitStack()
        self.engs = {"pe": nc.tensor, "act": nc.scalar, "dve": nc.vector, "pool": nc.gpsimd, "sp": nc.sync}
        self.semobj = {}
        self.cnt = {}
        for k in self.engs:
            self.semobj[k] = self.st.enter_context(nc.semaphore("s_" + k))
            self.cnt[k] = 0
        self.seen = {k: {} for k in self.engs}
        self.dcnt = [0] * self.NDS
        for i in range(self.NDS):
            self.semobj[("d", i)] = self.st.enter_context(nc.semaphore("s_d%d" % i))
        self.dnext = 0
        self.nins = 0

    def sb(self, name, shape, dt, st=None):
        self.uid = getattr(self, "uid", 0) + 1
        return (st or self.st).enter_context(self.nc.sbuf_tensor("%s_%d" % (name, self.uid), shape, dt))

    def ps(self, name, shape, dt, st=None):
        return (st or self.st).enter_context(self.nc.psum_tensor(name, shape, dt))

    def _wait(self, eng, key, val):
        if self.seen[eng].get(key, 0) >= val:
            return
        self.engs[eng].wait_ge(self.semobj[key], val)
        self.seen[eng][key] = val
        self.nins += 1

    def _deps(self, eng, reads, writes, extra=None, is_dma=False):
        need = {}

        def add(k, v, kind):
            if k == eng:
                if eng == "pe" or kind == "war":
                    return
            if need.get(k, 0) < v:
                need[k] = v
        if extra is not None:
            add(extra[0], extra[1], "raw")
        for b in reads:
            if b.w is not None:
                add(b.w[0], b.w[1], "raw")
            for k, v in b.wx:
                add(k, v, "raw")
        for b in writes:
            if is_dma and b.w is not None and isinstance(b.w[0], tuple) and not b.r:
                continue
            if b.w is not None:
                add(b.w[0], b.w[1], "waw")
            for k, v in b.wx:
                add(k, v, "waw")
            for k, v in b.r.items():
                add(k, v, "war")
        items = [(k, v) for k, v in need.items() if self.seen[eng].get(k, 0) < v]
        if not items:
            return None
        for k, v in items[:-1]:
            self._wait(eng, k, v)
        return items[-1]

    def _embed(self, eng, ins, last):
        if last is not None:
            ins._wait_ge(self.semobj[last[0]], last[1])
            self.seen[eng][last[0]] = last[1]

    def _mark(self, key, val, reads, writes, is_dma=False):
        for b in reads:
            b.r[key] = val
        for b in writes:
            if is_dma and b.w is not None and isinstance(b.w[0], tuple) and not b.r:
                b.wx.append(b.w)
            else:
                b.wx = []
            b.w = (key, val)
            b.r = {}

    def op(self, eng, fn, reads=(), writes=()):
        last = self._deps(eng, reads, writes)
        ins = fn(self.engs[eng])
        self._embed(eng, ins, last)
        self.cnt[eng] += 1
        ins.then_inc(self.semobj[eng], 1)
        self.nins += 1
        self._mark(eng, self.cnt[eng], reads, writes)

    def dma(self, eng, out, in_, reads=(), writes=()):
        i = self.dnext
        self.dnext = (i + 1) % self.NDS
        key = ("d", i)
        last = self._deps(eng, reads, writes, extra=(key, self.dcnt[i]) if self.dcnt[i] > 0 else None, is_dma=True)
        ins = self.engs[eng].dma_start(out=out, in_=in_)
        self._embed(eng, ins, last)
        self.dcnt[i] += 16
        ins.then_inc(self.semobj[key], 16)
        self.nins += 1
        self._mark(key, self.dcnt[i], reads, writes, is_dma=True)

    def collective(self, src, dst, groups):
        if "cc" not in self.semobj:
            self.semobj["cc"] = self.st.enter_context(self.nc.semaphore("s_cc"))
            self.cccnt = 0
        self.barrier(["pool"])
        ins = self.nc.gpsimd.collective_compute("AllGather", ALU.bypass, replica_groups=groups, ins=[src.opt()], outs=[dst.opt()])
        self.cccnt += 1
        ins.then_inc(self.semobj["cc"])
        self.nins += 1
        self._wait("pool", "cc", self.cccnt)

    def barrier(self, engs=None):
        engs = engs or list(self.engs)
        for e in engs:
            for k in self.engs:
                if k != e and self.cnt[k] > 0:
                    self._wait(e, k, self.cnt[k])
            for i in range(self.NDS):
                if self.dcnt[i] > 0:
                    self._wait(e, ("d", i), self.dcnt[i])
            if getattr(self, "cccnt", 0) > 0:
                self._wait(e, "cc", self.cccnt)

    def mm(self, out, lhsT, rhs, start, stop, reads, writes):
        self.op("pe", lambda e: e.matmul(out, lhsT=lhsT, rhs=rhs, start=start, stop=stop), reads, writes)

    def tr(self, out, in_, ident, reads, writes):
        self.op("pe", lambda e: e.transpose(out, in_, ident), reads, writes)

    def act(self, out, in_, func, reads, writes, bias=None, scale=1.0, accum=None):
        kw = {}
        if bias is not None:
            kw["bias"] = bias
        if accum is not None:
            kw["accum_out"] = accum
        self.op("act", lambda e: e.activation(out=out, in_=in_, func=func, scale=scale, **kw), reads, writes)

    def ts(self, eng, out, in0, s1, s2, op0, op1, reads, writes, accum=None):
        kw = {}
        if accum is not None:
            kw["accum_out"] = accum
        if op1 is None:
            self.op(eng, lambda e: e.tensor_scalar(out=out, in0=in0, scalar1=s1, scalar2=None, op0=op0, **kw), reads, writes)
        else:
            self.op(eng, lambda e: e.tensor_scalar(out=out, in0=in0, scalar1=s1, scalar2=s2, op0=op0, op1=op1, **kw), reads, writes)

    def tt(self, eng, out, in0, in1, op, reads, writes):
        self.op(eng, lambda e: e.tensor_tensor(out=out, in0=in0, in1=in1, op=op), reads, writes)

    def stt(self, eng, out, in0, scalar, in1, op0, op1, reads, writes):
        self.op(eng, lambda e: e.scalar_tensor_tensor(out=out, in0=in0, scalar=scalar, in1=in1, op0=op0, op1=op1), reads, writes)

    def copy(self, eng, out, in_, reads, writes):
        if eng == "act":
            self.op("act", lambda e: e.copy(out=out, in_=in_), reads, writes)
        else:
            self.op(eng, lambda e: e.tensor_copy(out=out, in_=in_), reads, writes)

    def memset(self, eng, ap, val, writes):
        self.op(eng, lambda e: e.memset(ap, val), (), writes)

    def finish(self):
        self.barrier(["sp"])
        self.st.close()


def _bufs(n):
    return [Buf() for _ in range(n)]


class Ctx:
    pass


class Gat:
    def __init__(self, R, mono=None, S=None, chunks=None):
        self.R, self.mono, self.S, self.chunks = R, mono, S, chunks

    def rows(self, c, r0, n):
        if self.mono is not None:
            return [(self.mono[c * self.R + r0:c * self.R + r0 + n, :], 0, n)]
        out = []
        r = r0
        while r < r0 + n:
            k, o = r // self.S, r % self.S
            m = min(self.S - o, r0 + n - r)
            out.append((self.chunks[k][c * self.S + o:c * self.S + o + m, :], r - r0, m))
            r += m
        return out


def setup_common(p, J, consts):
    nc = p.nc
    g = Ctx()
    g.J = J
    g.T = J * 128
    g.banks = [p.ps("bank%d" % i, [128, 512], F32) for i in range(8)]
    g.bb = _bufs(8)
    g.identf = p.sb("identf", [128, 128], F32)
    g.identb = p.sb("identb", [128, 128], BF16)
    g.b_ident = Buf()
    p.dma("sp", g.identf[:], consts["identf"], (), [g.b_ident])
    p.copy("dve", g.identb[:], g.identf[:], [g.b_ident], [g.b_ident])
    return g


def transpose_block(p, g, src, src_b, dst_bf, dst_b, blk, dst_f32=None, dst_f32_b=None, bank0=0):
    for half in range(2):
        bk = bank0 + half
        for q in range(4):
            dc = half * 4 + q
            p.tr(g.banks[bk][:, q * 128:(q + 1) * 128], src[:, dc * 128:(dc + 1) * 128], g.identf[:],
                 [src_b, g.b_ident], [g.bb[bk]])
        outv = dst_bf[:, half * 4:(half + 1) * 4, blk * 128:(blk + 1) * 128]
        inv = g.banks[bk][:].rearrange("p (q t) -> p q t", q=4)
        if dst_f32 is None:
            p.copy("act" if half == 0 else "dve", outv, inv, [g.bb[bk]], [dst_b])
        else:
            f32v = dst_f32[:, half * 4:(half + 1) * 4, :]
            p.copy("dve", f32v, inv, [g.bb[bk]], [dst_f32_b])
            p.copy("act", outv, f32v, [dst_f32_b], [dst_b])


def emit_A(p, g, X, w_in, forget_b, consts, KT_loc, V_loc, HT_loc, LF_loc, QT_loc, IW_loc):
    J, T = g.J, g.T
    st = ExitStack()
    xT = p.sb("a_xT", [128, 8, T], BF16, st)
    b_xT = Buf()
    win = p.sb("a_win", [128, 8, NIN], BF16, st)
    b_win = _bufs(8)
    xt = [p.sb("a_xt%d" % i, [128, D], F32, st) for i in range(2)]
    b_xt = _bufs(2)
    fb = p.sb("a_fb", [128, 4], F32, st)
    b_fb = Buf()
    dqm = p.sb("a_dqm", [128, 2], F32, st)
    b_dqm = Buf()
    p.dma("sp", fb[:], forget_b.partition_broadcast(128), (), [b_fb])
    p.dma("sp", dqm[:], consts["dqmask"], (), [b_dqm])
    w_r = w_in.rearrange("(dc q) f -> q dc f", q=128)
    for dc in range(8):
        p.dma("pool", win[:, dc, :], w_r[:, dc, :], (), [b_win[dc]])
    Xr = X.rearrange("(j q) d -> q j d", q=128)
    for j in range(J):
        s = j % 2
        p.dma("sp", xt[s][:], Xr[:, j, :], (), [b_xt[s]])
        transpose_block(p, g, xt[s], b_xt[s], xT, b_xT, j)

    ev = p.sb("a_ev", [128, 4, 512], BF16, st)
    b_ev = _bufs(4)
    evf = p.sb("a_evf", [128, 2, 512], F32, st)
    b_evf = _bufs(2)
    sg = p.sb("a_sg", [128, 2, 512], F32, st)
    b_sg = _bufs(2)
    ntg = T // 512 if T >= 512 else 1
    TG = min(512, T)
    state = {"bk": 2, "ev": 0, "evf": 0}

    def fm_chunk(col0, ncols, tg):
        bk = state["bk"]
        state["bk"] = 2 + (bk - 2 + 1) % 4
        for dc in range(8):
            p.mm(g.banks[bk][0:ncols, 0:TG], win[:, dc, col0:col0 + ncols], xT[:, dc, tg * TG:(tg + 1) * TG],
                 dc == 0, dc == 7, [b_win[dc], b_xT], [g.bb[bk]])
        return bk

    def ev_slot():
        s = state["ev"]
        state["ev"] = (s + 1) % 4
        return s

    for tg in range(ntg):
        tsl = slice(tg * TG, (tg + 1) * TG)
        qspec = [(0, C_FQ, None), (1, C_FQ + 128, None), (2, C_DQ, 0), (3, C_DQ + 128, 0), (4, C_DQ, 1), (5, C_DQ + 128, 1),
                 (6, C_SQ, None), (7, C_SQ + 128, None), (8, C_IQ, None), (9, C_IQ + 128, None)]
        cache = {}
        for qi, col0, m in qspec:
            if m == 1 and col0 in cache:
                bk = cache[col0]
            else:
                bk = fm_chunk(col0, 128, tg)
                cache[col0] = bk
            s = ev_slot()
            if m is None:
                p.copy("act", ev[:, s, 0:TG], g.banks[bk][:, 0:TG], [g.bb[bk]], [b_ev[s]])
            else:
                p.ts("dve", ev[:, s, 0:TG], g.banks[bk][:, 0:TG], dqm[:, m:m + 1], None, ALU.mult, None,
                     [g.bb[bk], b_dqm], [b_ev[s]])
            p.dma("sp", QT_loc[qi, :, tsl], ev[:, s, 0:TG], [b_ev[s]], ())
        for ki, col0, ncols in [(0, C_FK, 128), (1, C_FK + 128, 128), (2, C_DK, 128), (3, C_DK + 128, 128),
                                (4, C_SK, 128), (5, C_SK + 128, 128), (6, C_IK, 64)]:
            bk = fm_chunk(col0, ncols, tg)
            s = ev_slot()
            p.copy("act" if ki % 2 == 0 else "dve", ev[0:ncols, s, 0:TG], g.banks[bk][0:ncols, 0:TG], [g.bb[bk]], [b_ev[s]])
            p.dma("sp", KT_loc[ki * 128:ki * 128 + ncols, tsl], ev[0:ncols, s, 0:TG], [b_ev[s]], ())
        for cc in range(2):
            bka = fm_chunk(C_CU + cc * 128, 128, tg)
            bkg = fm_chunk(C_CU + 256 + cc * 128, 128, tg)
            s = state["evf"]
            state["evf"] = 1 - s
            p.act(sg[:, s, 0:TG], g.banks[bkg][:, 0:TG], AF.Sigmoid, [g.bb[bkg]], [b_sg[s]])
            p.tt("dve", evf[:, s, 0:TG], g.banks[bka][:, 0:TG], sg[:, s, 0:TG], ALU.mult, [g.bb[bka], b_sg[s]], [b_evf[s]])
            p.dma("sp", HT_loc[cc * 128:(cc + 1) * 128, tsl], evf[:, s, 0:TG], [b_evf[s]], ())

    vt = p.sb("a_vt", [128, 2, 768], BF16, st)
    b_vt = _bufs(2)
    lf = p.sb("a_lf", [128, J, 4], F32, st)
    b_lf = Buf()
    iw = p.sb("a_iw", [128, J, 4], F32, st)
    b_iw = Buf()
    for j in range(J):
        s = j % 2
        lt = slice(j * 128, (j + 1) * 128)
        for gi, col0 in enumerate([C_FV, C_DV]):
            for dc in range(8):
                p.mm(g.banks[0][:, gi * 256:(gi + 1) * 256], xT[:, dc, lt], win[:, dc, col0:col0 + 256], dc == 0, dc == 7,
                     [b_win[dc], b_xT], [g.bb[0]])
        for dc in range(8):
            p.mm(g.banks[1][:, 0:256], xT[:, dc, lt], win[:, dc, C_SV:C_SV + 256], dc == 0, dc == 7, [b_win[dc], b_xT], [g.bb[1]])
        for dc in range(8):
            p.mm(g.banks[1][:, 256:260], xT[:, dc, lt], win[:, dc, C_FF:C_FF + 4], dc == 0, dc == 7, [b_win[dc], b_xT], [g.bb[1]])
        for dc in range(8):
            p.mm(g.banks[1][:, 260:264], xT[:, dc, lt], win[:, dc, C_IW:C_IW + 4], dc == 0, dc == 7, [b_win[dc], b_xT], [g.bb[1]])
        p.copy("act", vt[:, s, 0:512], g.banks[0][:, 0:512], [g.bb[0]], [b_vt[s]])
        p.copy("dve", vt[:, s, 512:768], g.banks[1][:, 0:256], [g.bb[1]], [b_vt[s]])
        p.tt("dve", lf[:, j, :], g.banks[1][:, 256:260], fb[:], ALU.add, [g.bb[1], b_fb], [b_lf])
        p.copy("dve", iw[:, j, :], g.banks[1][:, 260:264], [g.bb[1]], [b_iw])
        p.dma("sp", V_loc[lt, :], vt[:, s, :], [b_vt[s]], ())
    lfv = lf[:].rearrange("p j h -> p (j h)")
    p.act(lfv, lfv, AF.Exp, [b_lf], [b_lf], scale=-1.0)
    p.act(lfv, lfv, AF.Ln, [b_lf], [b_lf], bias=1.0)
    p.ts("dve", lfv, lfv, -1.0, None, ALU.mult, None, [b_lf], [b_lf])
    p.tr(g.banks[2][0:J * 4, 0:128], lfv, g.identf[:], [b_lf, g.b_ident], [g.bb[2]])
    lft = p.sb("a_lft", [J * 4, 128], F32, st)
    b_lft = Buf()
    p.copy("dve", lft[:], g.banks[2][0:J * 4, 0:128], [g.bb[2]], [b_lft])
    p.dma("sp", LF_loc, lft[:], [b_lft], ())
    p.dma("sp", IW_loc, iw[:].rearrange("p j h -> p (j h)"), [b_iw], ())
    p.barrier()
    st.close()


def ln_rows(p, g, z, b_z, out, b_out, gt, bt, b_gb, tmp, b_tmp, st4, b_st4):
    p.memset("dve", st4[:, 0:2], 0.0, [b_st4])
    p.ts("dve", tmp, z, 1.0, 0.0, ALU.mult, ALU.add, [b_z], [b_tmp, b_st4], accum=st4[:, 0:1])
    p.tt("pool", tmp, z, z, ALU.mult, [b_z], [b_tmp])
    p.ts("dve", tmp, tmp, 1.0, 0.0, ALU.mult, ALU.add, [b_tmp], [b_tmp, b_st4], accum=st4[:, 1:2])
    p.ts("dve", st4[:, 0:1], st4[:, 0:1], 1.0 / D, None, ALU.mult, None, [b_st4], [b_st4])
    p.tt("dve", st4[:, 2:3], st4[:, 0:1], st4[:, 0:1], ALU.mult, [b_st4], [b_st4])
    p.stt("dve", st4[:, 1:2], st4[:, 1:2], 1.0 / D, st4[:, 2:3], ALU.mult, ALU.subtract, [b_st4], [b_st4])
    p.act(st4[:, 1:2], st4[:, 1:2], AF.Ln, [b_st4], [b_st4], bias=EPS)
    p.act(st4[:, 1:2], st4[:, 1:2], AF.Exp, [b_st4], [b_st4], scale=-0.5)
    p.stt("dve", st4[:, 2:3], st4[:, 0:1], -1.0, st4[:, 1:2], ALU.mult, ALU.mult, [b_st4], [b_st4])
    p.act(tmp, z, AF.Identity, [b_z, b_st4], [b_tmp], bias=st4[:, 2:3], scale=st4[:, 1:2])
    p.tt("dve", tmp, tmp, gt, ALU.mult, [b_tmp, b_gb], [b_tmp])
    p.tt("pool", out, tmp, bt, ALU.add, [b_tmp, b_gb], [b_out])


NOMASK = False


def emit_B(p, g, l, X, KT_all, V_all, HT_all, LF_all, QT_loc, IW_loc, HT_loc, NEGT, W, consts, x1, b_x1, lamc=None, dbg=None):
    J, T = g.J, g.T
    NB = 4 * J
    L = NB * 128
    lambda_init = 0.8 - 0.6 * math.exp(-0.3 * l)
    stB = ExitStack()
    yT = p.sb("b_yT", [128, 8, T], BF16, stB)
    b_yT = Buf()
    sel = p.sb("b_sel", [128, 4], F32, stB)
    b_c = Buf()
    p.dma("sp", sel[:], consts["sel"], (), [b_c])
    maskT = p.sb("b_maskT", [128, 4, 128], F32, stB)
    p.dma("sp", maskT[:], consts["maskT"].rearrange("c k q -> k c q"), (), [b_c])
    b31 = p.sb("b_b31", [128, 8], F32, stB)
    p.dma("sp", b31[:], W["rel31"].partition_broadcast(128), (), [b_c])
    ones = p.sb("b_ones", [128, 128], F32, stB)
    p.memset("pool", ones[:], 1.0, [b_c])

    st = ExitStack()
    hc = p.sb("c_hc", [128, 2, J, 158], F32, st)
    b_hc = Buf()
    cand = p.sb("c_cand", [128, 4, 2, J, 30], F32, st)
    b_cand = Buf()
    cw = p.sb("c_cw", [128, 2, 31], F32, st)
    cpar = p.sb("c_par", [128, 2, 3], F32, st)
    b_cw = Buf()
    cwr = p.sb("c_cwr", [31, 256], F32, st)
    p.dma("sp", cwr[:], W["conv_w"], (), [b_cw])
    for cc in range(2):
        p.tr(g.banks[0][:, cc * 32:cc * 32 + 31], cwr[:, cc * 128:(cc + 1) * 128], g.identf[0:31, 0:31], [b_cw, g.b_ident], [g.bb[0]])
    p.copy("dve", cw[:], g.banks[0][:, 0:64].rearrange("c (cc w) -> c cc w", cc=2)[:, :, 0:31], [g.bb[0]], [b_cw])
    for i, nm in enumerate(["conv_b", "conv_ln_g", "conv_ln_b"]):
        for cc in range(2):
            p.dma("sp", cpar[:, cc, i:i + 1], W[nm].rearrange("(cc c o) -> cc c o", c=128, o=1)[cc], (), [b_cw])
    p.memset("pool", cand[:], 0.0, [b_cand])
    for cc in range(2):
        p.dma("sp", hc[:, cc, :, 30:158], HT_loc[cc * 128:(cc + 1) * 128, :].rearrange("c (j t) -> c j t", t=128), (), [b_hc])
        for c in range(4):
            if c >= 1:
                for ap_, off, n in HT_all.rows(c - 1, cc * 128, 128):
                    p.dma("sp", cand[off:off + n, c, cc, :, :], ap_.rearrange("c (j t) -> c j t", t=128)[:, :, 98:128], (), [b_cand])
            elif J > 1:
                for ap_, off, n in HT_all.rows(3, cc * 128, 128):
                    p.dma("sp", cand[off:off + n, c, cc, 1:J, :], ap_.rearrange("c (j t) -> c j t", t=128)[:, 0:J - 1, 98:128], (), [b_cand])
    for cc in range(2):
        e = "dve"
        p.ts(e, hc[:, cc, :, 0:30], cand[:, 0, cc, :, :], sel[:, 0:1], None, ALU.mult, None, [b_cand, b_c], [b_hc])
        for c in range(1, 4):
            p.stt(e, hc[:, cc, :, 0:30], cand[:, c, cc, :, :], sel[:, c:c + 1], hc[:, cc, :, 0:30], ALU.mult, ALU.add,
                  [b_cand, b_c, b_hc], [b_hc])
    cacc = p.sb("c_acc", [128, 2, J, 128], F32, st)
    b_cacc = _bufs(2)
    for cc in range(2):
        e = "dve"
        p.ts(e, cacc[:, cc], hc[:, cc, :, 0:128], cw[:, cc, 0:1], cpar[:, cc, 0:1], ALU.mult, ALU.add, [b_hc, b_cw], [b_cacc[cc]])
        for w in range(1, 31):
            p.stt(e, cacc[:, cc], hc[:, cc, :, w:w + 128], cw[:, cc, w:w + 1], cacc[:, cc], ALU.mult, ALU.add,
                  [b_hc, b_cw, b_cacc[cc]], [b_cacc[cc]])
    TG = min(512, T)
    csq = p.sb("c_sq", [128, 2, TG], F32, st)
    b_csq = Buf()
    cmean = p.sb("c_mean", [128, TG], F32, st)
    crstd = p.sb("c_rstd", [128, TG], F32, st)
    b_cst = Buf()
    cd = p.sb("c_d", [128, 2, TG], F32, st)
    b_cd = _bufs(2)
    p.ts("dve", ones[:], ones[:], 1.0 / 256.0, None, ALU.mult, None, [b_c], [b_c])
    for tg in range(T // TG):
        flat = [cacc[:, cc].rearrange("c j t -> c (j t)")[:, tg * TG:(tg + 1) * TG] for cc in range(2)]
        for cc in range(2):
            p.tt("pool", csq[:, cc, :], flat[cc], flat[cc], ALU.mult, [b_cacc[cc]], [b_csq])
        for cc in range(2):
            p.mm(g.banks[0][:, 0:TG], ones[:], flat[cc], cc == 0, cc == 1, [b_c, b_cacc[cc]], [g.bb[0]])
        for cc in range(2):
            p.mm(g.banks[1][:, 0:TG], ones[:], csq[:, cc, :], cc == 0, cc == 1, [b_c, b_csq], [g.bb[1]])
        p.copy("act", cmean[:], g.banks[0][:, 0:TG], [g.bb[0]], [b_cst])
        p.tt("dve", crstd[:], cmean[:], cmean[:], ALU.mult, [b_cst], [b_cst])
        p.tt("dve", crstd[:], g.banks[1][:, 0:TG], crstd[:], ALU.subtract, [g.bb[1], b_cst], [b_cst])
        p.act(crstd[:], crstd[:], AF.Ln, [b_cst], [b_cst], bias=EPS)
        p.act(crstd[:], crstd[:], AF.Exp, [b_cst], [b_cst], scale=-0.5)
        for cc in range(2):
            p.tt("dve", cd[:, cc, :], flat[cc], cmean[:], ALU.subtract, [b_cacc[cc], b_cst], [b_cd[cc]])
            p.tt("dve", cd[:, cc, :], cd[:, cc, :], crstd[:], ALU.mult, [b_cd[cc], b_cst], [b_cd[cc]])
            p.ts("dve", cd[:, cc, :], cd[:, cc, :], cpar[:, cc, 1:2], cpar[:, cc, 2:3], ALU.mult, ALU.add, [b_cd[cc], b_cw], [b_cd[cc]])
            p.act(csq[:, cc, :], cd[:, cc, :], AF.Sigmoid, [b_cd[cc]], [b_csq])
            p.tt("dve", yT[:, 6 + cc, tg * TG:(tg + 1) * TG], cd[:, cc, :], csq[:, cc, :], ALU.mult, [b_cd[cc], b_csq], [b_yT])
    p.ts("dve", ones[:], ones[:], 256.0, None, ALU.mult, None, [b_c], [b_c])
    p.barrier()
    st.close()

    R = 4 * J * 4
    cn = p.sb("f_cn", [128, J, 4, 4], F32, stB)
    offs = p.sb("f_offs", [128, J, 4, 4], F32, stB)
    offl = p.sb("f_offl", [128, J, 4], F32, stB)
    vs = p.sb("f_vs", [128, J, 4, 4], F32, stB)
    bjr = p.sb("f_bjr", [128, J, J, 4], F32, stB)
    lam = p.sb("d_lam", [128, 4], F32, stB)
    dng = p.sb("d_g", [128, 64], F32, stB)
    b_f = Buf()
    st = ExitStack()
    triu = p.sb("f_triu", [128, 128], F32, st)
    p.dma("sp", triu[:], consts["triu"], (), [b_f])
    nt = (R + 127) // 128
    lfr = p.sb("f_lfr", [128, nt, 128], F32, st)
    lfn = p.sb("f_lfn", [128, R], F32, st)
    sc_a = p.sb("f_sca", [128, NB, 4], F32, st)
    sc_b = p.sb("f_scb", [128, NB, 4], F32, st)
    for t in range(nt):
        rr = min(128, R - t * 128)
        p.dma("sp", lfr[0:rr, t, :], LF_all[t * 128:t * 128 + rr, :], (), [b_f])
        p.tr(g.banks[0][:, t * 128:t * 128 + rr], lfr[0:rr, t, :], g.identf[0:rr, 0:rr], [b_f, g.b_ident], [g.bb[0]])
    p.copy("dve", lfn[:], g.banks[0][:, 0:R], [g.bb[0]], [b_f])
    p.mm(g.banks[1][:, 0:R], triu[:], lfn[:], True, True, [b_f], [g.bb[1]])
    p.memset("pool", ones[:], 1.0, [b_c])
    p.mm(g.banks[2][:, 0:R], ones[:], lfn[:], True, True, [b_f, b_c], [g.bb[2]])
    nat = lambda ap: ap.rearrange("p (c j h) -> p j c h", c=4, j=J)
    p.copy("dve", sc_a[:].rearrange("p (j c) h -> p j c h", c=4), nat(g.banks[2][:, 0:R]), [g.bb[2]], [b_f])
    p.copy("dve", offs[:], nat(g.banks[2][:, 0:R]), [g.bb[2]], [b_f])
    a, b = sc_a, sc_b
    s = 1
    while s < NB:
        p.copy("dve", b[:, 0:s, :], a[:, 0:s, :], [b_f], [b_f])
        p.tt("dve", b[:, s:NB, :], a[:, s:NB, :], a[:, 0:NB - s, :], ALU.add, [b_f], [b_f])
        a, b = b, a
        s *= 2
    offs_f = offs[:].rearrange("p j c h -> p (j c) h")
    p.tt("dve", offs_f, a[:], offs_f, ALU.subtract, [b_f], [b_f])
    p.tt("dve", cn[:], nat(g.banks[1][:, 0:R]), offs[:], ALU.add, [g.bb[1], b_f], [b_f])
    p.ts("dve", offl[:], offs[:, :, 0, :], sel[:, 0:1], None, ALU.mult, None, [b_f, b_c], [b_f])
    for c in range(1, 4):
        p.stt("dve", offl[:], offs[:, :, c, :], sel[:, c:c + 1], offl[:], ALU.mult, ALU.add, [b_f, b_c], [b_f])
    for c in range(4):
        p.tt("dve", vs[:, :, c, :], offs[:, :, 0, :], cn[:, :, c, :], ALU.subtract, [b_f], [b_f])
    vsf = vs[:].rearrange("p j c h -> p (j c h)")
    p.act(vsf, vsf, AF.Exp, [b_f], [b_f])
    for j in range(J):
        for h in range(4):
            p.ts("dve", bjr[:, j, :, h], offs[:, :, 0, h], -1.0, offl[:, j, h:h + 1], ALU.mult, ALU.add, [b_f], [b_f])
    dlf = p.sb("d_dl", [128, 128], F32, st)
    p.dma("sp", dlf[:], W["diff_lambda"].rearrange("a b -> (a b)").partition_broadcast(128), (), [b_f])
    dl = dlf[:].rearrange("p (a b) -> p a b", a=4)
    p.memset("dve", lam[:], 0.0, [b_f])
    p.dma("sp", dng[:], W["diff_norm_g"].partition_broadcast(128), (), [b_f])
    p.tt("dve", dl[:, 0, :], dl[:, 0, :], dl[:, 1, :], ALU.mult, [b_f], [b_f])
    p.tt("dve", dl[:, 2, :], dl[:, 2, :], dl[:, 3, :], ALU.mult, [b_f], [b_f])
    p.ts("dve", dl[:, 1, :], dl[:, 0, :], 1.0, 0.0, ALU.mult, ALU.add, [b_f], [b_f], accum=lam[:, 0:1])
    p.ts("dve", dl[:, 3, :], dl[:, 2, :], 1.0, 0.0, ALU.mult, ALU.add, [b_f], [b_f], accum=lam[:, 1:2])
    p.act(lam[:, 0:2], lam[:, 0:2], AF.Exp, [b_f], [b_f])
    p.tt("dve", lam[:, 2:3], lam[:, 1:2], lam[:, 0:1], ALU.subtract, [b_f], [b_f])
    if lamc is None:
        p.ts("dve", lam[:, 2:3], lam[:, 2:3], -lambda_init, None, ALU.add, None, [b_f], [b_f])
        p.ts("dve", dng[:], dng[:], 1.0 - lambda_init, None, ALU.mult, None, [b_f], [b_f])
    else:
        lc = p.sb("d_lc", [128, 2], F32, st)
        p.dma("sp", lc[:], lamc, (), [b_f])
        p.ts("dve", lam[:, 2:3], lam[:, 2:3], lc[:, 0:1], None, ALU.add, None, [b_f], [b_f])
        p.ts("dve", dng[:], dng[:], lc[:, 1:2], None, ALU.mult, None, [b_f], [b_f])
    p.barrier()
    st.close()

    stA = ExitStack()
    ALIAS = (J == 16)
    x1flat = x1[:].rearrange("p a b -> p (a b)")
    if ALIAS:
        kt = [x1flat[:, i * 4096:(i + 1) * 4096].bitcast(BF16) for i in range(2)]
        vt = [x1flat[:, 8192:8192 + 4160].bitcast(BF16).rearrange("p (a b c) -> p a b c", a=NB, b=2, c=65),
              p.sb("a_v1", [128, NB, 2, 65], BF16, stA)]
    else:
        kt = [p.sb("a_kt%d" % i, [128, L], BF16, stA) for i in range(2)]
        vt = [p.sb("a_v%d" % i, [128, NB, 2, 65], BF16, stA) for i in range(2)]
    qt = [p.sb("a_q%d" % i, [128, 2, T], BF16, stA) for i in range(2)]
    bti = [p.sb("a_bt%d" % i, [128, 2, 8, 128], F32, stA) for i in range(2)]
    b_kv = _bufs(2)
    pT = p.sb("a_pT", [128, 4, 512], BF16, stA)
    b_pT = _bufs(4)
    ssb = p.sb("a_ssb", [128, 2, 512], F32, stA)
    b_ssb = _bufs(2)
    ysb = p.sb("a_ysb", [128, 2, 128], F32, stA)
    b_ysb = _bufs(2)
    yd = p.sb("a_yd", [128, 2, 66], F32, stA)
    b_yd = _bufs(2)
    rec = p.sb("a_rec", [128, 8], F32, stA)
    b_rec = Buf()
    for i in range(2):
        p.memset("pool", vt[i][:, :, :, 64:65], 1.0, [b_kv[i]])
    rot = {"s": 0, "pt": 0, "ss": 0, "o": 0}
    SB = [0, 1, 2]
    OB = [3, 4, 5, 6]
    TB = 7

    def load_pair(slot, kind, pr):
        kch = kind * 2 + pr
        for c in range(4):
            for ap_, off, n in KT_all.rows(c, kch * 128, 128):
                dst = kt[slot][off:off + n, :].rearrange("r (j c t) -> r j c t", c=4, t=128)[:, :, c, :]
                p.dma("sp", dst, ap_.rearrange("r (j t) -> r j t", t=128), (), [b_kv[slot]])
            for hh in range(2):
                c0 = kind * 256 + pr * 128 + hh * 64
                for j_ in range(J):
                    for ap_, off, n in V_all.rows(c, j_ * 128, 128):
                        p.dma("sp", vt[slot][off:off + n, 4 * j_ + c, hh, 0:64], ap_[:, c0:c0 + 64], (), [b_kv[slot]])
        if kind == 0:
            p.dma("sp", qt[slot][:, 0, :], QT_loc[pr], (), [b_kv[slot]])
            for hh in range(2):
                h = pr * 2 + hh
                vsh = vs[:].rearrange("p j c h -> p (j c) h")[:, :, h:h + 1]
                p.tt("pool", vt[slot][:, :, hh, 0:64], vt[slot][:, :, hh, 0:64], vsh.to_broadcast([128, NB, 64]), ALU.mult,
                     [b_f, b_kv[slot]], [b_kv[slot]])
                p.copy("pool", vt[slot][:, :, hh, 64:65], vsh, [b_f, b_kv[slot]], [b_kv[slot]])
        elif kind == 1:
            p.dma("sp", qt[slot][:, 0, :], QT_loc[2 + pr], (), [b_kv[slot]])
            p.dma("sp", qt[slot][:, 1, :], QT_loc[4 + pr], (), [b_kv[slot]])
        else:
            p.dma("sp", qt[slot][:, 0, :], QT_loc[6 + pr], (), [b_kv[slot]])
        if kind > 0:
            hb = (kind - 1) * 4 + pr * 2
            p.dma("sp", bti[slot][:], W["biasT"][hb:hb + 2].rearrange("h t k q -> k h t q"), (), [b_kv[slot]])
            for hh in range(2):
                p.tt("pool", bti[slot][:, hh, 0:4, :], bti[slot][:, hh, 0:4, :], maskT[:], ALU.add, [b_kv[slot], b_c], [b_kv[slot]])
            p.memset("pool", vt[slot][:, :, :, 64:65], 1.0, [b_kv[slot]])

    def run_maps(slot, kind, pr, j, ngT=None, b_ng=None):
        nrow = j + 1
        if kind == 1:
            maps = [(hh, m) for hh in range(2) for m in range(2)]
        else:
            maps = [(hh, 0) for hh in range(2)]
        scale = (32 ** -0.5) if kind == 1 else 0.125
        obank = {}
        items = [(mi, r) for mi in range(len(maps)) for r in range(nrow)]
        pend = {}

        def qk(it):
            mi, r = it
            hh, m = maps[mi]
            sbk = SB[rot["s"] % 3]
            rot["s"] += 1
            rows = slice(hh * 64, hh * 64 + 64)
            for c in range(4):
                blk = 4 * r + c
                p.mm(g.banks[sbk][:, c * 128:(c + 1) * 128], kt[slot][rows, blk * 128:(blk + 1) * 128],
                     qt[slot][rows, m, j * 128:(j + 1) * 128], True, True, [b_kv[slot]], [g.bb[sbk]])
            ps_ = rot["pt"] % 4
            rot["pt"] += 1
            h = pr * 2 + hh
            if kind == 0:
                bias = bjr[:, j, r, h:h + 1]
                near = (r == j)
                bread = [b_f]
            else:
                gh = (kind - 1) * 4 + h
                bias = b31[:, gh:gh + 1]
                near = (r >= j - 1)
                bread = [b_c]
            if near:
                ss = rot["ss"] % 2
                rot["ss"] += 1
                for c in range(4):
                    if kind == 0:
                        tile_ = maskT[:, c, :]
                        rd = [b_c]
                    else:
                        tile_ = bti[slot][:, hh, (0 if r == j else 4) + c, :]
                        rd = [b_kv[slot]]
                    p.stt("dve", ssb[:, ss, c * 128:(c + 1) * 128], g.banks[sbk][:, c * 128:(c + 1) * 128], scale, tile_,
                          ALU.mult, ALU.add, [g.bb[sbk]] + rd, [b_ssb[ss]])
                if kind == 0:
                    p.act(pT[:, ps_, :], ssb[:, ss, :], AF.Exp, [b_ssb[ss]] + bread, [b_pT[ps_]], bias=bias)
                else:
                    p.act(pT[:, ps_, :], ssb[:, ss, :], AF.Exp, [b_ssb[ss]], [b_pT[ps_]])
            else:
                p.act(pT[:, ps_, :], g.banks[sbk][:, :], AF.Exp, [g.bb[sbk]] + bread, [b_pT[ps_]], bias=bias, scale=scale)
            if kind == 2:
                p.tt("pool", pT[:, ps_, :], pT[:, ps_, :], ngT[:, 4 * r:4 * r + 4, :].rearrange("k c q -> k (c q)"), ALU.mult,
                     [b_pT[ps_], b_ng], [b_pT[ps_]])
            pend[it] = ps_

        def pv(it):
            mi, r = it
            hh, m = maps[mi]
            ps_ = pend.pop(it)
            if r == 0:
                obank[mi] = OB[rot["o"] % 4]
                rot["o"] += 1
            ob = obank[mi]
            for c in range(4):
                blk = 4 * r + c
                p.mm(g.banks[ob][:, 0:65], pT[:, ps_, c * 128:(c + 1) * 128], vt[slot][:, blk, hh, :],
                     r == 0 and c == 0, r == nrow - 1 and c == 3, [b_pT[ps_], b_kv[slot]], [g.bb[ob]])
            if r == nrow - 1:
                finalize(mi)

        def finalize(mi):
            hh, m = maps[mi]
            ob = obank[mi]
            ys = (j * 2 + pr) % 2
            if kind != 1:
                col = mi
                p.op("dve", lambda e: e.reciprocal(out=rec[:, col:col + 1], in_=g.banks[ob][:, 64:65]), [g.bb[ob]], [b_rec])
                p.ts("dve", ysb[:, ys, hh * 64:(hh + 1) * 64], g.banks[ob][:, 0:64], rec[:, col:col + 1], None, ALU.mult, None,
                     [g.bb[ob], b_rec], [b_ysb[ys]])
            else:
                col = mi
                p.op("dve", lambda e: e.reciprocal(out=rec[:, col:col + 1], in_=g.banks[ob][:, 64:65]), [g.bb[ob]], [b_rec])
                if m == 0:
                    p.ts("dve", yd[:, hh, 0:64], g.banks[ob][:, 0:64], rec[:, col:col + 1], None, ALU.mult, None,
                         [g.bb[ob], b_rec], [b_yd[hh]])
                else:
                    p.ts("dve", ssb[:, 0, 0:64], g.banks[ob][:, 0:64], rec[:, col:col + 1], lam[:, 2:3], ALU.mult, ALU.mult,
                         [g.bb[ob], b_rec, b_f], [b_ssb[0]])
                    p.tt("dve", yd[:, hh, 0:64], yd[:, hh, 0:64], ssb[:, 0, 0:64], ALU.add, [b_ssb[0], b_yd[hh]], [b_yd[hh]])
                    p.tt("dve", ssb[:, 0, 64:128], yd[:, hh, 0:64], yd[:, hh, 0:64], ALU.mult, [b_yd[hh]], [b_ssb[0]])
                    p.memset("dve", yd[:, hh, 64:65], 0.0, [b_yd[hh]])
                    p.ts("dve", ssb[:, 0, 64:128], ssb[:, 0, 64:128], 1.0 / 64, 0.0, ALU.mult, ALU.add, [b_ssb[0]], [b_ssb[0], b_yd[hh]],
                         accum=yd[:, hh, 64:65])
                    p.act(yd[:, hh, 64:65], yd[:, hh, 64:65], AF.Ln, [b_yd[hh]], [b_yd[hh]], bias=EPS)
                    p.act(yd[:, hh, 64:65], yd[:, hh, 64:65], AF.Exp, [b_yd[hh]], [b_yd[hh]], scale=-0.5)
                    p.stt("dve", ysb[:, ys, hh * 64:(hh + 1) * 64], yd[:, hh, 0:64], yd[:, hh, 64:65], dng[:], ALU.mult, ALU.mult,
                          [b_yd[hh], b_f], [b_ysb[ys]])
            last = (mi == len(maps) - 1)
            if dbg is not None and kind == 1 and pr == 0 and j == 0 and mi == 1:
                def dd(name, ap, shape, bufs):
                    t_ = p.nc.dram_tensor("dbg_" + name, shape, F32, kind="ExternalOutput").ap()
                    p.dma("sp", t_, ap, bufs, ())
                dd("yd", yd[:], [128, 2, 66], [b_yd[0], b_yd[1]])
                dd("rec", rec[:], [128, 8], [b_rec])
                dd("bti", bti[slot][:, 0, 0, :], [128, 128], [b_kv[slot]])
                dd("lam", lam[:], [128, 4], [b_f])
                dd("dng", dng[:], [128, 64], [b_f])
                dd("ssb", ssb[:], [128, 2, 512], [b_ssb[0], b_ssb[1]])
                dd("ysb", ysb[:], [128, 2, 128], [b_ysb[0], b_ysb[1]])
            if last:
                p.tr(g.banks[TB][:, 0:128], ysb[:, ys, :], g.identf[:], [b_ysb[ys], g.b_ident], [g.bb[TB]])
                p.copy("act", yT[:, kind * 2 + pr, j * 128:(j + 1) * 128], g.banks[TB][:, 0:128], [g.bb[TB]], [b_yT])

        n = len(items)
        for i in range(n + 1):
            if i < n:
                qk(items[i])
            if i > 0:
                pv(items[i - 1])

    seq = [(0, 0), (0, 1), (1, 0), (1, 1)]
    load_pair(0, *seq[0])
    for i, (kind, pr) in enumerate(seq):
        if i + 1 < len(seq):
            load_pair((i + 1) % 2, *seq[i + 1])
        for j in range(J):
            run_maps(i % 2, kind, pr, j)
    p.barrier()

    st = ExitStack()
    if ALIAS:
        sc = x1flat[:, 0:8192]
        ik2 = x1flat[:, 8192:12288].bitcast(BF16)
        iq = x1flat[:, 12288:14336].bitcast(BF16).rearrange("p (a b) -> p a b", a=2)
    else:
        sc = p.sb("i_sc", [128, L], F32, st)
        ik2 = p.sb("i_ik2", [128, L], BF16, st)
        iq = p.sb("i_iq", [128, 2, T], BF16, st)
    b_sc = Buf()
    iwt = p.sb("i_iw", [128, J, 4], F32, st)
    mq = p.sb("i_mq", [128, 4, 128], F32, st)
    mqa = p.sb("i_mqa", [128, 4, 128], F32, st)
    b_i = Buf()
    if ALIAS:
        rl = x1flat[:, 14336:16384].rearrange("p (a b) -> p a b", a=4)
    else:
        rl = p.sb("i_rl", [128, 4, 512], F32, st)
    b_rl = _bufs(4)
    m8 = p.sb("i_m8", [128, 8], F32, st)
    b_m8 = Buf()
    nev = p.sb("i_nev", [128, 2, 512], BF16, st)
    b_nev = _bufs(2)
    for c in range(4):
        for ap_, off, n in KT_all.rows(c, 768, 64):
            for half in range(2):
                dst = ik2[half * 64 + off:half * 64 + off + n, :].rearrange("r (j c t) -> r j c t", c=4, t=128)[:, :, c, :]
                p.dma("sp", dst, ap_.rearrange("r (j t) -> r j t", t=128), (), [b_i])
    p.dma("sp", iq[:, 0, :], QT_loc[8], (), [b_i])
    p.dma("sp", iq[:, 1, :], QT_loc[9], (), [b_i])
    p.dma("sp", iwt[:].rearrange("p j h -> p (j h)"), IW_loc, (), [b_i])
    p.dma("sp", mq[:], consts["maskQ"].rearrange("c q k -> q c k"), (), [b_i])
    p.ts("dve", mqa[:], mq[:], 1.0e30, None, ALU.mult, None, [b_i], [b_i])
    for j in range(J):
        Nk = 512 * (j + 1)
        for r in range(j + 1):
            ks = slice(r * 512, (r + 1) * 512)
            for hi in range(4):
                rows = slice((hi % 2) * 64, (hi % 2) * 64 + 64)
                bk = hi
                p.mm(g.banks[bk][:, :], iq[rows, hi // 2, j * 128:(j + 1) * 128], ik2[rows, ks], True, True, [b_i], [g.bb[bk]])
                p.act(rl[:, hi, :], g.banks[bk][:, :], AF.Relu, [g.bb[bk]], [b_rl[hi]])
                if hi == 0:
                    p.ts("dve", sc[:, ks], rl[:, 0, :], iwt[:, j, 0:1], None, ALU.mult, None, [b_rl[0], b_i], [b_sc])
                else:
                    p.stt("dve", sc[:, ks], rl[:, hi, :], iwt[:, j, hi:hi + 1], sc[:, ks], ALU.mult, ALU.add, [b_rl[hi], b_i, b_sc], [b_sc])
            if r == j:
                p.tt("dve", sc[:, ks], sc[:, ks], mqa[:].rearrange("q c k -> q (c k)"), ALU.add, [b_sc, b_i], [b_sc])
        for it in range(32):
            p.op("dve", lambda e: e.max(out=m8[:], in_=sc[:, 0:Nk]), [b_sc], [b_m8])
            p.op("dve", lambda e: e.match_replace(out=sc[:, 0:Nk], in_to_replace=m8[:], in_values=sc[:, 0:Nk], imm_value=-3.0e38),
                 [b_sc, b_m8], [b_sc])
        p.ts("dve", sc[:, 0:Nk], sc[:, 0:Nk], -1.0e35, 1.0, ALU.is_le, ALU.subtract, [b_sc], [b_sc])
        ks = slice(j * 512, (j + 1) * 512)
        p.tt("dve", sc[:, ks], sc[:, ks], mq[:].rearrange("q c k -> q (c k)"), ALU.min, [b_sc, b_i], [b_sc])
        if dbg is not None and j == 0:
            t_ = p.nc.dram_tensor("dbg_sc", [128, 512], F32, kind="ExternalOutput").ap()
            p.dma("sp", t_, sc[:, 0:512], [b_sc], ())
        for r in range(j + 1):
            bk = 4 + (r % 2)
            for c in range(4):
                blk = 4 * r + c
                p.tr(g.banks[bk][:, c * 128:(c + 1) * 128], sc[:, blk * 128:(blk + 1) * 128], g.identf[:], [b_sc, g.b_ident], [g.bb[bk]])
            s = r % 2
            p.act(nev[:, s, :], g.banks[bk][:, :], AF.Identity, [g.bb[bk]], [b_nev[s]], bias=1.0)
            p.dma("sp", NEGT[j, :, 4 * r:4 * r + 4, :], nev[:, s, :].rearrange("k (c q) -> k c q", c=4), [b_nev[s]], ())
    p.barrier()
    st.close()

    st = ExitStack()
    ng = [p.sb("s_ng%d" % i, [128, NB, 128], BF16, st) for i in range(2)]
    b_ngs = _bufs(2)
    load_pair(0, 2, 0)
    load_pair(1, 2, 1)
    p.dma("sp", ng[0][:, 0:4, :], NEGT[0, :, 0:4, :], (), [b_ngs[0]])
    for j in range(J):
        if j + 1 < J:
            p.dma("sp", ng[(j + 1) % 2][:, 0:4 * (j + 2), :], NEGT[j + 1, :, 0:4 * (j + 2), :], (), [b_ngs[(j + 1) % 2]])
        for pr in range(2):
            run_maps(pr, 2, pr, j, ngT=ng[j % 2], b_ng=b_ngs[j % 2])
    p.barrier()
    st.close()
    stA.close()

    if dbg is not None:
        p.dma("sp", dbg, yT[:], [b_yT], ())
    st = ExitStack()
    wo = p.sb("o_wo", [128, 8, D], BF16, st)
    b_wo = Buf()
    p.dma("pool", wo[:], W["w_out"].rearrange("(fc f) m -> f fc m", f=128), (), [b_wo])
    gt = p.sb("o_g", [128, D], F32, st)
    bt = p.sb("o_b", [128, D], F32, st)
    b_gb = Buf()
    p.dma("sp", gt[:], W["ln1_g"].partition_broadcast(128), (), [b_gb])
    p.dma("sp", bt[:], W["ln1_b"].partition_broadcast(128), (), [b_gb])
    xin = [p.sb("o_x%d" % i, [128, D], F32, st) for i in range(2)]
    b_xin = _bufs(2)
    tmp = p.sb("o_tmp", [128, D], F32, st)
    b_tmp = Buf()
    st4 = p.sb("o_st4", [128, 4], F32, st)
    b_st4 = Buf()
    Xr = X.rearrange("(j q) d -> q j d", q=128)
    for j in range(J):
        s = j % 2
        p.dma("sp", xin[s][:], Xr[:, j, :], (), [b_xin[s]])
        for mh in range(2):
            bk = mh
            for fc in range(8):
                p.mm(g.banks[bk][:, :], yT[:, fc, j * 128:(j + 1) * 128], wo[:, fc, mh * 512:(mh + 1) * 512], fc == 0, fc == 7,
                     [b_yT, b_wo], [g.bb[bk]])
            p.stt("dve", xin[s][:, mh * 512:(mh + 1) * 512], xin[s][:, mh * 512:(mh + 1) * 512], ALPHA, g.banks[bk][:, :],
                  ALU.mult, ALU.add, [b_xin[s], g.bb[bk]], [b_xin[s]])
        ln_rows(p, g, xin[s][:], b_xin[s], x1[:, j, :], b_x1, gt[:], bt[:], b_gb, tmp[:], b_tmp, st4, b_st4)
    p.barrier()
    st.close()
    stB.close()

def emit_C(p, g, W, x1, b_x1, Xout):
    J, T = g.J, g.T
    st = ExitStack()
    xT = p.sb("m_xT", [128, 8, T], BF16, st)
    b_xT = Buf()
    xTf = p.sb("m_xTf", [128, 8, 128], F32, st)
    b_xTf = Buf()
    rw = p.sb("m_rw", [128, 8, NE], F32, st)
    b_w = Buf()
    p.dma("sp", rw[:], W["router_w"].rearrange("(dc q) e -> q dc e", q=128), (), [b_w])
    rb = p.sb("m_rb", [128, NE], F32, st)
    p.dma("sp", rb[:], W["router_b"].partition_broadcast(128), (), [b_w])
    bgu = p.sb("m_bgu", [128, 16, NE], F32, st)
    bd = p.sb("m_bd", [NE, D], F32, st)
    gates = p.sb("m_gates", [128, J, NE], F32, st)
    gT = p.sb("m_gT", [NE, 128], F32, st)
    lg = p.sb("m_lg", [128, NE], F32, st)
    m8 = p.sb("m_m8", [128, 8], F32, st)
    sm = p.sb("m_sm", [128, 4], F32, st)
    st0 = ExitStack()
    bgr = p.sb("m_bgr", [NE, 2 * D], F32, st0)
    p.dma("sp", bgr[:], W["b_gu"], (), [b_w])
    bgv = bgr[:].rearrange("e (jc q t) -> e jc t q", q=128, t=2)
    for jc in range(8):
        for t in range(2):
            i = jc * 2 + t
            p.tr(g.banks[7][:, i * NE:(i + 1) * NE], bgv[:, jc, t, :], g.identf[0:NE, 0:NE], [b_w, g.b_ident], [g.bb[7]])
    p.copy("dve", bgu[:], g.banks[7][:, :].rearrange("q (i e) -> q i e", e=NE), [g.bb[7]], [b_w])
    p.barrier()
    st0.close()
    p.dma("sp", bd[:], W["b_down"], (), [b_w])
    b_gates = Buf()
    b_gT = Buf()
    b_r = Buf()
    for j in range(J):
        transpose_block(p, g, x1[:, j, :], b_x1, xT, b_xT, j, dst_f32=xTf, dst_f32_b=b_xTf, bank0=0)
        for dc in range(8):
            p.mm(g.banks[2][:, 0:NE], xTf[:, dc, :], rw[:, dc, :], dc == 0, dc == 7, [b_xTf, b_w], [g.bb[2]])
        p.tt("dve", lg[:], g.banks[2][:, 0:NE], rb[:], ALU.add, [g.bb[2], b_w], [b_r])
        p.op("dve", lambda e: e.max(out=m8[:], in_=lg[:]), [b_r], [b_r])
        p.ts("dve", sm[:, 0:1], m8[:, 0:1], -1.0, None, ALU.mult, None, [b_r], [b_r])
        p.act(gates[:, j, :], lg[:], AF.Exp, [b_r], [b_gates], bias=sm[:, 0:1])
        p.ts("dve", lg[:], lg[:], m8[:, 3:4], None, ALU.is_ge, None, [b_r], [b_r])
        p.tt("dve", gates[:, j, :], gates[:, j, :], lg[:], ALU.mult, [b_r, b_gates], [b_gates])
        p.memset("dve", sm[:, 1:2], 0.0, [b_r])
        p.ts("dve", lg[:], gates[:, j, :], 1.0, 0.0, ALU.mult, ALU.add, [b_gates], [b_r], accum=sm[:, 1:2])
        p.op("dve", lambda e: e.reciprocal(out=sm[:, 2:3], in_=sm[:, 1:2]), [b_r], [b_r])
        p.ts("dve", gates[:, j, :], gates[:, j, :], sm[:, 2:3], None, ALU.mult, None, [b_r, b_gates], [b_gates])
        p.tr(g.banks[3][0:NE, 0:128], gates[:, j, :], g.identf[:], [b_gates, g.b_ident], [g.bb[3]])
        p.copy("dve", gT[:], g.banks[3][0:NE, 0:128], [g.bb[3]], [b_gT])
        for mh in range(2):
            p.mm(g.banks[4 + mh][:, :], gT[:], bd[:, mh * 512:(mh + 1) * 512], True, True, [b_gT, b_w], [g.bb[4 + mh]])
            p.stt("dve", x1[:, j, mh * 512:(mh + 1) * 512], x1[:, j, mh * 512:(mh + 1) * 512], ALPHA, g.banks[4 + mh][:, :],
                  ALU.mult, ALU.add, [b_x1, g.bb[4 + mh]], [b_x1])
    NW = 4
    st2 = ExitStack()
    st_save, st = st, st2
    wgu = [p.sb("m_wgu%d" % i, [128, 8, 256], BF16, st) for i in range(NW)]
    b_wgu = _bufs(NW)
    wd = [p.sb("m_wd%d" % i, [128, 8, D], BF16, st) for i in range(2)]
    b_wd = _bufs(2)
    hT = p.sb("m_hT", [128, 8, T], BF16, st)
    b_hT = _bufs(8)
    TG = min(512, T)
    ntg = T // TG
    gc = p.sb("m_gc", [128, 2, TG], F32, st)
    sg = p.sb("m_sg", [128, 2, TG], F32, st)
    uc = p.sb("m_uc", [128, 2, TG], F32, st)
    b_gc, b_sg, b_uc = _bufs(2), _bufs(2), _bufs(2)
    b_acc = _bufs(J)
    wq = {"n": 0}

    def load_gu(e, jc):
        s = wq["n"] % NW
        wq["n"] += 1
        src = W["w_gu"][e].rearrange("(dc q) f -> q dc f", q=128)[:, :, jc * 256:(jc + 1) * 256]
        p.dma("pool", wgu[s][:], src, (), [b_wgu[s]])
        return s

    def load_d(e):
        s = e % 2
        p.dma("pool", wd[s][:], W["w_down"][e].rearrange("(jc q) m -> q jc m", q=128), (), [b_wd[s]])

    pre = []
    for jc in range(2):
        pre.append(load_gu(0, jc))
    load_d(0)
    k = 0
    for e in range(NE):
        for jc in range(8):
            nxt = e * 8 + jc + 2
            if nxt < NE * 8:
                pre.append(load_gu(nxt // 8, nxt % 8))
            if jc == 4 and e + 1 < NE:
                load_d(e + 1)
            s = pre.pop(0)
            wv = wgu[s][:].rearrange("q dc (j t) -> q dc t j", t=2)
            for tg in range(ntg):
                kk = k % 2
                k += 1
                tsl = slice(tg * TG, (tg + 1) * TG)
                for t, bk in ((0, kk), (1, 2 + kk)):
                    for dc in range(8):
                        p.mm(g.banks[bk][:, 0:TG], wv[:, dc, t, :], xT[:, dc, tsl], dc == 0, dc == 7, [b_wgu[s], b_xT], [g.bb[bk]])
                p.ts("dve", gc[:, kk, :], g.banks[kk][:, 0:TG], bgu[:, jc * 2, e:e + 1], 7.0, ALU.add, ALU.min, [g.bb[kk], b_w], [b_gc[kk]])
                p.act(sg[:, kk, :], gc[:, kk, :], AF.Sigmoid, [b_gc[kk]], [b_sg[kk]], scale=1.702)
                p.act(uc[:, kk, :], g.banks[2 + kk][:, 0:TG], AF.Identity, [g.bb[2 + kk], b_w], [b_uc[kk]], bias=bgu[:, jc * 2 + 1, e:e + 1])
                p.ts("pool", uc[:, kk, :], uc[:, kk, :], 7.0, -7.0, ALU.min, ALU.max, [b_uc[kk]], [b_uc[kk]])
                p.tt("pool", gc[:, kk, :], gc[:, kk, :], sg[:, kk, :], ALU.mult, [b_gc[kk], b_sg[kk]], [b_gc[kk]])
                p.stt("dve", hT[:, jc, tsl], uc[:, kk, :], 1.0, gc[:, kk, :], ALU.add, ALU.mult, [b_uc[kk], b_gc[kk]], [b_hT[jc]])
        s = e % 2
        for j in range(J):
            for mh in range(2):
                bk = 4 + (j * 2 + mh) % 4
                for jc in range(8):
                    p.mm(g.banks[bk][:, :], hT[:, jc, j * 128:(j + 1) * 128], wd[s][:, jc, mh * 512:(mh + 1) * 512], jc == 0, jc == 7,
                         [b_hT[jc], b_wd[s]], [g.bb[bk]])
                p.stt("dve", x1[:, j, mh * 512:(mh + 1) * 512], g.banks[bk][:, :], gates[:, j, e:e + 1], x1[:, j, mh * 512:(mh + 1) * 512],
                      ALU.mult, ALU.add, [g.bb[bk], b_gates, b_acc[j], b_x1], [b_acc[j]])
    p.barrier()
    st2.close()
    st = st_save
    gt = p.sb("m_g", [128, D], F32, st)
    bt = p.sb("m_b", [128, D], F32, st)
    b_gb = Buf()
    p.dma("sp", gt[:], W["ln2_g"].partition_broadcast(128), (), [b_gb])
    p.dma("sp", bt[:], W["ln2_b"].partition_broadcast(128), (), [b_gb])
    tmp = p.sb("m_tmp", [128, D], F32, st)
    b_tmp = Buf()
    st4 = p.sb("m_st4", [128, 4], F32, st)
    b_st4 = Buf()
    Xr = Xout.rearrange("(j q) d -> q j d", q=128)
    for j in range(J):
        ln_rows(p, g, x1[:, j, :], b_acc[j], x1[:, j, :], b_acc[j], gt[:], bt[:], b_gb, tmp[:], b_tmp, st4, b_st4)
        p.dma("sp", Xr[:, j, :], x1[:, j, :], [b_acc[j]], ())
    p.barrier()
    st.close()


def _bucket(d):
    n = np.maximum(d, 0)
    nf = np.maximum(n, 1).astype(np.float32)
    large = 16 + (np.log(nf / np.float32(16)) / np.float32(math.log(8.0)) * np.float32(16)).astype(np.int32)
    large = np.minimum(large, 31)
    return np.where(n < 16, n, large)


def make_consts(c_me):
    q = np.arange(128)
    cs = {}
    cs["identf"] = np.eye(128, dtype=np.float32)
    dqm = np.zeros((128, 2), np.float32)
    dqm[:, 0] = ((q % 64) < 32)
    dqm[:, 1] = 1.0 - dqm[:, 0]
    cs["dqmask"] = dqm
    sel = np.zeros((128, 4), np.float32)
    sel[:, c_me] = 1.0
    cs["sel"] = sel
    cs["triu"] = np.triu(np.ones((128, 128), np.float32))
    mT = np.zeros((4, 128, 128), np.float32)
    for c in range(4):
        dl = c_me - c
        if dl < 0:
            mT[c] = 1.0
        elif dl == 0:
            mT[c] = (q[:, None] > q[None, :]).astype(np.float32)
    cs["maskT"] = mT * np.float32(NEG)
    cs["maskQ"] = -np.ascontiguousarray(mT.transpose(0, 2, 1))
    return cs


def bias_tile_index(c_me):
    q = np.arange(128)
    idx = np.zeros((8, 128, 128), np.int64)
    for t in range(8):
        dl = (c_me - t) if t < 4 else (4 + c_me - (t - 4))
        d = 128 * dl + q[None, :] - q[:, None]
        idx[t] = _bucket(d)
    return idx


CONST_SHAPES = {"identf": [128, 128], "dqmask": [128, 2], "sel": [128, 4], "triu": [128, 128],
                "maskT": [4, 128, 128], "maskQ": [4, 128, 128]}


def declare_consts(nc):
    return {k: nc.dram_tensor("c_" + k, shp, F32, kind="ExternalInput").ap() for k, shp in CONST_SHAPES.items()}


WSHAPES = {"rel31": [8], "biasT": [8, 8, 128, 128], "diff_lambda": [4, 32], "diff_norm_g": [64], "conv_w": [31, 256],
           "conv_b": [256], "conv_ln_g": [256], "conv_ln_b": [256], "w_out": [D, D], "ln1_g": [D], "ln1_b": [D],
           "router_w": [D, NE], "router_b": [NE], "w_gu": [NE, D, 2 * D], "b_gu": [NE, 2 * D], "w_down": [NE, D, D],
           "b_down": [NE, D], "ln2_g": [D], "ln2_b": [D]}


def build_A(J):
    nc = bass.Bass("TRN2", target_bir_lowering=False)
    T = J * 128
    X = nc.dram_tensor("X", [T, D], F32, kind="ExternalInput").ap()
    w_in = nc.dram_tensor("w_in", [D, NIN], F32, kind="ExternalInput").ap()
    forget_b = nc.dram_tensor("forget_b", [4], F32, kind="ExternalInput").ap()
    consts = declare_consts(nc)
    KT_loc = nc.dram_tensor("KT_loc", [KT_ROWS, T], BF16, kind="ExternalOutput").ap()
    V_loc = nc.dram_tensor("V_loc", [T, 768], BF16, kind="ExternalOutput").ap()
    HT_loc = nc.dram_tensor("HT_loc", [256, T], F32, kind="ExternalOutput").ap()
    LF_loc = nc.dram_tensor("LF_loc", [J * 4, 128], F32, kind="ExternalOutput").ap()
    QT_loc = nc.dram_tensor("QT_loc", [NQCH, 128, T], BF16, kind="ExternalOutput").ap()
    IW_loc = nc.dram_tensor("IW_loc", [128, J * 4], F32, kind="ExternalOutput").ap()
    p = Prog(nc)
    g = setup_common(p, J, consts)
    emit_A(p, g, X, w_in, forget_b, consts, KT_loc, V_loc, HT_loc, LF_loc, QT_loc, IW_loc)
    p.finish()
    return nc


def build_BC(J, do_C=True, debug=False):
    nc = bass.Bass("TRN2", target_bir_lowering=False)
    T = J * 128
    NB = 4 * J
    X = nc.dram_tensor("X", [T, D], F32, kind="ExternalInput").ap()
    KT_all = nc.dram_tensor("KT_all", [4 * KT_ROWS, T], BF16, kind="ExternalInput").ap()
    V_all = nc.dram_tensor("V_all", [4 * T, 768], BF16, kind="ExternalInput").ap()
    HT_all = nc.dram_tensor("HT_all", [4 * 256, T], F32, kind="ExternalInput").ap()
    LF_all = nc.dram_tensor("LF_all", [4 * J * 4, 128], F32, kind="ExternalInput").ap()
    QT_loc = nc.dram_tensor("QT_loc", [NQCH, 128, T], BF16, kind="ExternalInput").ap()
    IW_loc = nc.dram_tensor("IW_loc", [128, J * 4], F32, kind="ExternalInput").ap()
    HT_loc = nc.dram_tensor("HT_loc", [256, T], F32, kind="ExternalInput").ap()
    lamc = nc.dram_tensor("lamc", [128, 2], F32, kind="ExternalInput").ap()
    W = {k: nc.dram_tensor(k, shp, F32, kind="ExternalInput").ap() for k, shp in WSHAPES.items()}
    consts = declare_consts(nc)
    NEGT = nc.dram_tensor("NEGT", [J, 128, NB, 128], BF16, kind="Internal").ap()
    Xout = nc.dram_tensor("Xout", [T, D], F32, kind="ExternalOutput").ap()
    p = Prog(nc)
    g = setup_common(p, J, consts)
    x1 = p.sb("x1", [128, J, D], F32)
    b_x1 = Buf()
    dbg = nc.dram_tensor("dbg_yT", [128, 8, T], BF16, kind="ExternalOutput").ap() if debug else None
    emit_B(p, g, 0, X, Gat(KT_ROWS, mono=KT_all), Gat(T, mono=V_all), Gat(256, mono=HT_all), LF_all, QT_loc, IW_loc, HT_loc, NEGT, W,
           consts, x1, b_x1, lamc=lamc, dbg=dbg)
    if do_C:
        emit_C(p, g, W, x1, b_x1, Xout)
    else:
        p.dma("sp", Xout.rearrange("(j q) d -> q j d", q=128), x1[:], [b_x1], ())
    p.finish()
    return nc


def build_fused(J, nlayers=DEPTH):
    nc = bass.Bass("TRN2", target_bir_lowering=False)
    T = J * 128
    NB = 4 * J
    Xin = nc.dram_tensor("X", [T, D], F32, kind="ExternalInput").ap()
    w_in = nc.dram_tensor("w_in", [nlayers, D, NIN], F32, kind="ExternalInput").ap()
    forget_b = nc.dram_tensor("forget_b", [nlayers, 4], F32, kind="ExternalInput").ap()
    Wf = {}
    for k, shp in WSHAPES.items():
        if k in ("rel31", "biasT"):
            Wf[k] = nc.dram_tensor(k, shp, F32, kind="ExternalInput").ap()
        else:
            Wf[k] = nc.dram_tensor(k, [nlayers] + shp, F32, kind="ExternalInput").ap()
    consts = declare_consts(nc)
    Xout = nc.dram_tensor("Xout", [T, D], F32, kind="ExternalOutput").ap()
    Xbuf = nc.dram_tensor("Xbuf", [T, D], F32, kind="Internal").ap()
    KT_loc = nc.dram_tensor("KT_loc", [KT_ROWS, T], BF16, kind="Internal").ap()
    V_loc = nc.dram_tensor("V_loc", [T, 768], BF16, kind="Internal").ap()
    HT_loc = nc.dram_tensor("HT_loc", [256, T], F32, kind="Internal").ap()
    LF_loc = nc.dram_tensor("LF_loc", [J * 4, 128], F32, kind="Internal").ap()
    QT_loc = nc.dram_tensor("QT_loc", [NQCH, 128, T], BF16, kind="Internal").ap()
    IW_loc = nc.dram_tensor("IW_loc", [128, J * 4], F32, kind="Internal").ap()
    SK, SV, SH = 64, 128, 32
    def mk(name, R, S, W_, dt):
        return [Gat(R, S=S, chunks=[nc.dram_tensor("%s%d_%d" % (name, i, k), [4 * S, W_], dt, kind="Internal").ap()
                                    for k in range(R // S)]) for i in range(2)]
    KT_alls = mk("KTa", KT_ROWS, SK, T, BF16)
    V_alls = mk("Va", T, SV, 768, BF16)
    HT_alls = mk("HTa", 256, SH, T, F32)
    LF_alls = [nc.dram_tensor("LF_all%d" % i, [4 * J * 4, 128], F32, kind="Internal").ap() for i in range(2)]
    NEGT = nc.dram_tensor("NEGT", [J, 128, NB, 128], BF16, kind="Internal").ap()
    p = Prog(nc)
    g = setup_common(p, J, consts)
    x1 = p.sb("x1", [128, J, D], F32)
    b_x1 = Buf()
    groups = [[0, 1, 2, 3], [4, 5, 6, 7]]
    for l in range(nlayers):
        Xsrc = Xin if l == 0 else Xbuf
        Xdst = Xout if l == nlayers - 1 else Xbuf
        KT_all, V_all, HT_all, LF_all = KT_alls[l % 2], V_alls[l % 2], HT_alls[l % 2], LF_alls[l % 2]
        emit_A(p, g, Xsrc, w_in[l], forget_b[l], consts, KT_loc, V_loc, HT_loc, LF_loc, QT_loc, IW_loc)
        for src, gat in ((KT_loc, KT_all), (V_loc, V_all), (HT_loc, HT_all)):
            for k, ch in enumerate(gat.chunks):
                p.collective(src[k * gat.S:(k + 1) * gat.S, :], ch, groups)
        p.collective(LF_loc, LF_all, groups)
        p.barrier()
        W = {k: (v if k in ("rel31", "biasT") else v[l]) for k, v in Wf.items()}
        emit_B(p, g, l, Xsrc, KT_all, V_all, HT_all, LF_all, QT_loc, IW_loc, HT_loc, NEGT, W, consts, x1, b_x1)
        emit_C(p, g, W, x1, b_x1, Xdst)
    p.finish()
    return nc


J_FULL = 16
FUSED = True
_CACHE = {}


def _loc(a, J):
    L = a.shape[0]
    v = a.reshape(J, 4, 128, -1)
    return [np.ascontiguousarray(v[:, c].reshape(J * 128, -1)) for c in range(4)]


def _unloc(parts, J):
    Dd = parts[0].shape[-1]
    out = np.empty((J, 4, 128, Dd), parts[0].dtype)
    for c in range(4):
        out[:, c] = parts[c].reshape(J, 128, Dd)
    return out.reshape(J * 4 * 128, Dd)


def kernel(x, rel_bias, w_in, forget_b, diff_lambda, diff_norm_g, conv_w, conv_b, conv_ln_g, conv_ln_b, w_out,
           ln1_g, ln1_b, router_w, router_b, w_gu, b_gu, w_down, b_down, ln2_g, ln2_b):
    J = J_FULL
    f32 = lambda a: np.ascontiguousarray(np.asarray(a, dtype=np.float32))
    x = f32(x)
    rel_bias = f32(rel_bias)
    P = dict(w_in=f32(w_in), forget_b=f32(forget_b), diff_lambda=f32(diff_lambda), diff_norm_g=f32(diff_norm_g), conv_w=f32(conv_w),
             conv_b=f32(conv_b), conv_ln_g=f32(conv_ln_g), conv_ln_b=f32(conv_ln_b), w_out=f32(w_out), ln1_g=f32(ln1_g), ln1_b=f32(ln1_b),
             router_w=f32(router_w), router_b=f32(router_b), w_gu=f32(w_gu), b_gu=f32(b_gu), w_down=f32(w_down), b_down=f32(b_down),
             ln2_g=f32(ln2_g), ln2_b=f32(ln2_b))
    B = x.shape[0]
    cs = [make_consts(c) for c in range(4)]
    biasT = [np.ascontiguousarray(rel_bias[bias_tile_index(c)].transpose(3, 0, 1, 2)) for c in range(4)]
    rel31 = np.ascontiguousarray(rel_bias[31])
    xs = [None] * 8
    for b in range(B):
        for c, part in enumerate(_loc(x[b], J)):
            xs[4 * b + c] = part
    cores = list(range(8))
    if FUSED:
        if "F" not in _CACHE:
            _CACHE["F"] = build_fused(J)
        in_maps = []
        for r in cores:
            m = {"X": xs[r], "w_in": P["w_in"], "forget_b": P["forget_b"], "biasT": biasT[r % 4], "rel31": rel31}
            for k in WSHAPES:
                if k not in m:
                    m[k] = P[k]
            for k, v in cs[r % 4].items():
                m["c_" + k] = v
            in_maps.append(m)
        res = run_bass_kernel_spmd(_CACHE["F"], in_maps, core_ids=cores).results
        xs = [np.asarray(res[r]["Xout"]) for r in cores]
    else:
        if "A" not in _CACHE:
            _CACHE["A"] = build_A(J)
            _CACHE["BC"] = build_BC(J)
        for l in range(DEPTH):
            li = 0.8 - 0.6 * math.exp(-0.3 * l)
            lamc = np.tile(np.array([[-li, 1.0 - li]], np.float32), (128, 1))
            in_maps = []
            for r in cores:
                m = {"X": xs[r], "w_in": P["w_in"][l], "forget_b": P["forget_b"][l]}
                for k, v in cs[r % 4].items():
                    m["c_" + k] = v
                in_maps.append(m)
            resA = run_bass_kernel_spmd(_CACHE["A"], in_maps, core_ids=cores).results
            in_maps = []
            for r in cores:
                b = r // 4
                m = {"X": xs[r], "lamc": lamc, "biasT": biasT[r % 4], "rel31": rel31}
                for nm in ("KT", "V", "HT", "LF"):
                    m[nm + "_all"] = np.concatenate([np.asarray(resA[4 * b + cc][nm + "_loc"]) for cc in range(4)], 0)
                for nm in ("QT_loc", "IW_loc", "HT_loc"):
                    m[nm] = np.asarray(resA[r][nm])
                for k in WSHAPES:
                    if k not in m:
                        m[k] = P[k][l]
                for k, v in cs[r % 4].items():
                    m["c_" + k] = v
                in_maps.append(m)
            resB = run_bass_kernel_spmd(_CACHE["BC"], in_maps, core_ids=cores).results
            xs = [np.asarray(resB[r]["Xout"]) for r in cores]
    out = np.stack([_unloc([xs[4 * b + c] for c in range(4)], J) for b in range(B)], 0)
    return out.astype(np.float32)
```

```python
import math
from contextlib import ExitStack

import numpy as np
import concourse.bass as bass
import concourse.mybir as mybir
from concourse.bass_utils import run_bass_kernel_spmd

F32 = mybir.dt.float32
BF16 = mybir.dt.bfloat16
ALU = mybir.AluOpType
AF = mybir.ActivationFunctionType

D = 1024
NIN = 3144
DEPTH = 4
NE = 32
ALPHA = (2 * DEPTH) ** 0.25
EPS = 1e-5
NEG = -30000.0
REPL = -1.0e30
C_FQ, C_FK, C_FV, C_FF = 0, 256, 512, 768
C_DQ, C_DK, C_DV = 772, 1028, 1284
C_SQ, C_SK, C_SV = 1540, 1796, 2052
C_IQ, C_IK, C_IW, C_CU = 2308, 2564, 2628, 2632
KT_ROWS = 832
NQCH = 10


class Buf:
    __slots__ = ("w", "r", "wx")

    def __init__(self):
        self.w = None
        self.r = {}
        self.wx = []


class Prog:
    NDS = 24

    def __init__(self, nc):
        self.nc = nc
        self.st = ExitStack()
        self.engs = {"pe": nc.tensor, "act": nc.scalar, "dve": nc.vector, "pool": nc.gpsimd, "sp": nc.sync}
        self.semobj = {}
        self.cnt = {}
        for k in self.engs:
            self.semobj[k] = self.st.enter_context(nc.semaphore("s_" + k))
            self.cnt[k] = 0
        self.seen = {k: {} for k in self.engs}
        self.dcnt = [0] * self.NDS
        for i in range(self.NDS):
            self.semobj[("d", i)] = self.st.enter_context(nc.semaphore("s_d%d" % i))
        self.dnext = 0
        self.nins = 0

    def sb(self, name, shape, dt, st=None):
        self.uid = getattr(self, "uid", 0) + 1
        return (st or self.st).enter_context(self.nc.sbuf_tensor("%s_%d" % (name, self.uid), shape, dt))

    def ps(self, name, shape, dt, st=None):
        return (st or self.st).enter_context(self.nc.psum_tensor(name, shape, dt))

    def _wait(self, eng, key, val):
        if self.seen[eng].get(key, 0) >= val:
            return
        self.engs[eng].wait_ge(self.semobj[key], val)
        self.seen[eng][key] = val
        self.nins += 1

    def _deps(self, eng, reads, writes, extra=None, is_dma=False):
        need = {}

        def add(k, v, kind):
            if k == eng:
                if eng == "pe" or kind == "war":
                    return
            if need.get(k, 0) < v:
                need[k] = v
        if extra is not None:
            add(extra[0], extra[1], "raw")
        for b in reads:
            if b.w is not None:
                add(b.w[0], b.w[1], "raw")
            for k, v in b.wx:
                add(k, v, "raw")
        for b in writes:
            if is_dma and b.w is not None and isinstance(b.w[0], tuple) and not b.r:
                continue
            if b.w is not None:
                add(b.w[0], b.w[1], "waw")
            for k, v in b.wx:
                add(k, v, "waw")
            for k, v in b.r.items():
                add(k, v, "war")
        items = [(k, v) for k, v in need.items() if self.seen[eng].get(k, 0) < v]
        if not items:
            return None
        for k, v in items[:-1]:
            self._wait(eng, k, v)
        return items[-1]

    def _embed(self, eng, ins, last):
        if last is not None:
            ins._wait_ge(self.semobj[last[0]], last[1])
            self.seen[eng][last[0]] = last[1]

    def _mark(self, key, val, reads, writes, is_dma=False):
        for b in reads:
            b.r[key] = val
        for b in writes:
            if is_dma and b.w is not None and isinstance(b.w[0], tuple) and not b.r:
                b.wx.append(b.w)
            else:
                b.wx = []
            b.w = (key, val)
            b.r = {}

    def op(self, eng, fn, reads=(), writes=()):
        last = self._deps(eng, reads, writes)
        ins = fn(self.engs[eng])
        self._embed(eng, ins, last)
        self.cnt[eng] += 1
        ins.then_inc(self.semobj[eng], 1)
        self.nins += 1
        self._mark(eng, self.cnt[eng], reads, writes)

    def dma(self, eng, out, in_, reads=(), writes=()):
        i = self.dnext
        self.dnext = (i + 1) % self.NDS
        key = ("d", i)
        last = self._deps(eng, reads, writes, extra=(key, self.dcnt[i]) if self.dcnt[i] > 0 else None, is_dma=True)
        ins = self.engs[eng].dma_start(out=out, in_=in_)
        self._embed(eng, ins, last)
        self.dcnt[i] += 16
        ins.then_inc(self.semobj[key], 16)
        self.nins += 1
        self._mark(key, self.dcnt[i], reads, writes, is_dma=True)

    def collective(self, src, dst, groups):
        if "cc" not in self.semobj:
            self.semobj["cc"] = self.st.enter_context(self.nc.semaphore("s_cc"))
            self.cccnt = 0
        self.barrier(["pool"])
        ins = self.nc.gpsimd.collective_compute("AllGather", ALU.bypass, replica_groups=groups, ins=[src.opt()], outs=[dst.opt()])
        self.cccnt += 1
        ins.then_inc(self.semobj["cc"])
        self.nins += 1

    def barrier(self, engs=None):
        engs = engs or list(self.engs)
        for e in engs:
            for k in self.engs:
                if k != e and self.cnt[k] > 0:
                    self._wait(e, k, self.cnt[k])
            for i in range(self.NDS):
                if self.dcnt[i] > 0:
                    self._wait(e, ("d", i), self.dcnt[i])
            if getattr(self, "cccnt", 0) > 0 and e != "pool":
                self._wait(e, "cc", self.cccnt)

    def mm(self, out, lhsT, rhs, start, stop, reads, writes):
        self.op("pe", lambda e: e.matmul(out, lhsT=lhsT, rhs=rhs, start=start, stop=stop), reads, writes)

    def tr(self, out, in_, ident, reads, writes):
        self.op("pe", lambda e: e.transpose(out, in_, ident), reads, writes)

    def act(self, out, in_, func, reads, writes, bias=None, scale=1.0, accum=None):
        kw = {}
        if bias is not None:
            kw["bias"] = bias
        if accum is not None:
            kw["accum_out"] = accum
        self.op("act", lambda e: e.activation(out=out, in_=in_, func=func, scale=scale, **kw), reads, writes)

    def ts(self, eng, out, in0, s1, s2, op0, op1, reads, writes, accum=None):
        kw = {}
        if accum is not None:
            kw["accum_out"] = accum
        if op1 is None:
            self.op(eng, lambda e: e.tensor_scalar(out=out, in0=in0, scalar1=s1, scalar2=None, op0=op0, **kw), reads, writes)
        else:
            self.op(eng, lambda e: e.tensor_scalar(out=out, in0=in0, scalar1=s1, scalar2=s2, op0=op0, op1=op1, **kw), reads, writes)

    def tt(self, eng, out, in0, in1, op, reads, writes):
        self.op(eng, lambda e: e.tensor_tensor(out=out, in0=in0, in1=in1, op=op), reads, writes)

    def stt(self, eng, out, in0, scalar, in1, op0, op1, reads, writes):
        self.op(eng, lambda e: e.scalar_tensor_tensor(out=out, in0=in0, scalar=scalar, in1=in1, op0=op0, op1=op1), reads, writes)

    def copy(self, eng, out, in_, reads, writes):
        if eng == "act":
            self.op("act", lambda e: e.copy(out=out, in_=in_), reads, writes)
        else:
            self.op(eng, lambda e: e.tensor_copy(out=out, in_=in_), reads, writes)

    def memset(self, eng, ap, val, writes):
        self.op(eng, lambda e: e.memset(ap, val), (), writes)

    def finish(self):
        self.barrier(["sp"])
        self.st.close()


def _bufs(n):
    return [Buf() for _ in range(n)]


class Ctx:
    pass


class Gat:
    def __init__(self, R, mono=None, S=None, chunks=None):
        self.R, self.mono, self.S, self.chunks = R, mono, S, chunks

    def rows(self, c, r0, n):
        if self.mono is not None:
            return [(self.mono[c * self.R + r0:c * self.R + r0 + n, :], 0, n)]
        out = []
        r = r0
        while r < r0 + n:
            k, o = r // self.S, r % self.S
            m = min(self.S - o, r0 + n - r)
            out.append((self.chunks[k][c * self.S + o:c * self.S + o + m, :], r - r0, m))
            r += m
        return out


def setup_common(p, J, consts):
    nc = p.nc
    g = Ctx()
    g.J = J
    g.T = J * 128
    g.banks = [p.ps("bank%d" % i, [128, 512], F32) for i in range(8)]
    g.bb = _bufs(8)
    g.identf = p.sb("identf", [128, 128], F32)
    g.identb = p.sb("identb", [128, 128], BF16)
    g.b_ident = Buf()
    p.dma("sp", g.identf[:], consts["identf"], (), [g.b_ident])
    p.copy("dve", g.identb[:], g.identf[:], [g.b_ident], [g.b_ident])
    return g


def transpose_block(p, g, src, src_b, dst_bf, dst_b, blk, dst_f32=None, dst_f32_b=None, bank0=0):
    for half in range(2):
        bk = bank0 + half
        for q in range(4):
            dc = half * 4 + q
            p.tr(g.banks[bk][:, q * 128:(q + 1) * 128], src[:, dc * 128:(dc + 1) * 128], g.identf[:],
                 [src_b, g.b_ident], [g.bb[bk]])
        outv = dst_bf[:, half * 4:(half + 1) * 4, blk * 128:(blk + 1) * 128]
        inv = g.banks[bk][:].rearrange("p (q t) -> p q t", q=4)
        if dst_f32 is None:
            p.copy("act" if half == 0 else "dve", outv, inv, [g.bb[bk]], [dst_b])
        else:
            f32v = dst_f32[:, half * 4:(half + 1) * 4, :]
            p.copy("dve", f32v, inv, [g.bb[bk]], [dst_f32_b])
            p.copy("act", outv, f32v, [dst_f32_b], [dst_b])


def emit_A(p, g, X, w_in, forget_b, consts, KT_loc, V_loc, HT_loc, LF_loc, QT_loc, IW_loc):
    J, T = g.J, g.T
    st = ExitStack()
    xT = p.sb("a_xT", [128, 8, T], BF16, st)
    b_xT = Buf()
    win = p.sb("a_win", [128, 8, NIN], BF16, st)
    b_win = _bufs(8)
    xt = [p.sb("a_xt%d" % i, [128, D], F32, st) for i in range(2)]
    b_xt = _bufs(2)
    fb = p.sb("a_fb", [128, 4], F32, st)
    b_fb = Buf()
    dqm = p.sb("a_dqm", [128, 2], F32, st)
    b_dqm = Buf()
    p.dma("sp", fb[:], forget_b.partition_broadcast(128), (), [b_fb])
    p.dma("sp", dqm[:], consts["dqmask"], (), [b_dqm])
    w_r = w_in.rearrange("(dc q) f -> q dc f", q=128)
    for dc in range(8):
        p.dma("pool", win[:, dc, :], w_r[:, dc, :], (), [b_win[dc]])
    Xr = X.rearrange("(j q) d -> q j d", q=128)
    for j in range(J):
        s = j % 2
        p.dma("sp", xt[s][:], Xr[:, j, :], (), [b_xt[s]])
        transpose_block(p, g, xt[s], b_xt[s], xT, b_xT, j)

    ev = p.sb("a_ev", [128, 4, 512], BF16, st)
    b_ev = _bufs(4)
    evf = p.sb("a_evf", [128, 2, 512], F32, st)
    b_evf = _bufs(2)
    sg = p.sb("a_sg", [128, 2, 512], F32, st)
    b_sg = _bufs(2)
    ntg = T // 512 if T >= 512 else 1
    TG = min(512, T)
    state = {"bk": 2, "ev": 0, "evf": 0}

    def fm_chunk(col0, ncols, tg):
        bk = state["bk"]
        state["bk"] = 2 + (bk - 2 + 1) % 4
        for dc in range(8):
            p.mm(g.banks[bk][0:ncols, 0:TG], win[:, dc, col0:col0 + ncols], xT[:, dc, tg * TG:(tg + 1) * TG],
                 dc == 0, dc == 7, [b_win[dc], b_xT], [g.bb[bk]])
        return bk

    def ev_slot():
        s = state["ev"]
        state["ev"] = (s + 1) % 4
        return s

    for tg in range(ntg):
        tsl = slice(tg * TG, (tg + 1) * TG)
        qspec = [(0, C_FQ, None), (1, C_FQ + 128, None), (2, C_DQ, 0), (3, C_DQ + 128, 0), (4, C_DQ, 1), (5, C_DQ + 128, 1),
                 (6, C_SQ, None), (7, C_SQ + 128, None), (8, C_IQ, None), (9, C_IQ + 128, None)]
        cache = {}
        for qi, col0, m in qspec:
            if m == 1 and col0 in cache:
                bk = cache[col0]
            else:
                bk = fm_chunk(col0, 128, tg)
                cache[col0] = bk
            s = ev_slot()
            if m is None:
                p.copy("act", ev[:, s, 0:TG], g.banks[bk][:, 0:TG], [g.bb[bk]], [b_ev[s]])
            else:
                p.ts("dve", ev[:, s, 0:TG], g.banks[bk][:, 0:TG], dqm[:, m:m + 1], None, ALU.mult, None,
                     [g.bb[bk], b_dqm], [b_ev[s]])
            p.dma("sp", QT_loc[qi, :, tsl], ev[:, s, 0:TG], [b_ev[s]], ())
        for ki, col0, ncols in [(0, C_FK, 128), (1, C_FK + 128, 128), (2, C_DK, 128), (3, C_DK + 128, 128),
                                (4, C_SK, 128), (5, C_SK + 128, 128), (6, C_IK, 64)]:
            bk = fm_chunk(col0, ncols, tg)
            s = ev_slot()
            p.copy("act" if ki % 2 == 0 else "dve", ev[0:ncols, s, 0:TG], g.banks[bk][0:ncols, 0:TG], [g.bb[bk]], [b_ev[s]])
            p.dma("sp", KT_loc[ki * 128:ki * 128 + ncols, tsl], ev[0:ncols, s, 0:TG], [b_ev[s]], ())
        for cc in range(2):
            bka = fm_chunk(C_CU + cc * 128, 128, tg)
            bkg = fm_chunk(C_CU + 256 + cc * 128, 128, tg)
            s = state["evf"]
            state["evf"] = 1 - s
            p.act(sg[:, s, 0:TG], g.banks[bkg][:, 0:TG], AF.Sigmoid, [g.bb[bkg]], [b_sg[s]])
            p.tt("dve", evf[:, s, 0:TG], g.banks[bka][:, 0:TG], sg[:, s, 0:TG], ALU.mult, [g.bb[bka], b_sg[s]], [b_evf[s]])
            p.dma("sp", HT_loc[cc * 128:(cc + 1) * 128, tsl], evf[:, s, 0:TG], [b_evf[s]], ())

    vt = p.sb("a_vt", [128, 2, 768], BF16, st)
    b_vt = _bufs(2)
    lf = p.sb("a_lf", [128, J, 4], F32, st)
    b_lf = Buf()
    iw = p.sb("a_iw", [128, J, 4], F32, st)
    b_iw = Buf()
    for j in range(J):
        s = j % 2
        lt = slice(j * 128, (j + 1) * 128)
        for gi, col0 in enumerate([C_FV, C_DV]):
            for dc in range(8):
                p.mm(g.banks[0][:, gi * 256:(gi + 1) * 256], xT[:, dc, lt], win[:, dc, col0:col0 + 256], dc == 0, dc == 7,
                     [b_win[dc], b_xT], [g.bb[0]])
        for dc in range(8):
            p.mm(g.banks[1][:, 0:256], xT[:, dc, lt], win[:, dc, C_SV:C_SV + 256], dc == 0, dc == 7, [b_win[dc], b_xT], [g.bb[1]])
        for dc in range(8):
            p.mm(g.banks[1][:, 256:260], xT[:, dc, lt], win[:, dc, C_FF:C_FF + 4], dc == 0, dc == 7, [b_win[dc], b_xT], [g.bb[1]])
        for dc in range(8):
            p.mm(g.banks[1][:, 260:264], xT[:, dc, lt], win[:, dc, C_IW:C_IW + 4], dc == 0, dc == 7, [b_win[dc], b_xT], [g.bb[1]])
        p.copy("act", vt[:, s, 0:512], g.banks[0][:, 0:512], [g.bb[0]], [b_vt[s]])
        p.copy("dve", vt[:, s, 512:768], g.banks[1][:, 0:256], [g.bb[1]], [b_vt[s]])
        p.tt("dve", lf[:, j, :], g.banks[1][:, 256:260], fb[:], ALU.add, [g.bb[1], b_fb], [b_lf])
        p.copy("dve", iw[:, j, :], g.banks[1][:, 260:264], [g.bb[1]], [b_iw])
        p.dma("sp", V_loc[lt, :], vt[:, s, :], [b_vt[s]], ())
    lfv = lf[:].rearrange("p j h -> p (j h)")
    p.act(lfv, lfv, AF.Exp, [b_lf], [b_lf], scale=-1.0)
    p.act(lfv, lfv, AF.Ln, [b_lf], [b_lf], bias=1.0)
    p.ts("dve", lfv, lfv, -1.0, None, ALU.mult, None, [b_lf], [b_lf])
    p.tr(g.banks[2][0:J * 4, 0:128], lfv, g.identf[:], [b_lf, g.b_ident], [g.bb[2]])
    lft = p.sb("a_lft", [J * 4, 128], F32, st)
    b_lft = Buf()
    p.copy("dve", lft[:], g.banks[2][0:J * 4, 0:128], [g.bb[2]], [b_lft])
    p.dma("sp", LF_loc, lft[:], [b_lft], ())
    p.dma("sp", IW_loc, iw[:].rearrange("p j h -> p (j h)"), [b_iw], ())
    p.barrier()
    st.close()


def ln_rows(p, g, z, b_z, out, b_out, gt, bt, b_gb, tmp, b_tmp, st4, b_st4):
    p.memset("dve", st4[:, 0:2], 0.0, [b_st4])
    p.ts("dve", tmp, z, 1.0, 0.0, ALU.mult, ALU.add, [b_z], [b_tmp, b_st4], accum=st4[:, 0:1])
    p.tt("pool", tmp, z, z, ALU.mult, [b_z], [b_tmp])
    p.ts("dve", tmp, tmp, 1.0, 0.0, ALU.mult, ALU.add, [b_tmp], [b_tmp, b_st4], accum=st4[:, 1:2])
    p.ts("dve", st4[:, 0:1], st4[:, 0:1], 1.0 / D, None, ALU.mult, None, [b_st4], [b_st4])
    p.tt("dve", st4[:, 2:3], st4[:, 0:1], st4[:, 0:1], ALU.mult, [b_st4], [b_st4])
    p.stt("dve", st4[:, 1:2], st4[:, 1:2], 1.0 / D, st4[:, 2:3], ALU.mult, ALU.subtract, [b_st4], [b_st4])
    p.act(st4[:, 1:2], st4[:, 1:2], AF.Ln, [b_st4], [b_st4], bias=EPS)
    p.act(st4[:, 1:2], st4[:, 1:2], AF.Exp, [b_st4], [b_st4], scale=-0.5)
    p.stt("dve", st4[:, 2:3], st4[:, 0:1], -1.0, st4[:, 1:2], ALU.mult, ALU.mult, [b_st4], [b_st4])
    p.act(tmp, z, AF.Identity, [b_z, b_st4], [b_tmp], bias=st4[:, 2:3], scale=st4[:, 1:2])
    p.tt("dve", tmp, tmp, gt, ALU.mult, [b_tmp, b_gb], [b_tmp])
    p.tt("pool", out, tmp, bt, ALU.add, [b_tmp, b_gb], [b_out])


NOMASK = False


def emit_B(p, g, l, X, KT_all, V_all, HT_all, LF_all, QT_loc, IW_loc, HT_loc, NEGT, W, consts, x1, b_x1, lamc=None, dbg=None):
    J, T = g.J, g.T
    NB = 4 * J
    L = NB * 128
    lambda_init = 0.8 - 0.6 * math.exp(-0.3 * l)
    stB = ExitStack()
    yT = p.sb("b_yT", [128, 8, T], BF16, stB)
    b_yT = Buf()
    sel = p.sb("b_sel", [128, 4], F32, stB)
    b_c = Buf()
    p.dma("sp", sel[:], consts["sel"], (), [b_c])
    maskT = p.sb("b_maskT", [128, 4, 128], F32, stB)
    p.dma("sp", maskT[:], consts["maskT"].rearrange("c k q -> k c q"), (), [b_c])
    b31 = p.sb("b_b31", [128, 8], F32, stB)
    p.dma("sp", b31[:], W["rel31"].partition_broadcast(128), (), [b_c])
    ones = p.sb("b_ones", [128, 128], F32, stB)
    p.memset("pool", ones[:], 1.0, [b_c])

    st = ExitStack()
    hc = p.sb("c_hc", [128, 2, J, 158], F32, st)
    b_hc = Buf()
    cand = p.sb("c_cand", [128, 4, 2, J, 30], F32, st)
    b_cand = Buf()
    cw = p.sb("c_cw", [128, 2, 31], F32, st)
    cpar = p.sb("c_par", [128, 2, 3], F32, st)
    b_cw = Buf()
    cwr = p.sb("c_cwr", [31, 256], F32, st)
    p.dma("sp", cwr[:], W["conv_w"], (), [b_cw])
    for cc in range(2):
        p.tr(g.banks[0][:, cc * 32:cc * 32 + 31], cwr[:, cc * 128:(cc + 1) * 128], g.identf[0:31, 0:31], [b_cw, g.b_ident], [g.bb[0]])
    p.copy("dve", cw[:], g.banks[0][:, 0:64].rearrange("c (cc w) -> c cc w", cc=2)[:, :, 0:31], [g.bb[0]], [b_cw])
    for i, nm in enumerate(["conv_b", "conv_ln_g", "conv_ln_b"]):
        for cc in range(2):
            p.dma("sp", cpar[:, cc, i:i + 1], W[nm].rearrange("(cc c o) -> cc c o", c=128, o=1)[cc], (), [b_cw])
    p.memset("pool", cand[:], 0.0, [b_cand])
    for cc in range(2):
        p.dma("sp", hc[:, cc, :, 30:158], HT_loc[cc * 128:(cc + 1) * 128, :].rearrange("c (j t) -> c j t", t=128), (), [b_hc])
        for c in range(4):
            if c >= 1:
                for ap_, off, n in HT_all.rows(c - 1, cc * 128, 128):
                    p.dma("sp", cand[off:off + n, c, cc, :, :], ap_.rearrange("c (j t) -> c j t", t=128)[:, :, 98:128], (), [b_cand])
            elif J > 1:
                for ap_, off, n in HT_all.rows(3, cc * 128, 128):
                    p.dma("sp", cand[off:off + n, c, cc, 1:J, :], ap_.rearrange("c (j t) -> c j t", t=128)[:, 0:J - 1, 98:128], (), [b_cand])
    for cc in range(2):
        e = "dve"
        p.ts(e, hc[:, cc, :, 0:30], cand[:, 0, cc, :, :], sel[:, 0:1], None, ALU.mult, None, [b_cand, b_c], [b_hc])
        for c in range(1, 4):
            p.stt(e, hc[:, cc, :, 0:30], cand[:, c, cc, :, :], sel[:, c:c + 1], hc[:, cc, :, 0:30], ALU.mult, ALU.add,
                  [b_cand, b_c, b_hc], [b_hc])
    cacc = p.sb("c_acc", [128, 2, J, 128], F32, st)
    b_cacc = _bufs(2)
    for cc in range(2):
        e = "dve"
        p.ts(e, cacc[:, cc], hc[:, cc, :, 0:128], cw[:, cc, 0:1], cpar[:, cc, 0:1], ALU.mult, ALU.add, [b_hc, b_cw], [b_cacc[cc]])
        for w in range(1, 31):
            p.stt(e, cacc[:, cc], hc[:, cc, :, w:w + 128], cw[:, cc, w:w + 1], cacc[:, cc], ALU.mult, ALU.add,
                  [b_hc, b_cw, b_cacc[cc]], [b_cacc[cc]])
    TG = min(512, T)
    csq = p.sb("c_sq", [128, 2, TG], F32, st)
    b_csq = Buf()
    cmean = p.sb("c_mean", [128, TG], F32, st)
    crstd = p.sb("c_rstd", [128, TG], F32, st)
    b_cst = Buf()
    cd = p.sb("c_d", [128, 2, TG], F32, st)
    b_cd = _bufs(2)
    p.ts("dve", ones[:], ones[:], 1.0 / 256.0, None, ALU.mult, None, [b_c], [b_c])
    for tg in range(T // TG):
        flat = [cacc[:, cc].rearrange("c j t -> c (j t)")[:, tg * TG:(tg + 1) * TG] for cc in range(2)]
        for cc in range(2):
            p.tt("pool", csq[:, cc, :], flat[cc], flat[cc], ALU.mult, [b_cacc[cc]], [b_csq])
        for cc in range(2):
            p.mm(g.banks[0][:, 0:TG], ones[:], flat[cc], cc == 0, cc == 1, [b_c, b_cacc[cc]], [g.bb[0]])
        for cc in range(2):
            p.mm(g.banks[1][:, 0:TG], ones[:], csq[:, cc, :], cc == 0, cc == 1, [b_c, b_csq], [g.bb[1]])
        p.copy("act", cmean[:], g.banks[0][:, 0:TG], [g.bb[0]], [b_cst])
        p.tt("dve", crstd[:], cmean[:], cmean[:], ALU.mult, [b_cst], [b_cst])
        p.tt("dve", crstd[:], g.banks[1][:, 0:TG], crstd[:], ALU.subtract, [g.bb[1], b_cst], [b_cst])
        p.act(crstd[:], crstd[:], AF.Ln, [b_cst], [b_cst], bias=EPS)
        p.act(crstd[:], crstd[:], AF.Exp, [b_cst], [b_cst], scale=-0.5)
        for cc in range(2):
            p.tt("dve", cd[:, cc, :], flat[cc], cmean[:], ALU.subtract, [b_cacc[cc], b_cst], [b_cd[cc]])
            p.tt("dve", cd[:, cc, :], cd[:, cc, :], crstd[:], ALU.mult, [b_cd[cc], b_cst], [b_cd[cc]])
            p.ts("dve", cd[:, cc, :], cd[:, cc, :], cpar[:, cc, 1:2], cpar[:, cc, 2:3], ALU.mult, ALU.add, [b_cd[cc], b_cw], [b_cd[cc]])
            p.act(csq[:, cc, :], cd[:, cc, :], AF.Sigmoid, [b_cd[cc]], [b_csq])
            p.tt("dve", yT[:, 6 + cc, tg * TG:(tg + 1) * TG], cd[:, cc, :], csq[:, cc, :], ALU.mult, [b_cd[cc], b_csq], [b_yT])
    p.ts("dve", ones[:], ones[:], 256.0, None, ALU.mult, None, [b_c], [b_c])
    p.barrier()
    st.close()

    R = 4 * J * 4
    cn = p.sb("f_cn", [128, J, 4, 4], F32, stB)
    offs = p.sb("f_offs", [128, J, 4, 4], F32, stB)
    offl = p.sb("f_offl", [128, J, 4], F32, stB)
    vs = p.sb("f_vs", [128, J, 4, 4], F32, stB)
    bjr = p.sb("f_bjr", [128, J, J, 4], F32, stB)
    lam = p.sb("d_lam", [128, 4], F32, stB)
    dng = p.sb("d_g", [128, 64], F32, stB)
    b_f = Buf()
    st = ExitStack()
    triu = p.sb("f_triu", [128, 128], F32, st)
    p.dma("sp", triu[:], consts["triu"], (), [b_f])
    nt = (R + 127) // 128
    lfr = p.sb("f_lfr", [128, nt, 128], F32, st)
    lfn = p.sb("f_lfn", [128, R], F32, st)
    sc_a = p.sb("f_sca", [128, NB, 4], F32, st)
    sc_b = p.sb("f_scb", [128, NB, 4], F32, st)
    for t in range(nt):
        rr = min(128, R - t * 128)
        p.dma("sp", lfr[0:rr, t, :], LF_all[t * 128:t * 128 + rr, :], (), [b_f])
        p.tr(g.banks[0][:, t * 128:t * 128 + rr], lfr[0:rr, t, :], g.identf[0:rr, 0:rr], [b_f, g.b_ident], [g.bb[0]])
    p.copy("dve", lfn[:], g.banks[0][:, 0:R], [g.bb[0]], [b_f])
    p.mm(g.banks[1][:, 0:R], triu[:], lfn[:], True, True, [b_f], [g.bb[1]])
    p.memset("pool", ones[:], 1.0, [b_c])
    p.mm(g.banks[2][:, 0:R], ones[:], lfn[:], True, True, [b_f, b_c], [g.bb[2]])
    nat = lambda ap: ap.rearrange("p (c j h) -> p j c h", c=4, j=J)
    p.copy("dve", sc_a[:].rearrange("p (j c) h -> p j c h", c=4), nat(g.banks[2][:, 0:R]), [g.bb[2]], [b_f])
    p.copy("dve", offs[:], nat(g.banks[2][:, 0:R]), [g.bb[2]], [b_f])
    a, b = sc_a, sc_b
    s = 1
    while s < NB:
        p.copy("dve", b[:, 0:s, :], a[:, 0:s, :], [b_f], [b_f])
        p.tt("dve", b[:, s:NB, :], a[:, s:NB, :], a[:, 0:NB - s, :], ALU.add, [b_f], [b_f])
        a, b = b, a
        s *= 2
    offs_f = offs[:].rearrange("p j c h -> p (j c) h")
    p.tt("dve", offs_f, a[:], offs_f, ALU.subtract, [b_f], [b_f])
    p.tt("dve", cn[:], nat(g.banks[1][:, 0:R]), offs[:], ALU.add, [g.bb[1], b_f], [b_f])
    p.ts("dve", offl[:], offs[:, :, 0, :], sel[:, 0:1], None, ALU.mult, None, [b_f, b_c], [b_f])
    for c in range(1, 4):
        p.stt("dve", offl[:], offs[:, :, c, :], sel[:, c:c + 1], offl[:], ALU.mult, ALU.add, [b_f, b_c], [b_f])
    for c in range(4):
        p.tt("dve", vs[:, :, c, :], offs[:, :, 0, :], cn[:, :, c, :], ALU.subtract, [b_f], [b_f])
    vsf = vs[:].rearrange("p j c h -> p (j c h)")
    p.act(vsf, vsf, AF.Exp, [b_f], [b_f])
    for j in range(J):
        for h in range(4):
            p.ts("dve", bjr[:, j, :, h], offs[:, :, 0, h], -1.0, offl[:, j, h:h + 1], ALU.mult, ALU.add, [b_f], [b_f])
    dlf = p.sb("d_dl", [128, 128], F32, st)
    p.dma("sp", dlf[:], W["diff_lambda"].rearrange("a b -> (a b)").partition_broadcast(128), (), [b_f])
    dl = dlf[:].rearrange("p (a b) -> p a b", a=4)
    p.memset("dve", lam[:], 0.0, [b_f])
    p.dma("sp", dng[:], W["diff_norm_g"].partition_broadcast(128), (), [b_f])
    p.tt("dve", dl[:, 0, :], dl[:, 0, :], dl[:, 1, :], ALU.mult, [b_f], [b_f])
    p.tt("dve", dl[:, 2, :], dl[:, 2, :], dl[:, 3, :], ALU.mult, [b_f], [b_f])
    p.ts("dve", dl[:, 1, :], dl[:, 0, :], 1.0, 0.0, ALU.mult, ALU.add, [b_f], [b_f], accum=lam[:, 0:1])
    p.ts("dve", dl[:, 3, :], dl[:, 2, :], 1.0, 0.0, ALU.mult, ALU.add, [b_f], [b_f], accum=lam[:, 1:2])
    p.act(lam[:, 0:2], lam[:, 0:2], AF.Exp, [b_f], [b_f])
    p.tt("dve", lam[:, 2:3], lam[:, 1:2], lam[:, 0:1], ALU.subtract, [b_f], [b_f])
    if lamc is None:
        p.ts("dve", lam[:, 2:3], lam[:, 2:3], -lambda_init, None, ALU.add, None, [b_f], [b_f])
        p.ts("dve", dng[:], dng[:], 1.0 - lambda_init, None, ALU.mult, None, [b_f], [b_f])
    else:
        lc = p.sb("d_lc", [128, 2], F32, st)
        p.dma("sp", lc[:], lamc, (), [b_f])
        p.ts("dve", lam[:, 2:3], lam[:, 2:3], lc[:, 0:1], None, ALU.add, None, [b_f], [b_f])
        p.ts("dve", dng[:], dng[:], lc[:, 1:2], None, ALU.mult, None, [b_f], [b_f])
    p.barrier()
    st.close()

    stA = ExitStack()
    ALIAS = (J == 16)
    x1flat = x1[:].rearrange("p a b -> p (a b)")
    if ALIAS:
        kt = [x1flat[:, i * 4096:(i + 1) * 4096].bitcast(BF16) for i in range(2)]
        vt = [x1flat[:, 8192:8192 + 4160].bitcast(BF16).rearrange("p (a b c) -> p a b c", a=NB, b=2, c=65),
              p.sb("a_v1", [128, NB, 2, 65], BF16, stA)]
    else:
        kt = [p.sb("a_kt%d" % i, [128, L], BF16, stA) for i in range(2)]
        vt = [p.sb("a_v%d" % i, [128, NB, 2, 65], BF16, stA) for i in range(2)]
    qt = [p.sb("a_q%d" % i, [128, 2, T], BF16, stA) for i in range(2)]
    bti = [p.sb("a_bt%d" % i, [128, 2, 8, 128], F32, stA) for i in range(2)]
    b_kv = _bufs(2)
    pT = p.sb("a_pT", [128, 4, 512], BF16, stA)
    b_pT = _bufs(4)
    ssb = p.sb("a_ssb", [128, 2, 512], F32, stA)
    b_ssb = _bufs(2)
    ysb = p.sb("a_ysb", [128, 2, 128], F32, stA)
    b_ysb = _bufs(2)
    yd = p.sb("a_yd", [128, 2, 66], F32, stA)
    b_yd = _bufs(2)
    rec = p.sb("a_rec", [128, 8], F32, stA)
    b_rec = Buf()
    for i in range(2):
        p.memset("pool", vt[i][:, :, :, 64:65], 1.0, [b_kv[i]])
    rot = {"s": 0, "pt": 0, "ss": 0, "o": 0}
    SB = [0, 1, 2]
    OB = [3, 4, 5, 6]
    TB = 7

    def load_pair(slot, kind, pr):
        kch = kind * 2 + pr
        for c in range(4):
            for ap_, off, n in KT_all.rows(c, kch * 128, 128):
                dst = kt[slot][off:off + n, :].rearrange("r (j c t) -> r j c t", c=4, t=128)[:, :, c, :]
                p.dma("sp", dst, ap_.rearrange("r (j t) -> r j t", t=128), (), [b_kv[slot]])
            for hh in range(2):
                c0 = kind * 256 + pr * 128 + hh * 64
                for j_ in range(J):
                    for ap_, off, n in V_all.rows(c, j_ * 128, 128):
                        p.dma("sp", vt[slot][off:off + n, 4 * j_ + c, hh, 0:64], ap_[:, c0:c0 + 64], (), [b_kv[slot]])
        if kind == 0:
            p.dma("sp", qt[slot][:, 0, :], QT_loc[pr], (), [b_kv[slot]])
            for hh in range(2):
                h = pr * 2 + hh
                vsh = vs[:].rearrange("p j c h -> p (j c) h")[:, :, h:h + 1]
                p.tt("pool", vt[slot][:, :, hh, 0:64], vt[slot][:, :, hh, 0:64], vsh.to_broadcast([128, NB, 64]), ALU.mult,
                     [b_f, b_kv[slot]], [b_kv[slot]])
                p.copy("pool", vt[slot][:, :, hh, 64:65], vsh, [b_f, b_kv[slot]], [b_kv[slot]])
        elif kind == 1:
            p.dma("sp", qt[slot][:, 0, :], QT_loc[2 + pr], (), [b_kv[slot]])
            p.dma("sp", qt[slot][:, 1, :], QT_loc[4 + pr], (), [b_kv[slot]])
        else:
            p.dma("sp", qt[slot][:, 0, :], QT_loc[6 + pr], (), [b_kv[slot]])
        if kind > 0:
            hb = (kind - 1) * 4 + pr * 2
            p.dma("sp", bti[slot][:], W["biasT"][hb:hb + 2].rearrange("h t k q -> k h t q"), (), [b_kv[slot]])
            for hh in range(2):
                p.tt("pool", bti[slot][:, hh, 0:4, :], bti[slot][:, hh, 0:4, :], maskT[:], ALU.add, [b_kv[slot], b_c], [b_kv[slot]])
            p.memset("pool", vt[slot][:, :, :, 64:65], 1.0, [b_kv[slot]])

    def run_maps(slot, kind, pr, j, ngT=None, b_ng=None):
        nrow = j + 1
        if kind == 1:
            maps = [(hh, m) for hh in range(2) for m in range(2)]
        else:
            maps = [(hh, 0) for hh in range(2)]
        scale = (32 ** -0.5) if kind == 1 else 0.125
        obank = {}
        items = [(mi, r) for mi in range(len(maps)) for r in range(nrow)]
        pend = {}

        def qk(it):
            mi, r = it
            hh, m = maps[mi]
            sbk = SB[rot["s"] % 3]
            rot["s"] += 1
            rows = slice(hh * 64, hh * 64 + 64)
            for c in range(4):
                blk = 4 * r + c
                p.mm(g.banks[sbk][:, c * 128:(c + 1) * 128], kt[slot][rows, blk * 128:(blk + 1) * 128],
                     qt[slot][rows, m, j * 128:(j + 1) * 128], True, True, [b_kv[slot]], [g.bb[sbk]])
            ps_ = rot["pt"] % 4
            rot["pt"] += 1
            h = pr * 2 + hh
            if kind == 0:
                bias = bjr[:, j, r, h:h + 1]
                near = (r == j)
                bread = [b_f]
            else:
                gh = (kind - 1) * 4 + h
                bias = b31[:, gh:gh + 1]
                near = (r >= j - 1)
                bread = [b_c]
            if near:
                ss = rot["ss"] % 2
                rot["ss"] += 1
                for c in range(4):
                    if kind == 0:
                        tile_ = maskT[:, c, :]
                        rd = [b_c]
                    else:
                        tile_ = bti[slot][:, hh, (0 if r == j else 4) + c, :]
                        rd = [b_kv[slot]]
                    p.stt("dve", ssb[:, ss, c * 128:(c + 1) * 128], g.banks[sbk][:, c * 128:(c + 1) * 128], scale, tile_,
                          ALU.mult, ALU.add, [g.bb[sbk]] + rd, [b_ssb[ss]])
                if kind == 0:
                    p.act(pT[:, ps_, :], ssb[:, ss, :], AF.Exp, [b_ssb[ss]] + bread, [b_pT[ps_]], bias=bias)
                else:
                    p.act(pT[:, ps_, :], ssb[:, ss, :], AF.Exp, [b_ssb[ss]], [b_pT[ps_]])
            else:
                p.act(pT[:, ps_, :], g.banks[sbk][:, :], AF.Exp, [g.bb[sbk]] + bread, [b_pT[ps_]], bias=bias, scale=scale)
            if kind == 2:
                p.tt("pool", pT[:, ps_, :], pT[:, ps_, :], ngT[:, 4 * r:4 * r + 4, :].rearrange("k c q -> k (c q)"), ALU.mult,
                     [b_pT[ps_], b_ng], [b_pT[ps_]])
            pend[it] = ps_

        def pv(it):
            mi, r = it
            hh, m = maps[mi]
            ps_ = pend.pop(it)
            if r == 0:
                obank[mi] = OB[rot["o"] % 4]
                rot["o"] += 1
            ob = obank[mi]
            for c in range(4):
                blk = 4 * r + c
                p.mm(g.banks[ob][:, 0:65], pT[:, ps_, c * 128:(c + 1) * 128], vt[slot][:, blk, hh, :],
                     r == 0 and c == 0, r == nrow - 1 and c == 3, [b_pT[ps_], b_kv[slot]], [g.bb[ob]])
            if r == nrow - 1:
                finalize(mi)

        def finalize(mi):
            hh, m = maps[mi]
            ob = obank[mi]
            ys = (j * 2 + pr) % 2
            if kind != 1:
                col = mi
                p.op("dve", lambda e: e.reciprocal(out=rec[:, col:col + 1], in_=g.banks[ob][:, 64:65]), [g.bb[ob]], [b_rec])
                p.ts("dve", ysb[:, ys, hh * 64:(hh + 1) * 64], g.banks[ob][:, 0:64], rec[:, col:col + 1], None, ALU.mult, None,
                     [g.bb[ob], b_rec], [b_ysb[ys]])
            else:
                col = mi
                p.op("dve", lambda e: e.reciprocal(out=rec[:, col:col + 1], in_=g.banks[ob][:, 64:65]), [g.bb[ob]], [b_rec])
                if m == 0:
                    p.ts("dve", yd[:, hh, 0:64], g.banks[ob][:, 0:64], rec[:, col:col + 1], None, ALU.mult, None,
                         [g.bb[ob], b_rec], [b_yd[hh]])
                else:
                    p.ts("dve", ssb[:, 0, 0:64], g.banks[ob][:, 0:64], rec[:, col:col + 1], lam[:, 2:3], ALU.mult, ALU.mult,
                         [g.bb[ob], b_rec, b_f], [b_ssb[0]])
                    p.tt("dve", yd[:, hh, 0:64], yd[:, hh, 0:64], ssb[:, 0, 0:64], ALU.add, [b_ssb[0], b_yd[hh]], [b_yd[hh]])
                    p.tt("dve", ssb[:, 0, 64:128], yd[:, hh, 0:64], yd[:, hh, 0:64], ALU.mult, [b_yd[hh]], [b_ssb[0]])
                    p.memset("dve", yd[:, hh, 64:65], 0.0, [b_yd[hh]])
                    p.ts("dve", ssb[:, 0, 64:128], ssb[:, 0, 64:128], 1.0 / 64, 0.0, ALU.mult, ALU.add, [b_ssb[0]], [b_ssb[0], b_yd[hh]],
                         accum=yd[:, hh, 64:65])
                    p.act(yd[:, hh, 64:65], yd[:, hh, 64:65], AF.Ln, [b_yd[hh]], [b_yd[hh]], bias=EPS)
                    p.act(yd[:, hh, 64:65], yd[:, hh, 64:65], AF.Exp, [b_yd[hh]], [b_yd[hh]], scale=-0.5)
                    p.stt("dve", ysb[:, ys, hh * 64:(hh + 1) * 64], yd[:, hh, 0:64], yd[:, hh, 64:65], dng[:], ALU.mult, ALU.mult,
                          [b_yd[hh], b_f], [b_ysb[ys]])
            last = (mi == len(maps) - 1)
            if dbg is not None and kind == 1 and pr == 0 and j == 0 and mi == 1:
                def dd(name, ap, shape, bufs):
                    t_ = p.nc.dram_tensor("dbg_" + name, shape, F32, kind="ExternalOutput").ap()
                    p.dma("sp", t_, ap, bufs, ())
                dd("yd", yd[:], [128, 2, 66], [b_yd[0], b_yd[1]])
                dd("rec", rec[:], [128, 8], [b_rec])
                dd("bti", bti[slot][:, 0, 0, :], [128, 128], [b_kv[slot]])
                dd("lam", lam[:], [128, 4], [b_f])
                dd("dng", dng[:], [128, 64], [b_f])
                dd("ssb", ssb[:], [128, 2, 512], [b_ssb[0], b_ssb[1]])
                dd("ysb", ysb[:], [128, 2, 128], [b_ysb[0], b_ysb[1]])
            if last:
                p.tr(g.banks[TB][:, 0:128], ysb[:, ys, :], g.identf[:], [b_ysb[ys], g.b_ident], [g.bb[TB]])
                p.copy("act", yT[:, kind * 2 + pr, j * 128:(j + 1) * 128], g.banks[TB][:, 0:128], [g.bb[TB]], [b_yT])

        n = len(items)
        for i in range(n + 1):
            if i < n:
                qk(items[i])
            if i > 0:
                pv(items[i - 1])

    seq = [(0, 0), (0, 1), (1, 0), (1, 1)]
    load_pair(0, *seq[0])
    for i, (kind, pr) in enumerate(seq):
        if i + 1 < len(seq):
            load_pair((i + 1) % 2, *seq[i + 1])
        for j in range(J):
            run_maps(i % 2, kind, pr, j)
    p.barrier()

    st = ExitStack()
    if ALIAS:
        sc = x1flat[:, 0:8192]
        ik2 = x1flat[:, 8192:12288].bitcast(BF16)
        iq = x1flat[:, 12288:14336].bitcast(BF16).rearrange("p (a b) -> p a b", a=2)
    else:
        sc = p.sb("i_sc", [128, L], F32, st)
        ik2 = p.sb("i_ik2", [128, L], BF16, st)
        iq = p.sb("i_iq", [128, 2, T], BF16, st)
    b_sc = Buf()
    iwt = p.sb("i_iw", [128, J, 4], F32, st)
    mq = p.sb("i_mq", [128, 4, 128], F32, st)
    mqa = p.sb("i_mqa", [128, 4, 128], F32, st)
    b_i = Buf()
    if ALIAS:
        rl = x1flat[:, 14336:16384].rearrange("p (a b) -> p a b", a=4)
    else:
        rl = p.sb("i_rl", [128, 4, 512], F32, st)
    b_rl = _bufs(4)
    m8 = p.sb("i_m8", [128, 8], F32, st)
    b_m8 = Buf()
    nev = p.sb("i_nev", [128, 2, 512], BF16, st)
    b_nev = _bufs(2)
    for c in range(4):
        for ap_, off, n in KT_all.rows(c, 768, 64):
            for half in range(2):
                dst = ik2[half * 64 + off:half * 64 + off + n, :].rearrange("r (j c t) -> r j c t", c=4, t=128)[:, :, c, :]
                p.dma("sp", dst, ap_.rearrange("r (j t) -> r j t", t=128), (), [b_i])
    p.dma("sp", iq[:, 0, :], QT_loc[8], (), [b_i])
    p.dma("sp", iq[:, 1, :], QT_loc[9], (), [b_i])
    p.dma("sp", iwt[:].rearrange("p j h -> p (j h)"), IW_loc, (), [b_i])
    p.dma("sp", mq[:], consts["maskQ"].rearrange("c q k -> q c k"), (), [b_i])
    p.ts("dve", mqa[:], mq[:], 1.0e30, None, ALU.mult, None, [b_i], [b_i])
    for j in range(J):
        Nk = 512 * (j + 1)
        for r in range(j + 1):
            ks = slice(r * 512, (r + 1) * 512)
            for hi in range(4):
                rows = slice((hi % 2) * 64, (hi % 2) * 64 + 64)
                bk = hi
                p.mm(g.banks[bk][:, :], iq[rows, hi // 2, j * 128:(j + 1) * 128], ik2[rows, ks], True, True, [b_i], [g.bb[bk]])
                p.act(rl[:, hi, :], g.banks[bk][:, :], AF.Relu, [g.bb[bk]], [b_rl[hi]])
                if hi == 0:
                    p.ts("dve", sc[:, ks], rl[:, 0, :], iwt[:, j, 0:1], None, ALU.mult, None, [b_rl[0], b_i], [b_sc])
                else:
                    p.stt("dve", sc[:, ks], rl[:, hi, :], iwt[:, j, hi:hi + 1], sc[:, ks], ALU.mult, ALU.add, [b_rl[hi], b_i, b_sc], [b_sc])
            if r == j:
                p.tt("dve", sc[:, ks], sc[:, ks], mqa[:].rearrange("q c k -> q (c k)"), ALU.add, [b_sc, b_i], [b_sc])
        for it in range(32):
            p.op("dve", lambda e: e.max(out=m8[:], in_=sc[:, 0:Nk]), [b_sc], [b_m8])
            p.op("dve", lambda e: e.match_replace(out=sc[:, 0:Nk], in_to_replace=m8[:], in_values=sc[:, 0:Nk], imm_value=-3.0e38),
                 [b_sc, b_m8], [b_sc])
        p.ts("dve", sc[:, 0:Nk], sc[:, 0:Nk], -1.0e35, 1.0, ALU.is_le, ALU.subtract, [b_sc], [b_sc])
        ks = slice(j * 512, (j + 1) * 512)
        p.tt("dve", sc[:, ks], sc[:, ks], mq[:].rearrange("q c k -> q (c k)"), ALU.min, [b_sc, b_i], [b_sc])
        if dbg is not None and j == 0:
            t_ = p.nc.dram_tensor("dbg_sc", [128, 512], F32, kind="ExternalOutput").ap()
            p.dma("sp", t_, sc[:, 0:512], [b_sc], ())
        for r in range(j + 1):
            bk = 4 + (r % 2)
            for c in range(4):
                blk = 4 * r + c
                p.tr(g.banks[bk][:, c * 128:(c + 1) * 128], sc[:, blk * 128:(blk + 1) * 128], g.identf[:], [b_sc, g.b_ident], [g.bb[bk]])
            s = r % 2
            p.act(nev[:, s, :], g.banks[bk][:, :], AF.Identity, [g.bb[bk]], [b_nev[s]], bias=1.0)
            p.dma("sp", NEGT[j, :, 4 * r:4 * r + 4, :], nev[:, s, :].rearrange("k (c q) -> k c q", c=4), [b_nev[s]], ())
    p.barrier()
    st.close()

    st = ExitStack()
    ng = [p.sb("s_ng%d" % i, [128, NB, 128], BF16, st) for i in range(2)]
    b_ngs = _bufs(2)
    load_pair(0, 2, 0)
    load_pair(1, 2, 1)
    p.dma("sp", ng[0][:, 0:4, :], NEGT[0, :, 0:4, :], (), [b_ngs[0]])
    for j in range(J):
        if j + 1 < J:
            p.dma("sp", ng[(j + 1) % 2][:, 0:4 * (j + 2), :], NEGT[j + 1, :, 0:4 * (j + 2), :], (), [b_ngs[(j + 1) % 2]])
        for pr in range(2):
            run_maps(pr, 2, pr, j, ngT=ng[j % 2], b_ng=b_ngs[j % 2])
    p.barrier()
    st.close()
    stA.close()

    if dbg is not None:
        p.dma("sp", dbg, yT[:], [b_yT], ())
    st = ExitStack()
    wo = p.sb("o_wo", [128, 8, D], BF16, st)
    b_wo = Buf()
    p.dma("pool", wo[:], W["w_out"].rearrange("(fc f) m -> f fc m", f=128), (), [b_wo])
    gt = p.sb("o_g", [128, D], F32, st)
    bt = p.sb("o_b", [128, D], F32, st)
    b_gb = Buf()
    p.dma("sp", gt[:], W["ln1_g"].partition_broadcast(128), (), [b_gb])
    p.dma("sp", bt[:], W["ln1_b"].partition_broadcast(128), (), [b_gb])
    xin = [p.sb("o_x%d" % i, [128, D], F32, st) for i in range(2)]
    b_xin = _bufs(2)
    tmp = p.sb("o_tmp", [128, D], F32, st)
    b_tmp = Buf()
    st4 = p.sb("o_st4", [128, 4], F32, st)
    b_st4 = Buf()
    Xr = X.rearrange("(j q) d -> q j d", q=128)
    for j in range(J):
        s = j % 2
        p.dma("sp", xin[s][:], Xr[:, j, :], (), [b_xin[s]])
        for mh in range(2):
            bk = mh
            for fc in range(8):
                p.mm(g.banks[bk][:, :], yT[:, fc, j * 128:(j + 1) * 128], wo[:, fc, mh * 512:(mh + 1) * 512], fc == 0, fc == 7,
                     [b_yT, b_wo], [g.bb[bk]])
            p.stt("dve", xin[s][:, mh * 512:(mh + 1) * 512], xin[s][:, mh * 512:(mh + 1) * 512], ALPHA, g.banks[bk][:, :],
                  ALU.mult, ALU.add, [b_xin[s], g.bb[bk]], [b_xin[s]])
        ln_rows(p, g, xin[s][:], b_xin[s], x1[:, j, :], b_x1, gt[:], bt[:], b_gb, tmp[:], b_tmp, st4, b_st4)
    p.barrier()
    st.close()
    stB.close()

def emit_C(p, g, W, x1, b_x1, Xout):
    J, T = g.J, g.T
    st = ExitStack()
    xT = p.sb("m_xT", [128, 8, T], BF16, st)
    b_xT = Buf()
    xTf = p.sb("m_xTf", [128, 8, 128], F32, st)
    b_xTf = Buf()
    rw = p.sb("m_rw", [128, 8, NE], F32, st)
    b_w = Buf()
    p.dma("sp", rw[:], W["router_w"].rearrange("(dc q) e -> q dc e", q=128), (), [b_w])
    rb = p.sb("m_rb", [128, NE], F32, st)
    p.dma("sp", rb[:], W["router_b"].partition_broadcast(128), (), [b_w])
    bgu = p.sb("m_bgu", [128, 16, NE], F32, st)
    bd = p.sb("m_bd", [NE, D], F32, st)
    gates = p.sb("m_gates", [128, J, NE], F32, st)
    gT = p.sb("m_gT", [NE, 128], F32, st)
    lg = p.sb("m_lg", [128, NE], F32, st)
    m8 = p.sb("m_m8", [128, 8], F32, st)
    sm = p.sb("m_sm", [128, 4], F32, st)
    st0 = ExitStack()
    bgr = p.sb("m_bgr", [NE, 2 * D], F32, st0)
    p.dma("sp", bgr[:], W["b_gu"], (), [b_w])
    bgv = bgr[:].rearrange("e (jc q t) -> e jc t q", q=128, t=2)
    for jc in range(8):
        for t in range(2):
            i = jc * 2 + t
            p.tr(g.banks[7][:, i * NE:(i + 1) * NE], bgv[:, jc, t, :], g.identf[0:NE, 0:NE], [b_w, g.b_ident], [g.bb[7]])
    p.copy("dve", bgu[:], g.banks[7][:, :].rearrange("q (i e) -> q i e", e=NE), [g.bb[7]], [b_w])
    p.barrier()
    st0.close()
    p.dma("sp", bd[:], W["b_down"], (), [b_w])
    b_gates = Buf()
    b_gT = Buf()
    b_r = Buf()
    for j in range(J):
        transpose_block(p, g, x1[:, j, :], b_x1, xT, b_xT, j, dst_f32=xTf, dst_f32_b=b_xTf, bank0=0)
        for dc in range(8):
            p.mm(g.banks[2][:, 0:NE], xTf[:, dc, :], rw[:, dc, :], dc == 0, dc == 7, [b_xTf, b_w], [g.bb[2]])
        p.tt("dve", lg[:], g.banks[2][:, 0:NE], rb[:], ALU.add, [g.bb[2], b_w], [b_r])
        p.op("dve", lambda e: e.max(out=m8[:], in_=lg[:]), [b_r], [b_r])
        p.ts("dve", sm[:, 0:1], m8[:, 0:1], -1.0, None, ALU.mult, None, [b_r], [b_r])
        p.act(gates[:, j, :], lg[:], AF.Exp, [b_r], [b_gates], bias=sm[:, 0:1])
        p.ts("dve", lg[:], lg[:], m8[:, 3:4], None, ALU.is_ge, None, [b_r], [b_r])
        p.tt("dve", gates[:, j, :], gates[:, j, :], lg[:], ALU.mult, [b_r, b_gates], [b_gates])
        p.memset("dve", sm[:, 1:2], 0.0, [b_r])
        p.ts("dve", lg[:], gates[:, j, :], 1.0, 0.0, ALU.mult, ALU.add, [b_gates], [b_r], accum=sm[:, 1:2])
        p.op("dve", lambda e: e.reciprocal(out=sm[:, 2:3], in_=sm[:, 1:2]), [b_r], [b_r])
        p.ts("dve", gates[:, j, :], gates[:, j, :], sm[:, 2:3], None, ALU.mult, None, [b_r, b_gates], [b_gates])
        p.tr(g.banks[3][0:NE, 0:128], gates[:, j, :], g.identf[:], [b_gates, g.b_ident], [g.bb[3]])
        p.copy("dve", gT[:], g.banks[3][0:NE, 0:128], [g.bb[3]], [b_gT])
        for mh in range(2):
            p.mm(g.banks[4 + mh][:, :], gT[:], bd[:, mh * 512:(mh + 1) * 512], True, True, [b_gT, b_w], [g.bb[4 + mh]])
            p.stt("dve", x1[:, j, mh * 512:(mh + 1) * 512], x1[:, j, mh * 512:(mh + 1) * 512], ALPHA, g.banks[4 + mh][:, :],
                  ALU.mult, ALU.add, [b_x1, g.bb[4 + mh]], [b_x1])
    NW = 4
    st2 = ExitStack()
    st_save, st = st, st2
    wgu = [p.sb("m_wgu%d" % i, [128, 8, 256], BF16, st) for i in range(NW)]
    b_wgu = _bufs(NW)
    wd = [p.sb("m_wd%d" % i, [128, 8, D], BF16, st) for i in range(2)]
    b_wd = _bufs(2)
    hT = p.sb("m_hT", [128, 8, T], BF16, st)
    b_hT = _bufs(8)
    TG = min(512, T)
    ntg = T // TG
    gc = p.sb("m_gc", [128, 2, TG], F32, st)
    sg = p.sb("m_sg", [128, 2, TG], F32, st)
    uc = p.sb("m_uc", [128, 2, TG], F32, st)
    b_gc, b_sg, b_uc = _bufs(2), _bufs(2), _bufs(2)
    b_acc = _bufs(J)
    wq = {"n": 0}

    def load_gu(e, jc):
        s = wq["n"] % NW
        wq["n"] += 1
        src = W["w_gu"][e].rearrange("(dc q) f -> q dc f", q=128)[:, :, jc * 256:(jc + 1) * 256]
        p.dma("pool", wgu[s][:], src, (), [b_wgu[s]])
        return s

    def load_d(e):
        s = e % 2
        p.dma("pool", wd[s][:], W["w_down"][e].rearrange("(jc q) m -> q jc m", q=128), (), [b_wd[s]])

    pre = []
    for jc in range(2):
        pre.append(load_gu(0, jc))
    load_d(0)
    k = 0
    for e in range(NE):
        for jc in range(8):
            nxt = e * 8 + jc + 2
            if nxt < NE * 8:
                pre.append(load_gu(nxt // 8, nxt % 8))
            if jc == 4 and e + 1 < NE:
                load_d(e + 1)
            s = pre.pop(0)
            wv = wgu[s][:].rearrange("q dc (j t) -> q dc t j", t=2)
            for tg in range(ntg):
                kk = k % 2
                k += 1
                tsl = slice(tg * TG, (tg + 1) * TG)
                for t, bk in ((0, kk), (1, 2 + kk)):
                    for dc in range(8):
                        p.mm(g.banks[bk][:, 0:TG], wv[:, dc, t, :], xT[:, dc, tsl], dc == 0, dc == 7, [b_wgu[s], b_xT], [g.bb[bk]])
                p.ts("dve", gc[:, kk, :], g.banks[kk][:, 0:TG], bgu[:, jc * 2, e:e + 1], 7.0, ALU.add, ALU.min, [g.bb[kk], b_w], [b_gc[kk]])
                p.act(sg[:, kk, :], gc[:, kk, :], AF.Sigmoid, [b_gc[kk]], [b_sg[kk]], scale=1.702)
                p.act(uc[:, kk, :], g.banks[2 + kk][:, 0:TG], AF.Identity, [g.bb[2 + kk], b_w], [b_uc[kk]], bias=bgu[:, jc * 2 + 1, e:e + 1])
                p.ts("pool", uc[:, kk, :], uc[:, kk, :], 7.0, -7.0, ALU.min, ALU.max, [b_uc[kk]], [b_uc[kk]])
                p.tt("pool", gc[:, kk, :], gc[:, kk, :], sg[:, kk, :], ALU.mult, [b_gc[kk], b_sg[kk]], [b_gc[kk]])
                p.stt("dve", hT[:, jc, tsl], uc[:, kk, :], 1.0, gc[:, kk, :], ALU.add, ALU.mult, [b_uc[kk], b_gc[kk]], [b_hT[jc]])
        s = e % 2
        for j in range(J):
            for mh in range(2):
                bk = 4 + (j * 2 + mh) % 4
                for jc in range(8):
                    p.mm(g.banks[bk][:, :], hT[:, jc, j * 128:(j + 1) * 128], wd[s][:, jc, mh * 512:(mh + 1) * 512], jc == 0, jc == 7,
                         [b_hT[jc], b_wd[s]], [g.bb[bk]])
                p.stt("dve", x1[:, j, mh * 512:(mh + 1) * 512], g.banks[bk][:, :], gates[:, j, e:e + 1], x1[:, j, mh * 512:(mh + 1) * 512],
                      ALU.mult, ALU.add, [g.bb[bk], b_gates, b_acc[j], b_x1], [b_acc[j]])
    p.barrier()
    st2.close()
    st = st_save
    gt = p.sb("m_g", [128, D], F32, st)
    bt = p.sb("m_b", [128, D], F32, st)
    b_gb = Buf()
    p.dma("sp", gt[:], W["ln2_g"].partition_broadcast(128), (), [b_gb])
    p.dma("sp", bt[:], W["ln2_b"].partition_broadcast(128), (), [b_gb])
    tmp = p.sb("m_tmp", [128, D], F32, st)
    b_tmp = Buf()
    st4 = p.sb("m_st4", [128, 4], F32, st)
    b_st4 = Buf()
    Xr = Xout.rearrange("(j q) d -> q j d", q=128)
    for j in range(J):
        ln_rows(p, g, x1[:, j, :], b_acc[j], x1[:, j, :], b_acc[j], gt[:], bt[:], b_gb, tmp[:], b_tmp, st4, b_st4)
        p.dma("sp", Xr[:, j, :], x1[:, j, :], [b_acc[j]], ())
    p.barrier()
    st.close()


def _bucket(d):
    n = np.maximum(d, 0)
    nf = np.maximum(n, 1).astype(np.float32)
    large = 16 + (np.log(nf / np.float32(16)) / np.float32(math.log(8.0)) * np.float32(16)).astype(np.int32)
    large = np.minimum(large, 31)
    return np.where(n < 16, n, large)


def make_consts(c_me):
    q = np.arange(128)
    cs = {}
    cs["identf"] = np.eye(128, dtype=np.float32)
    dqm = np.zeros((128, 2), np.float32)
    dqm[:, 0] = ((q % 64) < 32)
    dqm[:, 1] = 1.0 - dqm[:, 0]
    cs["dqmask"] = dqm
    sel = np.zeros((128, 4), np.float32)
    sel[:, c_me] = 1.0
    cs["sel"] = sel
    cs["triu"] = np.triu(np.ones((128, 128), np.float32))
    mT = np.zeros((4, 128, 128), np.float32)
    for c in range(4):
        dl = c_me - c
        if dl < 0:
            mT[c] = 1.0
        elif dl == 0:
            mT[c] = (q[:, None] > q[None, :]).astype(np.float32)
    cs["maskT"] = mT * np.float32(NEG)
    cs["maskQ"] = -np.ascontiguousarray(mT.transpose(0, 2, 1))
    return cs


def bias_tile_index(c_me):
    q = np.arange(128)
    idx = np.zeros((8, 128, 128), np.int64)
    for t in range(8):
        dl = (c_me - t) if t < 4 else (4 + c_me - (t - 4))
        d = 128 * dl + q[None, :] - q[:, None]
        idx[t] = _bucket(d)
    return idx


CONST_SHAPES = {"identf": [128, 128], "dqmask": [128, 2], "sel": [128, 4], "triu": [128, 128],
                "maskT": [4, 128, 128], "maskQ": [4, 128, 128]}


def declare_consts(nc):
    return {k: nc.dram_tensor("c_" + k, shp, F32, kind="ExternalInput").ap() for k, shp in CONST_SHAPES.items()}


WSHAPES = {"rel31": [8], "biasT": [8, 8, 128, 128], "diff_lambda": [4, 32], "diff_norm_g": [64], "conv_w": [31, 256],
           "conv_b": [256], "conv_ln_g": [256], "conv_ln_b": [256], "w_out": [D, D], "ln1_g": [D], "ln1_b": [D],
           "router_w": [D, NE], "router_b": [NE], "w_gu": [NE, D, 2 * D], "b_gu": [NE, 2 * D], "w_down": [NE, D, D],
           "b_down": [NE, D], "ln2_g": [D], "ln2_b": [D]}


def build_A(J):
    nc = bass.Bass("TRN2", target_bir_lowering=False)
    T = J * 128
    X = nc.dram_tensor("X", [T, D], F32, kind="ExternalInput").ap()
    w_in = nc.dram_tensor("w_in", [D, NIN], F32, kind="ExternalInput").ap()
    forget_b = nc.dram_tensor("forget_b", [4], F32, kind="ExternalInput").ap()
    consts = declare_consts(nc)
    KT_loc = nc.dram_tensor("KT_loc", [KT_ROWS, T], BF16, kind="ExternalOutput").ap()
    V_loc = nc.dram_tensor("V_loc", [T, 768], BF16, kind="ExternalOutput").ap()
    HT_loc = nc.dram_tensor("HT_loc", [256, T], F32, kind="ExternalOutput").ap()
    LF_loc = nc.dram_tensor("LF_loc", [J * 4, 128], F32, kind="ExternalOutput").ap()
    QT_loc = nc.dram_tensor("QT_loc", [NQCH, 128, T], BF16, kind="ExternalOutput").ap()
    IW_loc = nc.dram_tensor("IW_loc", [128, J * 4], F32, kind="ExternalOutput").ap()
    p = Prog(nc)
    g = setup_common(p, J, consts)
    emit_A(p, g, X, w_in, forget_b, consts, KT_loc, V_loc, HT_loc, LF_loc, QT_loc, IW_loc)
    p.finish()
    return nc


def build_BC(J, do_C=True, debug=False):
    nc = bass.Bass("TRN2", target_bir_lowering=False)
    T = J * 128
    NB = 4 * J
    X = nc.dram_tensor("X", [T, D], F32, kind="ExternalInput").ap()
    KT_all = nc.dram_tensor("KT_all", [4 * KT_ROWS, T], BF16, kind="ExternalInput").ap()
    V_all = nc.dram_tensor("V_all", [4 * T, 768], BF16, kind="ExternalInput").ap()
    HT_all = nc.dram_tensor("HT_all", [4 * 256, T], F32, kind="ExternalInput").ap()
    LF_all = nc.dram_tensor("LF_all", [4 * J * 4, 128], F32, kind="ExternalInput").ap()
    QT_loc = nc.dram_tensor("QT_loc", [NQCH, 128, T], BF16, kind="ExternalInput").ap()
    IW_loc = nc.dram_tensor("IW_loc", [128, J * 4], F32, kind="ExternalInput").ap()
    HT_loc = nc.dram_tensor("HT_loc", [256, T], F32, kind="ExternalInput").ap()
    lamc = nc.dram_tensor("lamc", [128, 2], F32, kind="ExternalInput").ap()
    W = {k: nc.dram_tensor(k, shp, F32, kind="ExternalInput").ap() for k, shp in WSHAPES.items()}
    consts = declare_consts(nc)
    NEGT = nc.dram_tensor("NEGT", [J, 128, NB, 128], BF16, kind="Internal").ap()
    Xout = nc.dram_tensor("Xout", [T, D], F32, kind="ExternalOutput").ap()
    p = Prog(nc)
    g = setup_common(p, J, consts)
    x1 = p.sb("x1", [128, J, D], F32)
    b_x1 = Buf()
    dbg = nc.dram_tensor("dbg_yT", [128, 8, T], BF16, kind="ExternalOutput").ap() if debug else None
    emit_B(p, g, 0, X, Gat(KT_ROWS, mono=KT_all), Gat(T, mono=V_all), Gat(256, mono=HT_all), LF_all, QT_loc, IW_loc, HT_loc, NEGT, W,
           consts, x1, b_x1, lamc=lamc, dbg=dbg)
    if do_C:
        emit_C(p, g, W, x1, b_x1, Xout)
    else:
        p.dma("sp", Xout.rearrange("(j q) d -> q j d", q=128), x1[:], [b_x1], ())
    p.finish()
    return nc


def build_fused(J, nlayers=DEPTH):
    nc = bass.Bass("TRN2", target_bir_lowering=False)
    T = J * 128
    NB = 4 * J
    Xin = nc.dram_tensor("X", [T, D], F32, kind="ExternalInput").ap()
    w_in = nc.dram_tensor("w_in", [nlayers, D, NIN], F32, kind="ExternalInput").ap()
    forget_b = nc.dram_tensor("forget_b", [nlayers, 4], F32, kind="ExternalInput").ap()
    Wf = {}
    for k, shp in WSHAPES.items():
        if k in ("rel31", "biasT"):
            Wf[k] = nc.dram_tensor(k, shp, F32, kind="ExternalInput").ap()
        else:
            Wf[k] = nc.dram_tensor(k, [nlayers] + shp, F32, kind="ExternalInput").ap()
    consts = declare_consts(nc)
    Xout = nc.dram_tensor("Xout", [T, D], F32, kind="ExternalOutput").ap()
    Xbuf = nc.dram_tensor("Xbuf", [T, D], F32, kind="Internal").ap()
    KT_loc = nc.dram_tensor("KT_loc", [KT_ROWS, T], BF16, kind="Internal").ap()
    V_loc = nc.dram_tensor("V_loc", [T, 768], BF16, kind="Internal").ap()
    HT_loc = nc.dram_tensor("HT_loc", [256, T], F32, kind="Internal").ap()
    LF_loc = nc.dram_tensor("LF_loc", [J * 4, 128], F32, kind="Internal").ap()
    QT_loc = nc.dram_tensor("QT_loc", [NQCH, 128, T], BF16, kind="Internal").ap()
    IW_loc = nc.dram_tensor("IW_loc", [128, J * 4], F32, kind="Internal").ap()
    SK, SV, SH = 64, 128, 32
    def mk(name, R, S, W_, dt):
        return [Gat(R, S=S, chunks=[nc.dram_tensor("%s%d_%d" % (name, i, k), [4 * S, W_], dt, kind="Internal").ap()
                                    for k in range(R // S)]) for i in range(2)]
    KT_alls = mk("KTa", KT_ROWS, SK, T, BF16)
    V_alls = mk("Va", T, SV, 768, BF16)
    HT_alls = mk("HTa", 256, SH, T, F32)
    LF_alls = [nc.dram_tensor("LF_all%d" % i, [4 * J * 4, 128], F32, kind="Internal").ap() for i in range(2)]
    NEGT = nc.dram_tensor("NEGT", [J, 128, NB, 128], BF16, kind="Internal").ap()
    p = Prog(nc)
    g = setup_common(p, J, consts)
    x1 = p.sb("x1", [128, J, D], F32)
    b_x1 = Buf()
    groups = [[0, 1, 2, 3], [4, 5, 6, 7]]
    for l in range(nlayers):
        Xsrc = Xin if l == 0 else Xbuf
        Xdst = Xout if l == nlayers - 1 else Xbuf
        KT_all, V_all, HT_all, LF_all = KT_alls[l % 2], V_alls[l % 2], HT_alls[l % 2], LF_alls[l % 2]
        emit_A(p, g, Xsrc, w_in[l], forget_b[l], consts, KT_loc, V_loc, HT_loc, LF_loc, QT_loc, IW_loc)
        for src, gat in ((KT_loc, KT_all), (V_loc, V_all), (HT_loc, HT_all)):
            for k, ch in enumerate(gat.chunks):
                p.collective(src[k * gat.S:(k + 1) * gat.S, :], ch, groups)
        p.collective(LF_loc, LF_all, groups)
        p.barrier()
        W = {k: (v if k in ("rel31", "biasT") else v[l]) for k, v in Wf.items()}
        emit_B(p, g, l, Xsrc, KT_all, V_all, HT_all, LF_all, QT_loc, IW_loc, HT_loc, NEGT, W, consts, x1, b_x1)
        emit_C(p, g, W, x1, b_x1, Xdst)
    p.finish()
    return nc


J_FULL = 16
FUSED = True
_CACHE = {}


def _loc(a, J):
    L = a.shape[0]
    v = a.reshape(J, 4, 128, -1)
    return [np.ascontiguousarray(v[:, c].reshape(J * 128, -1)) for c in range(4)]


def _unloc(parts, J):
    Dd = parts[0].shape[-1]
    out = np.empty((J, 4, 128, Dd), parts[0].dtype)
    for c in range(4):
        out[:, c] = parts[c].reshape(J, 128, Dd)
    return out.reshape(J * 4 * 128, Dd)


def kernel(x, rel_bias, w_in, forget_b, diff_lambda, diff_norm_g, conv_w, conv_b, conv_ln_g, conv_ln_b, w_out,
           ln1_g, ln1_b, router_w, router_b, w_gu, b_gu, w_down, b_down, ln2_g, ln2_b):
    J = J_FULL
    f32 = lambda a: np.ascontiguousarray(np.asarray(a, dtype=np.float32))
    x = f32(x)
    rel_bias = f32(rel_bias)
    P = dict(w_in=f32(w_in), forget_b=f32(forget_b), diff_lambda=f32(diff_lambda), diff_norm_g=f32(diff_norm_g), conv_w=f32(conv_w),
             conv_b=f32(conv_b), conv_ln_g=f32(conv_ln_g), conv_ln_b=f32(conv_ln_b), w_out=f32(w_out), ln1_g=f32(ln1_g), ln1_b=f32(ln1_b),
             router_w=f32(router_w), router_b=f32(router_b), w_gu=f32(w_gu), b_gu=f32(b_gu), w_down=f32(w_down), b_down=f32(b_down),
             ln2_g=f32(ln2_g), ln2_b=f32(ln2_b))
    B = x.shape[0]
    cs = [make_consts(c) for c in range(4)]
    biasT = [np.ascontiguousarray(rel_bias[bias_tile_index(c)].transpose(3, 0, 1, 2)) for c in range(4)]
    rel31 = np.ascontiguousarray(rel_bias[31])
    xs = [None] * 8
    for b in range(B):
        for c, part in enumerate(_loc(x[b], J)):
            xs[4 * b + c] = part
    cores = list(range(8))
    if FUSED:
        if "F" not in _CACHE:
            _CACHE["F"] = build_fused(J)
        in_maps = []
        for r in cores:
            m = {"X": xs[r], "w_in": P["w_in"], "forget_b": P["forget_b"], "biasT": biasT[r % 4], "rel31": rel31}
            for k in WSHAPES:
                if k not in m:
                    m[k] = P[k]
            for k, v in cs[r % 4].items():
                m["c_" + k] = v
            in_maps.append(m)
        res = run_bass_kernel_spmd(_CACHE["F"], in_maps, core_ids=cores).results
        xs = [np.asarray(res[r]["Xout"]) for r in cores]
    else:
        if "A" not in _CACHE:
            _CACHE["A"] = build_A(J)
            _CACHE["BC"] = build_BC(J)
        for l in range(DEPTH):
            li = 0.8 - 0.6 * math.exp(-0.3 * l)
            lamc = np.tile(np.array([[-li, 1.0 - li]], np.float32), (128, 1))
            in_maps = []
            for r in cores:
                m = {"X": xs[r], "w_in": P["w_in"][l], "forget_b": P["forget_b"][l]}
                for k, v in cs[r % 4].items():
                    m["c_" + k] = v
                in_maps.append(m)
            resA = run_bass_kernel_spmd(_CACHE["A"], in_maps, core_ids=cores).results
            in_maps = []
            for r in cores:
                b = r // 4
                m = {"X": xs[r], "lamc": lamc, "biasT": biasT[r % 4], "rel31": rel31}
                for nm in ("KT", "V", "HT", "LF"):
                    m[nm + "_all"] = np.concatenate([np.asarray(resA[4 * b + cc][nm + "_loc"]) for cc in range(4)], 0)
                for nm in ("QT_loc", "IW_loc", "HT_loc"):
                    m[nm] = np.asarray(resA[r][nm])
                for k in WSHAPES:
                    if k not in m:
                        m[k] = P[k][l]
                for k, v in cs[r % 4].items():
                    m["c_" + k] = v
                in_maps.append(m)
            resB = run_bass_kernel_spmd(_CACHE["BC"], in_maps, core_ids=cores).results
            xs = [np.asarray(resB[r]["Xout"]) for r in cores]
    out = np.stack([_unloc([xs[4 * b + c] for c in range(4)], J) for b in range(B)], 0)
    return out.astype(np.float32)
```

```python
import math
from contextlib import ExitStack

import numpy as np
import concourse.bass as bass
import concourse.mybir as mybir
from concourse.bass_utils import run_bass_kernel_spmd

F32 = mybir.dt.float32
BF16 = mybir.dt.bfloat16
ALU = mybir.AluOpType
AF = mybir.ActivationFunctionType

D = 1024
NIN = 3144
DEPTH = 4
NE = 32
ALPHA = (2 * DEPTH) ** 0.25
EPS = 1e-5
NEG = -30000.0
REPL = -1.0e30
C_FQ, C_FK, C_FV, C_FF = 0, 256, 512, 768
C_DQ, C_DK, C_DV = 772, 1028, 1284
C_SQ, C_SK, C_SV = 1540, 1796, 2052
C_IQ, C_IK, C_IW, C_CU = 2308, 2564, 2628, 2632
KT_ROWS = 832
NQCH = 10


class Buf:
    __slots__ = ("w", "r", "wx")

    def __init__(self):
        self.w = None
        self.r = {}
        self.wx = []


class Prog:
    NDS = 24

    def __init__(self, nc):
        self.nc = nc
        self.st = ExitStack()
        self.engs = {"pe": nc.tensor, "act": nc.scalar, "dve": nc.vector, "pool": nc.gpsimd, "sp": nc.sync}
        self.semobj = {}
        self.cnt = {}
        for k in self.engs:
            self.semobj[k] = self.st.enter_context(nc.semaphore("s_" + k))
            self.cnt[k] = 0
        self.seen = {k: {} for k in self.engs}
        self.dcnt = [0] * self.NDS
        for i in range(self.NDS):
            self.semobj[("d", i)] = self.st.enter_context(nc.semaphore("s_d%d" % i))
        self.dnext = 0
        self.nins = 0

    def sb(self, name, shape, dt, st=None):
        self.uid = getattr(self, "uid", 0) + 1
        return (st or self.st).enter_context(self.nc.sbuf_tensor("%s_%d" % (name, self.uid), shape, dt))

    def ps(self, name, shape, dt, st=None):
        return (st or self.st).enter_context(self.nc.psum_tensor(name, shape, dt))

    def _wait(self, eng, key, val):
        if self.seen[eng].get(key, 0) >= val:
            return
        self.engs[eng].wait_ge(self.semobj[key], val)
        self.seen[eng][key] = val
        self.nins += 1

    def _deps(self, eng, reads, writes, extra=None, is_dma=False):
        need = {}

        def add(k, v, kind):
            if k == eng:
                if eng == "pe" or kind == "war":
                    return
            if need.get(k, 0) < v:
                need[k] = v
        if extra is not None:
            add(extra[0], extra[1], "raw")
        for b in reads:
            if b.w is not None:
                add(b.w[0], b.w[1], "raw")
            for k, v in b.wx:
                add(k, v, "raw")
        for b in writes:
            if is_dma and b.w is not None and isinstance(b.w[0], tuple) and not b.r:
                continue
            if b.w is not None:
                add(b.w[0], b.w[1], "waw")
            for k, v in b.wx:
                add(k, v, "waw")
            for k, v in b.r.items():
                add(k, v, "war")
        items = [(k, v) for k, v in need.items() if self.seen[eng].get(k, 0) < v]
        if not items:
            return None
        for k, v in items[:-1]:
            self._wait(eng, k, v)
        return items[-1]

    def _embed(self, eng, ins, last):
        if last is not None:
            ins._wait_ge(self.semobj[last[0]], last[1])
            self.seen[eng][last[0]] = last[1]

    def _mark(self, key, val, reads, writes, is_dma=False):
        for b in reads:
            b.r[key] = val
        for b in writes:
            if is_dma and b.w is not None and isinstance(b.w[0], tuple) and not b.r:
                b.wx.append(b.w)
            else:
                b.wx = []
            b.w = (key, val)
            b.r = {}

    def op(self, eng, fn, reads=(), writes=()):
        last = self._deps(eng, reads, writes)
        ins = fn(self.engs[eng])
        self._embed(eng, ins, last)
        self.cnt[eng] += 1
        ins.then_inc(self.semobj[eng], 1)
        self.nins += 1
        self._mark(eng, self.cnt[eng], reads, writes)

    def dma(self, eng, out, in_, reads=(), writes=()):
        i = self.dnext
        self.dnext = (i + 1) % self.NDS
        key = ("d", i)
        last = self._deps(eng, reads, writes, extra=(key, self.dcnt[i]) if self.dcnt[i] > 0 else None, is_dma=True)
        ins = self.engs[eng].dma_start(out=out, in_=in_)
        self._embed(eng, ins, last)
        self.dcnt[i] += 16
        ins.then_inc(self.semobj[key], 16)
        self.nins += 1
        self._mark(key, self.dcnt[i], reads, writes, is_dma=True)

    def collective(self, src, dst, groups):
        if "cc" not in self.semobj:
            self.semobj["cc"] = self.st.enter_context(self.nc.semaphore("s_cc"))
            self.cccnt = 0
        self.barrier(["pool"])
        ins = self.nc.gpsimd.collective_compute("AllGather", ALU.bypass, replica_groups=groups, ins=[src.opt()], outs=[dst.opt()])
        self.cccnt += 1
        ins.then_inc(self.semobj["cc"])
        self.nins += 1
        self._wait("pool", "cc", self.cccnt)

    def barrier(self, engs=None):
        engs = engs or list(self.engs)
        for e in engs:
            for k in self.engs:
                if k != e and self.cnt[k] > 0:
                    self._wait(e, k, self.cnt[k])
            for i in range(self.NDS):
                if self.dcnt[i] > 0:
                    self._wait(e, ("d", i), self.dcnt[i])
            if getattr(self, "cccnt", 0) > 0:
                self._wait(e, "cc", self.cccnt)

    def mm(self, out, lhsT, rhs, start, stop, reads, writes):
        self.op("pe", lambda e: e.matmul(out, lhsT=lhsT, rhs=rhs, start=start, stop=stop), reads, writes)

    def tr(self, out, in_, ident, reads, writes):
        self.op("pe", lambda e: e.transpose(out, in_, ident), reads, writes)

    def act(self, out, in_, func, reads, writes, bias=None, scale=1.0, accum=None):
        kw = {}
        if bias is not None:
            kw["bias"] = bias
        if accum is not None:
            kw["accum_out"] = accum
        self.op("act", lambda e: e.activation(out=out, in_=in_, func=func, scale=scale, **kw), reads, writes)

    def ts(self, eng, out, in0, s1, s2, op0, op1, reads, writes, accum=None):
        kw = {}
        if accum is not None:
            kw["accum_out"] = accum
        if op1 is None:
            self.op(eng, lambda e: e.tensor_scalar(out=out, in0=in0, scalar1=s1, scalar2=None, op0=op0, **kw), reads, writes)
        else:
            self.op(eng, lambda e: e.tensor_scalar(out=out, in0=in0, scalar1=s1, scalar2=s2, op0=op0, op1=op1, **kw), reads, writes)

    def tt(self, eng, out, in0, in1, op, reads, writes):
        self.op(eng, lambda e: e.tensor_tensor(out=out, in0=in0, in1=in1, op=op), reads, writes)

    def stt(self, eng, out, in0, scalar, in1, op0, op1, reads, writes):
        self.op(eng, lambda e: e.scalar_tensor_tensor(out=out, in0=in0, scalar=scalar, in1=in1, op0=op0, op1=op1), reads, writes)

    def copy(self, eng, out, in_, reads, writes):
        if eng == "act":
            self.op("act", lambda e: e.copy(out=out, in_=in_), reads, writes)
        else:
            self.op(eng, lambda e: e.tensor_copy(out=out, in_=in_), reads, writes)

    def memset(self, eng, ap, val, writes):
        self.op(eng, lambda e: e.memset(ap, val), (), writes)

    def finish(self):
        self.barrier(["sp"])
        self.st.close()


def _bufs(n):
    return [Buf() for _ in range(n)]


class Ctx:
    pass


class Gat:
    def __init__(self, R, mono=None, S=None, chunks=None):
        self.R, self.mono, self.S, self.chunks = R, mono, S, chunks

    def rows(self, c, r0, n):
        if self.mono is not None:
            return [(self.mono[c * self.R + r0:c * self.R + r0 + n, :], 0, n)]
        out = []
        r = r0
        while r < r0 + n:
            k, o = r // self.S, r % self.S
            m = min(self.S - o, r0 + n - r)
            out.append((self.chunks[k][c * self.S + o:c * self.S + o + m, :], r - r0, m))
            r += m
        return out


def setup_common(p, J, consts):
    nc = p.nc
    g = Ctx()
    g.J = J
    g.T = J * 128
    g.banks = [p.ps("bank%d" % i, [128, 512], F32) for i in range(8)]
    g.bb = _bufs(8)
    g.identf = p.sb("identf", [128, 128], F32)
    g.identb = p.sb("identb", [128, 128], BF16)
    g.b_ident = Buf()
    p.dma("sp", g.identf[:], consts["identf"], (), [g.b_ident])
    p.copy("dve", g.identb[:], g.identf[:], [g.b_ident], [g.b_ident])
    return g


def transpose_block(p, g, src, src_b, dst_bf, dst_b, blk, dst_f32=None, dst_f32_b=None, bank0=0):
    for half in range(2):
        bk = bank0 + half
        for q in range(4):
            dc = half * 4 + q
            p.tr(g.banks[bk][:, q * 128:(q + 1) * 128], src[:, dc * 128:(dc + 1) * 128], g.identf[:],
                 [src_b, g.b_ident], [g.bb[bk]])
        outv = dst_bf[:, half * 4:(half + 1) * 4, blk * 128:(blk + 1) * 128]
        inv = g.banks[bk][:].rearrange("p (q t) -> p q t", q=4)
        if dst_f32 is None:
            p.copy("act" if half == 0 else "dve", outv, inv, [g.bb[bk]], [dst_b])
        else:
            f32v = dst_f32[:, half * 4:(half + 1) * 4, :]
            p.copy("dve", f32v, inv, [g.bb[bk]], [dst_f32_b])
            p.copy("act", outv, f32v, [dst_f32_b], [dst_b])


def emit_A(p, g, X, w_in, forget_b, consts, KT_loc, V_loc, HT_loc, LF_loc, QT_loc, IW_loc):
    J, T = g.J, g.T
    st = ExitStack()
    xT = p.sb("a_xT", [128, 8, T], BF16, st)
    b_xT = Buf()
    win = p.sb("a_win", [128, 8, NIN], BF16, st)
    b_win = _bufs(8)
    xt = [p.sb("a_xt%d" % i, [128, D], F32, st) for i in range(2)]
    b_xt = _bufs(2)
    fb = p.sb("a_fb", [128, 4], F32, st)
    b_fb = Buf()
    dqm = p.sb("a_dqm", [128, 2], F32, st)
    b_dqm = Buf()
    p.dma("sp", fb[:], forget_b.partition_broadcast(128), (), [b_fb])
    p.dma("sp", dqm[:], consts["dqmask"], (), [b_dqm])
    w_r = w_in.rearrange("(dc q) f -> q dc f", q=128)
    for dc in range(8):
        p.dma("pool", win[:, dc, :], w_r[:, dc, :], (), [b_win[dc]])
    Xr = X.rearrange("(j q) d -> q j d", q=128)
    for j in range(J):
        s = j % 2
        p.dma("sp", xt[s][:], Xr[:, j, :], (), [b_xt[s]])
        transpose_block(p, g, xt[s], b_xt[s], xT, b_xT, j)

    ev = p.sb("a_ev", [128, 4, 512], BF16, st)
    b_ev = _bufs(4)
    evf = p.sb("a_evf", [128, 2, 512], F32, st)
    b_evf = _bufs(2)
    sg = p.sb("a_sg", [128, 2, 512], F32, st)
    b_sg = _bufs(2)
    ntg = T // 512 if T >= 512 else 1
    TG = min(512, T)
    state = {"bk": 2, "ev": 0, "evf": 0}

    def fm_chunk(col0, ncols, tg):
        bk = state["bk"]
        state["bk"] = 2 + (bk - 2 + 1) % 4
        for dc in range(8):
            p.mm(g.banks[bk][0:ncols, 0:TG], win[:, dc, col0:col0 + ncols], xT[:, dc, tg * TG:(tg + 1) * TG],
                 dc == 0, dc == 7, [b_win[dc], b_xT], [g.bb[bk]])
        return bk

    def ev_slot():
        s = state["ev"]
        state["ev"] = (s + 1) % 4
        return s

    for tg in range(ntg):
        tsl = slice(tg * TG, (tg + 1) * TG)
        qspec = [(0, C_FQ, None), (1, C_FQ + 128, None), (2, C_DQ, 0), (3, C_DQ + 128, 0), (4, C_DQ, 1), (5, C_DQ + 128, 1),
                 (6, C_SQ, None), (7, C_SQ + 128, None), (8, C_IQ, None), (9, C_IQ + 128, None)]
        cache = {}
        for qi, col0, m in qspec:
            if m == 1 and col0 in cache:
                bk = cache[col0]
            else:
                bk = fm_chunk(col0, 128, tg)
                cache[col0] = bk
            s = ev_slot()
            if m is None:
                p.copy("act", ev[:, s, 0:TG], g.banks[bk][:, 0:TG], [g.bb[bk]], [b_ev[s]])
            else:
                p.ts("dve", ev[:, s, 0:TG], g.banks[bk][:, 0:TG], dqm[:, m:m + 1], None, ALU.mult, None,
                     [g.bb[bk], b_dqm], [b_ev[s]])
            p.dma("sp", QT_loc[qi, :, tsl], ev[:, s, 0:TG], [b_ev[s]], ())
        for ki, col0, ncols in [(0, C_FK, 128), (1, C_FK + 128, 128), (2, C_DK, 128), (3, C_DK + 128, 128),
                                (4, C_SK, 128), (5, C_SK + 128, 128), (6, C_IK, 64)]:
            bk = fm_chunk(col0, ncols, tg)
            s = ev_slot()
            p.copy("act" if ki % 2 == 0 else "dve", ev[0:ncols, s, 0:TG], g.banks[bk][0:ncols, 0:TG], [g.bb[bk]], [b_ev[s]])
            p.dma("sp", KT_loc[ki * 128:ki * 128 + ncols, tsl], ev[0:ncols, s, 0:TG], [b_ev[s]], ())
        for cc in range(2):
            bka = fm_chunk(C_CU + cc * 128, 128, tg)
            bkg = fm_chunk(C_CU + 256 + cc * 128, 128, tg)
            s = state["evf"]
            state["evf"] = 1 - s
            p.act(sg[:, s, 0:TG], g.banks[bkg][:, 0:TG], AF.Sigmoid, [g.bb[bkg]], [b_sg[s]])
            p.tt("dve", evf[:, s, 0:TG], g.banks[bka][:, 0:TG], sg[:, s, 0:TG], ALU.mult, [g.bb[bka], b_sg[s]], [b_evf[s]])
            p.dma("sp", HT_loc[cc * 128:(cc + 1) * 128, tsl], evf[:, s, 0:TG], [b_evf[s]], ())

    vt = p.sb("a_vt", [128, 2, 768], BF16, st)
    b_vt = _bufs(2)
    lf = p.sb("a_lf", [128, J, 4], F32, st)
    b_lf = Buf()
    iw = p.sb("a_iw", [128, J, 4], F32, st)
    b_iw = Buf()
    for j in range(J):
        s = j % 2
        lt = slice(j * 128, (j + 1) * 128)
        for gi, col0 in enumerate([C_FV, C_DV]):
            for dc in range(8):
                p.mm(g.banks[0][:, gi * 256:(gi + 1) * 256], xT[:, dc, lt], win[:, dc, col0:col0 + 256], dc == 0, dc == 7,
                     [b_win[dc], b_xT], [g.bb[0]])
        for dc in range(8):
            p.mm(g.banks[1][:, 0:256], xT[:, dc, lt], win[:, dc, C_SV:C_SV + 256], dc == 0, dc == 7, [b_win[dc], b_xT], [g.bb[1]])
        for dc in range(8):
            p.mm(g.banks[1][:, 256:260], xT[:, dc, lt], win[:, dc, C_FF:C_FF + 4], dc == 0, dc == 7, [b_win[dc], b_xT], [g.bb[1]])
        for dc in range(8):
            p.mm(g.banks[1][:, 260:264], xT[:, dc, lt], win[:, dc, C_IW:C_IW + 4], dc == 0, dc == 7, [b_win[dc], b_xT], [g.bb[1]])
        p.copy("act", vt[:, s, 0:512], g.banks[0][:, 0:512], [g.bb[0]], [b_vt[s]])
        p.copy("dve", vt[:, s, 512:768], g.banks[1][:, 0:256], [g.bb[1]], [b_vt[s]])
        p.tt("dve", lf[:, j, :], g.banks[1][:, 256:260], fb[:], ALU.add, [g.bb[1], b_fb], [b_lf])
        p.copy("dve", iw[:, j, :], g.banks[1][:, 260:264], [g.bb[1]], [b_iw])
        p.dma("sp", V_loc[lt, :], vt[:, s, :], [b_vt[s]], ())
    lfv = lf[:].rearrange("p j h -> p (j h)")
    p.act(lfv, lfv, AF.Exp, [b_lf], [b_lf], scale=-1.0)
    p.act(lfv, lfv, AF.Ln, [b_lf], [b_lf], bias=1.0)
    p.ts("dve", lfv, lfv, -1.0, None, ALU.mult, None, [b_lf], [b_lf])
    p.tr(g.banks[2][0:J * 4, 0:128], lfv, g.identf[:], [b_lf, g.b_ident], [g.bb[2]])
    lft = p.sb("a_lft", [J * 4, 128], F32, st)
    b_lft = Buf()
    p.copy("dve", lft[:], g.banks[2][0:J * 4, 0:128], [g.bb[2]], [b_lft])
    p.dma("sp", LF_loc, lft[:], [b_lft], ())
    p.dma("sp", IW_loc, iw[:].rearrange("p j h -> p (j h)"), [b_iw], ())
    p.barrier()
    st.close()


def ln_rows(p, g, z, b_z, out, b_out, gt, bt, b_gb, tmp, b_tmp, st4, b_st4):
    p.memset("dve", st4[:, 0:2], 0.0, [b_st4])
    p.ts("dve", tmp, z, 1.0, 0.0, ALU.mult, ALU.add, [b_z], [b_tmp, b_st4], accum=st4[:, 0:1])
    p.tt("pool", tmp, z, z, ALU.mult, [b_z], [b_tmp])
    p.ts("dve", tmp, tmp, 1.0, 0.0, ALU.mult, ALU.add, [b_tmp], [b_tmp, b_st4], accum=st4[:, 1:2])
    p.ts("dve", st4[:, 0:1], st4[:, 0:1], 1.0 / D, None, ALU.mult, None, [b_st4], [b_st4])
    p.tt("dve", st4[:, 2:3], st4[:, 0:1], st4[:, 0:1], ALU.mult, [b_st4], [b_st4])
    p.stt("dve", st4[:, 1:2], st4[:, 1:2], 1.0 / D, st4[:, 2:3], ALU.mult, ALU.subtract, [b_st4], [b_st4])
    p.act(st4[:, 1:2], st4[:, 1:2], AF.Ln, [b_st4], [b_st4], bias=EPS)
    p.act(st4[:, 1:2], st4[:, 1:2], AF.Exp, [b_st4], [b_st4], scale=-0.5)
    p.stt("dve", st4[:, 2:3], st4[:, 0:1], -1.0, st4[:, 1:2], ALU.mult, ALU.mult, [b_st4], [b_st4])
    p.act(tmp, z, AF.Identity, [b_z, b_st4], [b_tmp], bias=st4[:, 2:3], scale=st4[:, 1:2])
    p.tt("dve", tmp, tmp, gt, ALU.mult, [b_tmp, b_gb], [b_tmp])
    p.tt("pool", out, tmp, bt, ALU.add, [b_tmp, b_gb], [b_out])


NOMASK = False


def emit_B(p, g, l, X, KT_all, V_all, HT_all, LF_all, QT_loc, IW_loc, HT_loc, NEGT, W, consts, x1, b_x1, lamc=None, dbg=None):
    J, T = g.J, g.T
    NB = 4 * J
    L = NB * 128
    lambda_init = 0.8 - 0.6 * math.exp(-0.3 * l)
    stB = ExitStack()
    yT = p.sb("b_yT", [128, 8, T], BF16, stB)
    b_yT = Buf()
    sel = p.sb("b_sel", [128, 4], F32, stB)
    b_c = Buf()
    p.dma("sp", sel[:], consts["sel"], (), [b_c])
    maskT = p.sb("b_maskT", [128, 4, 128], F32, stB)
    p.dma("sp", maskT[:], consts["maskT"].rearrange("c k q -> k c q"), (), [b_c])
    b31 = p.sb("b_b31", [128, 8], F32, stB)
    p.dma("sp", b31[:], W["rel31"].partition_broadcast(128), (), [b_c])
    ones = p.sb("b_ones", [128, 128], F32, stB)
    p.memset("pool", ones[:], 1.0, [b_c])

    st = ExitStack()
    hc = p.sb("c_hc", [128, 2, J, 158], F32, st)
    b_hc = Buf()
    cand = p.sb("c_cand", [128, 4, 2, J, 30], F32, st)
    b_cand = Buf()
    cw = p.sb("c_cw", [128, 2, 31], F32, st)
    cpar = p.sb("c_par", [128, 2, 3], F32, st)
    b_cw = Buf()
    cwr = p.sb("c_cwr", [31, 256], F32, st)
    p.dma("sp", cwr[:], W["conv_w"], (), [b_cw])
    for cc in range(2):
        p.tr(g.banks[0][:, cc * 32:cc * 32 + 31], cwr[:, cc * 128:(cc + 1) * 128], g.identf[0:31, 0:31], [b_cw, g.b_ident], [g.bb[0]])
    p.copy("dve", cw[:], g.banks[0][:, 0:64].rearrange("c (cc w) -> c cc w", cc=2)[:, :, 0:31], [g.bb[0]], [b_cw])
    for i, nm in enumerate(["conv_b", "conv_ln_g", "conv_ln_b"]):
        for cc in range(2):
            p.dma("sp", cpar[:, cc, i:i + 1], W[nm].rearrange("(cc c o) -> cc c o", c=128, o=1)[cc], (), [b_cw])
    p.memset("pool", cand[:], 0.0, [b_cand])
    for cc in range(2):
        p.dma("sp", hc[:, cc, :, 30:158], HT_loc[cc * 128:(cc + 1) * 128, :].rearrange("c (j t) -> c j t", t=128), (), [b_hc])
        for c in range(4):
            if c >= 1:
                for ap_, off, n in HT_all.rows(c - 1, cc * 128, 128):
                    p.dma("sp", cand[off:off + n, c, cc, :, :], ap_.rearrange("c (j t) -> c j t", t=128)[:, :, 98:128], (), [b_cand])
            elif J > 1:
                for ap_, off, n in HT_all.rows(3, cc * 128, 128):
                    p.dma("sp", cand[off:off + n, c, cc, 1:J, :], ap_.rearrange("c (j t) -> c j t", t=128)[:, 0:J - 1, 98:128], (), [b_cand])
    for cc in range(2):
        e = "dve"
        p.ts(e, hc[:, cc, :, 0:30], cand[:, 0, cc, :, :], sel[:, 0:1], None, ALU.mult, None, [b_cand, b_c], [b_hc])
        for c in range(1, 4):
            p.stt(e, hc[:, cc, :, 0:30], cand[:, c, cc, :, :], sel[:, c:c + 1], hc[:, cc, :, 0:30], ALU.mult, ALU.add,
                  [b_cand, b_c, b_hc], [b_hc])
    cacc = p.sb("c_acc", [128, 2, J, 128], F32, st)
    b_cacc = _bufs(2)
    for cc in range(2):
        e = "dve"
        p.ts(e, cacc[:, cc], hc[:, cc, :, 0:128], cw[:, cc, 0:1], cpar[:, cc, 0:1], ALU.mult, ALU.add, [b_hc, b_cw], [b_cacc[cc]])
        for w in range(1, 31):
            p.stt(e, cacc[:, cc], hc[:, cc, :, w:w + 128], cw[:, cc, w:w + 1], cacc[:, cc], ALU.mult, ALU.add,
                  [b_hc, b_cw, b_cacc[cc]], [b_cacc[cc]])
    TG = min(512, T)
    csq = p.sb("c_sq", [128, 2, TG], F32, st)
    b_csq = Buf()
    cmean = p.sb("c_mean", [128, TG], F32, st)
    crstd = p.sb("c_rstd", [128, TG], F32, st)
    b_cst = Buf()
    cd = p.sb("c_d", [128, 2, TG], F32, st)
    b_cd = _bufs(2)
    p.ts("dve", ones[:], ones[:], 1.0 / 256.0, None, ALU.mult, None, [b_c], [b_c])
    for tg in range(T // TG):
        flat = [cacc[:, cc].rearrange("c j t -> c (j t)")[:, tg * TG:(tg + 1) * TG] for cc in range(2)]
        for cc in range(2):
            p.tt("pool", csq[:, cc, :], flat[cc], flat[cc], ALU.mult, [b_cacc[cc]], [b_csq])
        for cc in range(2):
            p.mm(g.banks[0][:, 0:TG], ones[:], flat[cc], cc == 0, cc == 1, [b_c, b_cacc[cc]], [g.bb[0]])
        for cc in range(2):
            p.mm(g.banks[1][:, 0:TG], ones[:], csq[:, cc, :], cc == 0, cc == 1, [b_c, b_csq], [g.bb[1]])
        p.copy("act", cmean[:], g.banks[0][:, 0:TG], [g.bb[0]], [b_cst])
        p.tt("dve", crstd[:], cmean[:], cmean[:], ALU.mult, [b_cst], [b_cst])
        p.tt("dve", crstd[:], g.banks[1][:, 0:TG], crstd[:], ALU.subtract, [g.bb[1], b_cst], [b_cst])
        p.act(crstd[:], crstd[:], AF.Ln, [b_cst], [b_cst], bias=EPS)
        p.act(crstd[:], crstd[:], AF.Exp, [b_cst], [b_cst], scale=-0.5)
        for cc in range(2):
            p.tt("dve", cd[:, cc, :], flat[cc], cmean[:], ALU.subtract, [b_cacc[cc], b_cst], [b_cd[cc]])
            p.tt("dve", cd[:, cc, :], cd[:, cc, :], crstd[:], ALU.mult, [b_cd[cc], b_cst], [b_cd[cc]])
            p.ts("dve", cd[:, cc, :], cd[:, cc, :], cpar[:, cc, 1:2], cpar[:, cc, 2:3], ALU.mult, ALU.add, [b_cd[cc], b_cw], [b_cd[cc]])
            p.act(csq[:, cc, :], cd[:, cc, :], AF.Sigmoid, [b_cd[cc]], [b_csq])
            p.tt("dve", yT[:, 6 + cc, tg * TG:(tg + 1) * TG], cd[:, cc, :], csq[:, cc, :], ALU.mult, [b_cd[cc], b_csq], [b_yT])
    p.ts("dve", ones[:], ones[:], 256.0, None, ALU.mult, None, [b_c], [b_c])
    p.barrier()
    st.close()

    R = 4 * J * 4
    cn = p.sb("f_cn", [128, J, 4, 4], F32, stB)
    offs = p.sb("f_offs", [128, J, 4, 4], F32, stB)
    offl = p.sb("f_offl", [128, J, 4], F32, stB)
    vs = p.sb("f_vs", [128, J, 4, 4], F32, stB)
    bjr = p.sb("f_bjr", [128, J, J, 4], F32, stB)
    lam = p.sb("d_lam", [128, 4], F32, stB)
    dng = p.sb("d_g", [128, 64], F32, stB)
    b_f = Buf()
    st = ExitStack()
    triu = p.sb("f_triu", [128, 128], F32, st)
    p.dma("sp", triu[:], consts["triu"], (), [b_f])
    nt = (R + 127) // 128
    lfr = p.sb("f_lfr", [128, nt, 128], F32, st)
    lfn = p.sb("f_lfn", [128, R], F32, st)
    sc_a = p.sb("f_sca", [128, NB, 4], F32, st)
    sc_b = p.sb("f_scb", [128, NB, 4], F32, st)
    for t in range(nt):
        rr = min(128, R - t * 128)
        p.dma("sp", lfr[0:rr, t, :], LF_all[t * 128:t * 128 + rr, :], (), [b_f])
        p.tr(g.banks[0][:, t * 128:t * 128 + rr], lfr[0:rr, t, :], g.identf[0:rr, 0:rr], [b_f, g.b_ident], [g.bb[0]])
    p.copy("dve", lfn[:], g.banks[0][:, 0:R], [g.bb[0]], [b_f])
    p.mm(g.banks[1][:, 0:R], triu[:], lfn[:], True, True, [b_f], [g.bb[1]])
    p.memset("pool", ones[:], 1.0, [b_c])
    p.mm(g.banks[2][:, 0:R], ones[:], lfn[:], True, True, [b_f, b_c], [g.bb[2]])
    nat = lambda ap: ap.rearrange("p (c j h) -> p j c h", c=4, j=J)
    p.copy("dve", sc_a[:].rearrange("p (j c) h -> p j c h", c=4), nat(g.banks[2][:, 0:R]), [g.bb[2]], [b_f])
    p.copy("dve", offs[:], nat(g.banks[2][:, 0:R]), [g.bb[2]], [b_f])
    a, b = sc_a, sc_b
    s = 1
    while s < NB:
        p.copy("dve", b[:, 0:s, :], a[:, 0:s, :], [b_f], [b_f])
        p.tt("dve", b[:, s:NB, :], a[:, s:NB, :], a[:, 0:NB - s, :], ALU.add, [b_f], [b_f])
        a, b = b, a
        s *= 2
    offs_f = offs[:].rearrange("p j c h -> p (j c) h")
    p.tt("dve", offs_f, a[:], offs_f, ALU.subtract, [b_f], [b_f])
    p.tt("dve", cn[:], nat(g.banks[1][:, 0:R]), offs[:], ALU.add, [g.bb[1], b_f], [b_f])
    p.ts("dve", offl[:], offs[:, :, 0, :], sel[:, 0:1], None, ALU.mult, None, [b_f, b_c], [b_f])
    for c in range(1, 4):
        p.stt("dve", offl[:], offs[:, :, c, :], sel[:, c:c + 1], offl[:], ALU.mult, ALU.add, [b_f, b_c], [b_f])
    for c in range(4):
        p.tt("dve", vs[:, :, c, :], offs[:, :, 0, :], cn[:, :, c, :], ALU.subtract, [b_f], [b_f])
    vsf = vs[:].rearrange("p j c h -> p (j c h)")
    p.act(vsf, vsf, AF.Exp, [b_f], [b_f])
    for j in range(J):
        for h in range(4):
            p.ts("dve", bjr[:, j, :, h], offs[:, :, 0, h], -1.0, offl[:, j, h:h + 1], ALU.mult, ALU.add, [b_f], [b_f])
    dlf = p.sb("d_dl", [128, 128], F32, st)
    p.dma("sp", dlf[:], W["diff_lambda"].rearrange("a b -> (a b)").partition_broadcast(128), (), [b_f])
    dl = dlf[:].rearrange("p (a b) -> p a b", a=4)
    p.memset("dve", lam[:], 0.0, [b_f])
    p.dma("sp", dng[:], W["diff_norm_g"].partition_broadcast(128), (), [b_f])
    p.tt("dve", dl[:, 0, :], dl[:, 0, :], dl[:, 1, :], ALU.mult, [b_f], [b_f])
    p.tt("dve", dl[:, 2, :], dl[:, 2, :], dl[:, 3, :], ALU.mult, [b_f], [b_f])
    p.ts("dve", dl[:, 1, :], dl[:, 0, :], 1.0, 0.0, ALU.mult, ALU.add, [b_f], [b_f], accum=lam[:, 0:1])
    p.ts("dve", dl[:, 3, :], dl[:, 2, :], 1.0, 0.0, ALU.mult, ALU.add, [b_f], [b_f], accum=lam[:, 1:2])
    p.act(lam[:, 0:2], lam[:, 0:2], AF.Exp, [b_f], [b_f])
    p.tt("dve", lam[:, 2:3], lam[:, 1:2], lam[:, 0:1], ALU.subtract, [b_f], [b_f])
    if lamc is None:
        p.ts("dve", lam[:, 2:3], lam[:, 2:3], -lambda_init, None, ALU.add, None, [b_f], [b_f])
        p.ts("dve", dng[:], dng[:], 1.0 - lambda_init, None, ALU.mult, None, [b_f], [b_f])
    else:
        lc = p.sb("d_lc", [128, 2], F32, st)
        p.dma("sp", lc[:], lamc, (), [b_f])
        p.ts("dve", lam[:, 2:3], lam[:, 2:3], lc[:, 0:1], None, ALU.add, None, [b_f], [b_f])
        p.ts("dve", dng[:], dng[:], lc[:, 1:2], None, ALU.mult, None, [b_f], [b_f])
    p.barrier()
    st.close()

    stA = ExitStack()
    ALIAS = (J == 16)
    x1flat = x1[:].rearrange("p a b -> p (a b)")
    if ALIAS:
        kt = [x1flat[:, i * 4096:(i + 1) * 4096].bitcast(BF16) for i in range(2)]
        vt = [x1flat[:, 8192:8192 + 4160].bitcast(BF16).rearrange("p (a b c) -> p a b c", a=NB, b=2, c=65),
              p.sb("a_v1", [128, NB, 2, 65], BF16, stA)]
    else:
        kt = [p.sb("a_kt%d" % i, [128, L], BF16, stA) for i in range(2)]
        vt = [p.sb("a_v%d" % i, [128, NB, 2, 65], BF16, stA) for i in range(2)]
    qt = [p.sb("a_q%d" % i, [128, 2, T], BF16, stA) for i in range(2)]
    bti = [p.sb("a_bt%d" % i, [128, 2, 8, 128], F32, stA) for i in range(2)]
    b_kv = _bufs(2)
    pT = p.sb("a_pT", [128, 4, 512], BF16, stA)
    b_pT = _bufs(4)
    ssb = p.sb("a_ssb", [128, 2, 512], F32, stA)
    b_ssb = _bufs(2)
    ysb = p.sb("a_ysb", [128, 2, 128], F32, stA)
    b_ysb = _bufs(2)
    yd = p.sb("a_yd", [128, 2, 66], F32, stA)
    b_yd = _bufs(2)
    rec = p.sb("a_rec", [128, 8], F32, stA)
    b_rec = Buf()
    for i in range(2):
        p.memset("pool", vt[i][:, :, :, 64:65], 1.0, [b_kv[i]])
    rot = {"s": 0, "pt": 0, "ss": 0, "o": 0}
    SB = [0, 1, 2]
    OB = [3, 4, 5, 6]
    TB = 7

    def load_pair(slot, kind, pr):
        kch = kind * 2 + pr
        for c in range(4):
            for ap_, off, n in KT_all.rows(c, kch * 128, 128):
                dst = kt[slot][off:off + n, :].rearrange("r (j c t) -> r j c t", c=4, t=128)[:, :, c, :]
                p.dma("sp", dst, ap_.rearrange("r (j t) -> r j t", t=128), (), [b_kv[slot]])
            for hh in range(2):
                c0 = kind * 256 + pr * 128 + hh * 64
                for j_ in range(J):
                    for ap_, off, n in V_all.rows(c, j_ * 128, 128):
                        p.dma("sp", vt[slot][off:off + n, 4 * j_ + c, hh, 0:64], ap_[:, c0:c0 + 64], (), [b_kv[slot]])
        if kind == 0:
            p.dma("sp", qt[slot][:, 0, :], QT_loc[pr], (), [b_kv[slot]])
            for hh in range(2):
                h = pr * 2 + hh
                vsh = vs[:].rearrange("p j c h -> p (j c) h")[:, :, h:h + 1]
                p.tt("pool", vt[slot][:, :, hh, 0:64], vt[slot][:, :, hh, 0:64], vsh.to_broadcast([128, NB, 64]), ALU.mult,
                     [b_f, b_kv[slot]], [b_kv[slot]])
                p.copy("pool", vt[slot][:, :, hh, 64:65], vsh, [b_f, b_kv[slot]], [b_kv[slot]])
        elif kind == 1:
            p.dma("sp", qt[slot][:, 0, :], QT_loc[2 + pr], (), [b_kv[slot]])
            p.dma("sp", qt[slot][:, 1, :], QT_loc[4 + pr], (), [b_kv[slot]])
        else:
            p.dma("sp", qt[slot][:, 0, :], QT_loc[6 + pr], (), [b_kv[slot]])
        if kind > 0:
            hb = (kind - 1) * 4 + pr * 2
            p.dma("sp", bti[slot][:], W["biasT"][hb:hb + 2].rearrange("h t k q -> k h t q"), (), [b_kv[slot]])
            for hh in range(2):
                p.tt("pool", bti[slot][:, hh, 0:4, :], bti[slot][:, hh, 0:4, :], maskT[:], ALU.add, [b_kv[slot], b_c], [b_kv[slot]])
            p.memset("pool", vt[slot][:, :, :, 64:65], 1.0, [b_kv[slot]])

    def run_maps(slot, kind, pr, j, ngT=None, b_ng=None):
        nrow = j + 1
        if kind == 1:
            maps = [(hh, m) for hh in range(2) for m in range(2)]
        else:
            maps = [(hh, 0) for hh in range(2)]
        scale = (32 ** -0.5) if kind == 1 else 0.125
        obank = {}
        items = [(mi, r) for mi in range(len(maps)) for r in range(nrow)]
        pend = {}

        def qk(it):
            mi, r = it
            hh, m = maps[mi]
            sbk = SB[rot["s"] % 3]
            rot["s"] += 1
            rows = slice(hh * 64, hh * 64 + 64)
            for c in range(4):
                blk = 4 * r + c
                p.mm(g.banks[sbk][:, c * 128:(c + 1) * 128], kt[slot][rows, blk * 128:(blk + 1) * 128],
                     qt[slot][rows, m, j * 128:(j + 1) * 128], True, True, [b_kv[slot]], [g.bb[sbk]])
            ps_ = rot["pt"] % 4
            rot["pt"] += 1
            h = pr * 2 + hh
            if kind == 0:
                bias = bjr[:, j, r, h:h + 1]
                near = (r == j)
                bread = [b_f]
            else:
                gh = (kind - 1) * 4 + h
                bias = b31[:, gh:gh + 1]
                near = (r >= j - 1)
                bread = [b_c]
            if near:
                ss = rot["ss"] % 2
                rot["ss"] += 1
                for c in range(4):
                    if kind == 0:
                        tile_ = maskT[:, c, :]
                        rd = [b_c]
                    else:
                        tile_ = bti[slot][:, hh, (0 if r == j else 4) + c, :]
                        rd = [b_kv[slot]]
                    p.stt("dve", ssb[:, ss, c * 128:(c + 1) * 128], g.banks[sbk][:, c * 128:(c + 1) * 128], scale, tile_,
                          ALU.mult, ALU.add, [g.bb[sbk]] + rd, [b_ssb[ss]])
                if kind == 0:
                    p.act(pT[:, ps_, :], ssb[:, ss, :], AF.Exp, [b_ssb[ss]] + bread, [b_pT[ps_]], bias=bias)
                else:
                    p.act(pT[:, ps_, :], ssb[:, ss, :], AF.Exp, [b_ssb[ss]], [b_pT[ps_]])
            else:
                p.act(pT[:, ps_, :], g.banks[sbk][:, :], AF.Exp, [g.bb[sbk]] + bread, [b_pT[ps_]], bias=bias, scale=scale)
            if kind == 2:
                p.tt("pool", pT[:, ps_, :], pT[:, ps_, :], ngT[:, 4 * r:4 * r + 4, :].rearrange("k c q -> k (c q)"), ALU.mult,
                     [b_pT[ps_], b_ng], [b_pT[ps_]])
            pend[it] = ps_

        def pv(it):
            mi, r = it
            hh, m = maps[mi]
            ps_ = pend.pop(it)
            if r == 0:
                obank[mi] = OB[rot["o"] % 4]
                rot["o"] += 1
            ob = obank[mi]
            for c in range(4):
                blk = 4 * r + c
                p.mm(g.banks[ob][:, 0:65], pT[:, ps_, c * 128:(c + 1) * 128], vt[slot][:, blk, hh, :],
                     r == 0 and c == 0, r == nrow - 1 and c == 3, [b_pT[ps_], b_kv[slot]], [g.bb[ob]])
            if r == nrow - 1:
                finalize(mi)

        def finalize(mi):
            hh, m = maps[mi]
            ob = obank[mi]
            ys = (j * 2 + pr) % 2
            if kind != 1:
                col = mi
                p.op("dve", lambda e: e.reciprocal(out=rec[:, col:col + 1], in_=g.banks[ob][:, 64:65]), [g.bb[ob]], [b_rec])
                p.ts("dve", ysb[:, ys, hh * 64:(hh + 1) * 64], g.banks[ob][:, 0:64], rec[:, col:col + 1], None, ALU.mult, None,
                     [g.bb[ob], b_rec], [b_ysb[ys]])
            else:
                col = mi
                p.op("dve", lambda e: e.reciprocal(out=rec[:, col:col + 1], in_=g.banks[ob][:, 64:65]), [g.bb[ob]], [b_rec])
                if m == 0:
                    p.ts("dve", yd[:, hh, 0:64], g.banks[ob][:, 0:64], rec[:, col:col + 1], None, ALU.mult, None,
                         [g.bb[ob], b_rec], [b_yd[hh]])
                else:
                    p.ts("dve", ssb[:, 0, 0:64], g.banks[ob][:, 0:64], rec[:, col:col + 1], lam[:, 2:3], ALU.mult, ALU.mult,
                         [g.bb[ob], b_rec, b_f], [b_ssb[0]])
                    p.tt("dve", yd[:, hh, 0:64], yd[:, hh, 0:64], ssb[:, 0, 0:64], ALU.add, [b_ssb[0], b_yd[hh]], [b_yd[hh]])
                    p.tt("dve", ssb[:, 0, 64:128], yd[:, hh, 0:64], yd[:, hh, 0:64], ALU.mult, [b_yd[hh]], [b_ssb[0]])
                    p.memset("dve", yd[:, hh, 64:65], 0.0, [b_yd[hh]])
                    p.ts("dve", ssb[:, 0, 64:128], ssb[:, 0, 64:128], 1.0 / 64, 0.0, ALU.mult, ALU.add, [b_ssb[0]], [b_ssb[0], b_yd[hh]],
                         accum=yd[:, hh, 64:65])
                    p.act(yd[:, hh, 64:65], yd[:, hh, 64:65], AF.Ln, [b_yd[hh]], [b_yd[hh]], bias=EPS)
                    p.act(yd[:, hh, 64:65], yd[:, hh, 64:65], AF.Exp, [b_yd[hh]], [b_yd[hh]], scale=-0.5)
                    p.stt("dve", ysb[:, ys, hh * 64:(hh + 1) * 64], yd[:, hh, 0:64], yd[:, hh, 64:65], dng[:], ALU.mult, ALU.mult,
                          [b_yd[hh], b_f], [b_ysb[ys]])
            last = (mi == len(maps) - 1)
            if dbg is not None and kind == 1 and pr == 0 and j == 0 and mi == 1:
                def dd(name, ap, shape, bufs):
                    t_ = p.nc.dram_tensor("dbg_" + name, shape, F32, kind="ExternalOutput").ap()
                    p.dma("sp", t_, ap, bufs, ())
                dd("yd", yd[:], [128, 2, 66], [b_yd[0], b_yd[1]])
                dd("rec", rec[:], [128, 8], [b_rec])
                dd("bti", bti[slot][:, 0, 0, :], [128, 128], [b_kv[slot]])
                dd("lam", lam[:], [128, 4], [b_f])
                dd("dng", dng[:], [128, 64], [b_f])
                dd("ssb", ssb[:], [128, 2, 512], [b_ssb[0], b_ssb[1]])
                dd("ysb", ysb[:], [128, 2, 128], [b_ysb[0], b_ysb[1]])
            if last:
                p.tr(g.banks[TB][:, 0:128], ysb[:, ys, :], g.identf[:], [b_ysb[ys], g.b_ident], [g.bb[TB]])
                p.copy("act", yT[:, kind * 2 + pr, j * 128:(j + 1) * 128], g.banks[TB][:, 0:128], [g.bb[TB]], [b_yT])

        n = len(items)
        for i in range(n + 1):
            if i < n:
                qk(items[i])
            if i > 0:
                pv(items[i - 1])

    seq = [(0, 0), (0, 1), (1, 0), (1, 1)]
    load_pair(0, *seq[0])
    for i, (kind, pr) in enumerate(seq):
        if i + 1 < len(seq):
            load_pair((i + 1) % 2, *seq[i + 1])
        for j in range(J):
            run_maps(i % 2, kind, pr, j)
    p.barrier()

    st = ExitStack()
    if ALIAS:
        sc = x1flat[:, 0:8192]
        ik2 = x1flat[:, 8192:12288].bitcast(BF16)
        iq = x1flat[:, 12288:14336].bitcast(BF16).rearrange("p (a b) -> p a b", a=2)
    else:
        sc = p.sb("i_sc", [128, L], F32, st)
        ik2 = p.sb("i_ik2", [128, L], BF16, st)
        iq = p.sb("i_iq", [128, 2, T], BF16, st)
    b_sc = Buf()
    iwt = p.sb("i_iw", [128, J, 4], F32, st)
    mq = p.sb("i_mq", [128, 4, 128], F32, st)
    mqa = p.sb("i_mqa", [128, 4, 128], F32, st)
    b_i = Buf()
    if ALIAS:
        rl = x1flat[:, 14336:16384].rearrange("p (a b) -> p a b", a=4)
    else:
        rl = p.sb("i_rl", [128, 4, 512], F32, st)
    b_rl = _bufs(4)
    m8 = p.sb("i_m8", [128, 8], F32, st)
    b_m8 = Buf()
    nev = p.sb("i_nev", [128, 2, 512], BF16, st)
    b_nev = _bufs(2)
    for c in range(4):
        for ap_, off, n in KT_all.rows(c, 768, 64):
            for half in range(2):
                dst = ik2[half * 64 + off:half * 64 + off + n, :].rearrange("r (j c t) -> r j c t", c=4, t=128)[:, :, c, :]
                p.dma("sp", dst, ap_.rearrange("r (j t) -> r j t", t=128), (), [b_i])
    p.dma("sp", iq[:, 0, :], QT_loc[8], (), [b_i])
    p.dma("sp", iq[:, 1, :], QT_loc[9], (), [b_i])
    p.dma("sp", iwt[:].rearrange("p j h -> p (j h)"), IW_loc, (), [b_i])
    p.dma("sp", mq[:], consts["maskQ"].rearrange("c q k -> q c k"), (), [b_i])
    p.ts("dve", mqa[:], mq[:], 1.0e30, None, ALU.mult, None, [b_i], [b_i])
    for j in range(J):
        Nk = 512 * (j + 1)
        for r in range(j + 1):
            ks = slice(r * 512, (r + 1) * 512)
            for hi in range(4):
                rows = slice((hi % 2) * 64, (hi % 2) * 64 + 64)
                bk = hi
                p.mm(g.banks[bk][:, :], iq[rows, hi // 2, j * 128:(j + 1) * 128], ik2[rows, ks], True, True, [b_i], [g.bb[bk]])
                p.act(rl[:, hi, :], g.banks[bk][:, :], AF.Relu, [g.bb[bk]], [b_rl[hi]])
                if hi == 0:
                    p.ts("dve", sc[:, ks], rl[:, 0, :], iwt[:, j, 0:1], None, ALU.mult, None, [b_rl[0], b_i], [b_sc])
                else:
                    p.stt("dve", sc[:, ks], rl[:, hi, :], iwt[:, j, hi:hi + 1], sc[:, ks], ALU.mult, ALU.add, [b_rl[hi], b_i, b_sc], [b_sc])
            if r == j:
                p.tt("dve", sc[:, ks], sc[:, ks], mqa[:].rearrange("q c k -> q (c k)"), ALU.add, [b_sc, b_i], [b_sc])
        for it in range(32):
            p.op("dve", lambda e: e.max(out=m8[:], in_=sc[:, 0:Nk]), [b_sc], [b_m8])
            p.op("dve", lambda e: e.match_replace(out=sc[:, 0:Nk], in_to_replace=m8[:], in_values=sc[:, 0:Nk], imm_value=-3.0e38),
                 [b_sc, b_m8], [b_sc])
        p.ts("dve", sc[:, 0:Nk], sc[:, 0:Nk], -1.0e35, 1.0, ALU.is_le, ALU.subtract, [b_sc], [b_sc])
        ks = slice(j * 512, (j + 1) * 512)
        p.tt("dve", sc[:, ks], sc[:, ks], mq[:].rearrange("q c k -> q (c k)"), ALU.min, [b_sc, b_i], [b_sc])
        if dbg is not None and j == 0:
            t_ = p.nc.dram_tensor("dbg_sc", [128, 512], F32, kind="ExternalOutput").ap()
            p.dma("sp", t_, sc[:, 0:512], [b_sc], ())
        for r in range(j + 1):
            bk = 4 + (r % 2)
            for c in range(4):
                blk = 4 * r + c
                p.tr(g.banks[bk][:, c * 128:(c + 1) * 128], sc[:, blk * 128:(blk + 1) * 128], g.identf[:], [b_sc, g.b_ident], [g.bb[bk]])
            s = r % 2
            p.act(nev[:, s, :], g.banks[bk][:, :], AF.Identity, [g.bb[bk]], [b_nev[s]], bias=1.0)
            p.dma("sp", NEGT[j, :, 4 * r:4 * r + 4, :], nev[:, s, :].rearrange("k (c q) -> k c q", c=4), [b_nev[s]], ())
    p.barrier()
    st.close()

    st = ExitStack()
    ng = [p.sb("s_ng%d" % i, [128, NB, 128], BF16, st) for i in range(2)]
    b_ngs = _bufs(2)
    load_pair(0, 2, 0)
    load_pair(1, 2, 1)
    p.dma("sp", ng[0][:, 0:4, :], NEGT[0, :, 0:4, :], (), [b_ngs[0]])
    for j in range(J):
        if j + 1 < J:
            p.dma("sp", ng[(j + 1) % 2][:, 0:4 * (j + 2), :], NEGT[j + 1, :, 0:4 * (j + 2), :], (), [b_ngs[(j + 1) % 2]])
        for pr in range(2):
            run_maps(pr, 2, pr, j, ngT=ng[j % 2], b_ng=b_ngs[j % 2])
    p.barrier()
    st.close()
    stA.close()

    if dbg is not None:
        p.dma("sp", dbg, yT[:], [b_yT], ())
    st = ExitStack()
    wo = p.sb("o_wo", [128, 8, D], BF16, st)
    b_wo = Buf()
    p.dma("pool", wo[:], W["w_out"].rearrange("(fc f) m -> f fc m", f=128), (), [b_wo])
    gt = p.sb("o_g", [128, D], F32, st)
    bt = p.sb("o_b", [128, D], F32, st)
    b_gb = Buf()
    p.dma("sp", gt[:], W["ln1_g"].partition_broadcast(128), (), [b_gb])
    p.dma("sp", bt[:], W["ln1_b"].partition_broadcast(128), (), [b_gb])
    xin = [p.sb("o_x%d" % i, [128, D], F32, st) for i in range(2)]
    b_xin = _bufs(2)
    tmp = p.sb("o_tmp", [128, D], F32, st)
    b_tmp = Buf()
    st4 = p.sb("o_st4", [128, 4], F32, st)
    b_st4 = Buf()
    Xr = X.rearrange("(j q) d -> q j d", q=128)
    for j in range(J):
        s = j % 2
        p.dma("sp", xin[s][:], Xr[:, j, :], (), [b_xin[s]])
        for mh in range(2):
            bk = mh
            for fc in range(8):
                p.mm(g.banks[bk][:, :], yT[:, fc, j * 128:(j + 1) * 128], wo[:, fc, mh * 512:(mh + 1) * 512], fc == 0, fc == 7,
                     [b_yT, b_wo], [g.bb[bk]])
            p.stt("dve", xin[s][:, mh * 512:(mh + 1) * 512], xin[s][:, mh * 512:(mh + 1) * 512], ALPHA, g.banks[bk][:, :],
                  ALU.mult, ALU.add, [b_xin[s], g.bb[bk]], [b_xin[s]])
        ln_rows(p, g, xin[s][:], b_xin[s], x1[:, j, :], b_x1, gt[:], bt[:], b_gb, tmp[:], b_tmp, st4, b_st4)
    p.barrier()
    st.close()
    stB.close()

def emit_C(p, g, W, x1, b_x1, Xout):
    J, T = g.J, g.T
    st = ExitStack()
    xT = p.sb("m_xT", [128, 8, T], BF16, st)
    b_xT = Buf()
    xTf = p.sb("m_xTf", [128, 8, 128], F32, st)
    b_xTf = Buf()
    rw = p.sb("m_rw", [128, 8, NE], F32, st)
    b_w = Buf()
    p.dma("sp", rw[:], W["router_w"].rearrange("(dc q) e -> q dc e", q=128), (), [b_w])
    rb = p.sb("m_rb", [128, NE], F32, st)
    p.dma("sp", rb[:], W["router_b"].partition_broadcast(128), (), [b_w])
    bgu = p.sb("m_bgu", [128, 16, NE], F32, st)
    bd = p.sb("m_bd", [NE, D], F32, st)
    gates = p.sb("m_gates", [128, J, NE], F32, st)
    gT = p.sb("m_gT", [NE, 128], F32, st)
    lg = p.sb("m_lg", [128, NE], F32, st)
    m8 = p.sb("m_m8", [128, 8], F32, st)
    sm = p.sb("m_sm", [128, 4], F32, st)
    st0 = ExitStack()
    bgr = p.sb("m_bgr", [NE, 2 * D], F32, st0)
    p.dma("sp", bgr[:], W["b_gu"], (), [b_w])
    bgv = bgr[:].rearrange("e (jc q t) -> e jc t q", q=128, t=2)
    for jc in range(8):
        for t in range(2):
            i = jc * 2 + t
            p.tr(g.banks[7][:, i * NE:(i + 1) * NE], bgv[:, jc, t, :], g.identf[0:NE, 0:NE], [b_w, g.b_ident], [g.bb[7]])
    p.copy("dve", bgu[:], g.banks[7][:, :].rearrange("q (i e) -> q i e", e=NE), [g.bb[7]], [b_w])
    p.barrier()
    st0.close()
    p.dma("sp", bd[:], W["b_down"], (), [b_w])
    b_gates = Buf()
    b_gT = Buf()
    b_r = Buf()
    for j in range(J):
        transpose_block(p, g, x1[:, j, :], b_x1, xT, b_xT, j, dst_f32=xTf, dst_f32_b=b_xTf, bank0=0)
        for dc in range(8):
            p.mm(g.banks[2][:, 0:NE], xTf[:, dc, :], rw[:, dc, :], dc == 0, dc == 7, [b_xTf, b_w], [g.bb[2]])
        p.tt("dve", lg[:], g.banks[2][:, 0:NE], rb[:], ALU.add, [g.bb[2], b_w], [b_r])
        p.op("dve", lambda e: e.max(out=m8[:], in_=lg[:]), [b_r], [b_r])
        p.ts("dve", sm[:, 0:1], m8[:, 0:1], -1.0, None, ALU.mult, None, [b_r], [b_r])
        p.act(gates[:, j, :], lg[:], AF.Exp, [b_r], [b_gates], bias=sm[:, 0:1])
        p.ts("dve", lg[:], lg[:], m8[:, 3:4], None, ALU.is_ge, None, [b_r], [b_r])
        p.tt("dve", gates[:, j, :], gates[:, j, :], lg[:], ALU.mult, [b_r, b_gates], [b_gates])
        p.memset("dve", sm[:, 1:2], 0.0, [b_r])
        p.ts("dve", lg[:], gates[:, j, :], 1.0, 0.0, ALU.mult, ALU.add, [b_gates], [b_r], accum=sm[:, 1:2])
        p.op("dve", lambda e: e.reciprocal(out=sm[:, 2:3], in_=sm[:, 1:2]), [b_r], [b_r])
        p.ts("dve", gates[:, j, :], gates[:, j, :], sm[:, 2:3], None, ALU.mult, None, [b_r, b_gates], [b_gates])
        p.tr(g.banks[3][0:NE, 0:128], gates[:, j, :], g.identf[:], [b_gates, g.b_ident], [g.bb[3]])
        p.copy("dve", gT[:], g.banks[3][0:NE, 0:128], [g.bb[3]], [b_gT])
        for mh in range(2):
            p.mm(g.banks[4 + mh][:, :], gT[:], bd[:, mh * 512:(mh + 1) * 512], True, True, [b_gT, b_w], [g.bb[4 + mh]])
            p.stt("dve", x1[:, j, mh * 512:(mh + 1) * 512], x1[:, j, mh * 512:(mh + 1) * 512], ALPHA, g.banks[4 + mh][:, :],
                  ALU.mult, ALU.add, [b_x1, g.bb[4 + mh]], [b_x1])
    NW = 4
    st2 = ExitStack()
    st_save, st = st, st2
    wgu = [p.sb("m_wgu%d" % i, [128, 8, 256], BF16, st) for i in range(NW)]
    b_wgu = _bufs(NW)
    wd = [p.sb("m_wd%d" % i, [128, 8, D], BF16, st) for i in range(2)]
    b_wd = _bufs(2)
    hT = p.sb("m_hT", [128, 8, T], BF16, st)
    b_hT = _bufs(8)
    TG = min(512, T)
    ntg = T // TG
    gc = p.sb("m_gc", [128, 2, TG], F32, st)
    sg = p.sb("m_sg", [128, 2, TG], F32, st)
    uc = p.sb("m_uc", [128, 2, TG], F32, st)
    b_gc, b_sg, b_uc = _bufs(2), _bufs(2), _bufs(2)
    b_acc = _bufs(J)
    wq = {"n": 0}

    def load_gu(e, jc):
        s = wq["n"] % NW
        wq["n"] += 1
        src = W["w_gu"][e].rearrange("(dc q) f -> q dc f", q=128)[:, :, jc * 256:(jc + 1) * 256]
        p.dma("pool", wgu[s][:], src, (), [b_wgu[s]])
        return s

    def load_d(e):
        s = e % 2
        p.dma("pool", wd[s][:], W["w_down"][e].rearrange("(jc q) m -> q jc m", q=128), (), [b_wd[s]])

    pre = []
    for jc in range(2):
        pre.append(load_gu(0, jc))
    load_d(0)
    k = 0
    for e in range(NE):
        for jc in range(8):
            nxt = e * 8 + jc + 2
            if nxt < NE * 8:
                pre.append(load_gu(nxt // 8, nxt % 8))
            if jc == 4 and e + 1 < NE:
                load_d(e + 1)
            s = pre.pop(0)
            wv = wgu[s][:].rearrange("q dc (j t) -> q dc t j", t=2)
            for tg in range(ntg):
                kk = k % 2
                k += 1
                tsl = slice(tg * TG, (tg + 1) * TG)
                for t, bk in ((0, kk), (1, 2 + kk)):
                    for dc in range(8):
                        p.mm(g.banks[bk][:, 0:TG], wv[:, dc, t, :], xT[:, dc, tsl], dc == 0, dc == 7, [b_wgu[s], b_xT], [g.bb[bk]])
                p.ts("dve", gc[:, kk, :], g.banks[kk][:, 0:TG], bgu[:, jc * 2, e:e + 1], 7.0, ALU.add, ALU.min, [g.bb[kk], b_w], [b_gc[kk]])
                p.act(sg[:, kk, :], gc[:, kk, :], AF.Sigmoid, [b_gc[kk]], [b_sg[kk]], scale=1.702)
                p.act(uc[:, kk, :], g.banks[2 + kk][:, 0:TG], AF.Identity, [g.bb[2 + kk], b_w], [b_uc[kk]], bias=bgu[:, jc * 2 + 1, e:e + 1])
                p.ts("pool", uc[:, kk, :], uc[:, kk, :], 7.0, -7.0, ALU.min, ALU.max, [b_uc[kk]], [b_uc[kk]])
                p.tt("pool", gc[:, kk, :], gc[:, kk, :], sg[:, kk, :], ALU.mult, [b_gc[kk], b_sg[kk]], [b_gc[kk]])
                p.stt("dve", hT[:, jc, tsl], uc[:, kk, :], 1.0, gc[:, kk, :], ALU.add, ALU.mult, [b_uc[kk], b_gc[kk]], [b_hT[jc]])
        s = e % 2
        for j in range(J):
            for mh in range(2):
                bk = 4 + (j * 2 + mh) % 4
                for jc in range(8):
                    p.mm(g.banks[bk][:, :], hT[:, jc, j * 128:(j + 1) * 128], wd[s][:, jc, mh * 512:(mh + 1) * 512], jc == 0, jc == 7,
                         [b_hT[jc], b_wd[s]], [g.bb[bk]])
                p.stt("dve", x1[:, j, mh * 512:(mh + 1) * 512], g.banks[bk][:, :], gates[:, j, e:e + 1], x1[:, j, mh * 512:(mh + 1) * 512],
                      ALU.mult, ALU.add, [g.bb[bk], b_gates, b_acc[j], b_x1], [b_acc[j]])
    p.barrier()
    st2.close()
    st = st_save
    gt = p.sb("m_g", [128, D], F32, st)
    bt = p.sb("m_b", [128, D], F32, st)
    b_gb = Buf()
    p.dma("sp", gt[:], W["ln2_g"].partition_broadcast(128), (), [b_gb])
    p.dma("sp", bt[:], W["ln2_b"].partition_broadcast(128), (), [b_gb])
    tmp = p.sb("m_tmp", [128, D], F32, st)
    b_tmp = Buf()
    st4 = p.sb("m_st4", [128, 4], F32, st)
    b_st4 = Buf()
    Xr = Xout.rearrange("(j q) d -> q j d", q=128)
    for j in range(J):
        ln_rows(p, g, x1[:, j, :], b_acc[j], x1[:, j, :], b_acc[j], gt[:], bt[:], b_gb, tmp[:], b_tmp, st4, b_st4)
        p.dma("sp", Xr[:, j, :], x1[:, j, :], [b_acc[j]], ())
    p.barrier()
    st.close()


def _bucket(d):
    n = np.maximum(d, 0)
    nf = np.maximum(n, 1).astype(np.float32)
    large = 16 + (np.log(nf / np.float32(16)) / np.float32(math.log(8.0)) * np.float32(16)).astype(np.int32)
    large = np.minimum(large, 31)
    return np.where(n < 16, n, large)


def make_consts(c_me):
    q = np.arange(128)
    cs = {}
    cs["identf"] = np.eye(128, dtype=np.float32)
    dqm = np.zeros((128, 2), np.float32)
    dqm[:, 0] = ((q % 64) < 32)
    dqm[:, 1] = 1.0 - dqm[:, 0]
    cs["dqmask"] = dqm
    sel = np.zeros((128, 4), np.float32)
    sel[:, c_me] = 1.0
    cs["sel"] = sel
    cs["triu"] = np.triu(np.ones((128, 128), np.float32))
    mT = np.zeros((4, 128, 128), np.float32)
    for c in range(4):
        dl = c_me - c
        if dl < 0:
            mT[c] = 1.0
        elif dl == 0:
            mT[c] = (q[:, None] > q[None, :]).astype(np.float32)
    cs["maskT"] = mT * np.float32(NEG)
    cs["maskQ"] = -np.ascontiguousarray(mT.transpose(0, 2, 1))
    return cs


def bias_tile_index(c_me):
    q = np.arange(128)
    idx = np.zeros((8, 128, 128), np.int64)
    for t in range(8):
        dl = (c_me - t) if t < 4 else (4 + c_me - (t - 4))
        d = 128 * dl + q[None, :] - q[:, None]
        idx[t] = _bucket(d)
    return idx


CONST_SHAPES = {"identf": [128, 128], "dqmask": [128, 2], "sel": [128, 4], "triu": [128, 128],
                "maskT": [4, 128, 128], "maskQ": [4, 128, 128]}


def declare_consts(nc):
    return {k: nc.dram_tensor("c_" + k, shp, F32, kind="ExternalInput").ap() for k, shp in CONST_SHAPES.items()}


WSHAPES = {"rel31": [8], "biasT": [8, 8, 128, 128], "diff_lambda": [4, 32], "diff_norm_g": [64], "conv_w": [31, 256],
           "conv_b": [256], "conv_ln_g": [256], "conv_ln_b": [256], "w_out": [D, D], "ln1_g": [D], "ln1_b": [D],
           "router_w": [D, NE], "router_b": [NE], "w_gu": [NE, D, 2 * D], "b_gu": [NE, 2 * D], "w_down": [NE, D, D],
           "b_down": [NE, D], "ln2_g": [D], "ln2_b": [D]}


def build_A(J):
    nc = bass.Bass("TRN2", target_bir_lowering=False)
    T = J * 128
    X = nc.dram_tensor("X", [T, D], F32, kind="ExternalInput").ap()
    w_in = nc.dram_tensor("w_in", [D, NIN], F32, kind="ExternalInput").ap()
    forget_b = nc.dram_tensor("forget_b", [4], F32, kind="ExternalInput").ap()
    consts = declare_consts(nc)
    KT_loc = nc.dram_tensor("KT_loc", [KT_ROWS, T], BF16, kind="ExternalOutput").ap()
    V_loc = nc.dram_tensor("V_loc", [T, 768], BF16, kind="ExternalOutput").ap()
    HT_loc = nc.dram_tensor("HT_loc", [256, T], F32, kind="ExternalOutput").ap()
    LF_loc = nc.dram_tensor("LF_loc", [J * 4, 128], F32, kind="ExternalOutput").ap()
    QT_loc = nc.dram_tensor("QT_loc", [NQCH, 128, T], BF16, kind="ExternalOutput").ap()
    IW_loc = nc.dram_tensor("IW_loc", [128, J * 4], F32, kind="ExternalOutput").ap()
    p = Prog(nc)
    g = setup_common(p, J, consts)
    emit_A(p, g, X, w_in, forget_b, consts, KT_loc, V_loc, HT_loc, LF_loc, QT_loc, IW_loc)
    p.finish()
    return nc


def build_BC(J, do_C=True, debug=False):
    nc = bass.Bass("TRN2", target_bir_lowering=False)
    T = J * 128
    NB = 4 * J
    X = nc.dram_tensor("X", [T, D], F32, kind="ExternalInput").ap()
    KT_all = nc.dram_tensor("KT_all", [4 * KT_ROWS, T], BF16, kind="ExternalInput").ap()
    V_all = nc.dram_tensor("V_all", [4 * T, 768], BF16, kind="ExternalInput").ap()
    HT_all = nc.dram_tensor("HT_all", [4 * 256, T], F32, kind="ExternalInput").ap()
    LF_all = nc.dram_tensor("LF_all", [4 * J * 4, 128], F32, kind="ExternalInput").ap()
    QT_loc = nc.dram_tensor("QT_loc", [NQCH, 128, T], BF16, kind="ExternalInput").ap()
    IW_loc = nc.dram_tensor("IW_loc", [128, J * 4], F32, kind="ExternalInput").ap()
    HT_loc = nc.dram_tensor("HT_loc", [256, T], F32, kind="ExternalInput").ap()
    lamc = nc.dram_tensor("lamc", [128, 2], F32, kind="ExternalInput").ap()
    W = {k: nc.dram_tensor(k, shp, F32, kind="ExternalInput").ap() for k, shp in WSHAPES.items()}
    consts = declare_consts(nc)
    NEGT = nc.dram_tensor("NEGT", [J, 128, NB, 128], BF16, kind="Internal").ap()
    Xout = nc.dram_tensor("Xout", [T, D], F32, kind="ExternalOutput").ap()
    p = Prog(nc)
    g = setup_common(p, J, consts)
    x1 = p.sb("x1", [128, J, D], F32)
    b_x1 = Buf()
    dbg = nc.dram_tensor("dbg_yT", [128, 8, T], BF16, kind="ExternalOutput").ap() if debug else None
    emit_B(p, g, 0, X, Gat(KT_ROWS, mono=KT_all), Gat(T, mono=V_all), Gat(256, mono=HT_all), LF_all, QT_loc, IW_loc, HT_loc, NEGT, W,
           consts, x1, b_x1, lamc=lamc, dbg=dbg)
    if do_C:
        emit_C(p, g, W, x1, b_x1, Xout)
    else:
        p.dma("sp", Xout.rearrange("(j q) d -> q j d", q=128), x1[:], [b_x1], ())
    p.finish()
    return nc


def build_fused(J, nlayers=DEPTH):
    nc = bass.Bass("TRN2", target_bir_lowering=False)
    T = J * 128
    NB = 4 * J
    Xin = nc.dram_tensor("X", [T, D], F32, kind="ExternalInput").ap()
    w_in = nc.dram_tensor("w_in", [nlayers, D, NIN], F32, kind="ExternalInput").ap()
    forget_b = nc.dram_tensor("forget_b", [nlayers, 4], F32, kind="ExternalInput").ap()
    Wf = {}
    for k, shp in WSHAPES.items():
        if k in ("rel31", "biasT"):
            Wf[k] = nc.dram_tensor(k, shp, F32, kind="ExternalInput").ap()
        else:
            Wf[k] = nc.dram_tensor(k, [nlayers] + shp, F32, kind="ExternalInput").ap()
    consts = declare_consts(nc)
    Xout = nc.dram_tensor("Xout", [T, D], F32, kind="ExternalOutput").ap()
    Xbuf = nc.dram_tensor("Xbuf", [T, D], F32, kind="Internal").ap()
    KT_loc = nc.dram_tensor("KT_loc", [KT_ROWS, T], BF16, kind="Internal").ap()
    V_loc = nc.dram_tensor("V_loc", [T, 768], BF16, kind="Internal").ap()
    HT_loc = nc.dram_tensor("HT_loc", [256, T], F32, kind="Internal").ap()
    LF_loc = nc.dram_tensor("LF_loc", [J * 4, 128], F32, kind="Internal").ap()
    QT_loc = nc.dram_tensor("QT_loc", [NQCH, 128, T], BF16, kind="Internal").ap()
    IW_loc = nc.dram_tensor("IW_loc", [128, J * 4], F32, kind="Internal").ap()
    SK, SV, SH = 104, 256, 32
    def mk(name, R, S, W_, dt):
        return [Gat(R, S=S, chunks=[nc.dram_tensor("%s%d_%d" % (name, i, k), [4 * S, W_], dt, kind="Internal").ap()
                                    for k in range(R // S)]) for i in range(2)]
    KT_alls = mk("KTa", KT_ROWS, SK, T, BF16)
    V_alls = mk("Va", T, SV, 768, BF16)
    HT_alls = mk("HTa", 256, SH, T, F32)
    LF_alls = [nc.dram_tensor("LF_all%d" % i, [4 * J * 4, 128], F32, kind="Internal").ap() for i in range(2)]
    NEGT = nc.dram_tensor("NEGT", [J, 128, NB, 128], BF16, kind="Internal").ap()
    p = Prog(nc)
    g = setup_common(p, J, consts)
    x1 = p.sb("x1", [128, J, D], F32)
    b_x1 = Buf()
    groups = [[0, 1, 2, 3], [4, 5, 6, 7]]
    for l in range(nlayers):
        Xsrc = Xin if l == 0 else Xbuf
        Xdst = Xout if l == nlayers - 1 else Xbuf
        KT_all, V_all, HT_all, LF_all = KT_alls[l % 2], V_alls[l % 2], HT_alls[l % 2], LF_alls[l % 2]
        emit_A(p, g, Xsrc, w_in[l], forget_b[l], consts, KT_loc, V_loc, HT_loc, LF_loc, QT_loc, IW_loc)
        for src, gat in ((KT_loc, KT_all), (V_loc, V_all), (HT_loc, HT_all)):
            for k, ch in enumerate(gat.chunks):
                p.collective(src[k * gat.S:(k + 1) * gat.S, :], ch, groups)
        p.collective(LF_loc, LF_all, groups)
        p.barrier()
        W = {k: (v if k in ("rel31", "biasT") else v[l]) for k, v in Wf.items()}
        emit_B(p, g, l, Xsrc, KT_all, V_all, HT_all, LF_all, QT_loc, IW_loc, HT_loc, NEGT, W, consts, x1, b_x1)
        emit_C(p, g, W, x1, b_x1, Xdst)
    p.finish()
    return nc


J_FULL = 16
FUSED = True
_CACHE = {}


def _loc(a, J):
    L = a.shape[0]
    v = a.reshape(J, 4, 128, -1)
    return [np.ascontiguousarray(v[:, c].reshape(J * 128, -1)) for c in range(4)]


def _unloc(parts, J):
    Dd = parts[0].shape[-1]
    out = np.empty((J, 4, 128, Dd), parts[0].dtype)
    for c in range(4):
        out[:, c] = parts[c].reshape(J, 128, Dd)
    return out.reshape(J * 4 * 128, Dd)


def kernel(x, rel_bias, w_in, forget_b, diff_lambda, diff_norm_g, conv_w, conv_b, conv_ln_g, conv_ln_b, w_out,
           ln1_g, ln1_b, router_w, router_b, w_gu, b_gu, w_down, b_down, ln2_g, ln2_b):
    J = J_FULL
    f32 = lambda a: np.ascontiguousarray(np.asarray(a, dtype=np.float32))
    x = f32(x)
    rel_bias = f32(rel_bias)
    P = dict(w_in=f32(w_in), forget_b=f32(forget_b), diff_lambda=f32(diff_lambda), diff_norm_g=f32(diff_norm_g), conv_w=f32(conv_w),
             conv_b=f32(conv_b), conv_ln_g=f32(conv_ln_g), conv_ln_b=f32(conv_ln_b), w_out=f32(w_out), ln1_g=f32(ln1_g), ln1_b=f32(ln1_b),
             router_w=f32(router_w), router_b=f32(router_b), w_gu=f32(w_gu), b_gu=f32(b_gu), w_down=f32(w_down), b_down=f32(b_down),
             ln2_g=f32(ln2_g), ln2_b=f32(ln2_b))
    B = x.shape[0]
    cs = [make_consts(c) for c in range(4)]
    biasT = [np.ascontiguousarray(rel_bias[bias_tile_index(c)].transpose(3, 0, 1, 2)) for c in range(4)]
    rel31 = np.ascontiguousarray(rel_bias[31])
    xs = [None] * 8
    for b in range(B):
        for c, part in enumerate(_loc(x[b], J)):
            xs[4 * b + c] = part
    cores = list(range(8))
    if FUSED:
        if "F" not in _CACHE:
            _CACHE["F"] = build_fused(J)
        in_maps = []
        for r in cores:
            m = {"X": xs[r], "w_in": P["w_in"], "forget_b": P["forget_b"], "biasT": biasT[r % 4], "rel31": rel31}
            for k in WSHAPES:
                if k not in m:
                    m[k] = P[k]
            for k, v in cs[r % 4].items():
                m["c_" + k] = v
            in_maps.append(m)
        res = run_bass_kernel_spmd(_CACHE["F"], in_maps, core_ids=cores).results
        xs = [np.asarray(res[r]["Xout"]) for r in cores]
    else:
        if "A" not in _CACHE:
            _CACHE["A"] = build_A(J)
            _CACHE["BC"] = build_BC(J)
        for l in range(DEPTH):
            li = 0.8 - 0.6 * math.exp(-0.3 * l)
            lamc = np.tile(np.array([[-li, 1.0 - li]], np.float32), (128, 1))
            in_maps = []
            for r in cores:
                m = {"X": xs[r], "w_in": P["w_in"][l], "forget_b": P["forget_b"][l]}
                for k, v in cs[r % 4].items():
                    m["c_" + k] = v
                in_maps.append(m)
            resA = run_bass_kernel_spmd(_CACHE["A"], in_maps, core_ids=cores).results
            in_maps = []
            for r in cores:
                b = r // 4
                m = {"X": xs[r], "lamc": lamc, "biasT": biasT[r % 4], "rel31": rel31}
                for nm in ("KT", "V", "HT", "LF"):
                    m[nm + "_all"] = np.concatenate([np.asarray(resA[4 * b + cc][nm + "_loc"]) for cc in range(4)], 0)
                for nm in ("QT_loc", "IW_loc", "HT_loc"):
                    m[nm] = np.asarray(resA[r][nm])
                for k in WSHAPES:
                    if k not in m:
                        m[k] = P[k][l]
                for k, v in cs[r % 4].items():
                    m["c_" + k] = v
                in_maps.append(m)
            resB = run_bass_kernel_spmd(_CACHE["BC"], in_maps, core_ids=cores).results
            xs = [np.asarray(resB[r]["Xout"]) for r in cores]
    out = np.stack([_unloc([xs[4 * b + c] for c in range(4)], J) for b in range(B)], 0)
    return out.astype(np.float32)
```

```python
import math
from contextlib import ExitStack

import numpy as np
import concourse.bass as bass
import concourse.mybir as mybir
from concourse.bass_utils import run_bass_kernel_spmd

F32 = mybir.dt.float32
BF16 = mybir.dt.bfloat16
ALU = mybir.AluOpType
AF = mybir.ActivationFunctionType

D = 1024
NIN = 3144
DEPTH = 4
NE = 32
ALPHA = (2 * DEPTH) ** 0.25
EPS = 1e-5
NEG = -30000.0
REPL = -1.0e30
C_FQ, C_FK, C_FV, C_FF = 0, 256, 512, 768
C_DQ, C_DK, C_DV = 772, 1028, 1284
C_SQ, C_SK, C_SV = 1540, 1796, 2052
C_IQ, C_IK, C_IW, C_CU = 2308, 2564, 2628, 2632
KT_ROWS = 832
NQCH = 10


class Buf:
    __slots__ = ("w", "r", "wx")

    def __init__(self):
        self.w = None
        self.r = {}
        self.wx = []


class Prog:
    NDS = 24

    def __init__(self, nc):
        self.nc = nc
        self.st = ExitStack()
        self.engs = {"pe": nc.tensor, "act": nc.scalar, "dve": nc.vector, "pool": nc.gpsimd, "sp": nc.sync}
        self.semobj = {}
        self.cnt = {}
        for k in self.engs:
            self.semobj[k] = self.st.enter_context(nc.semaphore("s_" + k))
            self.cnt[k] = 0
        self.seen = {k: {} for k in self.engs}
        self.dcnt = [0] * self.NDS
        for i in range(self.NDS):
            self.semobj[("d", i)] = self.st.enter_context(nc.semaphore("s_d%d" % i))
        self.dnext = 0
        self.nins = 0

    def sb(self, name, shape, dt, st=None):
        self.uid = getattr(self, "uid", 0) + 1
        return (st or self.st).enter_context(self.nc.sbuf_tensor("%s_%d" % (name, self.uid), shape, dt))

    def ps(self, name, shape, dt, st=None):
        return (st or self.st).enter_context(self.nc.psum_tensor(name, shape, dt))

    def _wait(self, eng, key, val):
        if self.seen[eng].get(key, 0) >= val:
            return
        self.engs[eng].wait_ge(self.semobj[key], val)
        self.seen[eng][key] = val
        self.nins += 1

    def _deps(self, eng, reads, writes, extra=None, is_dma=False):
        need = {}

        def add(k, v, kind):
            if k == eng:
                if eng == "pe" or kind == "war":
                    return
            if need.get(k, 0) < v:
                need[k] = v
        if extra is not None:
            add(extra[0], extra[1], "raw")
        for b in reads:
            if b.w is not None:
                add(b.w[0], b.w[1], "raw")
            for k, v in b.wx:
                add(k, v, "raw")
        for b in writes:
            if is_dma and b.w is not None and isinstance(b.w[0], tuple) and not b.r:
                continue
            if b.w is not None:
                add(b.w[0], b.w[1], "waw")
            for k, v in b.wx:
                add(k, v, "waw")
            for k, v in b.r.items():
                add(k, v, "war")
        items = [(k, v) for k, v in need.items() if self.seen[eng].get(k, 0) < v]
        if not items:
            return None
        for k, v in items[:-1]:
            self._wait(eng, k, v)
        return items[-1]

    def _embed(self, eng, ins, last):
        if last is not None:
            ins._wait_ge(self.semobj[last[0]], last[1])
            self.seen[eng][last[0]] = last[1]

    def _mark(self, key, val, reads, writes, is_dma=False):
        for b in reads:
            b.r[key] = val
        for b in writes:
            if is_dma and b.w is not None and isinstance(b.w[0], tuple) and not b.r:
                b.wx.append(b.w)
            else:
                b.wx = []
            b.w = (key, val)
            b.r = {}

    def op(self, eng, fn, reads=(), writes=()):
        last = self._deps(eng, reads, writes)
        ins = fn(self.engs[eng])
        self._embed(eng, ins, last)
        self.cnt[eng] += 1
        ins.then_inc(self.semobj[eng], 1)
        self.nins += 1
        self._mark(eng, self.cnt[eng], reads, writes)

    def dma(self, eng, out, in_, reads=(), writes=()):
        i = self.dnext
        self.dnext = (i + 1) % self.NDS
        key = ("d", i)
        last = self._deps(eng, reads, writes, extra=(key, self.dcnt[i]) if self.dcnt[i] > 0 else None, is_dma=True)
        ins = self.engs[eng].dma_start(out=out, in_=in_)
        self._embed(eng, ins, last)
        self.dcnt[i] += 16
        ins.then_inc(self.semobj[key], 16)
        self.nins += 1
        self._mark(key, self.dcnt[i], reads, writes, is_dma=True)

    def collective(self, src, dst, groups):
        if "cc" not in self.semobj:
            self.semobj["cc"] = self.st.enter_context(self.nc.semaphore("s_cc"))
            self.cccnt = 0
        self.barrier(["pool"])
        ins = self.nc.gpsimd.collective_compute("AllGather", ALU.bypass, replica_groups=groups, ins=[src.opt()], outs=[dst.opt()])
        self.cccnt += 1
        ins.then_inc(self.semobj["cc"])
        self.nins += 1
        self._wait("pool", "cc", self.cccnt)

    def barrier(self, engs=None):
        engs = engs or list(self.engs)
        for e in engs:
            for k in self.engs:
                if k != e and self.cnt[k] > 0:
                    self._wait(e, k, self.cnt[k])
            for i in range(self.NDS):
                if self.dcnt[i] > 0:
                    self._wait(e, ("d", i), self.dcnt[i])
            if getattr(self, "cccnt", 0) > 0:
                self._wait(e, "cc", self.cccnt)

    def mm(self, out, lhsT, rhs, start, stop, reads, writes):
        self.op("pe", lambda e: e.matmul(out, lhsT=lhsT, rhs=rhs, start=start, stop=stop), reads, writes)

    def tr(self, out, in_, ident, reads, writes):
        self.op("pe", lambda e: e.transpose(out, in_, ident), reads, writes)

    def act(self, out, in_, func, reads, writes, bias=None, scale=1.0, accum=None):
        kw = {}
        if bias is not None:
            kw["bias"] = bias
        if accum is not None:
            kw["accum_out"] = accum
        self.op("act", lambda e: e.activation(out=out, in_=in_, func=func, scale=scale, **kw), reads, writes)

    def ts(self, eng, out, in0, s1, s2, op0, op1, reads, writes, accum=None):
        kw = {}
        if accum is not None:
            kw["accum_out"] = accum
        if op1 is None:
            self.op(eng, lambda e: e.tensor_scalar(out=out, in0=in0, scalar1=s1, scalar2=None, op0=op0, **kw), reads, writes)
        else:
            self.op(eng, lambda e: e.tensor_scalar(out=out, in0=in0, scalar1=s1, scalar2=s2, op0=op0, op1=op1, **kw), reads, writes)

    def tt(self, eng, out, in0, in1, op, reads, writes):
        self.op(eng, lambda e: e.tensor_tensor(out=out, in0=in0, in1=in1, op=op), reads, writes)

    def stt(self, eng, out, in0, scalar, in1, op0, op1, reads, writes):
        self.op(eng, lambda e: e.scalar_tensor_tensor(out=out, in0=in0, scalar=scalar, in1=in1, op0=op0, op1=op1), reads, writes)

    def copy(self, eng, out, in_, reads, writes):
        if eng == "act":
            self.op("act", lambda e: e.copy(out=out, in_=in_), reads, writes)
        else:
            self.op(eng, lambda e: e.tensor_copy(out=out, in_=in_), reads, writes)

    def memset(self, eng, ap, val, writes):
        self.op(eng, lambda e: e.memset(ap, val), (), writes)

    def finish(self):
        self.barrier(["sp"])
        self.st.close()


def _bufs(n):
    return [Buf() for _ in range(n)]


class Ctx:
    pass


class Gat:
    def __init__(self, R, mono=None, S=None, chunks=None):
        self.R, self.mono, self.S, self.chunks = R, mono, S, chunks

    def rows(self, c, r0, n):
        if self.mono is not None:
            return [(self.mono[c * self.R + r0:c * self.R + r0 + n, :], 0, n)]
        out = []
        r = r0
        while r < r0 + n:
            k, o = r // self.S, r % self.S
            m = min(self.S - o, r0 + n - r)
            out.append((self.chunks[k][c * self.S + o:c * self.S + o + m, :], r - r0, m))
            r += m
        return out


def setup_common(p, J, consts):
    nc = p.nc
    g = Ctx()
    g.J = J
    g.T = J * 128
    g.banks = [p.ps("bank%d" % i, [128, 512], F32) for i in range(8)]
    g.bb = _bufs(8)
    g.identf = p.sb("identf", [128, 128], F32)
    g.identb = p.sb("identb", [128, 128], BF16)
    g.b_ident = Buf()
    p.dma("sp", g.identf[:], consts["identf"], (), [g.b_ident])
    p.copy("dve", g.identb[:], g.identf[:], [g.b_ident], [g.b_ident])
    return g


def transpose_block(p, g, src, src_b, dst_bf, dst_b, blk, dst_f32=None, dst_f32_b=None, bank0=0):
    for half in range(2):
        bk = bank0 + half
        for q in range(4):
            dc = half * 4 + q
            p.tr(g.banks[bk][:, q * 128:(q + 1) * 128], src[:, dc * 128:(dc + 1) * 128], g.identf[:],
                 [src_b, g.b_ident], [g.bb[bk]])
        outv = dst_bf[:, half * 4:(half + 1) * 4, blk * 128:(blk + 1) * 128]
        inv = g.banks[bk][:].rearrange("p (q t) -> p q t", q=4)
        if dst_f32 is None:
            p.copy("act" if half == 0 else "dve", outv, inv, [g.bb[bk]], [dst_b])
        else:
            f32v = dst_f32[:, half * 4:(half + 1) * 4, :]
            p.copy("dve", f32v, inv, [g.bb[bk]], [dst_f32_b])
            p.copy("act", outv, f32v, [dst_f32_b], [dst_b])


def emit_A(p, g, X, w_in, forget_b, consts, KT_loc, V_loc, HT_loc, LF_loc, QT_loc, IW_loc):
    J, T = g.J, g.T
    st = ExitStack()
    xT = p.sb("a_xT", [128, 8, T], BF16, st)
    b_xT = Buf()
    win = p.sb("a_win", [128, 8, NIN], BF16, st)
    b_win = _bufs(8)
    xt = [p.sb("a_xt%d" % i, [128, D], F32, st) for i in range(2)]
    b_xt = _bufs(2)
    fb = p.sb("a_fb", [128, 4], F32, st)
    b_fb = Buf()
    dqm = p.sb("a_dqm", [128, 2], F32, st)
    b_dqm = Buf()
    p.dma("sp", fb[:], forget_b.partition_broadcast(128), (), [b_fb])
    p.dma("sp", dqm[:], consts["dqmask"], (), [b_dqm])
    w_r = w_in.rearrange("(dc q) f -> q dc f", q=128)
    for dc in range(8):
        p.dma("pool", win[:, dc, :], w_r[:, dc, :], (), [b_win[dc]])
    Xr = X.rearrange("(j q) d -> q j d", q=128)
    for j in range(J):
        s = j % 2
        p.dma("sp", xt[s][:], Xr[:, j, :], (), [b_xt[s]])
        transpose_block(p, g, xt[s], b_xt[s], xT, b_xT, j)

    ev = p.sb("a_ev", [128, 4, 512], BF16, st)
    b_ev = _bufs(4)
    evf = p.sb("a_evf", [128, 2, 512], F32, st)
    b_evf = _bufs(2)
    sg = p.sb("a_sg", [128, 2, 512], F32, st)
    b_sg = _bufs(2)
    ntg = T // 512 if T >= 512 else 1
    TG = min(512, T)
    state = {"bk": 2, "ev": 0, "evf": 0}

    def fm_chunk(col0, ncols, tg):
        bk = state["bk"]
        state["bk"] = 2 + (bk - 2 + 1) % 4
        for dc in range(8):
            p.mm(g.banks[bk][0:ncols, 0:TG], win[:, dc, col0:col0 + ncols], xT[:, dc, tg * TG:(tg + 1) * TG],
                 dc == 0, dc == 7, [b_win[dc], b_xT], [g.bb[bk]])
        return bk

    def ev_slot():
        s = state["ev"]
        state["ev"] = (s + 1) % 4
        return s

    for tg in range(ntg):
        tsl = slice(tg * TG, (tg + 1) * TG)
        qspec = [(0, C_FQ, None), (1, C_FQ + 128, None), (2, C_DQ, 0), (3, C_DQ + 128, 0), (4, C_DQ, 1), (5, C_DQ + 128, 1),
                 (6, C_SQ, None), (7, C_SQ + 128, None), (8, C_IQ, None), (9, C_IQ + 128, None)]
        cache = {}
        for qi, col0, m in qspec:
            if m == 1 and col0 in cache:
                bk = cache[col0]
            else:
                bk = fm_chunk(col0, 128, tg)
                cache[col0] = bk
            s = ev_slot()
            if m is None:
                p.copy("act", ev[:, s, 0:TG], g.banks[bk][:, 0:TG], [g.bb[bk]], [b_ev[s]])
            else:
                p.ts("dve", ev[:, s, 0:TG], g.banks[bk][:, 0:TG], dqm[:, m:m + 1], None, ALU.mult, None,
                     [g.bb[bk], b_dqm], [b_ev[s]])
            p.dma("sp", QT_loc[qi, :, tsl], ev[:, s, 0:TG], [b_ev[s]], ())
        for ki, col0, ncols in [(0, C_FK, 128), (1, C_FK + 128, 128), (2, C_DK, 128), (3, C_DK + 128, 128),
                                (4, C_SK, 128), (5, C_SK + 128, 128), (6, C_IK, 64)]:
            bk = fm_chunk(col0, ncols, tg)
            s = ev_slot()
            p.copy("act" if ki % 2 == 0 else "dve", ev[0:ncols, s, 0:TG], g.banks[bk][0:ncols, 0:TG], [g.bb[bk]], [b_ev[s]])
            p.dma("sp", KT_loc[ki * 128:ki * 128 + ncols, tsl], ev[0:ncols, s, 0:TG], [b_ev[s]], ())
        for cc in range(2):
            bka = fm_chunk(C_CU + cc * 128, 128, tg)
            bkg = fm_chunk(C_CU + 256 + cc * 128, 128, tg)
            s = state["evf"]
            state["evf"] = 1 - s
            p.act(sg[:, s, 0:TG], g.banks[bkg][:, 0:TG], AF.Sigmoid, [g.bb[bkg]], [b_sg[s]])
            p.tt("dve", evf[:, s, 0:TG], g.banks[bka][:, 0:TG], sg[:, s, 0:TG], ALU.mult, [g.bb[bka], b_sg[s]], [b_evf[s]])
            p.dma("sp", HT_loc[cc * 128:(cc + 1) * 128, tsl], evf[:, s, 0:TG], [b_evf[s]], ())

    vt = p.sb("a_vt", [128, 2, 768], BF16, st)
    b_vt = _bufs(2)
    lf = p.sb("a_lf", [128, J, 4], F32, st)
    b_lf = Buf()
    iw = p.sb("a_iw", [128, J, 4], F32, st)
    b_iw = Buf()
    for j in range(J):
        s = j % 2
        lt = slice(j * 128, (j + 1) * 128)
        for gi, col0 in enumerate([C_FV, C_DV]):
            for dc in range(8):
                p.mm(g.banks[0][:, gi * 256:(gi + 1) * 256], xT[:, dc, lt], win[:, dc, col0:col0 + 256], dc == 0, dc == 7,
                     [b_win[dc], b_xT], [g.bb[0]])
        for dc in range(8):
            p.mm(g.banks[1][:, 0:256], xT[:, dc, lt], win[:, dc, C_SV:C_SV + 256], dc == 0, dc == 7, [b_win[dc], b_xT], [g.bb[1]])
        for dc in range(8):
            p.mm(g.banks[1][:, 256:260], xT[:, dc, lt], win[:, dc, C_FF:C_FF + 4], dc == 0, dc == 7, [b_win[dc], b_xT], [g.bb[1]])
        for dc in range(8):
            p.mm(g.banks[1][:, 260:264], xT[:, dc, lt], win[:, dc, C_IW:C_IW + 4], dc == 0, dc == 7, [b_win[dc], b_xT], [g.bb[1]])
        p.copy("act", vt[:, s, 0:512], g.banks[0][:, 0:512], [g.bb[0]], [b_vt[s]])
        p.copy("dve", vt[:, s, 512:768], g.banks[1][:, 0:256], [g.bb[1]], [b_vt[s]])
        p.tt("dve", lf[:, j, :], g.banks[1][:, 256:260], fb[:], ALU.add, [g.bb[1], b_fb], [b_lf])
        p.copy("dve", iw[:, j, :], g.banks[1][:, 260:264], [g.bb[1]], [b_iw])
        p.dma("sp", V_loc[lt, :], vt[:, s, :], [b_vt[s]], ())
    lfv = lf[:].rearrange("p j h -> p (j h)")
    p.act(lfv, lfv, AF.Exp, [b_lf], [b_lf], scale=-1.0)
    p.act(lfv, lfv, AF.Ln, [b_lf], [b_lf], bias=1.0)
    p.ts("dve", lfv, lfv, -1.0, None, ALU.mult, None, [b_lf], [b_lf])
    p.tr(g.banks[2][0:J * 4, 0:128], lfv, g.identf[:], [b_lf, g.b_ident], [g.bb[2]])
    lft = p.sb("a_lft", [J * 4, 128], F32, st)
    b_lft = Buf()
    p.copy("dve", lft[:], g.banks[2][0:J * 4, 0:128], [g.bb[2]], [b_lft])
    p.dma("sp", LF_loc, lft[:], [b_lft], ())
    p.dma("sp", IW_loc, iw[:].rearrange("p j h -> p (j h)"), [b_iw], ())
    p.barrier()
    st.close()


def ln_rows(p, g, z, b_z, out, b_out, gt, bt, b_gb, tmp, b_tmp, st4, b_st4):
    p.memset("dve", st4[:, 0:2], 0.0, [b_st4])
    p.ts("dve", tmp, z, 1.0, 0.0, ALU.mult, ALU.add, [b_z], [b_tmp, b_st4], accum=st4[:, 0:1])
    p.tt("pool", tmp, z, z, ALU.mult, [b_z], [b_tmp])
    p.ts("dve", tmp, tmp, 1.0, 0.0, ALU.mult, ALU.add, [b_tmp], [b_tmp, b_st4], accum=st4[:, 1:2])
    p.ts("dve", st4[:, 0:1], st4[:, 0:1], 1.0 / D, None, ALU.mult, None, [b_st4], [b_st4])
    p.tt("dve", st4[:, 2:3], st4[:, 0:1], st4[:, 0:1], ALU.mult, [b_st4], [b_st4])
    p.stt("dve", st4[:, 1:2], st4[:, 1:2], 1.0 / D, st4[:, 2:3], ALU.mult, ALU.subtract, [b_st4], [b_st4])
    p.act(st4[:, 1:2], st4[:, 1:2], AF.Ln, [b_st4], [b_st4], bias=EPS)
    p.act(st4[:, 1:2], st4[:, 1:2], AF.Exp, [b_st4], [b_st4], scale=-0.5)
    p.stt("dve", st4[:, 2:3], st4[:, 0:1], -1.0, st4[:, 1:2], ALU.mult, ALU.mult, [b_st4], [b_st4])
    p.act(tmp, z, AF.Identity, [b_z, b_st4], [b_tmp], bias=st4[:, 2:3], scale=st4[:, 1:2])
    p.tt("dve", tmp, tmp, gt, ALU.mult, [b_tmp, b_gb], [b_tmp])
    p.tt("pool", out, tmp, bt, ALU.add, [b_tmp, b_gb], [b_out])


NOMASK = False


def emit_B(p, g, l, X, KT_all, V_all, HT_all, LF_all, QT_loc, IW_loc, HT_loc, NEGT, W, consts, x1, b_x1, lamc=None, dbg=None):
    J, T = g.J, g.T
    NB = 4 * J
    L = NB * 128
    lambda_init = 0.8 - 0.6 * math.exp(-0.3 * l)
    stB = ExitStack()
    yT = p.sb("b_yT", [128, 8, T], BF16, stB)
    b_yT = Buf()
    sel = p.sb("b_sel", [128, 4], F32, stB)
    b_c = Buf()
    p.dma("sp", sel[:], consts["sel"], (), [b_c])
    maskT = p.sb("b_maskT", [128, 4, 128], F32, stB)
    p.dma("sp", maskT[:], consts["maskT"].rearrange("c k q -> k c q"), (), [b_c])
    b31 = p.sb("b_b31", [128, 8], F32, stB)
    p.dma("sp", b31[:], W["rel31"].partition_broadcast(128), (), [b_c])
    ones = p.sb("b_ones", [128, 128], F32, stB)
    p.memset("pool", ones[:], 1.0, [b_c])

    st = ExitStack()
    hc = p.sb("c_hc", [128, 2, J, 158], F32, st)
    b_hc = Buf()
    cand = p.sb("c_cand", [128, 4, 2, J, 30], F32, st)
    b_cand = Buf()
    cw = p.sb("c_cw", [128, 2, 31], F32, st)
    cpar = p.sb("c_par", [128, 2, 3], F32, st)
    b_cw = Buf()
    cwr = p.sb("c_cwr", [31, 256], F32, st)
    p.dma("sp", cwr[:], W["conv_w"], (), [b_cw])
    for cc in range(2):
        p.tr(g.banks[0][:, cc * 32:cc * 32 + 31], cwr[:, cc * 128:(cc + 1) * 128], g.identf[0:31, 0:31], [b_cw, g.b_ident], [g.bb[0]])
    p.copy("dve", cw[:], g.banks[0][:, 0:64].rearrange("c (cc w) -> c cc w", cc=2)[:, :, 0:31], [g.bb[0]], [b_cw])
    for i, nm in enumerate(["conv_b", "conv_ln_g", "conv_ln_b"]):
        for cc in range(2):
            p.dma("sp", cpar[:, cc, i:i + 1], W[nm].rearrange("(cc c o) -> cc c o", c=128, o=1)[cc], (), [b_cw])
    p.memset("pool", cand[:], 0.0, [b_cand])
    for cc in range(2):
        p.dma("sp", hc[:, cc, :, 30:158], HT_loc[cc * 128:(cc + 1) * 128, :].rearrange("c (j t) -> c j t", t=128), (), [b_hc])
        for c in range(4):
            if c >= 1:
                for ap_, off, n in HT_all.rows(c - 1, cc * 128, 128):
                    p.dma("sp", cand[off:off + n, c, cc, :, :], ap_.rearrange("c (j t) -> c j t", t=128)[:, :, 98:128], (), [b_cand])
            elif J > 1:
                for ap_, off, n in HT_all.rows(3, cc * 128, 128):
                    p.dma("sp", cand[off:off + n, c, cc, 1:J, :], ap_.rearrange("c (j t) -> c j t", t=128)[:, 0:J - 1, 98:128], (), [b_cand])
    for cc in range(2):
        e = "dve"
        p.ts(e, hc[:, cc, :, 0:30], cand[:, 0, cc, :, :], sel[:, 0:1], None, ALU.mult, None, [b_cand, b_c], [b_hc])
        for c in range(1, 4):
            p.stt(e, hc[:, cc, :, 0:30], cand[:, c, cc, :, :], sel[:, c:c + 1], hc[:, cc, :, 0:30], ALU.mult, ALU.add,
                  [b_cand, b_c, b_hc], [b_hc])
    cacc = p.sb("c_acc", [128, 2, J, 128], F32, st)
    b_cacc = _bufs(2)
    for cc in range(2):
        e = "dve"
        p.ts(e, cacc[:, cc], hc[:, cc, :, 0:128], cw[:, cc, 0:1], cpar[:, cc, 0:1], ALU.mult, ALU.add, [b_hc, b_cw], [b_cacc[cc]])
        for w in range(1, 31):
            p.stt(e, cacc[:, cc], hc[:, cc, :, w:w + 128], cw[:, cc, w:w + 1], cacc[:, cc], ALU.mult, ALU.add,
                  [b_hc, b_cw, b_cacc[cc]], [b_cacc[cc]])
    TG = min(512, T)
    csq = p.sb("c_sq", [128, 2, TG], F32, st)
    b_csq = Buf()
    cmean = p.sb("c_mean", [128, TG], F32, st)
    crstd = p.sb("c_rstd", [128, TG], F32, st)
    b_cst = Buf()
    cd = p.sb("c_d", [128, 2, TG], F32, st)
    b_cd = _bufs(2)
    p.ts("dve", ones[:], ones[:], 1.0 / 256.0, None, ALU.mult, None, [b_c], [b_c])
    for tg in range(T // TG):
        flat = [cacc[:, cc].rearrange("c j t -> c (j t)")[:, tg * TG:(tg + 1) * TG] for cc in range(2)]
        for cc in range(2):
            p.tt("pool", csq[:, cc, :], flat[cc], flat[cc], ALU.mult, [b_cacc[cc]], [b_csq])
        for cc in range(2):
            p.mm(g.banks[0][:, 0:TG], ones[:], flat[cc], cc == 0, cc == 1, [b_c, b_cacc[cc]], [g.bb[0]])
        for cc in range(2):
            p.mm(g.banks[1][:, 0:TG], ones[:], csq[:, cc, :], cc == 0, cc == 1, [b_c, b_csq], [g.bb[1]])
        p.copy("act", cmean[:], g.banks[0][:, 0:TG], [g.bb[0]], [b_cst])
        p.tt("dve", crstd[:], cmean[:], cmean[:], ALU.mult, [b_cst], [b_cst])
        p.tt("dve", crstd[:], g.banks[1][:, 0:TG], crstd[:], ALU.subtract, [g.bb[1], b_cst], [b_cst])
        p.act(crstd[:], crstd[:], AF.Ln, [b_cst], [b_cst], bias=EPS)
        p.act(crstd[:], crstd[:], AF.Exp, [b_cst], [b_cst], scale=-0.5)
        for cc in range(2):
            p.tt("dve", cd[:, cc, :], flat[cc], cmean[:], ALU.subtract, [b_cacc[cc], b_cst], [b_cd[cc]])
            p.tt("dve", cd[:, cc, :], cd[:, cc, :], crstd[:], ALU.mult, [b_cd[cc], b_cst], [b_cd[cc]])
            p.ts("dve", cd[:, cc, :], cd[:, cc, :], cpar[:, cc, 1:2], cpar[:, cc, 2:3], ALU.mult, ALU.add, [b_cd[cc], b_cw], [b_cd[cc]])
            p.act(csq[:, cc, :], cd[:, cc, :], AF.Sigmoid, [b_cd[cc]], [b_csq])
            p.tt("dve", yT[:, 6 + cc, tg * TG:(tg + 1) * TG], cd[:, cc, :], csq[:, cc, :], ALU.mult, [b_cd[cc], b_csq], [b_yT])
    p.ts("dve", ones[:], ones[:], 256.0, None, ALU.mult, None, [b_c], [b_c])
    p.barrier()
    st.close()

    R = 4 * J * 4
    cn = p.sb("f_cn", [128, J, 4, 4], F32, stB)
    offs = p.sb("f_offs", [128, J, 4, 4], F32, stB)
    offl = p.sb("f_offl", [128, J, 4], F32, stB)
    vs = p.sb("f_vs", [128, J, 4, 4], F32, stB)
    bjr = p.sb("f_bjr", [128, J, J, 4], F32, stB)
    lam = p.sb("d_lam", [128, 4], F32, stB)
    dng = p.sb("d_g", [128, 64], F32, stB)
    b_f = Buf()
    st = ExitStack()
    triu = p.sb("f_triu", [128, 128], F32, st)
    p.dma("sp", triu[:], consts["triu"], (), [b_f])
    nt = (R + 127) // 128
    lfr = p.sb("f_lfr", [128, nt, 128], F32, st)
    lfn = p.sb("f_lfn", [128, R], F32, st)
    sc_a = p.sb("f_sca", [128, NB, 4], F32, st)
    sc_b = p.sb("f_scb", [128, NB, 4], F32, st)
    for t in range(nt):
        rr = min(128, R - t * 128)
        p.dma("sp", lfr[0:rr, t, :], LF_all[t * 128:t * 128 + rr, :], (), [b_f])
        p.tr(g.banks[0][:, t * 128:t * 128 + rr], lfr[0:rr, t, :], g.identf[0:rr, 0:rr], [b_f, g.b_ident], [g.bb[0]])
    p.copy("dve", lfn[:], g.banks[0][:, 0:R], [g.bb[0]], [b_f])
    p.mm(g.banks[1][:, 0:R], triu[:], lfn[:], True, True, [b_f], [g.bb[1]])
    p.memset("pool", ones[:], 1.0, [b_c])
    p.mm(g.banks[2][:, 0:R], ones[:], lfn[:], True, True, [b_f, b_c], [g.bb[2]])
    nat = lambda ap: ap.rearrange("p (c j h) -> p j c h", c=4, j=J)
    p.copy("dve", sc_a[:].rearrange("p (j c) h -> p j c h", c=4), nat(g.banks[2][:, 0:R]), [g.bb[2]], [b_f])
    p.copy("dve", offs[:], nat(g.banks[2][:, 0:R]), [g.bb[2]], [b_f])
    a, b = sc_a, sc_b
    s = 1
    while s < NB:
        p.copy("dve", b[:, 0:s, :], a[:, 0:s, :], [b_f], [b_f])
        p.tt("dve", b[:, s:NB, :], a[:, s:NB, :], a[:, 0:NB - s, :], ALU.add, [b_f], [b_f])
        a, b = b, a
        s *= 2
    offs_f = offs[:].rearrange("p j c h -> p (j c) h")
    p.tt("dve", offs_f, a[:], offs_f, ALU.subtract, [b_f], [b_f])
    p.tt("dve", cn[:], nat(g.banks[1][:, 0:R]), offs[:], ALU.add, [g.bb[1], b_f], [b_f])
    p.ts("dve", offl[:], offs[:, :, 0, :], sel[:, 0:1], None, ALU.mult, None, [b_f, b_c], [b_f])
    for c in range(1, 4):
        p.stt("dve", offl[:], offs[:, :, c, :], sel[:, c:c + 1], offl[:], ALU.mult, ALU.add, [b_f, b_c], [b_f])
    for c in range(4):
        p.tt("dve", vs[:, :, c, :], offs[:, :, 0, :], cn[:, :, c, :], ALU.subtract, [b_f], [b_f])
    vsf = vs[:].rearrange("p j c h -> p (j c h)")
    p.act(vsf, vsf, AF.Exp, [b_f], [b_f])
    for j in range(J):
        for h in range(4):
            p.ts("dve", bjr[:, j, :, h], offs[:, :, 0, h], -1.0, offl[:, j, h:h + 1], ALU.mult, ALU.add, [b_f], [b_f])
    dlf = p.sb("d_dl", [128, 128], F32, st)
    p.dma("sp", dlf[:], W["diff_lambda"].rearrange("a b -> (a b)").partition_broadcast(128), (), [b_f])
    dl = dlf[:].rearrange("p (a b) -> p a b", a=4)
    p.memset("dve", lam[:], 0.0, [b_f])
    p.dma("sp", dng[:], W["diff_norm_g"].partition_broadcast(128), (), [b_f])
    p.tt("dve", dl[:, 0, :], dl[:, 0, :], dl[:, 1, :], ALU.mult, [b_f], [b_f])
    p.tt("dve", dl[:, 2, :], dl[:, 2, :], dl[:, 3, :], ALU.mult, [b_f], [b_f])
    p.ts("dve", dl[:, 1, :], dl[:, 0, :], 1.0, 0.0, ALU.mult, ALU.add, [b_f], [b_f], accum=lam[:, 0:1])
    p.ts("dve", dl[:, 3, :], dl[:, 2, :], 1.0, 0.0, ALU.mult, ALU.add, [b_f], [b_f], accum=lam[:, 1:2])
    p.act(lam[:, 0:2], lam[:, 0:2], AF.Exp, [b_f], [b_f])
    p.tt("dve", lam[:, 2:3], lam[:, 1:2], lam[:, 0:1], ALU.subtract, [b_f], [b_f])
    if lamc is None:
        p.ts("dve", lam[:, 2:3], lam[:, 2:3], -lambda_init, None, ALU.add, None, [b_f], [b_f])
        p.ts("dve", dng[:], dng[:], 1.0 - lambda_init, None, ALU.mult, None, [b_f], [b_f])
    else:
        lc = p.sb("d_lc", [128, 2], F32, st)
        p.dma("sp", lc[:], lamc, (), [b_f])
        p.ts("dve", lam[:, 2:3], lam[:, 2:3], lc[:, 0:1], None, ALU.add, None, [b_f], [b_f])
        p.ts("dve", dng[:], dng[:], lc[:, 1:2], None, ALU.mult, None, [b_f], [b_f])
    p.barrier()
    st.close()

    stA = ExitStack()
    ALIAS = (J == 16)
    x1flat = x1[:].rearrange("p a b -> p (a b)")
    if ALIAS:
        kt = [x1flat[:, i * 4096:(i + 1) * 4096].bitcast(BF16) for i in range(2)]
        vt = [x1flat[:, 8192:8192 + 4160].bitcast(BF16).rearrange("p (a b c) -> p a b c", a=NB, b=2, c=65),
              p.sb("a_v1", [128, NB, 2, 65], BF16, stA)]
    else:
        kt = [p.sb("a_kt%d" % i, [128, L], BF16, stA) for i in range(2)]
        vt = [p.sb("a_v%d" % i, [128, NB, 2, 65], BF16, stA) for i in range(2)]
    qt = [p.sb("a_q%d" % i, [128, 2, T], BF16, stA) for i in range(2)]
    bti = [p.sb("a_bt%d" % i, [128, 2, 8, 128], F32, stA) for i in range(2)]
    b_kv = _bufs(2)
    pT = p.sb("a_pT", [128, 4, 512], BF16, stA)
    b_pT = _bufs(4)
    ssb = p.sb("a_ssb", [128, 2, 512], F32, stA)
    b_ssb = _bufs(2)
    ysb = p.sb("a_ysb", [128, 2, 128], F32, stA)
    b_ysb = _bufs(2)
    yd = p.sb("a_yd", [128, 2, 66], F32, stA)
    b_yd = _bufs(2)
    rec = p.sb("a_rec", [128, 8], F32, stA)
    b_rec = Buf()
    for i in range(2):
        p.memset("pool", vt[i][:, :, :, 64:65], 1.0, [b_kv[i]])
    rot = {"s": 0, "pt": 0, "ss": 0, "o": 0}
    SB = [0, 1, 2]
    OB = [3, 4, 5, 6]
    TB = 7

    def load_pair(slot, kind, pr):
        kch = kind * 2 + pr
        for c in range(4):
            for ap_, off, n in KT_all.rows(c, kch * 128, 128):
                dst = kt[slot][off:off + n, :].rearrange("r (j c t) -> r j c t", c=4, t=128)[:, :, c, :]
                p.dma("sp", dst, ap_.rearrange("r (j t) -> r j t", t=128), (), [b_kv[slot]])
            for hh in range(2):
                c0 = kind * 256 + pr * 128 + hh * 64
                for j_ in range(J):
                    for ap_, off, n in V_all.rows(c, j_ * 128, 128):
                        p.dma("sp", vt[slot][off:off + n, 4 * j_ + c, hh, 0:64], ap_[:, c0:c0 + 64], (), [b_kv[slot]])
        if kind == 0:
            p.dma("sp", qt[slot][:, 0, :], QT_loc[pr], (), [b_kv[slot]])
            for hh in range(2):
                h = pr * 2 + hh
                vsh = vs[:].rearrange("p j c h -> p (j c) h")[:, :, h:h + 1]
                p.tt("pool", vt[slot][:, :, hh, 0:64], vt[slot][:, :, hh, 0:64], vsh.to_broadcast([128, NB, 64]), ALU.mult,
                     [b_f, b_kv[slot]], [b_kv[slot]])
                p.copy("pool", vt[slot][:, :, hh, 64:65], vsh, [b_f, b_kv[slot]], [b_kv[slot]])
        elif kind == 1:
            p.dma("sp", qt[slot][:, 0, :], QT_loc[2 + pr], (), [b_kv[slot]])
            p.dma("sp", qt[slot][:, 1, :], QT_loc[4 + pr], (), [b_kv[slot]])
        else:
            p.dma("sp", qt[slot][:, 0, :], QT_loc[6 + pr], (), [b_kv[slot]])
        if kind > 0:
            hb = (kind - 1) * 4 + pr * 2
            p.dma("sp", bti[slot][:], W["biasT"][hb:hb + 2].rearrange("h t k q -> k h t q"), (), [b_kv[slot]])
            for hh in range(2):
                p.tt("pool", bti[slot][:, hh, 0:4, :], bti[slot][:, hh, 0:4, :], maskT[:], ALU.add, [b_kv[slot], b_c], [b_kv[slot]])
            p.memset("pool", vt[slot][:, :, :, 64:65], 1.0, [b_kv[slot]])

    def run_maps(slot, kind, pr, j, ngT=None, b_ng=None):
        nrow = j + 1
        if kind == 1:
            maps = [(hh, m) for hh in range(2) for m in range(2)]
        else:
            maps = [(hh, 0) for hh in range(2)]
        scale = (32 ** -0.5) if kind == 1 else 0.125
        obank = {}
        items = [(mi, r) for mi in range(len(maps)) for r in range(nrow)]
        pend = {}

        def qk(it):
            mi, r = it
            hh, m = maps[mi]
            sbk = SB[rot["s"] % 3]
            rot["s"] += 1
            rows = slice(hh * 64, hh * 64 + 64)
            for c in range(4):
                blk = 4 * r + c
                p.mm(g.banks[sbk][:, c * 128:(c + 1) * 128], kt[slot][rows, blk * 128:(blk + 1) * 128],
                     qt[slot][rows, m, j * 128:(j + 1) * 128], True, True, [b_kv[slot]], [g.bb[sbk]])
            ps_ = rot["pt"] % 4
            rot["pt"] += 1
            h = pr * 2 + hh
            if kind == 0:
                bias = bjr[:, j, r, h:h + 1]
                near = (r == j)
                bread = [b_f]
            else:
                gh = (kind - 1) * 4 + h
                bias = b31[:, gh:gh + 1]
                near = (r >= j - 1)
                bread = [b_c]
            if near:
                ss = rot["ss"] % 2
                rot["ss"] += 1
                for c in range(4):
                    if kind == 0:
                        tile_ = maskT[:, c, :]
                        rd = [b_c]
                    else:
                        tile_ = bti[slot][:, hh, (0 if r == j else 4) + c, :]
                        rd = [b_kv[slot]]
                    p.stt("dve", ssb[:, ss, c * 128:(c + 1) * 128], g.banks[sbk][:, c * 128:(c + 1) * 128], scale, tile_,
                          ALU.mult, ALU.add, [g.bb[sbk]] + rd, [b_ssb[ss]])
                if kind == 0:
                    p.act(pT[:, ps_, :], ssb[:, ss, :], AF.Exp, [b_ssb[ss]] + bread, [b_pT[ps_]], bias=bias)
                else:
                    p.act(pT[:, ps_, :], ssb[:, ss, :], AF.Exp, [b_ssb[ss]], [b_pT[ps_]])
            else:
                p.act(pT[:, ps_, :], g.banks[sbk][:, :], AF.Exp, [g.bb[sbk]] + bread, [b_pT[ps_]], bias=bias, scale=scale)
            if kind == 2:
                p.tt("pool", pT[:, ps_, :], pT[:, ps_, :], ngT[:, 4 * r:4 * r + 4, :].rearrange("k c q -> k (c q)"), ALU.mult,
                     [b_pT[ps_], b_ng], [b_pT[ps_]])
            pend[it] = ps_

        def pv(it):
            mi, r = it
            hh, m = maps[mi]
            ps_ = pend.pop(it)
            if r == 0:
                obank[mi] = OB[rot["o"] % 4]
                rot["o"] += 1
            ob = obank[mi]
            for c in range(4):
                blk = 4 * r + c
                p.mm(g.banks[ob][:, 0:65], pT[:, ps_, c * 128:(c + 1) * 128], vt[slot][:, blk, hh, :],
                     r == 0 and c == 0, r == nrow - 1 and c == 3, [b_pT[ps_], b_kv[slot]], [g.bb[ob]])
            if r == nrow - 1:
                finalize(mi)

        def finalize(mi):
            hh, m = maps[mi]
            ob = obank[mi]
            ys = (j * 2 + pr) % 2
            if kind != 1:
                col = mi
                p.op("dve", lambda e: e.reciprocal(out=rec[:, col:col + 1], in_=g.banks[ob][:, 64:65]), [g.bb[ob]], [b_rec])
                p.ts("dve", ysb[:, ys, hh * 64:(hh + 1) * 64], g.banks[ob][:, 0:64], rec[:, col:col + 1], None, ALU.mult, None,
                     [g.bb[ob], b_rec], [b_ysb[ys]])
            else:
                col = mi
                p.op("dve", lambda e: e.reciprocal(out=rec[:, col:col + 1], in_=g.banks[ob][:, 64:65]), [g.bb[ob]], [b_rec])
                if m == 0:
                    p.ts("dve", yd[:, hh, 0:64], g.banks[ob][:, 0:64], rec[:, col:col + 1], None, ALU.mult, None,
                         [g.bb[ob], b_rec], [b_yd[hh]])
                else:
                    p.ts("dve", ssb[:, 0, 0:64], g.banks[ob][:, 0:64], rec[:, col:col + 1], lam[:, 2:3], ALU.mult, ALU.mult,
                         [g.bb[ob], b_rec, b_f], [b_ssb[0]])
                    p.tt("dve", yd[:, hh, 0:64], yd[:, hh, 0:64], ssb[:, 0, 0:64], ALU.add, [b_ssb[0], b_yd[hh]], [b_yd[hh]])
                    p.tt("dve", ssb[:, 0, 64:128], yd[:, hh, 0:64], yd[:, hh, 0:64], ALU.mult, [b_yd[hh]], [b_ssb[0]])
                    p.memset("dve", yd[:, hh, 64:65], 0.0, [b_yd[hh]])
                    p.ts("dve", ssb[:, 0, 64:128], ssb[:, 0, 64:128], 1.0 / 64, 0.0, ALU.mult, ALU.add, [b_ssb[0]], [b_ssb[0], b_yd[hh]],
                         accum=yd[:, hh, 64:65])
                    p.act(yd[:, hh, 64:65], yd[:, hh, 64:65], AF.Ln, [b_yd[hh]], [b_yd[hh]], bias=EPS)
                    p.act(yd[:, hh, 64:65], yd[:, hh, 64:65], AF.Exp, [b_yd[hh]], [b_yd[hh]], scale=-0.5)
                    p.stt("dve", ysb[:, ys, hh * 64:(hh + 1) * 64], yd[:, hh, 0:64], yd[:, hh, 64:65], dng[:], ALU.mult, ALU.mult,
                          [b_yd[hh], b_f], [b_ysb[ys]])
            last = (mi == len(maps) - 1)
            if dbg is not None and kind == 1 and pr == 0 and j == 0 and mi == 1:
                def dd(name, ap, shape, bufs):
                    t_ = p.nc.dram_tensor("dbg_" + name, shape, F32, kind="ExternalOutput").ap()
                    p.dma("sp", t_, ap, bufs, ())
                dd("yd", yd[:], [128, 2, 66], [b_yd[0], b_yd[1]])
                dd("rec", rec[:], [128, 8], [b_rec])
                dd("bti", bti[slot][:, 0, 0, :], [128, 128], [b_kv[slot]])
                dd("lam", lam[:], [128, 4], [b_f])
                dd("dng", dng[:], [128, 64], [b_f])
                dd("ssb", ssb[:], [128, 2, 512], [b_ssb[0], b_ssb[1]])
                dd("ysb", ysb[:], [128, 2, 128], [b_ysb[0], b_ysb[1]])
            if last:
                p.tr(g.banks[TB][:, 0:128], ysb[:, ys, :], g.identf[:], [b_ysb[ys], g.b_ident], [g.bb[TB]])
                p.copy("act", yT[:, kind * 2 + pr, j * 128:(j + 1) * 128], g.banks[TB][:, 0:128], [g.bb[TB]], [b_yT])

        n = len(items)
        for i in range(n + 1):
            if i < n:
                qk(items[i])
            if i > 0:
                pv(items[i - 1])

    seq = [(0, 0), (0, 1), (1, 0), (1, 1)]
    load_pair(0, *seq[0])
    for i, (kind, pr) in enumerate(seq):
        if i + 1 < len(seq):
            load_pair((i + 1) % 2, *seq[i + 1])
        for j in range(J):
            run_maps(i % 2, kind, pr, j)
    p.barrier()

    st = ExitStack()
    if ALIAS:
        sc = x1flat[:, 0:8192]
        ik2 = x1flat[:, 8192:12288].bitcast(BF16)
        iq = x1flat[:, 12288:14336].bitcast(BF16).rearrange("p (a b) -> p a b", a=2)
    else:
        sc = p.sb("i_sc", [128, L], F32, st)
        ik2 = p.sb("i_ik2", [128, L], BF16, st)
        iq = p.sb("i_iq", [128, 2, T], BF16, st)
    b_sc = Buf()
    iwt = p.sb("i_iw", [128, J, 4], F32, st)
    mq = p.sb("i_mq", [128, 4, 128], F32, st)
    mqa = p.sb("i_mqa", [128, 4, 128], F32, st)
    b_i = Buf()
    if ALIAS:
        rl = x1flat[:, 14336:16384].rearrange("p (a b) -> p a b", a=4)
    else:
        rl = p.sb("i_rl", [128, 4, 512], F32, st)
    b_rl = _bufs(4)
    m8 = p.sb("i_m8", [128, 8], F32, st)
    b_m8 = Buf()
    nev = p.sb("i_nev", [128, 2, 512], BF16, st)
    b_nev = _bufs(2)
    for c in range(4):
        for ap_, off, n in KT_all.rows(c, 768, 64):
            for half in range(2):
                dst = ik2[half * 64 + off:half * 64 + off + n, :].rearrange("r (j c t) -> r j c t", c=4, t=128)[:, :, c, :]
                p.dma("sp", dst, ap_.rearrange("r (j t) -> r j t", t=128), (), [b_i])
    p.dma("sp", iq[:, 0, :], QT_loc[8], (), [b_i])
    p.dma("sp", iq[:, 1, :], QT_loc[9], (), [b_i])
    p.dma("sp", iwt[:].rearrange("p j h -> p (j h)"), IW_loc, (), [b_i])
    p.dma("sp", mq[:], consts["maskQ"].rearrange("c q k -> q c k"), (), [b_i])
    p.ts("dve", mqa[:], mq[:], 1.0e30, None, ALU.mult, None, [b_i], [b_i])
    for j in range(J):
        Nk = 512 * (j + 1)
        for r in range(j + 1):
            ks = slice(r * 512, (r + 1) * 512)
            for hi in range(4):
                rows = slice((hi % 2) * 64, (hi % 2) * 64 + 64)
                bk = hi
                p.mm(g.banks[bk][:, :], iq[rows, hi // 2, j * 128:(j + 1) * 128], ik2[rows, ks], True, True, [b_i], [g.bb[bk]])
                p.act(rl[:, hi, :], g.banks[bk][:, :], AF.Relu, [g.bb[bk]], [b_rl[hi]])
                if hi == 0:
                    p.ts("dve", sc[:, ks], rl[:, 0, :], iwt[:, j, 0:1], None, ALU.mult, None, [b_rl[0], b_i], [b_sc])
                else:
                    p.stt("dve", sc[:, ks], rl[:, hi, :], iwt[:, j, hi:hi + 1], sc[:, ks], ALU.mult, ALU.add, [b_rl[hi], b_i, b_sc], [b_sc])
            if r == j:
                p.tt("dve", sc[:, ks], sc[:, ks], mqa[:].rearrange("q c k -> q (c k)"), ALU.add, [b_sc, b_i], [b_sc])
        for it in range(32):
            p.op("dve", lambda e: e.max(out=m8[:], in_=sc[:, 0:Nk]), [b_sc], [b_m8])
            p.op("dve", lambda e: e.match_replace(out=sc[:, 0:Nk], in_to_replace=m8[:], in_values=sc[:, 0:Nk], imm_value=-3.0e38),
                 [b_sc, b_m8], [b_sc])
        p.ts("dve", sc[:, 0:Nk], sc[:, 0:Nk], -1.0e35, 1.0, ALU.is_le, ALU.subtract, [b_sc], [b_sc])
        ks = slice(j * 512, (j + 1) * 512)
        p.tt("dve", sc[:, ks], sc[:, ks], mq[:].rearrange("q c k -> q (c k)"), ALU.min, [b_sc, b_i], [b_sc])
        if dbg is not None and j == 0:
            t_ = p.nc.dram_tensor("dbg_sc", [128, 512], F32, kind="ExternalOutput").ap()
            p.dma("sp", t_, sc[:, 0:512], [b_sc], ())
        for r in range(j + 1):
            bk = 4 + (r % 2)
            for c in range(4):
                blk = 4 * r + c
                p.tr(g.banks[bk][:, c * 128:(c + 1) * 128], sc[:, blk * 128:(blk + 1) * 128], g.identf[:], [b_sc, g.b_ident], [g.bb[bk]])
            s = r % 2
            p.act(nev[:, s, :], g.banks[bk][:, :], AF.Identity, [g.bb[bk]], [b_nev[s]], bias=1.0)
            p.dma("sp", NEGT[j, :, 4 * r:4 * r + 4, :], nev[:, s, :].rearrange("k (c q) -> k c q", c=4), [b_nev[s]], ())
    p.barrier()
    st.close()

    st = ExitStack()
    ng = [p.sb("s_ng%d" % i, [128, NB, 128], BF16, st) for i in range(2)]
    b_ngs = _bufs(2)
    load_pair(0, 2, 0)
    load_pair(1, 2, 1)
    p.dma("sp", ng[0][:, 0:4, :], NEGT[0, :, 0:4, :], (), [b_ngs[0]])
    for j in range(J):
        if j + 1 < J:
            p.dma("sp", ng[(j + 1) % 2][:, 0:4 * (j + 2), :], NEGT[j + 1, :, 0:4 * (j + 2), :], (), [b_ngs[(j + 1) % 2]])
        for pr in range(2):
            run_maps(pr, 2, pr, j, ngT=ng[j % 2], b_ng=b_ngs[j % 2])
    p.barrier()
    st.close()
    stA.close()

    if dbg is not None:
        p.dma("sp", dbg, yT[:], [b_yT], ())
    st = ExitStack()
    wo = p.sb("o_wo", [128, 8, D], BF16, st)
    b_wo = Buf()
    p.dma("pool", wo[:], W["w_out"].rearrange("(fc f) m -> f fc m", f=128), (), [b_wo])
    gt = p.sb("o_g", [128, D], F32, st)
    bt = p.sb("o_b", [128, D], F32, st)
    b_gb = Buf()
    p.dma("sp", gt[:], W["ln1_g"].partition_broadcast(128), (), [b_gb])
    p.dma("sp", bt[:], W["ln1_b"].partition_broadcast(128), (), [b_gb])
    xin = [p.sb("o_x%d" % i, [128, D], F32, st) for i in range(2)]
    b_xin = _bufs(2)
    tmp = p.sb("o_tmp", [128, D], F32, st)
    b_tmp = Buf()
    st4 = p.sb("o_st4", [128, 4], F32, st)
    b_st4 = Buf()
    Xr = X.rearrange("(j q) d -> q j d", q=128)
    for j in range(J):
        s = j % 2
        p.dma("sp", xin[s][:], Xr[:, j, :], (), [b_xin[s]])
        for mh in range(2):
            bk = mh
            for fc in range(8):
                p.mm(g.banks[bk][:, :], yT[:, fc, j * 128:(j + 1) * 128], wo[:, fc, mh * 512:(mh + 1) * 512], fc == 0, fc == 7,
                     [b_yT, b_wo], [g.bb[bk]])
            p.stt("dve", xin[s][:, mh * 512:(mh + 1) * 512], xin[s][:, mh * 512:(mh + 1) * 512], ALPHA, g.banks[bk][:, :],
                  ALU.mult, ALU.add, [b_xin[s], g.bb[bk]], [b_xin[s]])
        ln_rows(p, g, xin[s][:], b_xin[s], x1[:, j, :], b_x1, gt[:], bt[:], b_gb, tmp[:], b_tmp, st4, b_st4)
    p.barrier()
    st.close()
    stB.close()

def emit_C(p, g, W, x1, b_x1, Xout):
    J, T = g.J, g.T
    st = ExitStack()
    xT = p.sb("m_xT", [128, 8, T], BF16, st)
    b_xT = Buf()
    xTf = p.sb("m_xTf", [128, 8, 128], F32, st)
    b_xTf = Buf()
    rw = p.sb("m_rw", [128, 8, NE], F32, st)
    b_w = Buf()
    p.dma("sp", rw[:], W["router_w"].rearrange("(dc q) e -> q dc e", q=128), (), [b_w])
    rb = p.sb("m_rb", [128, NE], F32, st)
    p.dma("sp", rb[:], W["router_b"].partition_broadcast(128), (), [b_w])
    bgu = p.sb("m_bgu", [128, 16, NE], F32, st)
    bd = p.sb("m_bd", [NE, D], F32, st)
    gates = p.sb("m_gates", [128, J, NE], F32, st)
    gT = p.sb("m_gT", [NE, 128], F32, st)
    lg = p.sb("m_lg", [128, NE], F32, st)
    m8 = p.sb("m_m8", [128, 8], F32, st)
    sm = p.sb("m_sm", [128, 4], F32, st)
    st0 = ExitStack()
    bgr = p.sb("m_bgr", [NE, 2 * D], F32, st0)
    p.dma("sp", bgr[:], W["b_gu"], (), [b_w])
    bgv = bgr[:].rearrange("e (jc q t) -> e jc t q", q=128, t=2)
    for jc in range(8):
        for t in range(2):
            i = jc * 2 + t
            p.tr(g.banks[7][:, i * NE:(i + 1) * NE], bgv[:, jc, t, :], g.identf[0:NE, 0:NE], [b_w, g.b_ident], [g.bb[7]])
    p.copy("dve", bgu[:], g.banks[7][:, :].rearrange("q (i e) -> q i e", e=NE), [g.bb[7]], [b_w])
    p.barrier()
    st0.close()
    p.dma("sp", bd[:], W["b_down"], (), [b_w])
    b_gates = Buf()
    b_gT = Buf()
    b_r = Buf()
    for j in range(J):
        transpose_block(p, g, x1[:, j, :], b_x1, xT, b_xT, j, dst_f32=xTf, dst_f32_b=b_xTf, bank0=0)
        for dc in range(8):
            p.mm(g.banks[2][:, 0:NE], xTf[:, dc, :], rw[:, dc, :], dc == 0, dc == 7, [b_xTf, b_w], [g.bb[2]])
        p.tt("dve", lg[:], g.banks[2][:, 0:NE], rb[:], ALU.add, [g.bb[2], b_w], [b_r])
        p.op("dve", lambda e: e.max(out=m8[:], in_=lg[:]), [b_r], [b_r])
        p.ts("dve", sm[:, 0:1], m8[:, 0:1], -1.0, None, ALU.mult, None, [b_r], [b_r])
        p.act(gates[:, j, :], lg[:], AF.Exp, [b_r], [b_gates], bias=sm[:, 0:1])
        p.ts("dve", lg[:], lg[:], m8[:, 3:4], None, ALU.is_ge, None, [b_r], [b_r])
        p.tt("dve", gates[:, j, :], gates[:, j, :], lg[:], ALU.mult, [b_r, b_gates], [b_gates])
        p.memset("dve", sm[:, 1:2], 0.0, [b_r])
        p.ts("dve", lg[:], gates[:, j, :], 1.0, 0.0, ALU.mult, ALU.add, [b_gates], [b_r], accum=sm[:, 1:2])
        p.op("dve", lambda e: e.reciprocal(out=sm[:, 2:3], in_=sm[:, 1:2]), [b_r], [b_r])
        p.ts("dve", gates[:, j, :], gates[:, j, :], sm[:, 2:3], None, ALU.mult, None, [b_r, b_gates], [b_gates])
        p.tr(g.banks[3][0:NE, 0:128], gates[:, j, :], g.identf[:], [b_gates, g.b_ident], [g.bb[3]])
        p.copy("dve", gT[:], g.banks[3][0:NE, 0:128], [g.bb[3]], [b_gT])
        for mh in range(2):
            p.mm(g.banks[4 + mh][:, :], gT[:], bd[:, mh * 512:(mh + 1) * 512], True, True, [b_gT, b_w], [g.bb[4 + mh]])
            p.stt("dve", x1[:, j, mh * 512:(mh + 1) * 512], x1[:, j, mh * 512:(mh + 1) * 512], ALPHA, g.banks[4 + mh][:, :],
                  ALU.mult, ALU.add, [b_x1, g.bb[4 + mh]], [b_x1])
    NW = 4
    st2 = ExitStack()
    st_save, st = st, st2
    wgu = [p.sb("m_wgu%d" % i, [128, 8, 256], BF16, st) for i in range(NW)]
    b_wgu = _bufs(NW)
    wd = [p.sb("m_wd%d" % i, [128, 8, D], BF16, st) for i in range(2)]
    b_wd = _bufs(2)
    hT = p.sb("m_hT", [128, 8, T], BF16, st)
    b_hT = _bufs(8)
    TG = min(512, T)
    ntg = T // TG
    gc = p.sb("m_gc", [128, 2, TG], F32, st)
    sg = p.sb("m_sg", [128, 2, TG], F32, st)
    uc = p.sb("m_uc", [128, 2, TG], F32, st)
    b_gc, b_sg, b_uc = _bufs(2), _bufs(2), _bufs(2)
    b_acc = _bufs(J)
    wq = {"n": 0}

    def load_gu(e, jc):
        s = wq["n"] % NW
        wq["n"] += 1
        src = W["w_gu"][e].rearrange("(dc q) f -> q dc f", q=128)[:, :, jc * 256:(jc + 1) * 256]
        p.dma("pool", wgu[s][:], src, (), [b_wgu[s]])
        return s

    def load_d(e):
        s = e % 2
        p.dma("pool", wd[s][:], W["w_down"][e].rearrange("(jc q) m -> q jc m", q=128), (), [b_wd[s]])

    pre = []
    for jc in range(2):
        pre.append(load_gu(0, jc))
    load_d(0)
    k = 0
    for e in range(NE):
        for jc in range(8):
            nxt = e * 8 + jc + 2
            if nxt < NE * 8:
                pre.append(load_gu(nxt // 8, nxt % 8))
            if jc == 4 and e + 1 < NE:
                load_d(e + 1)
            s = pre.pop(0)
            wv = wgu[s][:].rearrange("q dc (j t) -> q dc t j", t=2)
            for tg in range(ntg):
                kk = k % 2
                k += 1
                tsl = slice(tg * TG, (tg + 1) * TG)
                for t, bk in ((0, kk), (1, 2 + kk)):
                    for dc in range(8):
                        p.mm(g.banks[bk][:, 0:TG], wv[:, dc, t, :], xT[:, dc, tsl], dc == 0, dc == 7, [b_wgu[s], b_xT], [g.bb[bk]])
                p.ts("dve", gc[:, kk, :], g.banks[kk][:, 0:TG], bgu[:, jc * 2, e:e + 1], 7.0, ALU.add, ALU.min, [g.bb[kk], b_w], [b_gc[kk]])
                p.act(sg[:, kk, :], gc[:, kk, :], AF.Sigmoid, [b_gc[kk]], [b_sg[kk]], scale=1.702)
                p.act(uc[:, kk, :], g.banks[2 + kk][:, 0:TG], AF.Identity, [g.bb[2 + kk], b_w], [b_uc[kk]], bias=bgu[:, jc * 2 + 1, e:e + 1])
                p.ts("dve", uc[:, kk, :], uc[:, kk, :], 7.0, -7.0, ALU.min, ALU.max, [b_uc[kk]], [b_uc[kk]])
                p.tt("dve", gc[:, kk, :], gc[:, kk, :], sg[:, kk, :], ALU.mult, [b_gc[kk], b_sg[kk]], [b_gc[kk]])
                p.stt("dve", hT[:, jc, tsl], uc[:, kk, :], 1.0, gc[:, kk, :], ALU.add, ALU.mult, [b_uc[kk], b_gc[kk]], [b_hT[jc]])
        s = e % 2
        for j in range(J):
            for mh in range(2):
                bk = 4 + (j * 2 + mh) % 4
                for jc in range(8):
                    p.mm(g.banks[bk][:, :], hT[:, jc, j * 128:(j + 1) * 128], wd[s][:, jc, mh * 512:(mh + 1) * 512], jc == 0, jc == 7,
                         [b_hT[jc], b_wd[s]], [g.bb[bk]])
                p.stt("dve", x1[:, j, mh * 512:(mh + 1) * 512], g.banks[bk][:, :], gates[:, j, e:e + 1], x1[:, j, mh * 512:(mh + 1) * 512],
                      ALU.mult, ALU.add, [g.bb[bk], b_gates, b_acc[j], b_x1], [b_acc[j]])
    p.barrier()
    st2.close()
    st = st_save
    gt = p.sb("m_g", [128, D], F32, st)
    bt = p.sb("m_b", [128, D], F32, st)
    b_gb = Buf()
    p.dma("sp", gt[:], W["ln2_g"].partition_broadcast(128), (), [b_gb])
    p.dma("sp", bt[:], W["ln2_b"].partition_broadcast(128), (), [b_gb])
    tmp = p.sb("m_tmp", [128, D], F32, st)
    b_tmp = Buf()
    st4 = p.sb("m_st4", [128, 4], F32, st)
    b_st4 = Buf()
    Xr = Xout.rearrange("(j q) d -> q j d", q=128)
    for j in range(J):
        ln_rows(p, g, x1[:, j, :], b_acc[j], x1[:, j, :], b_acc[j], gt[:], bt[:], b_gb, tmp[:], b_tmp, st4, b_st4)
        p.dma("sp", Xr[:, j, :], x1[:, j, :], [b_acc[j]], ())
    p.barrier()
    st.close()


def _bucket(d):
    n = np.maximum(d, 0)
    nf = np.maximum(n, 1).astype(np.float32)
    large = 16 + (np.log(nf / np.float32(16)) / np.float32(math.log(8.0)) * np.float32(16)).astype(np.int32)
    large = np.minimum(large, 31)
    return np.where(n < 16, n, large)


def make_consts(c_me):
    q = np.arange(128)
    cs = {}
    cs["identf"] = np.eye(128, dtype=np.float32)
    dqm = np.zeros((128, 2), np.float32)
    dqm[:, 0] = ((q % 64) < 32)
    dqm[:, 1] = 1.0 - dqm[:, 0]
    cs["dqmask"] = dqm
    sel = np.zeros((128, 4), np.float32)
    sel[:, c_me] = 1.0
    cs["sel"] = sel
    cs["triu"] = np.triu(np.ones((128, 128), np.float32))
    mT = np.zeros((4, 128, 128), np.float32)
    for c in range(4):
        dl = c_me - c
        if dl < 0:
            mT[c] = 1.0
        elif dl == 0:
            mT[c] = (q[:, None] > q[None, :]).astype(np.float32)
    cs["maskT"] = mT * np.float32(NEG)
    cs["maskQ"] = -np.ascontiguousarray(mT.transpose(0, 2, 1))
    return cs


def bias_tile_index(c_me):
    q = np.arange(128)
    idx = np.zeros((8, 128, 128), np.int64)
    for t in range(8):
        dl = (c_me - t) if t < 4 else (4 + c_me - (t - 4))
        d = 128 * dl + q[None, :] - q[:, None]
        idx[t] = _bucket(d)
    return idx


CONST_SHAPES = {"identf": [128, 128], "dqmask": [128, 2], "sel": [128, 4], "triu": [128, 128],
                "maskT": [4, 128, 128], "maskQ": [4, 128, 128]}


def declare_consts(nc):
    return {k: nc.dram_tensor("c_" + k, shp, F32, kind="ExternalInput").ap() for k, shp in CONST_SHAPES.items()}


WSHAPES = {"rel31": [8], "biasT": [8, 8, 128, 128], "diff_lambda": [4, 32], "diff_norm_g": [64], "conv_w": [31, 256],
           "conv_b": [256], "conv_ln_g": [256], "conv_ln_b": [256], "w_out": [D, D], "ln1_g": [D], "ln1_b": [D],
           "router_w": [D, NE], "router_b": [NE], "w_gu": [NE, D, 2 * D], "b_gu": [NE, 2 * D], "w_down": [NE, D, D],
           "b_down": [NE, D], "ln2_g": [D], "ln2_b": [D]}


def build_A(J):
    nc = bass.Bass("TRN2", target_bir_lowering=False)
    T = J * 128
    X = nc.dram_tensor("X", [T, D], F32, kind="ExternalInput").ap()
    w_in = nc.dram_tensor("w_in", [D, NIN], F32, kind="ExternalInput").ap()
    forget_b = nc.dram_tensor("forget_b", [4], F32, kind="ExternalInput").ap()
    consts = declare_consts(nc)
    KT_loc = nc.dram_tensor("KT_loc", [KT_ROWS, T], BF16, kind="ExternalOutput").ap()
    V_loc = nc.dram_tensor("V_loc", [T, 768], BF16, kind="ExternalOutput").ap()
    HT_loc = nc.dram_tensor("HT_loc", [256, T], F32, kind="ExternalOutput").ap()
    LF_loc = nc.dram_tensor("LF_loc", [J * 4, 128], F32, kind="ExternalOutput").ap()
    QT_loc = nc.dram_tensor("QT_loc", [NQCH, 128, T], BF16, kind="ExternalOutput").ap()
    IW_loc = nc.dram_tensor("IW_loc", [128, J * 4], F32, kind="ExternalOutput").ap()
    p = Prog(nc)
    g = setup_common(p, J, consts)
    emit_A(p, g, X, w_in, forget_b, consts, KT_loc, V_loc, HT_loc, LF_loc, QT_loc, IW_loc)
    p.finish()
    return nc


def build_BC(J, do_C=True, debug=False):
    nc = bass.Bass("TRN2", target_bir_lowering=False)
    T = J * 128
    NB = 4 * J
    X = nc.dram_tensor("X", [T, D], F32, kind="ExternalInput").ap()
    KT_all = nc.dram_tensor("KT_all", [4 * KT_ROWS, T], BF16, kind="ExternalInput").ap()
    V_all = nc.dram_tensor("V_all", [4 * T, 768], BF16, kind="ExternalInput").ap()
    HT_all = nc.dram_tensor("HT_all", [4 * 256, T], F32, kind="ExternalInput").ap()
    LF_all = nc.dram_tensor("LF_all", [4 * J * 4, 128], F32, kind="ExternalInput").ap()
    QT_loc = nc.dram_tensor("QT_loc", [NQCH, 128, T], BF16, kind="ExternalInput").ap()
    IW_loc = nc.dram_tensor("IW_loc", [128, J * 4], F32, kind="ExternalInput").ap()
    HT_loc = nc.dram_tensor("HT_loc", [256, T], F32, kind="ExternalInput").ap()
    lamc = nc.dram_tensor("lamc", [128, 2], F32, kind="ExternalInput").ap()
    W = {k: nc.dram_tensor(k, shp, F32, kind="ExternalInput").ap() for k, shp in WSHAPES.items()}
    consts = declare_consts(nc)
    NEGT = nc.dram_tensor("NEGT", [J, 128, NB, 128], BF16, kind="Internal").ap()
    Xout = nc.dram_tensor("Xout", [T, D], F32, kind="ExternalOutput").ap()
    p = Prog(nc)
    g = setup_common(p, J, consts)
    x1 = p.sb("x1", [128, J, D], F32)
    b_x1 = Buf()
    dbg = nc.dram_tensor("dbg_yT", [128, 8, T], BF16, kind="ExternalOutput").ap() if debug else None
    emit_B(p, g, 0, X, Gat(KT_ROWS, mono=KT_all), Gat(T, mono=V_all), Gat(256, mono=HT_all), LF_all, QT_loc, IW_loc, HT_loc, NEGT, W,
           consts, x1, b_x1, lamc=lamc, dbg=dbg)
    if do_C:
        emit_C(p, g, W, x1, b_x1, Xout)
    else:
        p.dma("sp", Xout.rearrange("(j q) d -> q j d", q=128), x1[:], [b_x1], ())
    p.finish()
    return nc


def build_fused(J, nlayers=DEPTH):
    nc = bass.Bass("TRN2", target_bir_lowering=False)
    T = J * 128
    NB = 4 * J
    Xin = nc.dram_tensor("X", [T, D], F32, kind="ExternalInput").ap()
    w_in = nc.dram_tensor("w_in", [nlayers, D, NIN], F32, kind="ExternalInput").ap()
    forget_b = nc.dram_tensor("forget_b", [nlayers, 4], F32, kind="ExternalInput").ap()
    Wf = {}
    for k, shp in WSHAPES.items():
        if k in ("rel31", "biasT"):
            Wf[k] = nc.dram_tensor(k, shp, F32, kind="ExternalInput").ap()
        else:
            Wf[k] = nc.dram_tensor(k, [nlayers] + shp, F32, kind="ExternalInput").ap()
    consts = declare_consts(nc)
    Xout = nc.dram_tensor("Xout", [T, D], F32, kind="ExternalOutput").ap()
    Xbuf = nc.dram_tensor("Xbuf", [T, D], F32, kind="Internal").ap()
    KT_loc = nc.dram_tensor("KT_loc", [KT_ROWS, T], BF16, kind="Internal").ap()
    V_loc = nc.dram_tensor("V_loc", [T, 768], BF16, kind="Internal").ap()
    HT_loc = nc.dram_tensor("HT_loc", [256, T], F32, kind="Internal").ap()
    LF_loc = nc.dram_tensor("LF_loc", [J * 4, 128], F32, kind="Internal").ap()
    QT_loc = nc.dram_tensor("QT_loc", [NQCH, 128, T], BF16, kind="Internal").ap()
    IW_loc = nc.dram_tensor("IW_loc", [128, J * 4], F32, kind="Internal").ap()
    SK, SV, SH = 104, 256, 32
    def mk(name, R, S, W_, dt):
        return [Gat(R, S=S, chunks=[nc.dram_tensor("%s%d_%d" % (name, i, k), [4 * S, W_], dt, kind="Internal").ap()
                                    for k in range(R // S)]) for i in range(2)]
    KT_alls = mk("KTa", KT_ROWS, SK, T, BF16)
    V_alls = mk("Va", T, SV, 768, BF16)
    HT_alls = mk("HTa", 256, SH, T, F32)
    LF_alls = [nc.dram_tensor("LF_all%d" % i, [4 * J * 4, 128], F32, kind="Internal").ap() for i in range(2)]
    NEGT = nc.dram_tensor("NEGT", [J, 128, NB, 128], BF16, kind="Internal").ap()
    p = Prog(nc)
    g = setup_common(p, J, consts)
    x1 = p.sb("x1", [128, J, D], F32)
    b_x1 = Buf()
    groups = [[0, 1, 2, 3], [4, 5, 6, 7]]
    for l in range(nlayers):
        Xsrc = Xin if l == 0 else Xbuf
        Xdst = Xout if l == nlayers - 1 else Xbuf
        KT_all, V_all, HT_all, LF_all = KT_alls[l % 2], V_alls[l % 2], HT_alls[l % 2], LF_alls[l % 2]
        emit_A(p, g, Xsrc, w_in[l], forget_b[l], consts, KT_loc, V_loc, HT_loc, LF_loc, QT_loc, IW_loc)
        for src, gat in ((KT_loc, KT_all), (V_loc, V_all), (HT_loc, HT_all)):
            for k, ch in enumerate(gat.chunks):
                p.collective(src[k * gat.S:(k + 1) * gat.S, :], ch, groups)
        p.collective(LF_loc, LF_all, groups)
        p.barrier()
        W = {k: (v if k in ("rel31", "biasT") else v[l]) for k, v in Wf.items()}
        emit_B(p, g, l, Xsrc, KT_all, V_all, HT_all, LF_all, QT_loc, IW_loc, HT_loc, NEGT, W, consts, x1, b_x1)
        emit_C(p, g, W, x1, b_x1, Xdst)
    p.finish()
    return nc


J_FULL = 16
FUSED = True
_CACHE = {}


def _loc(a, J):
    L = a.shape[0]
    v = a.reshape(J, 4, 128, -1)
    return [np.ascontiguousarray(v[:, c].reshape(J * 128, -1)) for c in range(4)]


def _unloc(parts, J):
    Dd = parts[0].shape[-1]
    out = np.empty((J, 4, 128, Dd), parts[0].dtype)
    for c in range(4):
        out[:, c] = parts[c].reshape(J, 128, Dd)
    return out.reshape(J * 4 * 128, Dd)


def kernel(x, rel_bias, w_in, forget_b, diff_lambda, diff_norm_g, conv_w, conv_b, conv_ln_g, conv_ln_b, w_out,
           ln1_g, ln1_b, router_w, router_b, w_gu, b_gu, w_down, b_down, ln2_g, ln2_b):
    J = J_FULL
    f32 = lambda a: np.ascontiguousarray(np.asarray(a, dtype=np.float32))
    x = f32(x)
    rel_bias = f32(rel_bias)
    P = dict(w_in=f32(w_in), forget_b=f32(forget_b), diff_lambda=f32(diff_lambda), diff_norm_g=f32(diff_norm_g), conv_w=f32(conv_w),
             conv_b=f32(conv_b), conv_ln_g=f32(conv_ln_g), conv_ln_b=f32(conv_ln_b), w_out=f32(w_out), ln1_g=f32(ln1_g), ln1_b=f32(ln1_b),
             router_w=f32(router_w), router_b=f32(router_b), w_gu=f32(w_gu), b_gu=f32(b_gu), w_down=f32(w_down), b_down=f32(b_down),
             ln2_g=f32(ln2_g), ln2_b=f32(ln2_b))
    B = x.shape[0]
    cs = [make_consts(c) for c in range(4)]
    biasT = [np.ascontiguousarray(rel_bias[bias_tile_index(c)].transpose(3, 0, 1, 2)) for c in range(4)]
    rel31 = np.ascontiguousarray(rel_bias[31])
    xs = [None] * 8
    for b in range(B):
        for c, part in enumerate(_loc(x[b], J)):
            xs[4 * b + c] = part
    cores = list(range(8))
    if FUSED:
        if "F" not in _CACHE:
            _CACHE["F"] = build_fused(J)
        in_maps = []
        for r in cores:
            m = {"X": xs[r], "w_in": P["w_in"], "forget_b": P["forget_b"], "biasT": biasT[r % 4], "rel31": rel31}
            for k in WSHAPES:
                if k not in m:
                    m[k] = P[k]
            for k, v in cs[r % 4].items():
                m["c_" + k] = v
            in_maps.append(m)
        res = run_bass_kernel_spmd(_CACHE["F"], in_maps, core_ids=cores).results
        xs = [np.asarray(res[r]["Xout"]) for r in cores]
    else:
        if "A" not in _CACHE:
            _CACHE["A"] = build_A(J)
            _CACHE["BC"] = build_BC(J)
        for l in range(DEPTH):
            li = 0.8 - 0.6 * math.exp(-0.3 * l)
            lamc = np.tile(np.array([[-li, 1.0 - li]], np.float32), (128, 1))
            in_maps = []
            for r in cores:
                m = {"X": xs[r], "w_in": P["w_in"][l], "forget_b": P["forget_b"][l]}
                for k, v in cs[r % 4].items():
                    m["c_" + k] = v
                in_maps.append(m)
            resA = run_bass_kernel_spmd(_CACHE["A"], in_maps, core_ids=cores).results
            in_maps = []
            for r in cores:
                b = r // 4
                m = {"X": xs[r], "lamc": lamc, "biasT": biasT[r % 4], "rel31": rel31}
                for nm in ("KT", "V", "HT", "LF"):
                    m[nm + "_all"] = np.concatenate([np.asarray(resA[4 * b + cc][nm + "_loc"]) for cc in range(4)], 0)
                for nm in ("QT_loc", "IW_loc", "HT_loc"):
                    m[nm] = np.asarray(resA[r][nm])
                for k in WSHAPES:
                    if k not in m:
                        m[k] = P[k][l]
                for k, v in cs[r % 4].items():
                    m["c_" + k] = v
                in_maps.append(m)
            resB = run_bass_kernel_spmd(_CACHE["BC"], in_maps, core_ids=cores).results
            xs = [np.asarray(resB[r]["Xout"]) for r in cores]
    out = np.stack([_unloc([xs[4 * b + c] for c in range(4)], J) for b in range(B)], 0)
    return out.astype(np.float32)
```
